# Optimizing a Trainium2 kernel written in Bass

```python
import math
import jax, jax.numpy as jnp
from jax import lax
import numpy as np

D_MODEL = 2048
BATCH = 1
SEQ = 8192
DEPTH = 4

GRID_W = 64
CTX_LEN = 256
N_MIXERS = 4
ROPE_BASE = 10000.0
EPS = 1e-6
NEG_INF = -1e30
BLOCK = 128
HEAD_DIM = 128

SWA_HEADS = D_MODEL // HEAD_DIM
SWA_KV_HEADS = SWA_HEADS // 4
SWA_WINDOW = BLOCK
SWA_WIDTH = SWA_HEADS * HEAD_DIM
SWA_IN = SWA_WIDTH + 2 * SWA_KV_HEADS * HEAD_DIM + SWA_WIDTH

MLA_HEADS = D_MODEL // HEAD_DIM
MLA_Q_RANK = 512
MLA_KV_RANK = 256
MLA_NOPE = 128
MLA_ROPE = 64
MLA_V = 128
MLA_WIDTH = MLA_HEADS * MLA_V
MLA_IN = MLA_Q_RANK + MLA_KV_RANK + MLA_ROPE + MLA_WIDTH

HYENA_WIDTH = D_MODEL
HYENA_ORDER = 2
HYENA_BANDS = 16
HYENA_EMB = 1 + 2 * HYENA_BANDS
HYENA_HIDDEN = 64
HYENA_CONV = 3
HYENA_DECAY_TARGET = 1e-2
HYENA_FAST_DECAY = 0.3
HYENA_SLOW_DECAY = 1.5
HYENA_IN = (HYENA_ORDER + 1) * HYENA_WIDTH + HYENA_WIDTH

DIFF_HEADS = D_MODEL // (2 * HEAD_DIM)
DIFF_WIDTH = DIFF_HEADS * 2 * HEAD_DIM
DIFF_IN = 4 * DIFF_WIDTH

kernel_name = 'hybrid_diffusion_interleaved_swa_mla_hyena_diff'


def rmsnorm(x, g):
    xf = x.astype(jnp.float32)
    y = xf * lax.rsqrt(jnp.mean(xf * xf, axis=-1, keepdims=True) + EPS)
    return (y * g.astype(jnp.float32)).astype(x.dtype)


def axial_rope_tables(n_tok, rot_dim):
    rows = n_tok // GRID_W
    row = jnp.repeat(jnp.arange(rows, dtype=jnp.float32), GRID_W)
    col = jnp.tile(jnp.arange(GRID_W, dtype=jnp.float32), rows)
    half = rot_dim // 2
    inv = 1.0 / (ROPE_BASE ** (jnp.arange(0, half, 2, dtype=jnp.float32) / half))
    ar = row[:, None] * inv[None, :]
    ac = col[:, None] * inv[None, :]
    ang = jnp.concatenate([ar, ar, ac, ac], axis=-1)
    return jnp.cos(ang), jnp.sin(ang)


def apply_axial_rope(x, cos, sin):
    half = x.shape[-1] // 2
    qr = half // 2
    a, b = x[..., :half], x[..., half:]
    rot = jnp.concatenate([-a[..., qr:], a[..., :qr], -b[..., qr:], b[..., :qr]], axis=-1)
    return x * cos[:, None].astype(x.dtype) + rot * sin[:, None].astype(x.dtype)


def sweep_query_blocks(fn, *qs):
    B, N = qs[0].shape[:2]
    nb = N // BLOCK
    blocks = tuple(jnp.moveaxis(q.reshape((B, nb, BLOCK) + q.shape[2:]), 1, 0) for q in qs)
    out = lax.map(lambda a: fn(*a), blocks)
    return jnp.moveaxis(out, 0, 1).reshape((B, N) + out.shape[3:])


def swa_mixer(h, hc, w_in, q_g, k_g, sink, w_out, need_ctx):
    B, N, _ = h.shape
    C = hc.shape[1]
    Hq, Hk, d = SWA_HEADS, SWA_KV_HEADS, HEAD_DIM
    G = Hq // Hk
    cuts = [Hq * d, Hq * d + Hk * d, Hq * d + 2 * Hk * d]
    q, k, v, gate = jnp.split(h @ w_in, cuts, axis=-1)
    cos, sin = axial_rope_tables(N, d)
    q = apply_axial_rope(rmsnorm(q.reshape(B, N, Hq, d), q_g), cos, sin)
    k = apply_axial_rope(rmsnorm(k.reshape(B, N, Hk, d), k_g), cos, sin)
    v = v.reshape(B, N, Hk, d)
    if need_ctx:
        cq, ck, cv, cgate = jnp.split(hc @ w_in, cuts, axis=-1)
    else:
        ck, cv = jnp.split(hc @ w_in[:, cuts[0]:cuts[2]], 2, axis=-1)
    ck = rmsnorm(ck.reshape(B, C, Hk, d), k_g)
    cv = cv.reshape(B, C, Hk, d)
    scale = d ** -0.5
    sink_l = sink.astype(jnp.float32).reshape(Hk, G)
    nb = N // BLOCK

    def bands(t):
        tp = jnp.pad(t, ((0, 0), (BLOCK, BLOCK), (0, 0), (0, 0))).reshape(B, nb + 2, BLOCK, Hk, d)
        return jnp.concatenate([tp[:, :-2], tp[:, 1:-1], tp[:, 2:]], axis=2)

    kb, vb = bands(k), bands(v)
    qb = q.reshape(B, nb, BLOCK, Hk, G, d)
    qi = jnp.arange(BLOCK)[:, None]
    kj = jnp.arange(3 * BLOCK)[None, :]
    in_band = (kj >= qi + BLOCK - SWA_WINDOW) & (kj <= qi + BLOCK + SWA_WINDOW)
    kpos = (jnp.arange(nb)[:, None] - 1) * BLOCK + jnp.arange(3 * BLOCK)[None, :]
    mask = in_band[None] & ((kpos >= 0) & (kpos < N))[:, None, :]
    s_loc = jnp.einsum('bnqhgd,bnkhd->bnhgqk', qb, kb).astype(jnp.float32) * scale
    s_loc = jnp.where(mask[None, :, None, None], s_loc, NEG_INF)
    s_ctx = jnp.einsum('bnqhgd,bchd->bnhgqc', qb, ck).astype(jnp.float32) * scale
    s_sink = jnp.broadcast_to(sink_l[None, None, :, :, None, None], s_ctx.shape[:-1] + (1,))
    p = jax.nn.softmax(jnp.concatenate([s_loc, s_ctx, s_sink], axis=-1), axis=-1).astype(v.dtype)
    o = (jnp.einsum('bnhgqk,bnkhd->bnqhgd', p[..., :3 * BLOCK], vb)
         + jnp.einsum('bnhgqc,bchd->bnqhgd', p[..., 3 * BLOCK:3 * BLOCK + C], cv)).reshape(B, N, Hq * d)
    out = (o * jax.nn.silu(gate)) @ w_out
    out_c = None
    if need_ctx:
        cqh = rmsnorm(cq.reshape(B, C, Hq, d), q_g).reshape(B, C, Hk, G, d)
        sc = jnp.einsum('bqhgd,bkhd->bhgqk', cqh, ck).astype(jnp.float32) * scale
        sc_sink = jnp.broadcast_to(sink_l[None, :, :, None, None], sc.shape[:-1] + (1,))
        pc = jax.nn.softmax(jnp.concatenate([sc, sc_sink], axis=-1), axis=-1)[..., :C].astype(cv.dtype)
        oc = jnp.einsum('bhgqk,bkhd->bqhgd', pc, cv).reshape(B, C, Hq * d)
        out_c = (oc * jax.nn.silu(cgate)) @ w_out
    return out, out_c


def mla_attend(qn, qp, kn, kp, v):
    scale = (MLA_NOPE + MLA_ROPE) ** -0.5
    s = (jnp.einsum('bqhd,bkhd->bhqk', qn, kn) + jnp.einsum('bqhr,bkr->bhqk', qp, kp)).astype(jnp.float32) * scale
    p = jax.nn.softmax(s, axis=-1).astype(v.dtype)
    return jnp.einsum('bhqk,bkhd->bqhd', p, v)


def mla_mixer(h, hc, w_in, qa_g, kva_g, w_qb, w_kvb, qn_nope_g, qn_pe_g, kn_nope_g, kn_pe_g, w_out, need_ctx):
    B, N, _ = h.shape
    C = hc.shape[1]
    H = MLA_HEADS
    cuts = [MLA_Q_RANK, MLA_Q_RANK + MLA_KV_RANK, MLA_Q_RANK + MLA_KV_RANK + MLA_ROPE]

    def queries(c_q):
        S = c_q.shape[1]
        qf = (rmsnorm(c_q, qa_g) @ w_qb).reshape(B, S, H, MLA_NOPE + MLA_ROPE)
        return rmsnorm(qf[..., :MLA_NOPE], qn_nope_g), rmsnorm(qf[..., MLA_NOPE:], qn_pe_g)

    def keys_values(c_kv, k_rope):
        S = c_kv.shape[1]
        kv = (rmsnorm(c_kv, kva_g) @ w_kvb).reshape(B, S, H, MLA_NOPE + MLA_V)
        return rmsnorm(kv[..., :MLA_NOPE], kn_nope_g), rmsnorm(k_rope, kn_pe_g), kv[..., MLA_NOPE:]

    lat_cq, lat_ckv, lat_kr, gate = jnp.split(h @ w_in, cuts, axis=-1)
    q_nope, q_pe = queries(lat_cq)
    k_nope, k_pe, v = keys_values(lat_ckv, lat_kr)
    cos, sin = axial_rope_tables(N, MLA_ROPE)
    q_pe = apply_axial_rope(q_pe, cos, sin)
    k_pe = apply_axial_rope(k_pe[:, :, None], cos, sin)[:, :, 0]
    if need_ctx:
        ctx_cq, ctx_ckv, ctx_kr, cgate = jnp.split(hc @ w_in, cuts, axis=-1)
    else:
        ctx_ckv, ctx_kr = jnp.split(hc @ w_in[:, cuts[0]:cuts[2]], [MLA_KV_RANK], axis=-1)
    ck_nope, ck_pe, cv = keys_values(ctx_ckv, ctx_kr)
    kn_all = jnp.concatenate([k_nope, ck_nope], axis=1)
    kp_all = jnp.concatenate([k_pe, ck_pe], axis=1)
    v_all = jnp.concatenate([v, cv], axis=1)
    o = sweep_query_blocks(lambda a, b: mla_attend(a, b, kn_all, kp_all, v_all), q_nope, q_pe)
    out = (o.reshape(B, N, MLA_WIDTH) * jax.nn.silu(gate)) @ w_out
    out_c = None
    if need_ctx:
        cq_nope, cq_pe = queries(ctx_cq)
        oc = mla_attend(cq_nope, cq_pe, ck_nope, ck_pe, cv).reshape(B, C, MLA_WIDTH)
        out_c = (oc * jax.nn.silu(cgate)) @ w_out
    return out, out_c


def centred_conv3(u, w, b):
    up = jnp.pad(u, ((0, 0), (1, 1), (0, 0)))
    return up[:, :-2] * w[0] + up[:, 1:-1] * w[1] + up[:, 2:] * w[2] + b


def hyena_filter_spectrum(L, f_w1, f_b1, f_w2, f_b2, f_freq, f_w3):
    f32 = jnp.float32
    pos = jnp.arange(L, dtype=f32)
    t = pos / max(L - 1, 1)
    bands = jnp.linspace(1e-4, HYENA_BANDS - 1, HYENA_BANDS, dtype=f32)
    ang = (2.0 * math.pi / L) * pos[:, None] * bands[None, :]
    z = jnp.concatenate([t[:, None], jnp.cos(ang), -jnp.sin(ang)], axis=-1)
    freq = f_freq.astype(f32)
    hf = jnp.sin(freq[0] * (z @ f_w1.astype(f32) + f_b1.astype(f32)))
    hf = jnp.sin(freq[1] * (hf @ f_w2.astype(f32) + f_b2.astype(f32)))
    hf = (hf @ f_w3.astype(f32)).reshape(L, HYENA_ORDER, 2, HYENA_WIDTH)
    max_decay = math.log(HYENA_DECAY_TARGET) / HYENA_FAST_DECAY
    min_decay = math.log(HYENA_DECAY_TARGET) / HYENA_SLOW_DECAY
    deltas = jnp.abs(jnp.linspace(min_decay, max_decay, HYENA_WIDTH, dtype=f32))
    hf = hf * jnp.exp(-t[:, None] * deltas[None, :])[:, None, None, :]
    fwd = hf[:, :, 0]
    bwd = hf[:0:-1, :, 1]
    kc = jnp.concatenate([fwd, jnp.zeros((1, HYENA_ORDER, HYENA_WIDTH), f32), bwd], axis=0)
    kc = kc * lax.rsqrt(jnp.sum(kc * kc, axis=0, keepdims=True) + EPS)
    return jnp.fft.rfft(kc, axis=0)


def bidir_fftconv(u, spec, skip):
    L = u.shape[1]
    y = jnp.fft.irfft(jnp.fft.rfft(u, n=2 * L, axis=1) * spec[None], n=2 * L, axis=1)[:, :L]
    return y + u * skip


def hyena_branch(hin, w_in, conv_w, conv_b, f_w1, f_b1, f_w2, f_b2, f_freq, f_w3, skip, w_out):
    L = hin.shape[1]
    proj = hin @ w_in
    u, gate = proj[..., :(HYENA_ORDER + 1) * HYENA_WIDTH], proj[..., (HYENA_ORDER + 1) * HYENA_WIDTH:]
    u = centred_conv3(u, conv_w, conv_b).astype(jnp.float32)
    v, x1, x2 = jnp.split(u, 3, axis=-1)
    spec = hyena_filter_spectrum(L, f_w1, f_b1, f_w2, f_b2, f_freq, f_w3)
    sk = skip.astype(jnp.float32)
    z = x1 * bidir_fftconv(v, spec[:, 0], sk[0])
    z = x2 * bidir_fftconv(z, spec[:, 1], sk[1])
    return (z.astype(hin.dtype) * jax.nn.silu(gate)) @ w_out


def diff_attend(q, k, v, lam):
    s = jnp.einsum('bqhid,bkhid->bihqk', q, k).astype(jnp.float32) * (HEAD_DIM ** -0.5)
    p = jax.nn.softmax(s, axis=-1)
    a = (p[:, 0] - lam * p[:, 1]).astype(v.dtype)
    return jnp.einsum('bhqk,bkhe->bqhe', a, v)


def diff_mixer(h, hc, w_in, q_g, k_g, lq1, lk1, lq2, lk2, subln_g, w_out, lam_init, need_ctx):
    B, N, _ = h.shape
    C = hc.shape[1]
    H, d, W = DIFF_HEADS, HEAD_DIM, DIFF_WIDTH
    f32 = jnp.float32

    def qk(t, g):
        return rmsnorm(t.reshape(B, t.shape[1], H, 2, d), g)

    q, k, v, gate = jnp.split(h @ w_in, 4, axis=-1)
    cos, sin = axial_rope_tables(N, d)

    def rope2(t):
        return apply_axial_rope(t.reshape(B, N, 2 * H, d), cos, sin).reshape(B, N, H, 2, d)

    q = rope2(qk(q, q_g))
    k = rope2(qk(k, k_g))
    v = v.reshape(B, N, H, 2 * d)
    if need_ctx:
        cq, ck, cv, cgate = jnp.split(hc @ w_in, 4, axis=-1)
    else:
        ck, cv = jnp.split(hc @ w_in[:, W:3 * W], 2, axis=-1)
    ck = qk(ck, k_g)
    cv = cv.reshape(B, C, H, 2 * d)
    lam = (jnp.exp(jnp.sum(lq1.astype(f32) * lk1.astype(f32)))
           - jnp.exp(jnp.sum(lq2.astype(f32) * lk2.astype(f32))) + lam_init)
    k_all = jnp.concatenate([k, ck], axis=1)
    v_all = jnp.concatenate([v, cv], axis=1)

    def finish(o, g):
        o = (rmsnorm(o, subln_g) * (1.0 - lam_init)).reshape(B, o.shape[1], W)
        return (o * jax.nn.silu(g)) @ w_out

    o = sweep_query_blocks(lambda qb: diff_attend(qb, k_all, v_all, lam), q)
    out = finish(o, gate)
    out_c = None
    if need_ctx:
        out_c = finish(diff_attend(qk(cq, q_g), ck, cv, lam), cgate)
    return out, out_c


def setup_inputs(seed: int = 0) -> dict:
    key = jax.random.key(seed)
    ks = iter(jax.random.split(key, 64))
    f32 = jnp.float32

    def nrm(shape, std):
        return jax.random.normal(next(ks), shape, f32) * std

    def gain(shape):
        return 1.0 + nrm(shape, 0.05)

    nA, nB, nC, nD = (len(range(m, DEPTH, N_MIXERS)) for m in range(N_MIXERS))
    D = D_MODEL
    d = HEAD_DIM
    W = HYENA_WIDTH
    return {
        'x': nrm((BATCH, SEQ, D), 1.0),
        'c': nrm((BATCH, D), 1.0),
        'ctx': nrm((BATCH, CTX_LEN, D), 1.0),
        'c_ctx': nrm((D,), 1.0),
        'norm_g': gain((DEPTH, D)),
        'ada_w': nrm((DEPTH, D, 3 * D), 0.5 * D ** -0.5),
        'ada_b': nrm((DEPTH, 3 * D), 0.02),
        'swa_w_in': nrm((nA, D, SWA_IN), D ** -0.5),
        'swa_q_g': gain((nA, d)),
        'swa_k_g': gain((nA, d)),
        'swa_sink': nrm((nA, SWA_HEADS), 1.0),
        'swa_w_out': nrm((nA, SWA_WIDTH, D), SWA_WIDTH ** -0.5),
        'mla_w_in': nrm((nB, D, MLA_IN), D ** -0.5),
        'mla_qa_g': gain((nB, MLA_Q_RANK)),
        'mla_kva_g': gain((nB, MLA_KV_RANK)),
        'mla_w_qb': nrm((nB, MLA_Q_RANK, MLA_HEADS * (MLA_NOPE + MLA_ROPE)), MLA_Q_RANK ** -0.5),
        'mla_w_kvb': nrm((nB, MLA_KV_RANK, MLA_HEADS * (MLA_NOPE + MLA_V)), MLA_KV_RANK ** -0.5),
        'mla_qn_nope_g': gain((nB, MLA_NOPE)),
        'mla_qn_pe_g': gain((nB, MLA_ROPE)),
        'mla_kn_nope_g': gain((nB, MLA_NOPE)),
        'mla_kn_pe_g': gain((nB, MLA_ROPE)),
        'mla_w_out': nrm((nB, MLA_WIDTH, D), MLA_WIDTH ** -0.5),
        'hyena_w_in': nrm((nC, D, HYENA_IN), D ** -0.5),
        'hyena_conv_w': nrm((nC, HYENA_CONV, (HYENA_ORDER + 1) * W), 0.5),
        'hyena_conv_b': nrm((nC, (HYENA_ORDER + 1) * W), 0.02),
        'hyena_f_w1': nrm((nC, HYENA_EMB, HYENA_HIDDEN), HYENA_EMB ** -0.5),
        'hyena_f_b1': nrm((nC, HYENA_HIDDEN), 0.2),
        'hyena_f_w2': nrm((nC, HYENA_HIDDEN, HYENA_HIDDEN), HYENA_HIDDEN ** -0.5),
        'hyena_f_b2': nrm((nC, HYENA_HIDDEN), 0.2),
        'hyena_f_freq': gain((nC, 2, HYENA_HIDDEN)),
        'hyena_f_w3': nrm((nC, HYENA_HIDDEN, HYENA_ORDER * 2 * W), HYENA_HIDDEN ** -0.5),
        'hyena_skip': nrm((nC, HYENA_ORDER, W), 0.5),
        'hyena_w_out': nrm((nC, W, D), W ** -0.5),
        'diff_w_in': nrm((nD, D, DIFF_IN), D ** -0.5),
        'diff_q_g': gain((nD, d)),
        'diff_k_g': gain((nD, d)),
        'diff_lq1': nrm((nD, d), 0.1),
        'diff_lk1': nrm((nD, d), 0.1),
        'diff_lq2': nrm((nD, d), 0.1),
        'diff_lk2': nrm((nD, d), 0.1),
        'diff_subln_g': gain((nD, 2 * d)),
        'diff_w_out': nrm((nD, DIFF_WIDTH, D), DIFF_WIDTH ** -0.5),
    }


def reference(x, c, ctx, c_ctx, norm_g, ada_w, ada_b,
              swa_w_in, swa_q_g, swa_k_g, swa_sink, swa_w_out,
              mla_w_in, mla_qa_g, mla_kva_g, mla_w_qb, mla_w_kvb,
              mla_qn_nope_g, mla_qn_pe_g, mla_kn_nope_g, mla_kn_pe_g, mla_w_out,
              hyena_w_in, hyena_conv_w, hyena_conv_b, hyena_f_w1, hyena_f_b1, hyena_f_w2,
              hyena_f_b2, hyena_f_freq, hyena_f_w3, hyena_skip, hyena_w_out,
              diff_w_in, diff_q_g, diff_k_g, diff_lq1, diff_lk1, diff_lq2, diff_lk2,
              diff_subln_g, diff_w_out):
    cond = jax.nn.silu(c)[:, None, :]
    cond_ctx = jax.nn.silu(c_ctx)
    for i in range(DEPTH):
        kind, j = i % N_MIXERS, i // N_MIXERS
        need_ctx = i < DEPTH - 1
        shift, scale, gate = jnp.split(cond @ ada_w[i] + ada_b[i], 3, axis=-1)
        h = rmsnorm(x, norm_g[i]) * (1.0 + scale) + shift
        hc = None
        if need_ctx or kind != 2:
            shift_c, scale_c, gate_c = jnp.split(cond_ctx @ ada_w[i] + ada_b[i], 3, axis=-1)
            hc = rmsnorm(ctx, norm_g[i]) * (1.0 + scale_c) + shift_c
        if kind == 0:
            o, oc = swa_mixer(h, hc, swa_w_in[j], swa_q_g[j], swa_k_g[j], swa_sink[j], swa_w_out[j], need_ctx)
        elif kind == 1:
            o, oc = mla_mixer(h, hc, mla_w_in[j], mla_qa_g[j], mla_kva_g[j], mla_w_qb[j], mla_w_kvb[j],
                              mla_qn_nope_g[j], mla_qn_pe_g[j], mla_kn_nope_g[j], mla_kn_pe_g[j],
                              mla_w_out[j], need_ctx)
        elif kind == 2:
            hy = (hyena_w_in[j], hyena_conv_w[j], hyena_conv_b[j], hyena_f_w1[j], hyena_f_b1[j],
                  hyena_f_w2[j], hyena_f_b2[j], hyena_f_freq[j], hyena_f_w3[j], hyena_skip[j], hyena_w_out[j])
            o = hyena_branch(h, *hy)
            oc = hyena_branch(hc, *hy) if need_ctx else None
        else:
            lam_init = 0.8 - 0.6 * math.exp(-0.3 * i)
            o, oc = diff_mixer(h, hc, diff_w_in[j], diff_q_g[j], diff_k_g[j], diff_lq1[j], diff_lk1[j],
                               diff_lq2[j], diff_lk2[j], diff_subln_g[j], diff_w_out[j], lam_init, need_ctx)
        x = x + gate * o
        if need_ctx:
            ctx = ctx + gate_c * oc
    return x
```

```python
import math
import numpy as np
import ml_dtypes
import concourse.bass as bass
import concourse.mybir as mybir
from concourse.bass_utils import run_bass_kernel_spmd

F32 = mybir.dt.float32
BF16 = mybir.dt.bfloat16
ALU = mybir.AluOpType
AF = mybir.ActivationFunctionType
AX = mybir.AxisListType
NCORES = 8


class Buf:
    _n = 0

    def __init__(self, name):
        Buf._n += 1
        self.name = f"{name}_{Buf._n}"
        self.writes = {}
        self.reads = {}
        self.dsem = None
        self.dcnt = 0


class Prog:
    def __init__(self, nc):
        self.nc = nc
        self.lists = {e: [] for e in ("pe", "act", "dve", "pool", "sp")}
        self.esem = {}
        self.ecnt = {e: 0 for e in self.lists}
        self.seen = {e: {} for e in self.lists}
        self.sems = {}
        self.out_events = []
        self.dbufs = []
        for e in ("pe", "act", "dve", "pool"):
            self.esem[e] = nc.alloc_semaphore(f"es_{e}")
            self.sems[("e", e)] = self.esem[e]

    def _wait(self, eng, ev):
        if ev is None:
            return
        key, val = ev
        if self.seen[eng].get(key, 0) >= val:
            return
        self.seen[eng][key] = val
        sem = self.sems[key]
        self.lists[eng].append(lambda e, sem=sem, val=val: e.wait_ge(sem, val))

    def _deps(self, eng, reads, writes, acc=False):
        for b in reads:
            for k, v in b.writes.items():
                self._wait(eng, (k, v))
        for b in writes:
            for k, v in b.writes.items():
                if acc and k == ("e", eng):
                    continue
                self._wait(eng, (k, v))
            for k, v in b.reads.items():
                self._wait(eng, (k, v))

    def _mark(self, ev, reads, writes):
        k, v = ev
        for b in reads:
            if b.reads.get(k, 0) < v:
                b.reads[k] = v
        for b in writes:
            if b.writes.get(k, 0) < v:
                b.writes[k] = v
            b.reads = {}

    def op(self, eng, fn, reads=(), writes=(), acc=False):
        self._deps(eng, reads, writes, acc)
        sem = self.esem[eng]
        self.ecnt[eng] += 1
        ev = (("e", eng), self.ecnt[eng])
        self.lists[eng].append(lambda e, fn=fn, sem=sem: fn(e).then_inc(sem, 1))
        self._mark(ev, reads, writes)
        return ev

    def dma(self, q, out, in_, sb, reads=(), writes=(), is_output=False, **kw):
        self._deps(q, reads, writes)
        if sb.dsem is None:
            sb.dsem = self.nc.alloc_semaphore(f"ds_{sb.name}")
            self.sems[("d", sb.name)] = sb.dsem
        sb.dcnt += 1
        if not hasattr(self, "dbufs"):
            self.dbufs = []
        if sb not in self.dbufs:
            self.dbufs.append(sb)
        ev = (("d", sb.name), 16 * sb.dcnt)
        sem = sb.dsem
        self.lists[q].append(lambda e, out=out, in_=in_, sem=sem, kw=kw: e.dma_start(out=out, in_=in_, **kw).then_inc(sem, 16))
        self._mark(ev, reads, writes)
        if is_output:
            self.out_events.append(ev)
        return ev

    def barrier(self):
        for eng in self.lists:
            for e2, cnt in self.ecnt.items():
                if e2 in self.esem and cnt > 0:
                    self._wait(eng, (("e", e2), cnt))
            for b in self.dbufs:
                self._wait(eng, (("d", b.name), 16 * b.dcnt))

    def finish(self):
        for ev in self.out_events:
            self._wait("sp", ev)
        nc = self.nc
        lists = self.lists
        with nc.Block() as block:
            @block.tensor
            def _(e):
                for f in lists["pe"]:
                    f(e)

            @block.scalar
            def _(e):
                for f in lists["act"]:
                    f(e)

            @block.vector
            def _(e):
                for f in lists["dve"]:
                    f(e)

            @block.gpsimd
            def _(e):
                for f in lists["pool"]:
                    f(e)

            @block.sync
            def _(e):
                for f in lists["sp"]:
                    f(e)


class T:
    def __init__(self, h, b):
        self.h = h
        self.b = b

    def __getitem__(self, key):
        return self.h[key]


class KB:
    def __init__(self):
        self.nc = bass.Bass("TRN2", target_bir_lowering=False)
        self.p = Prog(self.nc)
        self.in_names = []
        self.out_names = []
        self._ps = [T(self.nc.alloc_psum_tensor(f"psb{i}", [128, 512], F32), Buf(f"psb{i}")) for i in range(8)]
        self._psi = 0
        self._n = 0

    def inp(self, name, shape, dt=F32):
        self.in_names.append(name)
        return self.nc.dram_tensor(name, list(shape), dt, kind="ExternalInput").ap()

    def out(self, name, shape, dt=F32):
        self.out_names.append(name)
        return T(self.nc.dram_tensor(name, list(shape), dt, kind="ExternalOutput").ap(), Buf(name))

    def scratch(self, name, shape, dt=F32):
        return T(self.nc.dram_tensor(name, list(shape), dt, kind="Internal").ap(), Buf(name))

    def sb(self, name, shape, dt=F32):
        self._n += 1
        nm = f"{name}_{self._n}"
        return T(self.nc.alloc_sbuf_tensor(nm, list(shape), dt), Buf(nm))

    def ps(self):
        pool = getattr(self, "_pool", list(range(8)))
        t = self._ps[pool[self._psi % len(pool)]]
        self._psi += 1
        return t

    def bank(self, i):
        return self._ps[i]

    def op(self, eng, fn, reads=(), writes=(), acc=False):
        return self.p.op(eng, fn, [t.b for t in reads], [t.b for t in writes], acc)

    def load(self, dst, dst_ap, src_ap, q="sp", src=None, **kw):
        rd = [src.b] if src is not None else []
        return self.p.dma(q, dst_ap, src_ap, dst.b, reads=rd, writes=[dst.b], **kw)

    def store(self, dst, dst_ap, src, src_ap, q="sp", is_output=True, **kw):
        return self.p.dma(q, dst_ap, src_ap, src.b, reads=[src.b], writes=[dst.b], is_output=is_output, **kw)

    def finish(self):
        self.p.finish()
        return self.nc

    def const_load(self, name, shape, dt=F32, q="sp"):
        ap = self.inp(name, shape, dt)
        t = self.sb(name, shape, dt)
        self.load(t, t[:], ap, q=q)
        return t

    def rstd_from_ss(self, dst, dst_ap, ss, ss_ap, inv_n, eps=1e-6):
        self.op("dve", lambda e: e.tensor_scalar(dst_ap, ss_ap, inv_n, eps, ALU.mult, ALU.add), reads=[ss], writes=[dst])
        self.op("act", lambda e: e.activation(dst_ap, dst_ap, AF.Sqrt), reads=[dst], writes=[dst])
        self.op("dve", lambda e: e.reciprocal(dst_ap, dst_ap), reads=[dst], writes=[dst])


def run_prog(kb, in_maps):
    res = run_bass_kernel_spmd(kb.nc, in_maps, core_ids=list(range(NCORES)))
    return res.results


class WStream:
    def __init__(self, kb, name, kc, ncol_max, cast_eng="pool"):
        self.kb = kb
        self.kc = kc
        self.ncol_max = ncol_max
        self.stg = [kb.sb(f"{name}_stg{i}", [128, kc, ncol_max], F32) for i in range(2)]
        self.wb = [kb.sb(f"{name}_wb{i}", [128, kc, ncol_max], BF16) for i in range(2)]
        self.n = 0
        self.cast_eng = cast_eng

    def issue(self, w_ap, c0, ncol):
        kb = self.kb
        s = self.n % 2
        self.n += 1
        stg, wb = self.stg[s], self.wb[s]
        gi = c0 // self.ncol_max
        h = max(1, self.kc // 2)
        for k0 in range(0, self.kc, h):
            kb.load(stg, stg[:, k0:k0 + h, :], w_ap[gi, :, k0:k0 + h, :], q="sp")
        eng = self.cast_eng
        kb.op(eng, lambda e: e.tensor_copy(wb[:, :, :ncol], stg[:, :, :ncol]), reads=[stg], writes=[wb])
        return wb


def build_mod():
    kb = KB()
    cs = kb.const_load("cs", [128, 16, 2])
    adaw = kb.inp("adaw", [4, 2048, 768])
    bias = kb.const_load("adab", [2, 4, 768])
    modo = kb.out("modo", [2, 4 * 768])
    kb.op("act", lambda e: e.activation(cs[:], cs[:], AF.Silu), reads=[cs], writes=[cs])
    wt = [kb.sb(f"adaw{i}", [128, 16, 768], F32) for i in range(2)]
    osb = kb.sb("osb", [2, 4 * 768], F32)
    for i in range(4):
        w = wt[i % 2]
        for g in range(4):
            kb.load(w, w[:, 4 * g:4 * g + 4, :], adaw[i, 512 * g:512 * (g + 1), :].rearrange("(kc p) n -> p kc n", p=128), q="sp")
        for nb in range(2):
            ps = kb.ps()
            for kc in range(16):
                kb.op("pe", lambda e, ps=ps, w=w, kc=kc, nb=nb: e.matmul(ps[0:2, 0:384], cs[:, kc, :], w[:, kc, nb * 384:(nb + 1) * 384],
                                                                       start=(kc == 0), stop=(kc == 15)),
                      reads=[cs, w], writes=[ps], acc=True)
            c0 = i * 768 + nb * 384
            kb.op("dve", lambda e, ps=ps, c0=c0, i=i, nb=nb: e.tensor_tensor(osb[0:2, c0:c0 + 384], ps[0:2, 0:384],
                                                                           bias[0:2, i, nb * 384:(nb + 1) * 384], ALU.add),
                  reads=[ps, bias], writes=[osb])
    kb.store(modo, modo[:], osb, osb[:])
    kb.finish()
    return kb


def run_mod(inp):
    kb = build_mod()
    c = inp["c"].reshape(2048)
    cc = inp["c_ctx"].reshape(2048)
    cs = np.stack([c.reshape(16, 128).T, cc.reshape(16, 128).T], axis=-1).astype(np.float32)
    maps = []
    for r in range(NCORES):
        sl = slice(768 * r, 768 * (r + 1))
        adab = np.ascontiguousarray(np.broadcast_to(inp["ada_b"][None, :, sl], (2, 4, 768)))
        maps.append({"cs": np.ascontiguousarray(cs), "adaw": np.ascontiguousarray(inp["ada_w"][:, :, sl]), "adab": adab})
    res = run_prog(kb, maps)
    mod = np.zeros((4, 2, 6144), np.float32)
    for r in range(NCORES):
        mod[:, :, 768 * r:768 * (r + 1)] = res[r]["modo"].reshape(2, 4, 768).transpose(1, 0, 2)
    return mod


TC = 32


def rope_tables(positions, rot_dim):
    row = (positions // 64).astype(np.float32)
    col = (positions % 64).astype(np.float32)
    half = rot_dim // 2
    inv = (1.0 / (10000.0 ** (np.arange(0, half, 2, dtype=np.float32) / half))).astype(np.float32)
    ar = row[:, None] * inv[None, :]
    ac = col[:, None] * inv[None, :]
    ang = np.concatenate([ar, ar, ac, ac], axis=-1)
    return np.cos(ang).T.astype(np.float32), np.sin(ang).T.astype(np.float32)


def rope_matrix(rot_dim, reps):
    half = rot_dim // 2
    qr = half // 2
    R = np.zeros((rot_dim, rot_dim), np.float32)
    for seg in range(2):
        o = seg * half
        for i in range(qr):
            R[o + qr + i, o + i] = -1.0
            R[o + i, o + qr + i] = 1.0
    full = np.zeros((rot_dim * reps, rot_dim * reps), np.float32)
    for r in range(reps):
        full[r * rot_dim:(r + 1) * rot_dim, r * rot_dim:(r + 1) * rot_dim] = R
    return full


def blockdiag_ones(bs):
    m = np.zeros((128, 128), np.float32)
    for r in range(128 // bs):
        m[r * bs:(r + 1) * bs, r * bs:(r + 1) * bs] = 1.0
    return m


class ProjCtx:
    def __init__(self, kb, tm, halo):
        self.kb = kb
        self.tm = tm
        self.halo = halo
        self.ttot = tm + TC
        tiles = []
        c = 0
        while c < tm:
            w = min(512, tm - c)
            tiles.append((c, w, False))
            c += w
        tiles.append((tm, TC, True))
        self.tiles = tiles
        self._tmp = {}

    def tmp(self, name, shape, dt, n=2):
        key = name
        if key not in self._tmp:
            self._tmp[key] = [[self.kb.sb(f"{name}{i}", shape, dt) for i in range(n)], 0]
        lst = self._tmp[key]
        t = lst[0][lst[1] % n]
        lst[1] += 1
        return t


def emit_modnorm(pc, xT_ap, modv):
    kb = pc.kb
    T_ = pc.ttot
    ones = kb.sb("ones32", [128, 128], F32)
    kb.op("pool", lambda e: e.memset(ones[:], 1.0), writes=[ones])
    AB = kb.sb("AB", [128, 2, 16], F32)
    for w_, (sc) in enumerate((2, 4)):
        kb.op("dve", lambda e, w_=w_, sc=sc: e.tensor_scalar(AB[:, w_, :], modv[:, sc, :], 1.0, None, ALU.add), reads=[modv], writes=[AB])
        kb.op("dve", lambda e, w_=w_: e.tensor_tensor(AB[:, w_, :], AB[:, w_, :], modv[:, 0, :], ALU.mult), reads=[AB, modv], writes=[AB])
    rstd = kb.sb("rstd", [128, T_], F32)
    pss = [kb.bank(i) for i in range(len(pc.tiles))]
    for kc in range(16):
        xc = pc.tmp("xc", [128, T_], F32, n=3)
        kb.load(xc, xc[:], xT_ap[128 * kc:128 * (kc + 1), :], q=("sp" if kc % 2 == 0 else "pool"))
        for ti, (c0, w, isc) in enumerate(pc.tiles):
            ps = pss[ti]
            sq = pc.tmp("sq32", [128, 512], F32)
            kb.op("act", lambda e, sq=sq, xc=xc, c0=c0, w=w: e.activation(sq[:, :w], xc[:, c0:c0 + w], AF.Square), reads=[xc], writes=[sq])
            kb.op("pe", lambda e, ps=ps, sq=sq, kc=kc, w=w: e.matmul(ps[:, :w], ones[:], sq[:, :w], start=(kc == 0), stop=(kc == 15)),
                  reads=[ones, sq], writes=[ps], acc=(kc != 0))
    for ti, (c0, w, isc) in enumerate(pc.tiles):
        kb.rstd_from_ss(rstd, rstd[:, c0:c0 + w], pss[ti], pss[ti][:, :w], 1.0 / 2048)
    hT = kb.sb("hT", [128, 16, T_], BF16)
    for kc in range(16):
        xc = pc.tmp("xc", [128, T_], F32, n=3)
        kb.load(xc, xc[:], xT_ap[128 * kc:128 * (kc + 1), :], q=("sp" if kc % 2 == 0 else "pool"))
        for (c0, w, isc) in pc.tiles:
            wi = 1 if isc else 0
            bi = 3 if isc else 1
            t = pc.tmp("mn32", [128, 512], F32)
            kb.op("dve", lambda e, t=t, xc=xc, kc=kc, c0=c0, w=w, wi=wi: e.scalar_tensor_tensor(t[:, :w], xc[:, c0:c0 + w], AB[:, wi, kc:kc + 1],
                                                                                             rstd[:, c0:c0 + w], ALU.mult, ALU.mult),
                  reads=[xc, AB, rstd], writes=[t])
            kb.op("act", lambda e, t=t, kc=kc, c0=c0, w=w, bi=bi: e.activation(hT[:, kc, c0:c0 + w], t[:, :w], AF.Identity, bias=modv[:, bi, kc:kc + 1]),
                  reads=[t, modv], writes=[hT])
    return hT


def emit_epilogue(pc, blk, accs, C):
    kb = pc.kb
    kind = blk["kind"]
    T_ = pc.ttot
    if kind == "keep":
        dst, idx = blk["dst"], blk["idx"]
        for ti, (c0, w, isc) in enumerate(pc.tiles):
            ps = accs[ti]
            eng = "act" if ti % 2 == 0 else "dve"
            if eng == "act":
                kb.op("act", lambda e, ps=ps, c0=c0, w=w: e.copy(dst[:, idx, c0:c0 + w], ps[:, :w]), reads=[ps], writes=[dst])
            else:
                kb.op("dve", lambda e, ps=ps, c0=c0, w=w: e.tensor_copy(dst[:, idx, c0:c0 + w], ps[:, :w]), reads=[ps], writes=[dst])
        return
    dt = blk.get("dt", F32)
    stage = pc.tmp("stg32" if dt == F32 else "stg16", [128, T_], dt, n=3)
    if kind == "resid":
        xs, gv, ob = blk["xs"], blk["gv"], blk["ob"]
        for ti, (c0, w, isc) in enumerate(pc.tiles):
            ps = accs[ti]
            wi = 1 if isc else 0
            kb.op("dve", lambda e, ps=ps, c0=c0, w=w, wi=wi: e.scalar_tensor_tensor(stage[:, c0:c0 + w], ps[:, :w], gv[:, wi, ob:ob + 1], xs[:, ob, c0:c0 + w], ALU.mult, ALU.add),
                  reads=[ps, gv, xs], writes=[stage])
    elif kind == "raw":
        for ti, (c0, w, isc) in enumerate(pc.tiles):
            ps = accs[ti]
            if ti % 2 == 0:
                kb.op("act", lambda e, ps=ps, c0=c0, w=w: e.copy(stage[:, c0:c0 + w], ps[:, :w]), reads=[ps], writes=[stage])
            else:
                kb.op("dve", lambda e, ps=ps, c0=c0, w=w: e.tensor_copy(stage[:, c0:c0 + w], ps[:, :w]), reads=[ps], writes=[stage])
    elif kind == "hn":
        bs, gain, rope = blk["bs"], blk["gain"], blk["rope"]
        onesb = C["ones128"] if bs == 128 else C["ones64"]

        def tile_gen(ti, c0, w, isc):
            ps = accs[ti]
            sqb = pc.tmp("sqb", [128, 512], BF16, n=3)
            kb.op("act", lambda e: e.activation(sqb[:, :w], ps[:, :w], AF.Square), reads=[ps], writes=[sqb])
            yield
            pss = kb.ps()
            kb.op("pe", lambda e: e.matmul(pss[:, :w], onesb[:], sqb[:, :w], start=True, stop=True), reads=[onesb, sqb], writes=[pss])
            yield
            t1 = pc.tmp("t1", [128, 512], F32, n=3)
            kb.op("dve", lambda e: e.tensor_scalar(t1[:, :w], pss[:, :w], 1.0 / bs, 1e-6, ALU.mult, ALU.add), reads=[pss], writes=[t1])
            yield
            kb.op("act", lambda e: e.activation(t1[:, :w], t1[:, :w], AF.Sqrt), reads=[t1], writes=[t1])
            yield
            kb.op("dve", lambda e: e.reciprocal(t1[:, :w], t1[:, :w]), reads=[t1], writes=[t1])
            yield
            if rope and not isc:
                yn = pc.tmp("yn", [128, 512], F32, n=3)
                kb.op("dve", lambda e: e.scalar_tensor_tensor(yn[:, :w], ps[:, :w], gain, t1[:, :w], ALU.mult, ALU.mult),
                      reads=[ps, t1, blk["gain_t"]], writes=[yn])
                yield
                ynb = pc.tmp("ynb", [128, 512], BF16, n=3)
                kb.op("act", lambda e: e.copy(ynb[:, :w], yn[:, :w]), reads=[yn], writes=[ynb])
                yield
                psr = kb.ps()
                Rm = C["rm128"] if bs == 128 else C["rm64"]
                kb.op("pe", lambda e: e.matmul(psr[:, :w], Rm[:], ynb[:, :w], start=True, stop=True), reads=[Rm, ynb], writes=[psr])
                cos, sin = (C["cos128"], C["sin128"]) if bs == 128 else (C["cos64"], C["sin64"])
                kb.op("pool", lambda e: e.tensor_tensor(yn[:, :w], yn[:, :w], cos[:, c0:c0 + w], ALU.mult), reads=[yn, cos], writes=[yn])
                yield
                o2 = pc.tmp("o2", [128, 512], F32, n=3)
                kb.op("dve", lambda e: e.tensor_tensor(o2[:, :w], psr[:, :w], sin[:, c0:c0 + w], ALU.mult), reads=[psr, sin], writes=[o2])
                yield
                kb.op("dve", lambda e: e.tensor_tensor(stage[:, c0:c0 + w], yn[:, :w], o2[:, :w], ALU.add), reads=[yn, o2], writes=[stage])
            else:
                kb.op("dve", lambda e: e.scalar_tensor_tensor(stage[:, c0:c0 + w], ps[:, :w], gain, t1[:, :w], ALU.mult, ALU.mult),
                      reads=[ps, t1, blk["gain_t"]], writes=[stage])
            yield

        gens = [tile_gen(ti, c0, w, isc) for ti, (c0, w, isc) in enumerate(pc.tiles)]
        while gens:
            for gen in list(gens):
                try:
                    next(gen)
                except StopIteration:
                    gens.remove(gen)
    elif kind == "conv3":
        cw = blk["cw"]
        ci = blk["ci"]
        hm = C["hm"]
        u = pc.tmp("u32", [128, T_], F32)
        for ti, (c0, w, isc) in enumerate(pc.tiles):
            ps = accs[ti]
            if ti % 2 == 0:
                kb.op("act", lambda e, ps=ps, c0=c0, w=w: e.copy(u[:, c0:c0 + w], ps[:, :w]), reads=[ps], writes=[u])
            else:
                kb.op("dve", lambda e, ps=ps, c0=c0, w=w: e.tensor_copy(u[:, c0:c0 + w], ps[:, :w]), reads=[ps], writes=[u])
        tm = pc.tm
        kb.op("dve", lambda e: e.tensor_scalar(u[:, 0:1], u[:, 0:1], hm[:, 0:1], None, ALU.mult), reads=[u, hm], writes=[u])
        kb.op("dve", lambda e: e.tensor_scalar(u[:, tm - 1:tm], u[:, tm - 1:tm], hm[:, 1:2], None, ALU.mult), reads=[u, hm], writes=[u])
        n = tm - 2
        kb.op("dve", lambda e: e.tensor_scalar(stage[:, 1:1 + n], u[:, 0:n], cw[:, ci, 0:1], cw[:, ci, 3:4], ALU.mult, ALU.add), reads=[u, cw], writes=[stage])
        kb.op("dve", lambda e: e.scalar_tensor_tensor(stage[:, 1:1 + n], u[:, 1:1 + n], cw[:, ci, 1:2], stage[:, 1:1 + n], ALU.mult, ALU.add),
              reads=[u, cw, stage], writes=[stage])
        kb.op("dve", lambda e: e.scalar_tensor_tensor(stage[:, 1:1 + n], u[:, 2:2 + n], cw[:, ci, 2:3], stage[:, 1:1 + n], ALU.mult, ALU.add),
              reads=[u, cw, stage], writes=[stage])
        kb.op("pool", lambda e: e.tensor_copy(stage[:, tm:tm + TC], u[:, tm:tm + TC]), reads=[u], writes=[stage])
    else:
        raise ValueError(kind)
    out, r0 = blk["out"], blk["r0"]
    h = pc.halo
    if h:
        kb.store(out, out[r0:r0 + 128, 0:pc.tm - 2], stage, stage[:, 1:pc.tm - 1], q="pool")
        kb.store(out, out[r0:r0 + 128, pc.tm - 2:pc.tm - 2 + TC], stage, stage[:, pc.tm:pc.tm + TC], q="pool")
    else:
        kb.store(out, out[r0:r0 + 128, :], stage, stage[:], q="pool")


def emit_proj_stage(pc, src, KC, w_ap, blocks, ws, C):
    kb = pc.kb
    BPG = ws.ncol_max // 128
    ngroups = (len(blocks) + BPG - 1) // BPG
    def issue(g):
        nb = min(BPG, len(blocks) - BPG * g)
        return ws.issue(w_ap, 128 * BPG * g, 128 * nb)
    pending = issue(0)
    for g in range(ngroups):
        cur = pending
        if g + 1 < ngroups:
            pending = issue(g + 1)
        for bi in range(min(BPG, len(blocks) - BPG * g)):
            blk = blocks[BPG * g + bi]
            accs = []
            for (c0, w, isc) in pc.tiles:
                ps = kb.ps()
                for kc in range(KC):
                    kb.op("pe", lambda e, ps=ps, cur=cur, kc=kc, bi=bi, c0=c0, w=w: e.matmul(ps[:, :w], cur[:, kc, bi * 128:(bi + 1) * 128], src[:, kc, c0:c0 + w],
                                                                                     start=(kc == 0), stop=(kc == KC - 1)),
                          reads=[cur, src], writes=[ps], acc=True)
                accs.append(ps)
            emit_epilogue(pc, blk, accs, C)


def emit_widenorm(pc, raw, nb, gain, dst, C, gcol0=0):
    kb = pc.kb
    for (c0, w, isc) in pc.tiles:
        ps = kb.ps()
        for b in range(nb):
            sqb = pc.tmp("sqb", [128, 512], BF16)
            kb.op("act", lambda e, sqb=sqb, b=b, c0=c0, w=w: e.activation(sqb[:, :w], raw[:, b, c0:c0 + w], AF.Square), reads=[raw], writes=[sqb])
            kb.op("pe", lambda e, ps=ps, sqb=sqb, b=b, w=w: e.matmul(ps[:, :w], C["ones128"][:], sqb[:, :w], start=(b == 0), stop=(b == nb - 1)),
                  reads=[C["ones128"], sqb], writes=[ps], acc=True)
        t1 = pc.tmp("t1", [128, 512], F32)
        kb.rstd_from_ss(t1, t1[:, :w], ps, ps[:, :w], 1.0 / (128 * nb))
        for b in range(nb):
            kb.op("dve", lambda e, b=b, t1=t1, c0=c0, w=w: e.scalar_tensor_tensor(dst[:, b, c0:c0 + w], raw[:, b, c0:c0 + w], gain[:, gcol0 + b:gcol0 + b + 1], t1[:, :w], ALU.mult, ALU.mult),
                  reads=[raw, gain, t1], writes=[dst])


def proj_consts(kb, need64=False, rope=True, tm=1024):
    C = {}
    def cbf(name, shape):
        t32 = kb.const_load(name, shape)
        tb = kb.sb(name + "b", shape, BF16)
        kb.op("dve", lambda e: e.tensor_copy(tb[:], t32[:]), reads=[t32], writes=[tb])
        return tb
    C["ones128"] = cbf("c_ones128", [128, 128])
    if rope:
        C["rm128"] = cbf("c_rm128", [128, 128])
        C["cos128"] = kb.const_load("c_cos128", [128, tm])
        C["sin128"] = kb.const_load("c_sin128", [128, tm], q="pool")
    if need64:
        C["ones64"] = cbf("c_ones64", [128, 128])
        C["rm64"] = cbf("c_rm64", [128, 128])
        C["cos64"] = kb.const_load("c_cos64", [128, tm])
        C["sin64"] = kb.const_load("c_sin64", [128, tm], q="pool")
    return C


def proj_const_inputs(r, need64=False, rope=True, tm=1024):
    d = {"c_ones128": blockdiag_ones(128)}
    pos = np.arange(1024 * r, 1024 * (r + 1))
    if rope:
        d["c_rm128"] = rope_matrix(128, 1)
        d["c_cos128"], d["c_sin128"] = rope_tables(pos, 128)
    if need64:
        d["c_ones64"] = blockdiag_ones(64)
        d["c_rm64"] = rope_matrix(64, 2)
        c, s = rope_tables(pos, 64)
        d["c_cos64"], d["c_sin64"] = np.concatenate([c, c], 0), np.concatenate([s, s], 0)
    return {k: np.ascontiguousarray(v, dtype=np.float32) for k, v in d.items()}


def modv_input(norm_g, mod_m, mod_c):
    def l(v):
        return v.reshape(16, 128).T
    return np.ascontiguousarray(np.stack([l(norm_g), l(mod_m[0:2048]), l(mod_m[2048:4096]), l(mod_c[0:2048]), l(mod_c[2048:4096])], axis=1), dtype=np.float32)


def xT_input(x2d, ctx2d, r, halo=0):
    lo, hi = 1024 * r - halo, 1024 * (r + 1) + halo
    cols = []
    if lo < 0:
        cols.append(np.zeros((2048, halo), np.float32))
    cols.append(x2d[max(lo, 0):min(hi, 8192)].T)
    if hi > 8192:
        cols.append(np.zeros((2048, halo), np.float32))
    cols.append(ctx2d[TC * r:TC * (r + 1)].T)
    return np.ascontiguousarray(np.concatenate(cols, axis=1), dtype=np.float32)


def hn_blocks(n, bs, gain_t, col, rope, out, r0=0, dt=BF16):
    return [dict(kind="hn", bs=bs, gain=gain_t[:, col:col + 1], gain_t=gain_t, rope=rope, out=out, r0=r0 + 128 * i, dt=dt) for i in range(n)]


def raw_blocks(n, out, r0=0, dt=F32):
    return [dict(kind="raw", out=out, r0=r0 + 128 * i, dt=dt) for i in range(n)]


def build_proj_swa(nkv=4):
    kb = KB()
    pc = ProjCtx(kb, 1024, 0)
    T_ = pc.ttot
    xT = kb.inp("xT", [2048, T_])
    modv = kb.const_load("modv", [128, 5, 16])
    gains = kb.const_load("gains", [128, 2])
    C = proj_consts(kb)
    w = kb.inp("w_in", [(4096 + 256 * nkv) // 256, 128, 16, 256])
    qT = kb.out("qT", [2048, T_], BF16)
    kT = kb.out("kT", [128 * nkv, T_], BF16)
    vT = kb.out("vT", [128 * nkv, T_], BF16)
    gT = kb.out("gT", [2048, T_], BF16)
    hT = emit_modnorm(pc, xT, modv)
    blocks = (hn_blocks(16, 128, gains, 0, True, qT) + hn_blocks(nkv, 128, gains, 1, True, kT)
              + raw_blocks(nkv, vT, dt=BF16) + raw_blocks(16, gT, dt=BF16))
    ws = WStream(kb, "w", 16, 256)
    emit_proj_stage(pc, hT, 16, w, blocks, ws, C)
    kb.finish()
    return kb


def col2(a, b):
    return np.ascontiguousarray(np.stack([a, b], axis=1), dtype=np.float32)


def build_tail():
    kb = KB()
    pc = ProjCtx(kb, 1024, 0)
    T_ = pc.ttot
    oT = kb.inp("oT", [2048, T_], BF16)
    gT = kb.inp("gT", [2048, T_], BF16)
    xT = kb.inp("xT", [2048, T_])
    gv = kb.const_load("gv", [128, 2, 16])
    w = kb.inp("w_out", [8, 128, 16, 256])
    xo = kb.out("xo", [2048, T_], F32)
    xs = kb.sb("xs", [128, 16, T_], F32)
    for g in range(4):
        kb.load(xs, xs[:, 4 * g:4 * g + 4, :], xT[512 * g:512 * (g + 1), :].rearrange("(kc p) t -> p kc t", p=128), q="pool")
    aT = kb.sb("aT", [128, 16, T_], BF16)
    for kc in range(16):
        ob_ = pc.tmp("o_in", [128, T_], BF16)
        gb_ = pc.tmp("g_in", [128, T_], BF16)
        kb.load(ob_, ob_[:], oT[128 * kc:128 * (kc + 1), :], q="sp")
        kb.load(gb_, gb_[:], gT[128 * kc:128 * (kc + 1), :], q="sp")
        sg = pc.tmp("sg", [128, T_], F32)
        kb.op("act", lambda e, sg=sg, gb_=gb_: e.activation(sg[:], gb_[:], AF.Silu), reads=[gb_], writes=[sg])
        kb.op("dve", lambda e, sg=sg, ob_=ob_, kc=kc: e.tensor_tensor(aT[:, kc, :], sg[:], ob_[:], ALU.mult), reads=[sg, ob_], writes=[aT])
    blocks = [dict(kind="resid", xs=xs, gv=gv, ob=i, out=xo, r0=128 * i, dt=F32) for i in range(16)]
    ws = WStream(kb, "w", 16, 256)
    emit_proj_stage(pc, aT, 16, w, blocks, ws, {})
    kb.finish()
    return kb


def gv_input(mod_m, mod_c):
    def l(v):
        return v.reshape(16, 128).T
    return np.ascontiguousarray(np.stack([l(mod_m[4096:6144]), l(mod_c[4096:6144])], axis=1), dtype=np.float32)


def run_tail(kb, oT_list, gT_list, xT_list, mod_m, mod_c, w_out):
    gv = gv_input(mod_m, mod_c)
    w_t = tile_weight(w_out)
    maps = [{"oT": oT_list[r], "gT": gT_list[r], "xT": xT_list[r], "gv": gv, "w_out": w_t} for r in range(NCORES)]
    res = run_prog(kb, maps)
    return [np.asarray(res[r]["xo"]) for r in range(NCORES)]


def split_xo(xo_list):
    x2d = np.concatenate([xo[:, :1024].T for xo in xo_list], axis=0)
    c2d = np.concatenate([xo[:, 1024:1024 + TC].T for xo in xo_list], axis=0)
    return np.ascontiguousarray(x2d), np.ascontiguousarray(c2d)


def build_swa_att():
    kb = KB()
    T_ = 1024 + TC
    qT = kb.inp("qT", [2048, T_], BF16)
    kL = kb.inp("kL", [512, 1280], BF16)
    vL = kb.inp("vL", [1280, 512], BF16)
    ckT = kb.inp("ckT", [512, 256], BF16)
    cvL = kb.inp("cvL", [256, 512], BF16)
    oT = kb.out("oT", [2048, T_], BF16)
    q_sb = kb.sb("q_sb", [128, 16, T_], BF16)
    for g4 in range(4):
        kb.load(q_sb, q_sb[:, 4 * g4:4 * g4 + 4, :], qT[512 * g4:512 * (g4 + 1), :].rearrange("(h p) t -> p h t", p=128))
    k_sb = kb.sb("k_sb", [128, 4, 1280], BF16)
    kb.load(k_sb, k_sb[:], kL.rearrange("(g p) t -> p g t", p=128), q="pool")
    v_sb = kb.sb("v_sb", [128, 10, 512], BF16)
    kb.load(v_sb, v_sb[:], vL.rearrange("(j p) c -> p j c", p=128))
    ck_sb = kb.sb("ck_sb", [128, 4, 256], BF16)
    kb.load(ck_sb, ck_sb[:], ckT.rearrange("(g p) t -> p g t", p=128), q="pool")
    cv_sb = kb.sb("cv_sb", [128, 2, 512], BF16)
    kb.load(cv_sb, cv_sb[:], cvL.rearrange("(j p) c -> p j c", p=128))
    masks32 = kb.const_load("masks", [128, 4, 128])
    masks = kb.sb("masksb", [128, 4, 128], BF16)
    kb.op("dve", lambda e: e.tensor_copy(masks[:], masks32[:]), reads=[masks32], writes=[masks])
    esink = kb.const_load("sinkb", [128, 16])
    kb.op("act", lambda e: e.activation(esink[:], esink[:], AF.Exp), reads=[esink], writes=[esink])
    ones = kb.sb("onesb", [128, 128], BF16)
    kb.op("pool", lambda e: e.memset(ones[:], 1.0), writes=[ones])
    o_sb = kb.sb("o_sb", [128, 16, T_], BF16)
    kb._pool = [0, 1, 2, 3]
    pts = [kb.sb(f"pt{i}", [128, 512], BF16) for i in range(10)]
    dens = [kb.sb(f"den{i}", [128, 512], F32) for i in range(2)]
    scale = 128.0 ** -0.5
    state = {"it": 0, "npt": 0}

    def iter_gen(g, qb):
        if qb < 8:
            nq = 128
            qc0 = qb * 128
            kblocks = [("l", qb, 2 if qb == 0 else 0), ("l", qb + 1, None), ("l", qb + 2, 3 if qb == 7 else 1), ("c", 0, None), ("c", 1, None)]
        else:
            nq = TC
            qc0 = 1024
            kblocks = [("c", 0, None), ("c", 1, None)]
        N = 4 * nq
        it = state["it"]
        state["it"] += 1
        pso = kb.bank(4 + it % 2)
        psd = kb.bank(6 + it % 2)
        den = dens[it % 2]
        for bi, (kind, j, mi) in enumerate(kblocks):
            pss = kb.ps()
            if kind == "l":
                lk = k_sb[:, g, j * 128:(j + 1) * 128]
                lv = v_sb[:, j, g * 128:(g + 1) * 128]
                kt, vt = k_sb, v_sb
            else:
                lk = ck_sb[:, g, j * 128:(j + 1) * 128]
                lv = cv_sb[:, j, g * 128:(g + 1) * 128]
                kt, vt = ck_sb, cv_sb
            qv = q_sb[:, 4 * g:4 * g + 4, qc0:qc0 + nq]
            kb.op("pe", lambda e, pss=pss, lk=lk, qv=qv: e.matmul(pss[:, :N], lk, qv, start=True, stop=True), reads=[kt, q_sb], writes=[pss])
            pt = pts[state["npt"] % len(pts)]
            state["npt"] += 1
            yield
            kb.op("act", lambda e, pt=pt, pss=pss: e.activation(pt[:, :N], pss[:, :N], AF.Exp, scale=scale), reads=[pss], writes=[pt])
            yield
            if mi is not None:
                kb.op("dve", lambda e, pt=pt, mi=mi: e.tensor_tensor(pt[:, :4 * nq].rearrange("p (h q) -> p h q", h=4),
                                                                   pt[:, :4 * nq].rearrange("p (h q) -> p h q", h=4),
                                                                   masks[:, mi, :].unsqueeze(1).to_broadcast([128, 4, 128]), ALU.mult),
                      reads=[pt, masks], writes=[pt])
                yield
            first, last = (bi == 0), (bi == len(kblocks) - 1)
            kb.op("pe", lambda e, lv=lv, pt=pt, first=first, last=last: e.matmul(pso[:, :N], lv, pt[:, :N], start=first, stop=last),
                  reads=[vt, pt], writes=[pso], acc=not first)
            kb.op("pe", lambda e, pt=pt, first=first, last=last: e.matmul(psd[:, :N], ones[:], pt[:, :N], start=first, stop=last),
                  reads=[ones, pt], writes=[psd], acc=not first)
            yield
        kb.op("dve", lambda e: e.tensor_tensor(den[:, :N].rearrange("p (h q) -> p h q", h=4),
                                               psd[:, :N].rearrange("p (h q) -> p h q", h=4),
                                               esink[:, 4 * g:4 * g + 4].unsqueeze(2).to_broadcast([128, 4, nq]), ALU.add),
              reads=[psd, esink], writes=[den])
        yield
        kb.op("dve", lambda e: e.reciprocal(den[:, :N], den[:, :N]), reads=[den], writes=[den])
        yield
        kb.op("dve", lambda e: e.tensor_tensor(o_sb[:, 4 * g:4 * g + 4, qc0:qc0 + nq],
                                               pso[:, :N].rearrange("p (h q) -> p h q", h=4),
                                               den[:, :N].rearrange("p (h q) -> p h q", h=4), ALU.mult),
              reads=[pso, den], writes=[o_sb])
        yield

    work = [(g, qb) for g in range(4) for qb in range(9)]
    for i in range(0, len(work), 2):
        gens = [iter_gen(*w) for w in work[i:i + 2]]
        while gens:
            for gen in list(gens):
                try:
                    next(gen)
                except StopIteration:
                    gens.remove(gen)
    for g4 in range(4):
        kb.store(oT, oT[512 * g4:512 * (g4 + 1), :].rearrange("(h p) t -> p h t", p=128), o_sb, o_sb[:, 4 * g4:4 * g4 + 4, :])
    kb.finish()
    return kb


def swa_masks(r):
    k = np.arange(128)[:, None]
    q = np.arange(128)[None, :]
    mlo = (k >= q).astype(np.float32)
    mhi = (k <= q).astype(np.float32)
    z = np.zeros_like(mlo)
    return np.ascontiguousarray(np.stack([mlo, mhi, z if r == 0 else mlo, z if r == NCORES - 1 else mhi], axis=1))


def bf(a):
    return np.ascontiguousarray(a, dtype=ml_dtypes.bfloat16)


def run_layer_swa(inp, mod, x2d, ctx2d, progs):
    li = 0
    kb = progs["proj_swa"]
    w_t = tile_weight(inp["swa_w_in"][0])
    maps = []
    for r in range(NCORES):
        m = {"xT": xT_input(x2d, ctx2d, r), "modv": modv_input(inp["norm_g"][li], mod[li, 0], mod[li, 1]),
             "gains": col2(inp["swa_q_g"][0], inp["swa_k_g"][0]), "w_in": w_t}
        m.update(proj_const_inputs(r))
        maps.append(m)
    res = run_prog(kb, maps)
    qT = [np.asarray(res[r]["qT"]) for r in range(NCORES)]
    kT = [np.asarray(res[r]["kT"]) for r in range(NCORES)]
    vT = [np.asarray(res[r]["vT"]) for r in range(NCORES)]
    gT = [np.asarray(res[r]["gT"]) for r in range(NCORES)]
    z = np.zeros((512, 128), kT[0].dtype)
    kfull = np.concatenate([z] + [k[:, :1024] for k in kT] + [z], axis=1)
    vfull = np.concatenate([z] + [v[:, :1024] for v in vT] + [z], axis=1)
    ckT = np.ascontiguousarray(np.concatenate([k[:, 1024:] for k in kT], axis=1))
    cvL = np.ascontiguousarray(np.concatenate([v[:, 1024:] for v in vT], axis=1).T)
    sinkb = np.ascontiguousarray(np.broadcast_to(inp["swa_sink"][0][None, :], (128, 16)), dtype=np.float32)
    maps = []
    for r in range(NCORES):
        sl = slice(1024 * r, 1024 * r + 1280)
        maps.append({"qT": qT[r], "kL": np.ascontiguousarray(kfull[:, sl]), "vL": np.ascontiguousarray(vfull[:, sl].T),
                     "ckT": ckT, "cvL": cvL, "masks": swa_masks(r), "sinkb": sinkb})
    res = run_prog(progs["swa_att"], maps)
    oT = [np.asarray(res[r]["oT"]) for r in range(NCORES)]
    xT = [xT_input(x2d, ctx2d, r) for r in range(NCORES)]
    xo = run_tail(progs["tail"], oT, gT, xT, mod[li, 0], mod[li, 1], inp["swa_w_out"][0])
    return split_xo(xo)


NTOK = 8192 + 256


def build_proj_mla():
    kb = KB()
    pc = ProjCtx(kb, 1024, 0)
    T_ = pc.ttot
    xT = kb.inp("xT", [2048, T_])
    modv = kb.const_load("modv", [128, 5, 16])
    gains = kb.const_load("gains", [128, 10])
    C = proj_consts(kb, need64=True, rope=False)
    w_in = kb.inp("w_in", [12, 128, 16, 256])
    w_qb = kb.inp("w_qb", [12, 128, 4, 256])
    w_kvb = kb.inp("w_kvb", [16, 128, 2, 256])
    qnT = kb.out("qnT", [2048, T_], BF16)
    qpeT = kb.out("qpeT", [1024, T_], BF16)
    knT = kb.out("knT", [2048, T_], BF16)
    vT = kb.out("vT", [2048, T_], BF16)
    kpeT = kb.out("kpeT", [128, T_], BF16)
    gT = kb.out("gT", [2048, T_], BF16)
    hT = emit_modnorm(pc, xT, modv)
    cq_raw = kb.sb("cq_raw", [128, 4, T_], F32)
    ckv_raw = kb.sb("ckv_raw", [128, 2, T_], F32)
    blocks = ([dict(kind="keep", dst=cq_raw, idx=i) for i in range(4)] + [dict(kind="keep", dst=ckv_raw, idx=i) for i in range(2)]
              + hn_blocks(1, 64, gains, 9, True, kpeT) + raw_blocks(16, gT, dt=BF16))
    ws1 = WStream(kb, "w1", 16, 256)
    emit_proj_stage(pc, hT, 16, w_in, blocks, ws1, C)
    cqn = kb.sb("cqn", [128, 4, T_], BF16)
    ckvn = kb.sb("ckvn", [128, 2, T_], BF16)
    emit_widenorm(pc, cq_raw, 4, gains, cqn, C, gcol0=0)
    emit_widenorm(pc, ckv_raw, 2, gains, ckvn, C, gcol0=4)
    ws2 = WStream(kb, "w2", 4, 256)
    emit_proj_stage(pc, cqn, 4, w_qb, hn_blocks(16, 128, gains, 6, False, qnT) + hn_blocks(8, 64, gains, 7, True, qpeT), ws2, C)
    ws3 = WStream(kb, "w3", 2, 256)
    emit_proj_stage(pc, ckvn, 2, w_kvb, hn_blocks(16, 128, gains, 8, False, knT) + raw_blocks(16, vT, dt=BF16), ws3, C)
    kb.finish()
    return kb


def mla_weight_layouts(inp):
    w_in = inp["mla_w_in"][0]
    w_in_re = np.concatenate([w_in[:, 0:768], w_in[:, 768:832], w_in[:, 768:832], w_in[:, 832:]], axis=1)
    wq = inp["mla_w_qb"][0].reshape(512, 16, 192)
    w_qb_re = np.concatenate([wq[:, :, :128].reshape(512, 2048), wq[:, :, 128:].reshape(512, 1024)], axis=1)
    wkv = inp["mla_w_kvb"][0].reshape(256, 16, 256)
    w_kvb_re = np.concatenate([wkv[:, :, :128].reshape(256, 2048), wkv[:, :, 128:].reshape(256, 2048)], axis=1)
    g = np.zeros((128, 10), np.float32)
    g[:, 0:4] = inp["mla_qa_g"][0].reshape(4, 128).T
    g[:, 4:6] = inp["mla_kva_g"][0].reshape(2, 128).T
    g[:, 6] = inp["mla_qn_nope_g"][0]
    g[:, 7] = np.tile(inp["mla_qn_pe_g"][0], 2)
    g[:, 8] = inp["mla_kn_nope_g"][0]
    g[:, 9] = np.tile(inp["mla_kn_pe_g"][0], 2)
    return (np.ascontiguousarray(w_in_re, dtype=np.float32), np.ascontiguousarray(w_qb_re, dtype=np.float32),
            np.ascontiguousarray(w_kvb_re, dtype=np.float32), g)


def gather_tokens(per_core, rows=None):
    return np.concatenate([a[:, :1024] for a in per_core] + [a[:, 1024:1024 + TC] for a in per_core], axis=1)


def scatter_tokens(full, r):
    return np.ascontiguousarray(np.concatenate([full[:, 1024 * r:1024 * (r + 1)], full[:, 8192 + TC * r:8192 + TC * (r + 1)]], axis=1))


def build_mla_att():
    kb = KB()
    qn = kb.inp("qn", [2, 128, NTOK], BF16)
    qpe = kb.inp("qpe", [128, NTOK], BF16)
    kn = kb.inp("kn", [2, 128, NTOK], BF16)
    kpe = kb.inp("kpe", [128, NTOK], BF16)
    v = kb.inp("v", [2, 128, 66, 128], BF16)
    oT = kb.out("oT", [2, 128, NTOK], BF16)
    qn_sb = [kb.sb(f"qn{i}", [128, NTOK], BF16) for i in range(2)]
    kn_sb = [kb.sb(f"kn{i}", [128, NTOK], BF16) for i in range(2)]
    v_sb = [kb.sb(f"v{i}", [128, 66, 128], BF16) for i in range(2)]
    qpe_sb = kb.sb("qpe", [128, NTOK], BF16)
    kpe_sb = kb.sb("kpe", [128, NTOK], BF16)
    o_sb = [kb.sb(f"o{i}", [128, NTOK], BF16) for i in range(2)]
    H = NTOK // 2
    for i in range(2):
        for hf in range(2):
            kb.load(kn_sb[i], kn_sb[i][:, hf * H:(hf + 1) * H], kn[i, :, hf * H:(hf + 1) * H], q="sp")
            kb.load(qn_sb[i], qn_sb[i][:, hf * H:(hf + 1) * H], qn[i, :, hf * H:(hf + 1) * H], q="pool")
            kb.load(v_sb[i], v_sb[i][:, 33 * hf:33 * (hf + 1), :], v[i, :, 33 * hf:33 * (hf + 1), :], q="sp")
        if i == 0:
            for hf in range(2):
                kb.load(kpe_sb, kpe_sb[:, hf * H:(hf + 1) * H], kpe[:, hf * H:(hf + 1) * H], q="pool")
                kb.load(qpe_sb, qpe_sb[:, hf * H:(hf + 1) * H], qpe[:, hf * H:(hf + 1) * H], q="pool")
    ones = kb.sb("onesb", [128, 128], BF16)
    kb.op("pool", lambda e: e.memset(ones[:], 1.0), writes=[ones])
    kb._pool = [0, 1, 2, 3]
    pts = [kb.sb(f"pt{i}", [128, 512], BF16) for i in range(4)]
    recs = [kb.sb(f"rec{i}", [128, 512], F32) for i in range(2)]
    paccs = [kb.sb(f"pacc{i}", [128, 512], F32) for i in range(2)]
    ones32 = kb.sb("ones32", [128, 128], F32)
    kb.op("pool", lambda e: e.memset(ones32[:], 1.0), writes=[ones32])
    scale = 192.0 ** -0.5
    it = 0
    npt = 0
    qtiles = [(512 * i, 512, list(range(66))) for i in range(16)] + [(8192, 256, [64, 65])]
    hmask = kb.const_load("hmask", [128, 2])
    qpm = kb.sb("qpm", [128, NTOK], BF16)
    for hh in range(2):
        K, Q, V, O = kn_sb[hh], qn_sb[hh], v_sb[hh], o_sb[hh]
        for hf in range(2):
            kb.op("dve", lambda e, hh=hh, hf=hf: e.tensor_scalar(qpm[:, hf * H:(hf + 1) * H], qpe_sb[:, hf * H:(hf + 1) * H], hmask[:, hh:hh + 1], None, ALU.mult),
                  reads=[qpe_sb, hmask], writes=[qpm])
        for (q0, N, blocks) in qtiles:
            pso = kb.bank(4 + it % 2)
            psd = kb.bank(6 + it % 2)
            rec = recs[it % 2]
            pacc = paccs[it % 2]
            it += 1
            def emit_s(j, N=N, K=K, Q=Q, q0=q0):
                pss = kb.ps()
                kb.op("pe", lambda e, pss=pss, j=j: e.matmul(pss[:, :N], K[:, j * 128:(j + 1) * 128], Q[:, q0:q0 + N], start=True, stop=False),
                      reads=[K, Q], writes=[pss])
                kb.op("pe", lambda e, pss=pss, j=j: e.matmul(pss[:, :N], kpe_sb[:, j * 128:(j + 1) * 128], qpm[:, q0:q0 + N], start=False, stop=True),
                      reads=[kpe_sb, qpm], writes=[pss], acc=True)
                return pss
            pendq = [emit_s(blocks[0])]
            if len(blocks) > 1:
                pendq.append(emit_s(blocks[1]))
            for bi, j in enumerate(blocks):
                pss = pendq.pop(0)
                pt = pts[npt % 4]
                npt += 1
                kb.op("act", lambda e, pt=pt, pss=pss, N=N: e.activation(pt[:, :N], pss[:, :N], AF.Exp, scale=scale), reads=[pss], writes=[pt])
                if bi + 2 < len(blocks):
                    pendq.append(emit_s(blocks[bi + 2]))
                first, last = (bi == 0), (bi == len(blocks) - 1)
                kb.op("pe", lambda e, pt=pt, j=j, first=first, last=last, N=N, V=V, pso=pso: e.matmul(pso[:, :N], V[:, j, :], pt[:, :N], start=first, stop=last),
                      reads=[V, pt], writes=[pso], acc=not first)
                if first:
                    kb.op("dve", lambda e, pt=pt, N=N, pacc=pacc: e.tensor_copy(pacc[:, :N], pt[:, :N]), reads=[pt], writes=[pacc])
                else:
                    kb.op("dve", lambda e, pt=pt, N=N, pacc=pacc: e.tensor_tensor(pacc[:, :N], pacc[:, :N], pt[:, :N], ALU.add), reads=[pt, pacc], writes=[pacc])
            kb.op("pe", lambda e, N=N, psd=psd, pacc=pacc: e.matmul(psd[:, :N], ones32[:], pacc[:, :N], start=True, stop=True), reads=[ones32, pacc], writes=[psd])
            kb.op("dve", lambda e, rec=rec, psd=psd, N=N: e.reciprocal(rec[:, :N], psd[:, :N]), reads=[psd], writes=[rec])
            kb.op("dve", lambda e, rec=rec, pso=pso, N=N, O=O, q0=q0: e.tensor_tensor(O[:, q0:q0 + N], pso[:, :N], rec[:, :N], ALU.mult), reads=[pso, rec], writes=[O])
        for hf in range(2):
            kb.store(oT, oT[hh, :, hf * H:(hf + 1) * H], O, O[:, hf * H:(hf + 1) * H], q="sp")
    kb.finish()
    return kb


def run_layer_mla(inp, mod, x2d, ctx2d, progs):
    li = 1
    w_in_re, w_qb_re, w_kvb_re, g = mla_weight_layouts(inp)
    w_in_re, w_qb_re, w_kvb_re = tile_weight(w_in_re), tile_weight(w_qb_re), tile_weight(w_kvb_re)
    maps = []
    for r in range(NCORES):
        m = {"xT": xT_input(x2d, ctx2d, r), "modv": modv_input(inp["norm_g"][li], mod[li, 0], mod[li, 1]),
             "gains": g, "w_in": w_in_re, "w_qb": w_qb_re, "w_kvb": w_kvb_re}
        m.update(proj_const_inputs(r, need64=True, rope=False))
        maps.append(m)
    res = run_prog(progs["proj_mla"], maps)
    names = ("qnT", "qpeT", "knT", "vT", "kpeT", "gT")
    full = {n: gather_tokens([np.asarray(res[r][n]) for r in range(NCORES)]) for n in names}
    return run_layer_mla_b(inp, mod, x2d, ctx2d, progs, full)


def run_layer_mla_b(inp, mod, x2d, ctx2d, progs, full):
    li = 1
    gT = [scatter_tokens(full["gT"], r) for r in range(NCORES)]
    maps = []
    for r in range(NCORES):
        hs = slice(256 * r, 256 * (r + 1))
        maps.append({"qn": np.ascontiguousarray(full["qnT"][hs].reshape(2, 128, NTOK)),
                     "qpe": np.ascontiguousarray(full["qpeT"][128 * r:128 * (r + 1)]),
                     "kn": np.ascontiguousarray(full["knT"][hs].reshape(2, 128, NTOK)),
                     "kpe": np.ascontiguousarray(full["kpeT"]),
                     "hmask": np.ascontiguousarray(np.stack([(np.arange(128) < 64), (np.arange(128) >= 64)], axis=1), dtype=np.float32),
                     "v": np.ascontiguousarray(full["vT"][hs].reshape(2, 128, 66, 128).transpose(0, 3, 2, 1))})
    res = run_prog(progs["mla_att"], maps)
    ofull = np.concatenate([np.asarray(res[r]["oT"]).reshape(256, NTOK) for r in range(NCORES)], axis=0)
    oT = [scatter_tokens(ofull, r) for r in range(NCORES)]
    xT = [xT_input(x2d, ctx2d, r) for r in range(NCORES)]
    xo = run_tail(progs["tail"], oT, gT, xT, mod[li, 0], mod[li, 1], inp["mla_w_out"][0])
    return split_xo(xo)


def build_diff_att(lam_init):
    kb = KB()
    q = kb.inp("q", [2, 128, NTOK], BF16)
    k = kb.inp("k", [2, 128, NTOK], BF16)
    v = kb.inp("v", [128, 66, 256], BF16)
    oT = kb.out("oT", [2, 128, NTOK], BF16)
    lqk = kb.const_load("lqk", [128, 4])
    sg = kb.const_load("subg", [128, 2])
    q_sb = [kb.sb(f"q{i}", [128, NTOK], BF16) for i in range(2)]
    k_sb = [kb.sb(f"k{i}", [128, NTOK], BF16) for i in range(2)]
    v_sb = kb.sb("v", [128, 66, 256], BF16)
    o_sb = kb.sb("o", [128, 2, NTOK], BF16)
    H = NTOK // 2
    for i in range(2):
        for hf in range(2):
            kb.load(k_sb[i], k_sb[i][:, hf * H:(hf + 1) * H], k[i, :, hf * H:(hf + 1) * H], q="sp")
            kb.load(q_sb[i], q_sb[i][:, hf * H:(hf + 1) * H], q[i, :, hf * H:(hf + 1) * H], q="pool")
    for hf in range(2):
        kb.load(v_sb, v_sb[:, 33 * hf:33 * (hf + 1), :], v[:, 33 * hf:33 * (hf + 1), :], q="sp")
    ones = kb.sb("onesb", [128, 128], BF16)
    kb.op("pool", lambda e: e.memset(ones[:], 1.0), writes=[ones])
    ones32 = kb.sb("ones32", [128, 128], F32)
    kb.op("pool", lambda e: e.memset(ones32[:], 1.0), writes=[ones32])
    prod = kb.sb("prod", [128, 2], F32)
    kb.op("dve", lambda e: e.tensor_tensor(prod[:, 0:1], lqk[:, 0:1], lqk[:, 1:2], ALU.mult), reads=[lqk], writes=[prod])
    kb.op("dve", lambda e: e.tensor_tensor(prod[:, 1:2], lqk[:, 2:3], lqk[:, 3:4], ALU.mult), reads=[lqk], writes=[prod])
    psl = kb.bank(0)
    kb.op("pe", lambda e: e.matmul(psl[:, 0:2], ones32[:], prod[:], start=True, stop=True), reads=[ones32, prod], writes=[psl])
    el = kb.sb("el", [128, 2], F32)
    kb.op("act", lambda e: e.activation(el[:], psl[:, 0:2], AF.Exp), reads=[psl], writes=[el])
    lam = kb.sb("lam", [128, 1], F32)
    kb.op("dve", lambda e: e.tensor_tensor(lam[:], el[:, 0:1], el[:, 1:2], ALU.subtract), reads=[el], writes=[lam])
    kb.op("dve", lambda e: e.tensor_scalar(lam[:], lam[:], float(lam_init), None, ALU.add), reads=[lam], writes=[lam])
    kb.op("dve", lambda e: e.tensor_scalar(sg[:], sg[:], float(1.0 - lam_init), None, ALU.mult), reads=[sg], writes=[sg])
    kb._pool = [0, 1, 7]
    pts = [kb.sb(f"pt{i}", [128, 512], BF16) for i in range(4)]
    recs = [kb.sb(f"rec{i}", [128, 512], F32) for i in range(2)]
    paccs = [kb.sb(f"pacc{i}", [128, 512], F32) for i in range(2)]
    ods = [kb.sb(f"od{i}", [128, 2, 256], F32) for i in range(2)]
    t2s = [kb.sb(f"t2{i}", [128, 256], F32) for i in range(2)]
    sqs = [kb.sb(f"sqd{i}", [128, 256], BF16) for i in range(2)]
    rss = [kb.sb(f"rs{i}", [128, 256], F32) for i in range(2)]
    scale = 128.0 ** -0.5
    NQ = 256
    qtiles = [(NQ * i, list(range(66))) for i in range(8192 // NQ)] + [(8192, [64, 65])]
    it = 0
    npt = 0
    for (q0, blocks) in qtiles:
        pso1 = kb.bank(2 + it % 2)
        pso2 = kb.bank(4 + it % 2)
        psd = kb.bank(6)
        rec, od, t2, rs = recs[it % 2], ods[it % 2], t2s[it % 2], rss[it % 2]
        pacc = paccs[it % 2]
        it += 1

        def emit_s(j, q0=q0):
            pss = kb.ps()
            kb.op("pe", lambda e, pss=pss, j=j: e.matmul(pss[:, 0:NQ], k_sb[0][:, j * 128:(j + 1) * 128], q_sb[0][:, q0:q0 + NQ], start=True, stop=True),
                  reads=[k_sb[0], q_sb[0]], writes=[pss])
            kb.op("pe", lambda e, pss=pss, j=j: e.matmul(pss[:, NQ:2 * NQ], k_sb[1][:, j * 128:(j + 1) * 128], q_sb[1][:, q0:q0 + NQ], start=True, stop=True),
                  reads=[k_sb[1], q_sb[1]], writes=[pss], acc=True)
            return pss
        pendq = [emit_s(blocks[0])]
        if len(blocks) > 1:
            pendq.append(emit_s(blocks[1]))
        for bi, j in enumerate(blocks):
            pss = pendq.pop(0)
            pt = pts[npt % 4]
            npt += 1
            kb.op("act", lambda e, pt=pt, pss=pss: e.activation(pt[:, :], pss[:, :], AF.Exp, scale=scale), reads=[pss], writes=[pt])
            if bi + 2 < len(blocks):
                pendq.append(emit_s(blocks[bi + 2]))
            first, last = (bi == 0), (bi == len(blocks) - 1)
            for mi, pso in enumerate((pso1, pso2)):
                for c in range(2):
                    kb.op("pe", lambda e, pt=pt, j=j, first=first, last=last, pso=pso, c=c, mi=mi: e.matmul(pso[:, c * NQ:(c + 1) * NQ], v_sb[:, j, c * 128:(c + 1) * 128],
                                                                                                 pt[:, mi * NQ:(mi + 1) * NQ], start=first, stop=last),
                          reads=[v_sb, pt], writes=[pso], acc=not (first and c == 0))
            if first:
                kb.op("dve", lambda e, pt=pt, pacc=pacc: e.tensor_copy(pacc[:, :], pt[:, :]), reads=[pt], writes=[pacc])
            else:
                kb.op("dve", lambda e, pt=pt, pacc=pacc: e.tensor_tensor(pacc[:, :], pacc[:, :], pt[:, :], ALU.add), reads=[pt, pacc], writes=[pacc])
        kb.op("pe", lambda e, psd=psd, pacc=pacc: e.matmul(psd[:, :], ones32[:], pacc[:, :], start=True, stop=True), reads=[ones32, pacc], writes=[psd])
        kb.op("dve", lambda e, rec=rec, psd=psd: e.reciprocal(rec[:, :], psd[:, :]), reads=[psd], writes=[rec])
        kb.op("dve", lambda e, rec=rec: e.tensor_scalar(rec[:, NQ:2 * NQ], rec[:, NQ:2 * NQ], lam[:, 0:1], None, ALU.mult), reads=[rec, lam], writes=[rec])
        for c in range(2):
            kb.op("dve", lambda e, rec=rec, od=od, pso1=pso1, c=c: e.tensor_tensor(od[:, c, :], pso1[:, c * NQ:(c + 1) * NQ], rec[:, 0:NQ], ALU.mult),
                  reads=[pso1, rec], writes=[od])
            kb.op("dve", lambda e, rec=rec, t2=t2, pso2=pso2, c=c: e.tensor_tensor(t2[:, :], pso2[:, c * NQ:(c + 1) * NQ], rec[:, NQ:2 * NQ], ALU.mult),
                  reads=[pso2, rec], writes=[t2])
            kb.op("dve", lambda e, od=od, t2=t2, c=c: e.tensor_tensor(od[:, c, :], od[:, c, :], t2[:, :], ALU.subtract), reads=[od, t2], writes=[od])
        pq = kb.ps()
        for c in range(2):
            sq = sqs[c]
            kb.op("act", lambda e, sq=sq, od=od, c=c: e.activation(sq[:, :], od[:, c, :], AF.Square), reads=[od], writes=[sq])
            kb.op("pe", lambda e, pq=pq, sq=sq, c=c: e.matmul(pq[:, 0:NQ], ones[:], sq[:, :], start=(c == 0), stop=(c == 1)), reads=[ones, sq], writes=[pq], acc=(c == 1))
        kb.rstd_from_ss(rs, rs[:, :], pq, pq[:, 0:NQ], 1.0 / 256)
        for c in range(2):
            kb.op("dve", lambda e, od=od, rs=rs, c=c, q0=q0: e.scalar_tensor_tensor(o_sb[:, c, q0:q0 + NQ], od[:, c, :], sg[:, c:c + 1], rs[:, :], ALU.mult, ALU.mult),
                  reads=[od, sg, rs], writes=[o_sb])
    for c in range(2):
        for hf in range(2):
            kb.store(oT, oT[c, :, hf * H:(hf + 1) * H], o_sb, o_sb[:, c, hf * H:(hf + 1) * H], q="sp")
    kb.finish()
    return kb


def run_layer_diff(inp, mod, x2d, ctx2d, progs):
    li = 3
    w_t = tile_weight(inp["diff_w_in"][0])
    maps = []
    for r in range(NCORES):
        m = {"xT": xT_input(x2d, ctx2d, r), "modv": modv_input(inp["norm_g"][li], mod[li, 0], mod[li, 1]),
             "gains": col2(inp["diff_q_g"][0], inp["diff_k_g"][0]), "w_in": w_t}
        m.update(proj_const_inputs(r))
        maps.append(m)
    res = run_prog(progs["proj_diff"], maps)
    full = {n: gather_tokens([np.asarray(res[r][n]) for r in range(NCORES)]) for n in ("qT", "kT", "vT", "gT")}
    return run_layer_diff_b(inp, mod, x2d, ctx2d, progs, full)


def run_layer_diff_b(inp, mod, x2d, ctx2d, progs, full):
    li = 3
    gT = [scatter_tokens(full["gT"], r) for r in range(NCORES)]
    lqk = np.ascontiguousarray(np.stack([inp["diff_lq1"][0], inp["diff_lk1"][0], inp["diff_lq2"][0], inp["diff_lk2"][0]], axis=1), dtype=np.float32)
    subg = np.ascontiguousarray(inp["diff_subln_g"][0].reshape(2, 128).T, dtype=np.float32)
    maps = []
    for r in range(NCORES):
        hs = slice(256 * r, 256 * (r + 1))
        maps.append({"q": np.ascontiguousarray(full["qT"][hs].reshape(2, 128, NTOK)),
                     "k": np.ascontiguousarray(full["kT"][hs].reshape(2, 128, NTOK)),
                     "v": np.ascontiguousarray(full["vT"][hs].reshape(256, 66, 128).transpose(2, 1, 0)),
                     "lqk": lqk, "subg": subg})
    res = run_prog(progs["diff_att"], maps)
    ofull = np.concatenate([np.asarray(res[r]["oT"]).reshape(256, NTOK) for r in range(NCORES)], axis=0)
    oT = [scatter_tokens(ofull, r) for r in range(NCORES)]
    xT = [xT_input(x2d, ctx2d, r) for r in range(NCORES)]
    xo = run_tail(progs["tail"], oT, gT, xT, mod[li, 0], mod[li, 1], inp["diff_w_out"][0])
    return split_xo(xo)


def build_proj_hyena():
    kb = KB()
    pc = ProjCtx(kb, 1026, 1)
    T_ = pc.ttot
    TO = 1024 + TC
    xT = kb.inp("xT", [2048, T_])
    modv = kb.const_load("modv", [128, 5, 16])
    cw = kb.const_load("cw", [128, 48, 4])
    hm = kb.const_load("hm", [128, 2])
    w = kb.inp("w_in", [32, 128, 16, 256])
    uT = kb.out("uT", [6144, TO], BF16)
    gT = kb.out("gT", [2048, TO], BF16)
    hT = emit_modnorm(pc, xT, modv)
    blocks = ([dict(kind="conv3", cw=cw, ci=i, out=uT, r0=128 * i, dt=BF16) for i in range(48)] + raw_blocks(16, gT, dt=BF16))
    ws = WStream(kb, "w", 16, 256)
    emit_proj_stage(pc, hT, 16, w, blocks, ws, {"hm": hm})
    kb.finish()
    return kb


LH = 8192
NF = 2 * LH
GC = 4
CPC = 256


def hyena_consts(L):
    n = np.arange(2 * L)
    pos = np.where(n < L, n, 2 * L - n).astype(np.float64)
    pos[L] = 0
    t = pos / max(L - 1, 1)
    bands = np.linspace(1e-4, 15, 16)
    ang = (2.0 * math.pi / L) * pos[:, None] * bands[None, :]
    z = np.concatenate([t[:, None], np.cos(ang), -np.sin(ang)], axis=-1)
    return np.ascontiguousarray(z.T, dtype=np.float32), t.astype(np.float32)


def hyena_deltas():
    max_decay = math.log(1e-2) / 0.3
    min_decay = math.log(1e-2) / 1.5
    return np.abs(np.linspace(min_decay, max_decay, 2048)).astype(np.float32)


def emit_sin(kb, tmp, out_ap, out_t, arg_ap, arg_t, shape_ap):
    s, c, t = tmp("sn_s"), tmp("sn_c"), tmp("sn_t")
    sa, ca, ta = shape_ap(s), shape_ap(c), shape_ap(t)
    kb.op("act", lambda e: e.activation(sa, arg_ap, AF.Sin, scale=0.125), reads=[arg_t], writes=[s])
    hp = shape_ap(kb.halfpi_t)
    kb.op("act", lambda e: e.activation(ca, arg_ap, AF.Sin, scale=0.125, bias=hp), reads=[arg_t, kb.halfpi_t], writes=[c])
    for k in range(3):
        last = (k == 2)
        if not last:
            kb.op("dve", lambda e: e.tensor_tensor(ta, sa, sa, ALU.mult), reads=[s], writes=[t])
        dst = out_ap if last else sa
        dst_t = out_t if last else s
        kb.op("dve", lambda e, dst=dst: e.scalar_tensor_tensor(dst, sa, 2.0, ca, ALU.mult, ALU.mult), reads=[s, c], writes=[dst_t])
        if not last:
            kb.op("dve", lambda e: e.tensor_scalar(ca, ta, -2.0, 1.0, ALU.mult, ALU.add), reads=[t], writes=[c])


def build_hyena_core(with_ctx=True, debug=False, ngroups_override=None, skip_b=False):
    kb = KB()
    tmps = {}

    def tmp(name, shape, dt=F32, n=2):
        if name not in tmps:
            tmps[name] = [[kb.sb(f"{name}{i}", shape, dt) for i in range(n)], 0]
        lst = tmps[name]
        t = lst[0][lst[1] % n]
        lst[1] += 1
        return t

    sig_in = [kb.inp(nm, [64, CPC, 128], BF16) for nm in ("v", "x1", "x2")]
    zout = kb.out("z", [64, CPC, 128], BF16)
    ZT = kb.inp("ZT", [33, NF])
    tprow = kb.inp("tprow", [1, NF])
    fw1 = kb.const_load("fw1", [33, 64])
    fb1f = kb.const_load("fb1f", [64, 2])
    fw2d = kb.const_load("fw2d", [64, 128])
    fb2f = kb.const_load("fb2f", [128, 2])
    w3s32 = kb.const_load("w3s", [128, 2, CPC])
    w3b = kb.sb("w3b", [128, 2, CPC], BF16)
    kb.op("dve", lambda e: e.tensor_copy(w3b[:], w3s32[:]), reads=[w3s32], writes=[w3b])
    ndelta = kb.const_load("deltac", [128, 2])
    kb.op("dve", lambda e: e.tensor_scalar(ndelta[:], ndelta[:], -1.0, None, ALU.mult), reads=[ndelta], writes=[ndelta])
    skipb = kb.const_load("skipb", [64, 2, CPC])
    halfpi = kb.sb("halfpi", [128, 1], F32)
    kb.op("pool", lambda e: e.memset(halfpi[:], math.pi / 2), writes=[halfpi])
    kb.halfpi, kb.halfpi_t = halfpi, halfpi

    def cbf(name, shape):
        t32 = kb.const_load(name, shape)
        tb = kb.sb(name + "b", shape, BF16)
        kb.op("dve", lambda e: e.tensor_copy(tb[:], t32[:]), reads=[t32], writes=[tb])
        return tb
    Ff = cbf("Ff", [128, 256])
    Fi1 = cbf("Fi1", [128, 256])
    Fi2 = cbf("Fi2", [128, 256])
    Fc = cbf("Fc", [128, 128])
    Fs = cbf("Fs", [128, 128])
    Fsn = cbf("Fsn", [128, 128])
    Tc = kb.const_load("Tc", [128, 128])
    Ts = kb.const_load("Ts", [128, 128])
    kcs = kb.out("kcs", [2, CPC, NF], F32) if debug else kb.scratch("kcs", [2, CPC, NF], F32)

    HT2 = kb.sb("HT2", [128, NF], BF16)
    CH = 2048
    def big(nm):
        if nm in ("kchA", "kchB"):
            lst = tmps["kch"][0]
            return lst[0] if nm == "kchA" else lst[1]
        return tmp(nm, [128, CH], F32, n=1)
    def mlp_block(z_dram_ap, wcols, dst_ap, dst_t):
        zt = tmp("zt", [33, CH], F32, n=1)
        kb.load(zt, zt[:, :wcols], z_dram_ap, q="sp")
        arg = big("arg")
        for sbk in range(wcols // 512):
            ps = kb.ps()
            sc = slice(sbk * 512, (sbk + 1) * 512)
            kb.op("pe", lambda e, ps=ps, zt=zt, sc=sc: e.matmul(ps[0:64, :], fw1[:, :], zt[:, sc], start=True, stop=True), reads=[fw1, zt], writes=[ps])
            kb.op("dve", lambda e, ps=ps, sc=sc, arg=arg: e.tensor_scalar(arg[0:64, sc], ps[0:64, :], fb1f[:, 0:1], fb1f[:, 1:2], ALU.add, ALU.mult),
                  reads=[ps, fb1f], writes=[arg])
        h1 = big("h1")
        emit_sin(kb, big, h1[0:64, :wcols], h1, arg[0:64, :wcols], arg, lambda t: t[0:64, :wcols] if t is not kb.halfpi_t else t[0:64, :])
        arg2 = big("arg")
        for sbk in range(wcols // 512):
            ps = kb.ps()
            sc = slice(sbk * 512, (sbk + 1) * 512)
            kb.op("pe", lambda e, ps=ps, h1=h1, sc=sc: e.matmul(ps[:, :], fw2d[:, :], h1[0:64, sc], start=True, stop=True), reads=[fw2d, h1], writes=[ps])
            kb.op("dve", lambda e, ps=ps, sc=sc, arg2=arg2: e.tensor_scalar(arg2[:, sc], ps[:, :], fb2f[:, 0:1], fb2f[:, 1:2], ALU.add, ALU.mult),
                  reads=[ps, fb2f], writes=[arg2])
        emit_sin(kb, big, dst_ap, dst_t, arg2[:, :wcols], arg2, lambda t: t[:, :wcols] if t is not kb.halfpi_t else t[:, :])

    for ch in range(NF // CH):
        cols = slice(ch * CH, (ch + 1) * CH)
        mlp_block(ZT[:, cols], CH, HT2[:, cols], HT2)
    kb.op("pool", lambda e: e.memset(HT2[0:64, LH:NF], 0.0), writes=[HT2])
    kb.op("pool", lambda e: e.memset(HT2[64:128, 0:LH + 1], 0.0), writes=[HT2])

    if debug:
        dH = kb.out("dHT2", [128, NF], BF16)
        kb.store(dH, dH[:], HT2, HT2[:])
    ssp = kb.sb("ssp", [128, 4, 8], F32)
    rc = kb.sb("rc", [128, 4], F32)
    for o in range(0 if not skip_b else 2, 2):
        for cb in range(2):
            oc = 2 * o + cb
            for ch in range(NF // CH):
                cols = slice(ch * CH, (ch + 1) * CH)
                tpb = big("sn_s")
                kb.load(tpb, tpb[:], tprow[0:1, cols].partition_broadcast(128), q="sp")
                dec = big("sn_c")
                kb.op("act", lambda e, dec=dec, tpb=tpb, cb=cb: e.activation(dec[:], tpb[:], AF.Exp, scale=ndelta[:, cb:cb + 1]), reads=[tpb, ndelta], writes=[dec])
                kch = tmp("kch", [128, CH], F32, n=2)
                for sbk in range(4):
                    ps = kb.ps()
                    sc = slice(sbk * 512, (sbk + 1) * 512)
                    gc = slice(ch * CH + sbk * 512, ch * CH + (sbk + 1) * 512)
                    kb.op("pe", lambda e, ps=ps, o=o, cb=cb, gc=gc: e.matmul(ps[:, :], w3b[:, o, cb * 128:(cb + 1) * 128], HT2[:, gc], start=True, stop=True),
                          reads=[w3b, HT2], writes=[ps])
                    kb.op("dve", lambda e, ps=ps, sc=sc, kch=kch, dec=dec: e.tensor_tensor(kch[:, sc], ps[:, :], dec[:, sc], ALU.mult), reads=[ps, dec], writes=[kch])
                sq = big("sn_t")
                kb.op("pool", lambda e, sq=sq, kch=kch: e.tensor_tensor(sq[:], kch[:], kch[:], ALU.mult), reads=[kch], writes=[sq])
                kb.op("dve", lambda e, sq=sq, oc=oc, ch=ch: e.reduce_sum(ssp[:, oc, ch:ch + 1], sq[:], AX.X), reads=[sq], writes=[ssp])
                kb.store(kcs, kcs[o, cb * 128:(cb + 1) * 128, cols], kch, kch[:], q="pool", is_output=False)
            kb.op("dve", lambda e, oc=oc: e.reduce_sum(rc[:, oc:oc + 1], ssp[:, oc, :], AX.X), reads=[ssp], writes=[rc])
            kb.rstd_from_ss(rc, rc[:, oc:oc + 1], rc, rc[:, oc:oc + 1], 1.0)
            for ch in range(NF // CH):
                cols = slice(ch * CH, (ch + 1) * CH)
                k2 = tmp("kch", [128, CH], F32, n=2)
                kb.load(k2, k2[:], kcs[o, cb * 128:(cb + 1) * 128, cols], q="sp", src=kcs)
                kb.op("dve", lambda e, k2=k2, oc=oc: e.tensor_scalar(k2[:], k2[:], rc[:, oc:oc + 1], None, ALU.mult), reads=[k2, rc], writes=[k2])
                kb.store(kcs, kcs[o, cb * 128:(cb + 1) * 128, cols], k2, k2[:], q="pool", is_output=False)


    if with_ctx:
        LC = 256
        uc = kb.inp("uc", [3, 2, 128, LC], BF16)
        cwc = kb.const_load("cwc", [128, 3, 2, 4])
        ZTc = kb.inp("ZTc", [33, 2 * LC])
        tpc = kb.const_load("tpc", [128, 4])
        kb.op("dve", lambda e: e.tensor_scalar(tpc[:], tpc[:], -1.0, None, ALU.mult), reads=[tpc], writes=[tpc])
        deltab = kb.const_load("deltab", [128, CPC])
        skipc = kb.const_load("skipc", [128, 2, CPC])
        ident = kb.const_load("ident", [128, 128])
        ones32 = kb.sb("ones32c", [128, 128], F32)
        kb.op("pool", lambda e: e.memset(ones32[:], 1.0), writes=[ones32])
        Dc_in = kb.inp("Dc", [128, 4, 512])
        Dsn_in = kb.inp("Dsn", [128, 4, 512])
        zc_out = kb.out("zc", [2, 128, CPC], BF16)
        Dcb = kb.sb("Dcb", [128, 4, 512], BF16)
        Dsnb = kb.sb("Dsnb", [128, 4, 512], BF16)
        for src_, dst_ in ((Dc_in, Dcb), (Dsn_in, Dsnb)):
            st_ = big("sn_s")
            kb.load(st_, st_[:].rearrange("p (a b) -> p a b", a=4), src_, q="sp")
            kb.op("dve", lambda e, st_=st_, dst_=dst_: e.tensor_copy(dst_[:].rearrange("p a b -> p (a b)"), st_[:]), reads=[st_], writes=[dst_])
        utT = big("kchA")
        ut = utT[:, 0:1536].rearrange("p (tb si c) -> p tb si c", tb=2, si=3)
        for si in range(3):
            for cb in range(2):
                ub = tmp("ucb", [128, LC], BF16, n=2)
                kb.load(ub, ub[:], uc[si, cb], q="sp")
                u32 = tmp("uc32", [128, LC], F32, n=2)
                kb.op("act", lambda e, u32=u32, ub=ub: e.copy(u32[:], ub[:]), reads=[ub], writes=[u32])
                o32 = tmp("oc32", [128, LC], F32, n=2)
                kb.op("dve", lambda e, o32=o32, u32=u32, si=si, cb=cb: e.tensor_scalar(o32[:], u32[:], cwc[:, si, cb, 1:2], cwc[:, si, cb, 3:4], ALU.mult, ALU.add),
                      reads=[u32, cwc], writes=[o32])
                kb.op("dve", lambda e, o32=o32, u32=u32, si=si, cb=cb: e.scalar_tensor_tensor(o32[:, 1:LC], u32[:, 0:LC - 1], cwc[:, si, cb, 0:1], o32[:, 1:LC], ALU.mult, ALU.add),
                      reads=[u32, cwc, o32], writes=[o32])
                kb.op("dve", lambda e, o32=o32, u32=u32, si=si, cb=cb: e.scalar_tensor_tensor(o32[:, 0:LC - 1], u32[:, 1:LC], cwc[:, si, cb, 2:3], o32[:, 0:LC - 1], ALU.mult, ALU.add),
                      reads=[u32, cwc, o32], writes=[o32])
                for tb in range(2):
                    ps = kb.ps()
                    kb.op("pe", lambda e, ps=ps, o32=o32, tb=tb: e.transpose(ps[:, 0:128], o32[:, tb * 128:(tb + 1) * 128], ident[:, :]), reads=[o32, ident], writes=[ps])
                    kb.op("act", lambda e, ps=ps, tb=tb, si=si, cb=cb: e.copy(ut[:, tb, si, cb * 128:(cb + 1) * 128], ps[:, 0:128]), reads=[ps], writes=[utT])
        HTc = kb.sb("HTc", [128, 2 * LC], BF16)
        mlp_block(ZTc[:, :], 2 * LC, HTc[:, :], HTc)
        kb.op("pool", lambda e: e.memset(HTc[0:64, LC:2 * LC], 0.0), writes=[HTc])
        kb.op("pool", lambda e: e.memset(HTc[64:128, 0:LC + 1], 0.0), writes=[HTc])
        kccT = big("sn_c")
        HcT = [big("sn_t"), big("kchB")]
        kccb = kb.sb("kccb", [128, 2, 4, CPC], BF16)
        for o in range(2):
            kcc = kccT[:, o * 1024:(o + 1) * 1024].rearrange("p (a b) -> p a b", a=4)
            pss = kb.ps()
            for nb in range(4):
                ps = kb.ps()
                kb.op("pe", lambda e, ps=ps, nb=nb, o=o: e.matmul(ps[:, 0:CPC], HTc[:, nb * 128:(nb + 1) * 128], w3b[:, o, :], start=True, stop=True), reads=[HTc, w3b], writes=[ps])
                dec = tmp("decc", [128, CPC], F32, n=2)
                kb.op("act", lambda e, dec=dec, nb=nb: e.activation(dec[:], deltab[:], AF.Exp, scale=tpc[:, nb:nb + 1]), reads=[deltab, tpc], writes=[dec])
                kb.op("dve", lambda e, ps=ps, dec=dec, kcc=kcc, nb=nb: e.tensor_tensor(kcc[:, nb, :], ps[:, 0:CPC], dec[:], ALU.mult), reads=[ps, dec], writes=[kccT])
                sq = tmp("sqc", [128, CPC], F32, n=2)
                kb.op("pool", lambda e, sq=sq, kcc=kcc, nb=nb: e.tensor_tensor(sq[:], kcc[:, nb, :], kcc[:, nb, :], ALU.mult), reads=[kccT], writes=[sq])
                kb.op("pe", lambda e, pss=pss, sq=sq, nb=nb: e.matmul(pss[:, 0:CPC], ones32[:], sq[:], start=(nb == 0), stop=(nb == 3)), reads=[ones32, sq], writes=[pss], acc=(nb != 0))
            rsc = tmp("rsc", [128, CPC], F32, n=2)
            kb.rstd_from_ss(rsc, rsc[:], pss, pss[:, 0:CPC], 1.0)
            for nb in range(4):
                kb.op("dve", lambda e, kcc=kcc, rsc=rsc, nb=nb, o=o: e.tensor_tensor(kccb[:, o, nb, :], kcc[:, nb, :], rsc[:], ALU.mult), reads=[kccT, rsc], writes=[kccb])
            Hc = HcT[o]
            for fb in range(4):
                for ri, Dm in enumerate((Dcb, Dsnb)):
                    ps = kb.ps()
                    for kbk in range(4):
                        kb.op("pe", lambda e, ps=ps, Dm=Dm, kbk=kbk, fb=fb, o=o: e.matmul(ps[:, 0:CPC], Dm[:, kbk, fb * 128:(fb + 1) * 128], kccb[:, o, kbk, :],
                                                                                     start=(kbk == 0), stop=(kbk == 3)),
                              reads=[Dm, kccb], writes=[ps], acc=(kbk != 0))
                    kb.op("act", lambda e, ps=ps, Hc=Hc, ri=ri, fb=fb: e.copy(Hc[:, ri * 1024 + fb * 256: ri * 1024 + (fb + 1) * 256], ps[:, 0:CPC]), reads=[ps], writes=[Hc])
        curb = kb.sb("curc", [128, 2, CPC], BF16)
        for tb in range(2):
            kb.op("act", lambda e, tb=tb: e.copy(curb[:, tb, :], ut[:, tb, 0, :]), reads=[utT], writes=[curb])
        cur32 = [ut[:, tb, 0, :] for tb in range(2)]
        cur32_t = utT
        zc32 = kb.sb("zc32", [128, 2, CPC], F32)
        Ycb = kb.sb("Ycb", [128, 2, 4, CPC], BF16)
        for o in range(2):
            Hc = HcT[o]
            for fb in range(4):
                pre, pim = kb.ps(), kb.ps()
                for ri, (Dm, pp) in enumerate(((Dcb, pre), (Dsnb, pim))):
                    for kbk in range(2):
                        kb.op("pe", lambda e, pp=pp, Dm=Dm, kbk=kbk, fb=fb: e.matmul(pp[:, 0:CPC], Dm[:, kbk, fb * 128:(fb + 1) * 128], curb[:, kbk, :],
                                                                                 start=(kbk == 0), stop=(kbk == 1)),
                              reads=[Dm, curb], writes=[pp], acc=(kbk != 0))
                hre = Hc[:, fb * 256:(fb + 1) * 256]
                him = Hc[:, 1024 + fb * 256:1024 + (fb + 1) * 256]
                t1, t2 = tmp("hmc", [128, CPC], F32, n=4), tmp("hmc", [128, CPC], F32, n=4)
                kb.op("dve", lambda e, t1=t1, pre=pre, hre=hre: e.tensor_tensor(t1[:], pre[:, 0:CPC], hre, ALU.mult), reads=[pre, Hc], writes=[t1])
                kb.op("dve", lambda e, t2=t2, pim=pim, him=him: e.tensor_tensor(t2[:], pim[:, 0:CPC], him, ALU.mult), reads=[pim, Hc], writes=[t2])
                kb.op("pool", lambda e, t1=t1, t2=t2, fb=fb: e.tensor_tensor(Ycb[:, 0, fb, :], t1[:], t2[:], ALU.subtract), reads=[t1, t2], writes=[Ycb])
                t3, t4 = tmp("hmc", [128, CPC], F32, n=4), tmp("hmc", [128, CPC], F32, n=4)
                kb.op("dve", lambda e, t3=t3, pre=pre, him=him: e.tensor_tensor(t3[:], pre[:, 0:CPC], him, ALU.mult), reads=[pre, Hc], writes=[t3])
                kb.op("dve", lambda e, t4=t4, pim=pim, hre=hre: e.tensor_tensor(t4[:], pim[:, 0:CPC], hre, ALU.mult), reads=[pim, Hc], writes=[t4])
                kb.op("pool", lambda e, t3=t3, t4=t4, fb=fb: e.tensor_tensor(Ycb[:, 1, fb, :], t3[:], t4[:], ALU.add), reads=[t3, t4], writes=[Ycb])
            for tb in range(2):
                py = kb.ps()
                n_ = 0
                for ri, Dm in enumerate((Dcb, Dsnb)):
                    for fb in range(4):
                        kb.op("pe", lambda e, py=py, Dm=Dm, fb=fb, tb=tb, ri=ri, n_=n_: e.matmul(py[:, 0:CPC], Dm[:, fb, tb * 128:(tb + 1) * 128], Ycb[:, ri, fb, :],
                                                                                        start=(n_ == 0), stop=(n_ == 7)),
                              reads=[Dm, Ycb], writes=[py], acc=(n_ != 0))
                        n_ += 1
                t = tmp("epc", [128, CPC], F32, n=2)
                c32 = cur32[tb]
                kb.op("pool", lambda e, t=t, c32=c32, o=o: e.tensor_tensor(t[:], c32, skipc[:, o, :], ALU.mult), reads=[cur32_t, skipc], writes=[t])
                kb.op("dve", lambda e, t=t, py=py: e.scalar_tensor_tensor(t[:], py[:, 0:CPC], 1.0 / (2 * LC), t[:], ALU.mult, ALU.add), reads=[py, t], writes=[t])
                kb.op("dve", lambda e, t=t, tb=tb, o=o: e.tensor_tensor(zc32[:, tb, :], t[:], ut[:, tb, 1 + o, :], ALU.mult), reads=[t, utT], writes=[zc32])
            if o == 0:
                for tb in range(2):
                    kb.op("act", lambda e, tb=tb: e.copy(curb[:, tb, :], zc32[:, tb, :]), reads=[zc32], writes=[curb])
                z1c = kb.sb("z1c", [128, 2, CPC], F32)
                kb.op("dve", lambda e: e.tensor_copy(z1c[:], zc32[:]), reads=[zc32], writes=[z1c])
                cur32 = [z1c[:, tb, :] for tb in range(2)]
                cur32_t = z1c
        zcb = kb.sb("zcb", [128, 2, CPC], BF16)
        kb.op("act", lambda e: e.copy(zcb[:], zc32[:]), reads=[zc32], writes=[zcb])
        kb.store(zc_out, zc_out.h.rearrange("tb p c -> p tb c"), zcb, zcb[:], q="sp")

    W = GC * 128
    kb._pool = [0, 1, 2, 3]
    kb.p.barrier()
    carve_src = [tmps[nm][0][0] for nm in ("arg", "h1", "sn_s", "sn_c", "sn_t")] + list(tmps["kch"][0])
    carve_pos = [0, 0]

    def carve(nelem32):
        ti, off = carve_pos
        if off + nelem32 > CH:
            ti, off = ti + 1, 0
        v = carve_src[ti].h[:, off:off + nelem32]
        carve_pos[0], carve_pos[1] = ti, off + nelem32
        return v

    def ctmp(name, shape, dt, n):
        if name not in tmps:
            lst = []
            nel = int(np.prod(shape[1:]))
            n32 = nel if dt == F32 else nel // 2
            for i in range(n):
                v = carve(n32)
                if dt != F32:
                    v = v.bitcast(BF16)
                if len(shape) == 3:
                    v = v.rearrange("p (a b) -> p a b", a=shape[1])
                elif len(shape) == 4:
                    v = v.rearrange("p (a b c) -> p a b c", a=shape[1], b=shape[2])
                lst.append(T(v, Buf(f"{name}{i}")))
            tmps[name] = [lst, 0]
        lst = tmps[name]
        t = lst[0][lst[1] % n]
        lst[1] += 1
        return t

    dbg_done = []

    def fwd_fft(src, src_ap_fn, K, par):
        A32 = ctmp("A32", [128, GC, 2, 128], F32, 3)
        for c2 in range(GC // 2):
            bank = kb.ps()
            for u in range(2):
                c = 2 * c2 + u
                kb.op("pe", lambda e, bank=bank, c=c, u=u: e.matmul(bank[:, u * 256:(u + 1) * 256], src_ap_fn(c), Ff[0:K, :], start=True, stop=True),
                      reads=[src, Ff], writes=[bank], acc=(u == 1))
            kb.op("act", lambda e, bank=bank, c2=c2, A32=A32: e.copy(A32[:, 2 * c2:2 * c2 + 2, :, :].rearrange("p c r k -> p (c r k)"), bank[:, :]), reads=[bank], writes=[A32])
            yield
        Ab = [ctmp("Ab", [128, GC, 128], BF16, 8) for _ in range(2)]
        if debug and K == 64 and not dbg_done:
            d_ = kb.out("dA32", [128, GC, 2, 128], F32)
            kb.store(d_, d_[:], A32, A32[:])
        yield from twiddle(A32, Ab, conj=False)
        if debug and K == 64 and not dbg_done:
            d_ = kb.out("dAb0", [128, GC, 128], BF16)
            kb.store(d_, d_[:], Ab[0], Ab[0][:])
        banks = []
        for qd in range(GC // 4):
            bre, bim = kb.bank(4 + 2 * par), kb.bank(5 + 2 * par)
            cs = slice(4 * qd, 4 * qd + 4)
            kb.op("pe", lambda e, bre=bre, cs=cs: e.matmul(bre[:, :], Fc[:, :], Ab[0][:, cs, :], start=True, stop=False), reads=[Fc, Ab[0]], writes=[bre])
            kb.op("pe", lambda e, bre=bre, cs=cs: e.matmul(bre[:, :], Fs[:, :], Ab[1][:, cs, :], start=False, stop=True), reads=[Fs, Ab[1]], writes=[bre], acc=True)
            kb.op("pe", lambda e, bim=bim, cs=cs: e.matmul(bim[:, :], Fc[:, :], Ab[1][:, cs, :], start=True, stop=False), reads=[Fc, Ab[1]], writes=[bim])
            kb.op("pe", lambda e, bim=bim, cs=cs: e.matmul(bim[:, :], Fsn[:, :], Ab[0][:, cs, :], start=False, stop=True), reads=[Fsn, Ab[0]], writes=[bim], acc=True)
            banks.append((bre, bim))
            yield
            if debug and K == 64 and not dbg_done and qd == 0:
                xs_ = kb.sb("dbgx", [128, 2, 512], F32)
                kb.op("dve", lambda e, bre=bre: e.tensor_copy(xs_[:, 0, :], bre[:, :]), reads=[bre], writes=[xs_])
                kb.op("dve", lambda e, bim=bim: e.tensor_copy(xs_[:, 1, :], bim[:, :]), reads=[bim], writes=[xs_])
                d_ = kb.out("dX", [128, 2, 512], F32)
                kb.store(d_, d_[:], xs_, xs_[:])
        if debug and K == 64:
            dbg_done.append(1)
        return banks

    def twiddle(A32, Ab, conj):
        are, aim = A32[:, :, 0, :], A32[:, :, 1, :]
        tcb = Tc[:, :].unsqueeze(1).to_broadcast([128, GC, 128])
        tsb = Ts[:, :].unsqueeze(1).to_broadcast([128, GC, 128])
        t1, t2 = ctmp("tw", [128, GC, 128], F32, 8), ctmp("tw", [128, GC, 128], F32, 8)
        kb.op("dve", lambda e: e.tensor_tensor(t1[:], are, tcb, ALU.mult), reads=[A32, Tc], writes=[t1])
        kb.op("pool", lambda e: e.tensor_tensor(t2[:], aim, tsb, ALU.mult), reads=[A32, Ts], writes=[t2])
        yield
        kb.op("dve", lambda e: e.tensor_tensor(Ab[0][:], t1[:], t2[:], ALU.subtract if conj else ALU.add), reads=[t1, t2], writes=[Ab[0]])
        t3, t4 = ctmp("tw", [128, GC, 128], F32, 8), ctmp("tw", [128, GC, 128], F32, 8)
        kb.op("dve", lambda e: e.tensor_tensor(t3[:], aim, tcb, ALU.mult), reads=[A32, Tc], writes=[t3])
        kb.op("pool", lambda e: e.tensor_tensor(t4[:], are, tsb, ALU.mult), reads=[A32, Ts], writes=[t4])
        yield
        kb.op("dve", lambda e: e.tensor_tensor(Ab[1][:], t3[:], t4[:], ALU.add if conj else ALU.subtract), reads=[t3, t4], writes=[Ab[1]])
        yield

    ngroups = CPC // GC if ngroups_override is None else ngroups_override

    def group_gen(g, par):
        c0 = g * GC
        sig = []
        for si in range(3):
            t = tmp(f"sig{si}", [64, GC, 128], BF16, n=2)
            kb.load(t, t[:], sig_in[si][:, c0:c0 + GC, :], q="sp")
            sig.append(t)
        H = []
        for o in range(2):
            k32 = tmp("k32", [128, GC, 128], F32, n=2)
            kb.load(k32, k32[:], kcs[o, c0:c0 + GC, :].rearrange("c (a b) -> a c b", b=128), q="pool", src=kcs)
            kbf = tmp("kbf", [128, GC, 128], BF16, n=4)
            kb.op("act", lambda e, kbf=kbf, k32=k32: e.copy(kbf[:], k32[:]), reads=[k32], writes=[kbf])
            banks = yield from fwd_fft(kbf, lambda c, kbf=kbf: kbf[:, c, :], 128, par)
            Hre, Him = tmp(f"Hre{o}", [128, GC, 128], F32, n=2), tmp(f"Him{o}", [128, GC, 128], F32, n=2)
            for qd, (bre, bim) in enumerate(banks):
                cs = slice(4 * qd, 4 * qd + 4)
                kb.op("act", lambda e, bre=bre, cs=cs, Hre=Hre: e.copy(Hre[:, cs, :].rearrange("p c k -> p (c k)"), bre[:, :]), reads=[bre], writes=[Hre])
                kb.op("act", lambda e, bim=bim, cs=cs, Him=Him: e.copy(Him[:, cs, :].rearrange("p c k -> p (c k)"), bim[:, :]), reads=[bim], writes=[Him])
            H.append((Hre, Him))
            yield
            if debug and g == 0:
                for nm_, t_ in ((f"dHre{o}", Hre), (f"dHim{o}", Him)):
                    d_ = kb.out(nm_, [128, GC, 128], F32)
                    kb.store(d_, d_[:], t_, t_[:])
        cur = sig[0]
        for o in range(2):
            Hre, Him = H[o]
            banks = yield from fwd_fft(cur, lambda c, cur=cur: cur[0:64, c, :], 64, par)
            Yb = [tmp("Yb", [128, GC, 128], BF16, n=4) for _ in range(2)]
            for qd, (bre, bim) in enumerate(banks):
                cs = slice(4 * qd, 4 * qd + 4)
                fl = lambda t, cs=cs: t[:, cs, :].rearrange("p c k -> p (c k)")
                t1, t2 = ctmp("hm", [128, 512], F32, 8), ctmp("hm", [128, 512], F32, 8)
                kb.op("dve", lambda e, t1=t1, bre=bre, fl=fl, Hre=Hre: e.tensor_tensor(t1[:], bre[:, :], fl(Hre), ALU.mult), reads=[bre, Hre], writes=[t1])
                kb.op("dve", lambda e, t2=t2, bim=bim, fl=fl, Him=Him: e.tensor_tensor(t2[:], bim[:, :], fl(Him), ALU.mult), reads=[bim, Him], writes=[t2])
                kb.op("pool", lambda e, t1=t1, t2=t2, fl=fl, Y0=Yb[0]: e.tensor_tensor(fl(Y0), t1[:], t2[:], ALU.subtract), reads=[t1, t2], writes=[Yb[0]])
                t3, t4 = ctmp("hm", [128, 512], F32, 8), ctmp("hm", [128, 512], F32, 8)
                kb.op("dve", lambda e, t3=t3, bre=bre, fl=fl, Him=Him: e.tensor_tensor(t3[:], bre[:, :], fl(Him), ALU.mult), reads=[bre, Him], writes=[t3])
                kb.op("dve", lambda e, t4=t4, bim=bim, fl=fl, Hre=Hre: e.tensor_tensor(t4[:], bim[:, :], fl(Hre), ALU.mult), reads=[bim, Hre], writes=[t4])
                kb.op("pool", lambda e, t3=t3, t4=t4, fl=fl, Y1=Yb[1]: e.tensor_tensor(fl(Y1), t3[:], t4[:], ALU.add), reads=[t3, t4], writes=[Yb[1]])
                yield
            B32 = ctmp("A32", [128, GC, 2, 128], F32, 3)
            for c2 in range(GC // 2):
                bank = kb.ps()
                for u in range(2):
                    c = 2 * c2 + u
                    kb.op("pe", lambda e, bank=bank, c=c, u=u, Y0=Yb[0]: e.matmul(bank[:, u * 256:(u + 1) * 256], Y0[:, c, :], Fi1[:, :], start=True, stop=False),
                          reads=[Yb[0], Fi1], writes=[bank], acc=(u == 1))
                    kb.op("pe", lambda e, bank=bank, c=c, u=u, Y1=Yb[1]: e.matmul(bank[:, u * 256:(u + 1) * 256], Y1[:, c, :], Fi2[:, :], start=False, stop=True),
                          reads=[Yb[1], Fi2], writes=[bank], acc=True)
                kb.op("act", lambda e, bank=bank, c2=c2, B32=B32: e.copy(B32[:, 2 * c2:2 * c2 + 2, :, :].rearrange("p c r k -> p (c r k)"), bank[:, :]), reads=[bank], writes=[B32])
                yield
            Bb = [ctmp("Ab", [128, GC, 128], BF16, 8) for _ in range(2)]
            yield from twiddle(B32, Bb, conj=True)
            xk = sig[1 + o]
            znew = tmp("zb", [64, GC, 128], BF16, n=2)
            for qd in range(GC // 4):
                yb = kb.bank(4 + 2 * par)
                cs = slice(4 * qd, 4 * qd + 4)
                kb.op("pe", lambda e, yb=yb, cs=cs, B0=Bb[0]: e.matmul(yb[0:64, :], Fc[:, 0:64], B0[:, cs, :], start=True, stop=False), reads=[Fc, Bb[0]], writes=[yb])
                kb.op("pe", lambda e, yb=yb, cs=cs, B1=Bb[1]: e.matmul(yb[0:64, :], Fsn[:, 0:64], B1[:, cs, :], start=False, stop=True), reads=[Fsn, Bb[1]], writes=[yb], acc=True)
                t = tmp("ep", [64, 4, 128], F32, n=2)
                sk = skipb[:, o, c0 + 4 * qd:c0 + 4 * qd + 4].unsqueeze(2).to_broadcast([64, 4, 128])
                kb.op("pool", lambda e, t=t, cs=cs, sk=sk, cur=cur: e.tensor_tensor(t[:], cur[0:64, cs, :], sk, ALU.mult), reads=[cur, skipb], writes=[t])
                kb.op("dve", lambda e, t=t, yb=yb: e.scalar_tensor_tensor(t[:].rearrange("p c k -> p (c k)"), yb[0:64, :], 1.0 / NF, t[:].rearrange("p c k -> p (c k)"), ALU.mult, ALU.add),
                      reads=[yb, t], writes=[t])
                kb.op("dve", lambda e, t=t, cs=cs, xk=xk, znew=znew: e.tensor_tensor(znew[:, cs, :], t[:], xk[0:64, cs, :], ALU.mult), reads=[t, xk], writes=[znew])
                yield
            if debug and g == 0:
                d_ = kb.out(f"dz{o}", [64, GC, 128], BF16)
                kb.store(d_, d_[:], znew, znew[:])
                d_ = kb.out(f"dY{o}", [128, GC, 128], BF16)
                kb.store(d_, d_[:], Yb[0], Yb[0][:])
                d_ = kb.out(f"dB{o}", [128, GC, 128], BF16)
                kb.store(d_, d_[:], Bb[0], Bb[0][:])
            cur = znew
        kb.store(zout, zout[:, c0:c0 + GC, :], cur, cur[:], q="pool")

    NCH = 2
    for g2 in range(0, ngroups, NCH):
        gens = [group_gen(g2 + p, p % 2) for p in range(NCH) if g2 + p < ngroups]
        while gens:
            for gen in list(gens):
                try:
                    next(gen)
                except StopIteration:
                    gens.remove(gen)
    kb.finish()
    return kb


def dft_consts():
    a = np.arange(128)
    th = 2 * math.pi * np.outer(a, a) / 128.0
    c, s = np.cos(th), np.sin(th)
    ph = 2 * math.pi * np.outer(a, a) / NF
    d = {"Ff": np.concatenate([c, -s], 1), "Fi1": np.concatenate([c, s], 1), "Fi2": np.concatenate([-s, c], 1),
         "Fc": c, "Fs": s, "Fsn": -s, "Tc": np.cos(ph), "Ts": np.sin(ph)}
    return {k: np.ascontiguousarray(v, dtype=np.float32) for k, v in d.items()}


def hyena_core_inputs(inp, r, u_main, u_ctx):
    sl = slice(CPC * r, CPC * (r + 1))
    d = {}
    d["uc"] = np.ascontiguousarray(np.stack([u_ctx[si * 2048 + CPC * r: si * 2048 + CPC * (r + 1)].reshape(2, 128, 256) for si in range(3)], axis=0))
    cwv = np.concatenate([inp["hyena_conv_w"][0], inp["hyena_conv_b"][0][None, :]], axis=0).reshape(4, 3, 2048)[:, :, sl]
    d["cwc"] = np.ascontiguousarray(cwv.reshape(4, 3, 2, 128).transpose(3, 1, 2, 0), dtype=np.float32)
    ZTc, tc = hyena_consts(256)
    d["ZTc"] = ZTc
    d["tpc"] = np.ascontiguousarray(tc.reshape(4, 128).T, dtype=np.float32)
    d["deltab"] = np.ascontiguousarray(np.broadcast_to(hyena_deltas()[sl][None, :], (128, CPC)), dtype=np.float32)
    d["skipc"] = np.ascontiguousarray(np.broadcast_to(inp["hyena_skip"][0][None, :, sl], (128, 2, CPC)), dtype=np.float32)
    d["ident"] = np.eye(128, dtype=np.float32)
    nn = np.arange(512)
    th = 2 * math.pi * np.outer(nn, nn) / 512.0
    d["Dc"] = np.ascontiguousarray(np.cos(th).reshape(4, 128, 512).transpose(1, 0, 2), dtype=np.float32)
    d["Dsn"] = np.ascontiguousarray((-np.sin(th)).reshape(4, 128, 512).transpose(1, 0, 2), dtype=np.float32)
    for si, nm in enumerate(("v", "x1", "x2")):
        a = u_main[si * 2048 + CPC * r: si * 2048 + CPC * (r + 1)]
        d[nm] = np.ascontiguousarray(a.reshape(CPC, 64, 128).transpose(1, 0, 2))
    ZT, t = hyena_consts(LH)
    d["ZT"] = ZT
    d["tprow"] = np.ascontiguousarray(t[None, :])
    f32 = lambda a: np.ascontiguousarray(a, dtype=np.float32)
    d["fw1"] = f32(inp["hyena_f_w1"][0])
    d["fb1f"] = f32(np.stack([inp["hyena_f_b1"][0], inp["hyena_f_freq"][0, 0]], axis=1))
    w2 = inp["hyena_f_w2"][0]
    d["fw2d"] = f32(np.concatenate([w2, w2], axis=1))
    d["fb2f"] = f32(np.stack([np.tile(inp["hyena_f_b2"][0], 2), np.tile(inp["hyena_f_freq"][0, 1], 2)], axis=1))
    w3 = inp["hyena_f_w3"][0].reshape(64, 2, 2, 2048)
    d["w3s"] = f32(np.concatenate([w3[:, :, 0, sl], w3[:, :, 1, sl]], axis=0))
    d["deltac"] = f32(hyena_deltas()[sl].reshape(2, 128).T)
    d["skipb"] = f32(np.broadcast_to(inp["hyena_skip"][0][None, :, sl], (64, 2, CPC)))
    d.update(dft_consts())
    return d


def run_layer_hyena(inp, mod, x2d, ctx2d, progs):
    li = 2
    w_t = tile_weight(inp["hyena_w_in"][0])
    cwv = np.concatenate([inp["hyena_conv_w"][0], inp["hyena_conv_b"][0][None, :]], axis=0)
    cw = np.ascontiguousarray(cwv.reshape(4, 48, 128).transpose(2, 1, 0), dtype=np.float32)
    maps = []
    for r in range(NCORES):
        hm = np.ones((128, 2), np.float32)
        if r == 0:
            hm[:, 0] = 0
        if r == NCORES - 1:
            hm[:, 1] = 0
        maps.append({"xT": xT_input(x2d, ctx2d, r, halo=1), "modv": modv_input(inp["norm_g"][li], mod[li, 0], mod[li, 1]),
                     "cw": cw, "hm": hm, "w_in": w_t})
    res = run_prog(progs["proj_hyena"], maps)
    full = {n: gather_tokens([np.asarray(res[r][n]) for r in range(NCORES)]) for n in ("uT", "gT")}
    return run_layer_hyena_b(inp, mod, x2d, ctx2d, progs, full)


def run_layer_hyena_b(inp, mod, x2d, ctx2d, progs, full):
    li = 2
    u_main = full["uT"][:, :8192]
    u_ctx = full["uT"][:, 8192:]
    maps = [hyena_core_inputs(inp, r, u_main, u_ctx) for r in range(NCORES)]
    res = run_prog(progs["hyena_core"], maps)
    zmain = np.concatenate([np.asarray(res[r]["z"]).transpose(1, 0, 2).reshape(CPC, 8192) for r in range(NCORES)], axis=0)
    zc = np.concatenate([np.asarray(res[r]["zc"]).reshape(256, CPC).T for r in range(NCORES)], axis=0)
    zfull = np.concatenate([zmain, zc], axis=1)
    oT = [scatter_tokens(zfull, r) for r in range(NCORES)]
    gT = [scatter_tokens(full["gT"], r) for r in range(NCORES)]
    xT = [xT_input(x2d, ctx2d, r) for r in range(NCORES)]
    xo = run_tail(progs["tail"], oT, gT, xT, mod[li, 0], mod[li, 1], inp["hyena_w_out"][0])
    return split_xo(xo)


def kernel(**inputs):
    inp = {k: np.asarray(v) for k, v in inputs.items()}
    mod = run_mod(inp)
    x2d = np.ascontiguousarray(inp["x"][0], dtype=np.float32)
    ctx2d = np.ascontiguousarray(inp["ctx"][0], dtype=np.float32)
    progs = {"proj_swa": build_proj_swa(4), "swa_att": build_swa_att(), "tail": build_tail()}
    x2d, ctx2d = run_layer_swa(inp, mod, x2d, ctx2d, progs)
    progs = {"proj_mla": build_proj_mla(), "mla_att": build_mla_att(), "tail": build_tail()}
    x2d, ctx2d = run_layer_mla(inp, mod, x2d, ctx2d, progs)
    progs = {"proj_hyena": build_proj_hyena(), "hyena_core": build_hyena_core(), "tail": build_tail()}
    x2d, ctx2d = run_layer_hyena(inp, mod, x2d, ctx2d, progs)
    lam_init = 0.8 - 0.6 * math.exp(-0.3 * 3)
    progs = {"proj_diff": build_proj_swa(16), "diff_att": build_diff_att(lam_init), "tail": build_tail()}
    x2d, ctx2d = run_layer_diff(inp, mod, x2d, ctx2d, progs)
    return np.ascontiguousarray(x2d[None], dtype=np.float32)


def tile_weight(w, ncol=256):
    K, N = w.shape
    ng = (N + ncol - 1) // ncol
    wp = np.zeros((K, ng * ncol), np.float32)
    wp[:, :N] = w
    return np.ascontiguousarray(wp.reshape(K // 128, 128, ng, ncol).transpose(2, 1, 0, 3))
```

```python
import math
import numpy as np
import ml_dtypes
import concourse.bass as bass
import concourse.mybir as mybir
from concourse.bass_utils import run_bass_kernel_spmd

F32 = mybir.dt.float32
BF16 = mybir.dt.bfloat16
ALU = mybir.AluOpType
AF = mybir.ActivationFunctionType
AX = mybir.AxisListType
NCORES = 8


class Buf:
    _n = 0

    def __init__(self, name):
        Buf._n += 1
        self.name = f"{name}_{Buf._n}"
        self.writes = {}
        self.reads = {}
        self.dsem = None
        self.dcnt = 0


class Prog:
    def __init__(self, nc):
        self.nc = nc
        self.lists = {e: [] for e in ("pe", "act", "dve", "pool", "sp")}
        self.esem = {}
        self.ecnt = {e: 0 for e in self.lists}
        self.seen = {e: {} for e in self.lists}
        self.sems = {}
        self.out_events = []
        self.dbufs = []
        for e in ("pe", "act", "dve", "pool"):
            self.esem[e] = nc.alloc_semaphore(f"es_{e}")
            self.sems[("e", e)] = self.esem[e]

    def _wait(self, eng, ev):
        if ev is None:
            return
        key, val = ev
        if self.seen[eng].get(key, 0) >= val:
            return
        self.seen[eng][key] = val
        sem = self.sems[key]
        self.lists[eng].append(lambda e, sem=sem, val=val: e.wait_ge(sem, val))

    def _deps(self, eng, reads, writes, acc=False):
        for b in reads:
            for k, v in b.writes.items():
                self._wait(eng, (k, v))
        for b in writes:
            for k, v in b.writes.items():
                if acc and k == ("e", eng):
                    continue
                self._wait(eng, (k, v))
            for k, v in b.reads.items():
                self._wait(eng, (k, v))

    def _mark(self, ev, reads, writes):
        k, v = ev
        for b in reads:
            if b.reads.get(k, 0) < v:
                b.reads[k] = v
        for b in writes:
            if b.writes.get(k, 0) < v:
                b.writes[k] = v
            b.reads = {}

    def op(self, eng, fn, reads=(), writes=(), acc=False):
        self._deps(eng, reads, writes, acc)
        sem = self.esem[eng]
        self.ecnt[eng] += 1
        ev = (("e", eng), self.ecnt[eng])
        self.lists[eng].append(lambda e, fn=fn, sem=sem: fn(e).then_inc(sem, 1))
        self._mark(ev, reads, writes)
        return ev

    def dma(self, q, out, in_, sb, reads=(), writes=(), is_output=False, **kw):
        self._deps(q, reads, writes)
        if sb.dsem is None:
            sb.dsem = self.nc.alloc_semaphore(f"ds_{sb.name}")
            self.sems[("d", sb.name)] = sb.dsem
        sb.dcnt += 1
        if not hasattr(self, "dbufs"):
            self.dbufs = []
        if sb not in self.dbufs:
            self.dbufs.append(sb)
        ev = (("d", sb.name), 16 * sb.dcnt)
        sem = sb.dsem
        self.lists[q].append(lambda e, out=out, in_=in_, sem=sem, kw=kw: e.dma_start(out=out, in_=in_, **kw).then_inc(sem, 16))
        self._mark(ev, reads, writes)
        if is_output:
            self.out_events.append(ev)
        return ev

    def barrier(self):
        for eng in self.lists:
            for e2, cnt in self.ecnt.items():
                if e2 in self.esem and cnt > 0:
                    self._wait(eng, (("e", e2), cnt))
            for b in self.dbufs:
                self._wait(eng, (("d", b.name), 16 * b.dcnt))

    def finish(self):
        for ev in self.out_events:
            self._wait("sp", ev)
        nc = self.nc
        lists = self.lists
        with nc.Block() as block:
            @block.tensor
            def _(e):
                for f in lists["pe"]:
                    f(e)

            @block.scalar
            def _(e):
                for f in lists["act"]:
                    f(e)

            @block.vector
            def _(e):
                for f in lists["dve"]:
                    f(e)

            @block.gpsimd
            def _(e):
                for f in lists["pool"]:
                    f(e)

            @block.sync
            def _(e):
                for f in lists["sp"]:
                    f(e)


class T:
    def __init__(self, h, b):
        self.h = h
        self.b = b

    def __getitem__(self, key):
        return self.h[key]


class KB:
    def __init__(self):
        self.nc = bass.Bass("TRN2", target_bir_lowering=False)
        self.p = Prog(self.nc)
        self.in_names = []
        self.out_names = []
        self._ps = [T(self.nc.alloc_psum_tensor(f"psb{i}", [128, 512], F32), Buf(f"psb{i}")) for i in range(8)]
        self._psi = 0
        self._n = 0

    def inp(self, name, shape, dt=F32):
        self.in_names.append(name)
        return self.nc.dram_tensor(name, list(shape), dt, kind="ExternalInput").ap()

    def out(self, name, shape, dt=F32):
        self.out_names.append(name)
        return T(self.nc.dram_tensor(name, list(shape), dt, kind="ExternalOutput").ap(), Buf(name))

    def scratch(self, name, shape, dt=F32):
        return T(self.nc.dram_tensor(name, list(shape), dt, kind="Internal").ap(), Buf(name))

    def sb(self, name, shape, dt=F32):
        self._n += 1
        nm = f"{name}_{self._n}"
        return T(self.nc.alloc_sbuf_tensor(nm, list(shape), dt), Buf(nm))

    def ps(self):
        pool = getattr(self, "_pool", list(range(8)))
        t = self._ps[pool[self._psi % len(pool)]]
        self._psi += 1
        return t

    def bank(self, i):
        return self._ps[i]

    def op(self, eng, fn, reads=(), writes=(), acc=False):
        return self.p.op(eng, fn, [t.b for t in reads], [t.b for t in writes], acc)

    def load(self, dst, dst_ap, src_ap, q="sp", src=None, **kw):
        rd = [src.b] if src is not None else []
        return self.p.dma(q, dst_ap, src_ap, dst.b, reads=rd, writes=[dst.b], **kw)

    def store(self, dst, dst_ap, src, src_ap, q="sp", is_output=True, **kw):
        return self.p.dma(q, dst_ap, src_ap, src.b, reads=[src.b], writes=[dst.b], is_output=is_output, **kw)

    def finish(self):
        self.p.finish()
        return self.nc

    def const_load(self, name, shape, dt=F32, q="sp"):
        ap = self.inp(name, shape, dt)
        t = self.sb(name, shape, dt)
        self.load(t, t[:], ap, q=q)
        return t

    def rstd_from_ss(self, dst, dst_ap, ss, ss_ap, inv_n, eps=1e-6):
        self.op("dve", lambda e: e.tensor_scalar(dst_ap, ss_ap, inv_n, eps, ALU.mult, ALU.add), reads=[ss], writes=[dst])
        self.op("act", lambda e: e.activation(dst_ap, dst_ap, AF.Sqrt), reads=[dst], writes=[dst])
        self.op("dve", lambda e: e.reciprocal(dst_ap, dst_ap), reads=[dst], writes=[dst])


def run_prog(kb, in_maps):
    res = run_bass_kernel_spmd(kb.nc, in_maps, core_ids=list(range(NCORES)))
    return res.results


class WStream:
    def __init__(self, kb, name, kc, ncol_max, cast_eng="pool"):
        self.kb = kb
        self.kc = kc
        self.ncol_max = ncol_max
        self.stg = [kb.sb(f"{name}_stg{i}", [128, kc, ncol_max], F32) for i in range(2)]
        self.wb = [kb.sb(f"{name}_wb{i}", [128, kc, ncol_max], BF16) for i in range(2)]
        self.n = 0
        self.cast_eng = cast_eng

    def issue(self, w_ap, c0, ncol):
        kb = self.kb
        s = self.n % 2
        self.n += 1
        stg, wb = self.stg[s], self.wb[s]
        gi = c0 // self.ncol_max
        h = max(1, self.kc // 2)
        for k0 in range(0, self.kc, h):
            kb.load(stg, stg[:, k0:k0 + h, :], w_ap[gi, :, k0:k0 + h, :], q="sp")
        eng = self.cast_eng
        kb.op(eng, lambda e: e.tensor_copy(wb[:, :, :ncol], stg[:, :, :ncol]), reads=[stg], writes=[wb])
        return wb


def build_mod():
    kb = KB()
    cs = kb.const_load("cs", [128, 16, 2])
    adaw = kb.inp("adaw", [4, 2048, 768])
    bias = kb.const_load("adab", [2, 4, 768])
    modo = kb.out("modo", [2, 4 * 768])
    kb.op("act", lambda e: e.activation(cs[:], cs[:], AF.Silu), reads=[cs], writes=[cs])
    wt = [kb.sb(f"adaw{i}", [128, 16, 768], F32) for i in range(2)]
    osb = kb.sb("osb", [2, 4 * 768], F32)
    for i in range(4):
        w = wt[i % 2]
        for g in range(4):
            kb.load(w, w[:, 4 * g:4 * g + 4, :], adaw[i, 512 * g:512 * (g + 1), :].rearrange("(kc p) n -> p kc n", p=128), q="sp")
        for nb in range(2):
            ps = kb.ps()
            for kc in range(16):
                kb.op("pe", lambda e, ps=ps, w=w, kc=kc, nb=nb: e.matmul(ps[0:2, 0:384], cs[:, kc, :], w[:, kc, nb * 384:(nb + 1) * 384],
                                                                       start=(kc == 0), stop=(kc == 15)),
                      reads=[cs, w], writes=[ps], acc=True)
            c0 = i * 768 + nb * 384
            kb.op("dve", lambda e, ps=ps, c0=c0, i=i, nb=nb: e.tensor_tensor(osb[0:2, c0:c0 + 384], ps[0:2, 0:384],
                                                                           bias[0:2, i, nb * 384:(nb + 1) * 384], ALU.add),
                  reads=[ps, bias], writes=[osb])
    kb.store(modo, modo[:], osb, osb[:])
    kb.finish()
    return kb


def run_mod(inp):
    kb = build_mod()
    c = inp["c"].reshape(2048)
    cc = inp["c_ctx"].reshape(2048)
    cs = np.stack([c.reshape(16, 128).T, cc.reshape(16, 128).T], axis=-1).astype(np.float32)
    maps = []
    for r in range(NCORES):
        sl = slice(768 * r, 768 * (r + 1))
        adab = np.ascontiguousarray(np.broadcast_to(inp["ada_b"][None, :, sl], (2, 4, 768)))
        maps.append({"cs": np.ascontiguousarray(cs), "adaw": np.ascontiguousarray(inp["ada_w"][:, :, sl]), "adab": adab})
    res = run_prog(kb, maps)
    mod = np.zeros((4, 2, 6144), np.float32)
    for r in range(NCORES):
        mod[:, :, 768 * r:768 * (r + 1)] = res[r]["modo"].reshape(2, 4, 768).transpose(1, 0, 2)
    return mod


TC = 32


def rope_tables(positions, rot_dim):
    row = (positions // 64).astype(np.float32)
    col = (positions % 64).astype(np.float32)
    half = rot_dim // 2
    inv = (1.0 / (10000.0 ** (np.arange(0, half, 2, dtype=np.float32) / half))).astype(np.float32)
    ar = row[:, None] * inv[None, :]
    ac = col[:, None] * inv[None, :]
    ang = np.concatenate([ar, ar, ac, ac], axis=-1)
    return np.cos(ang).T.astype(np.float32), np.sin(ang).T.astype(np.float32)


def rope_matrix(rot_dim, reps):
    half = rot_dim // 2
    qr = half // 2
    R = np.zeros((rot_dim, rot_dim), np.float32)
    for seg in range(2):
        o = seg * half
        for i in range(qr):
            R[o + qr + i, o + i] = -1.0
            R[o + i, o + qr + i] = 1.0
    full = np.zeros((rot_dim * reps, rot_dim * reps), np.float32)
    for r in range(reps):
        full[r * rot_dim:(r + 1) * rot_dim, r * rot_dim:(r + 1) * rot_dim] = R
    return full


def blockdiag_ones(bs):
    m = np.zeros((128, 128), np.float32)
    for r in range(128 // bs):
        m[r * bs:(r + 1) * bs, r * bs:(r + 1) * bs] = 1.0
    return m


class ProjCtx:
    def __init__(self, kb, tm, halo):
        self.kb = kb
        self.tm = tm
        self.halo = halo
        self.ttot = tm + TC
        tiles = []
        c = 0
        while c < tm:
            w = min(512, tm - c)
            tiles.append((c, w, False))
            c += w
        tiles.append((tm, TC, True))
        self.tiles = tiles
        self._tmp = {}

    def tmp(self, name, shape, dt, n=2):
        key = name
        if key not in self._tmp:
            self._tmp[key] = [[self.kb.sb(f"{name}{i}", shape, dt) for i in range(n)], 0]
        lst = self._tmp[key]
        t = lst[0][lst[1] % n]
        lst[1] += 1
        return t


def emit_modnorm(pc, xT_ap, modv):
    kb = pc.kb
    T_ = pc.ttot
    ones = kb.sb("ones32", [128, 128], F32)
    kb.op("pool", lambda e: e.memset(ones[:], 1.0), writes=[ones])
    AB = kb.sb("AB", [128, 2, 16], F32)
    for w_, (sc) in enumerate((2, 4)):
        kb.op("dve", lambda e, w_=w_, sc=sc: e.tensor_scalar(AB[:, w_, :], modv[:, sc, :], 1.0, None, ALU.add), reads=[modv], writes=[AB])
        kb.op("dve", lambda e, w_=w_: e.tensor_tensor(AB[:, w_, :], AB[:, w_, :], modv[:, 0, :], ALU.mult), reads=[AB, modv], writes=[AB])
    rstd = kb.sb("rstd", [128, T_], F32)
    pss = [kb.bank(i) for i in range(len(pc.tiles))]
    for kc in range(16):
        xc = pc.tmp("xc", [128, T_], F32, n=3)
        kb.load(xc, xc[:], xT_ap[128 * kc:128 * (kc + 1), :], q=("sp" if kc % 2 == 0 else "pool"))
        for ti, (c0, w, isc) in enumerate(pc.tiles):
            ps = pss[ti]
            sq = pc.tmp("sq32", [128, 512], F32)
            kb.op("act", lambda e, sq=sq, xc=xc, c0=c0, w=w: e.activation(sq[:, :w], xc[:, c0:c0 + w], AF.Square), reads=[xc], writes=[sq])
            kb.op("pe", lambda e, ps=ps, sq=sq, kc=kc, w=w: e.matmul(ps[:, :w], ones[:], sq[:, :w], start=(kc == 0), stop=(kc == 15)),
                  reads=[ones, sq], writes=[ps], acc=(kc != 0))
    for ti, (c0, w, isc) in enumerate(pc.tiles):
        kb.rstd_from_ss(rstd, rstd[:, c0:c0 + w], pss[ti], pss[ti][:, :w], 1.0 / 2048)
    hT = kb.sb("hT", [128, 16, T_], BF16)
    for kc in range(16):
        xc = pc.tmp("xc", [128, T_], F32, n=3)
        kb.load(xc, xc[:], xT_ap[128 * kc:128 * (kc + 1), :], q=("sp" if kc % 2 == 0 else "pool"))
        for (c0, w, isc) in pc.tiles:
            wi = 1 if isc else 0
            bi = 3 if isc else 1
            t = pc.tmp("mn32", [128, 512], F32)
            kb.op("dve", lambda e, t=t, xc=xc, kc=kc, c0=c0, w=w, wi=wi: e.scalar_tensor_tensor(t[:, :w], xc[:, c0:c0 + w], AB[:, wi, kc:kc + 1],
                                                                                             rstd[:, c0:c0 + w], ALU.mult, ALU.mult),
                  reads=[xc, AB, rstd], writes=[t])
            kb.op("act", lambda e, t=t, kc=kc, c0=c0, w=w, bi=bi: e.activation(hT[:, kc, c0:c0 + w], t[:, :w], AF.Identity, bias=modv[:, bi, kc:kc + 1]),
                  reads=[t, modv], writes=[hT])
    return hT


def emit_epilogue_gen(pc, blk, accs, C):
    kb = pc.kb
    kind = blk["kind"]
    T_ = pc.ttot
    if kind == "keep":
        dst, idx = blk["dst"], blk["idx"]
        for ti, (c0, w, isc) in enumerate(pc.tiles):
            ps = accs[ti]
            eng = "act" if ti % 2 == 0 else "dve"
            if eng == "act":
                kb.op("act", lambda e, ps=ps, c0=c0, w=w: e.copy(dst[:, idx, c0:c0 + w], ps[:, :w]), reads=[ps], writes=[dst])
            else:
                kb.op("dve", lambda e, ps=ps, c0=c0, w=w: e.tensor_copy(dst[:, idx, c0:c0 + w], ps[:, :w]), reads=[ps], writes=[dst])
            yield
        return
    dt = blk.get("dt", F32)
    stage = pc.tmp("stg32" if dt == F32 else "stg16", [128, T_], dt, n=3)
    if kind == "resid":
        xs, gv, ob = blk["xs"], blk["gv"], blk["ob"]
        for ti, (c0, w, isc) in enumerate(pc.tiles):
            ps = accs[ti]
            wi = 1 if isc else 0
            kb.op("dve", lambda e, ps=ps, c0=c0, w=w, wi=wi: e.scalar_tensor_tensor(stage[:, c0:c0 + w], ps[:, :w], gv[:, wi, ob:ob + 1], xs[:, ob, c0:c0 + w], ALU.mult, ALU.add),
                  reads=[ps, gv, xs], writes=[stage])
            yield
    elif kind == "raw":
        for ti, (c0, w, isc) in enumerate(pc.tiles):
            ps = accs[ti]
            if ti % 2 == 0:
                kb.op("act", lambda e, ps=ps, c0=c0, w=w: e.copy(stage[:, c0:c0 + w], ps[:, :w]), reads=[ps], writes=[stage])
            else:
                kb.op("dve", lambda e, ps=ps, c0=c0, w=w: e.tensor_copy(stage[:, c0:c0 + w], ps[:, :w]), reads=[ps], writes=[stage])
            yield
    elif kind == "hn":
        bs, gain, rope = blk["bs"], blk["gain"], blk["rope"]
        onesb = C["ones128"] if bs == 128 else C["ones64"]

        def tile_gen(ti, c0, w, isc):
            ps = accs[ti]
            sqb = pc.tmp("sqb", [128, 512], BF16, n=3)
            kb.op("act", lambda e: e.activation(sqb[:, :w], ps[:, :w], AF.Square), reads=[ps], writes=[sqb])
            yield
            pss = kb.ps()
            kb.op("pe", lambda e: e.matmul(pss[:, :w], onesb[:], sqb[:, :w], start=True, stop=True), reads=[onesb, sqb], writes=[pss])
            yield
            t1 = pc.tmp("t1", [128, 512], F32, n=3)
            kb.op("dve", lambda e: e.tensor_scalar(t1[:, :w], pss[:, :w], 1.0 / bs, 1e-6, ALU.mult, ALU.add), reads=[pss], writes=[t1])
            yield
            kb.op("act", lambda e: e.activation(t1[:, :w], t1[:, :w], AF.Sqrt), reads=[t1], writes=[t1])
            yield
            kb.op("dve", lambda e: e.reciprocal(t1[:, :w], t1[:, :w]), reads=[t1], writes=[t1])
            yield
            if rope and not isc:
                yn = pc.tmp("yn", [128, 512], F32, n=3)
                kb.op("dve", lambda e: e.scalar_tensor_tensor(yn[:, :w], ps[:, :w], gain, t1[:, :w], ALU.mult, ALU.mult),
                      reads=[ps, t1, blk["gain_t"]], writes=[yn])
                yield
                ynb = pc.tmp("ynb", [128, 512], BF16, n=3)
                kb.op("act", lambda e: e.copy(ynb[:, :w], yn[:, :w]), reads=[yn], writes=[ynb])
                yield
                psr = kb.ps()
                Rm = C["rm128"] if bs == 128 else C["rm64"]
                kb.op("pe", lambda e: e.matmul(psr[:, :w], Rm[:], ynb[:, :w], start=True, stop=True), reads=[Rm, ynb], writes=[psr])
                cos, sin = (C["cos128"], C["sin128"]) if bs == 128 else (C["cos64"], C["sin64"])
                kb.op("pool", lambda e: e.tensor_tensor(yn[:, :w], yn[:, :w], cos[:, c0:c0 + w], ALU.mult), reads=[yn, cos], writes=[yn])
                yield
                o2 = pc.tmp("o2", [128, 512], F32, n=3)
                kb.op("dve", lambda e: e.tensor_tensor(o2[:, :w], psr[:, :w], sin[:, c0:c0 + w], ALU.mult), reads=[psr, sin], writes=[o2])
                yield
                kb.op("dve", lambda e: e.tensor_tensor(stage[:, c0:c0 + w], yn[:, :w], o2[:, :w], ALU.add), reads=[yn, o2], writes=[stage])
            else:
                kb.op("dve", lambda e: e.scalar_tensor_tensor(stage[:, c0:c0 + w], ps[:, :w], gain, t1[:, :w], ALU.mult, ALU.mult),
                      reads=[ps, t1, blk["gain_t"]], writes=[stage])
            yield

        main = [tile_gen(ti, c0, w, isc) for ti, (c0, w, isc) in enumerate(pc.tiles) if not isc]
        rest = [tile_gen(ti, c0, w, isc) for ti, (c0, w, isc) in enumerate(pc.tiles) if isc]
        for grp in (main[0:2], main[2:] + rest):
            gens = list(grp)
            while gens:
                for gen in list(gens):
                    try:
                        next(gen)
                    except StopIteration:
                        gens.remove(gen)
                yield
    elif kind == "conv3":
        cw = blk["cw"]
        ci = blk["ci"]
        hm = C["hm"]
        u = pc.tmp("u32", [128, T_], F32)
        for ti, (c0, w, isc) in enumerate(pc.tiles):
            ps = accs[ti]
            if ti % 2 == 0:
                kb.op("act", lambda e, ps=ps, c0=c0, w=w: e.copy(u[:, c0:c0 + w], ps[:, :w]), reads=[ps], writes=[u])
            else:
                kb.op("dve", lambda e, ps=ps, c0=c0, w=w: e.tensor_copy(u[:, c0:c0 + w], ps[:, :w]), reads=[ps], writes=[u])
            yield
        tm = pc.tm
        kb.op("dve", lambda e: e.tensor_scalar(u[:, 0:1], u[:, 0:1], hm[:, 0:1], None, ALU.mult), reads=[u, hm], writes=[u])
        kb.op("dve", lambda e: e.tensor_scalar(u[:, tm - 1:tm], u[:, tm - 1:tm], hm[:, 1:2], None, ALU.mult), reads=[u, hm], writes=[u])
        n = tm - 2
        yield
        kb.op("dve", lambda e: e.tensor_scalar(stage[:, 1:1 + n], u[:, 0:n], cw[:, ci, 0:1], cw[:, ci, 3:4], ALU.mult, ALU.add), reads=[u, cw], writes=[stage])
        yield
        kb.op("dve", lambda e: e.scalar_tensor_tensor(stage[:, 1:1 + n], u[:, 1:1 + n], cw[:, ci, 1:2], stage[:, 1:1 + n], ALU.mult, ALU.add),
              reads=[u, cw, stage], writes=[stage])
        yield
        kb.op("dve", lambda e: e.scalar_tensor_tensor(stage[:, 1:1 + n], u[:, 2:2 + n], cw[:, ci, 2:3], stage[:, 1:1 + n], ALU.mult, ALU.add),
              reads=[u, cw, stage], writes=[stage])
        kb.op("pool", lambda e: e.tensor_copy(stage[:, tm:tm + TC], u[:, tm:tm + TC]), reads=[u], writes=[stage])
    else:
        raise ValueError(kind)
    out, r0 = blk["out"], blk["r0"]
    h = pc.halo
    if h:
        kb.store(out, out[r0:r0 + 128, 0:pc.tm - 2], stage, stage[:, 1:pc.tm - 1], q="pool")
        kb.store(out, out[r0:r0 + 128, pc.tm - 2:pc.tm - 2 + TC], stage, stage[:, pc.tm:pc.tm + TC], q="pool")
    else:
        kb.store(out, out[r0:r0 + 128, :], stage, stage[:], q="pool")


def emit_proj_stage(pc, src, KC, w_ap, blocks, ws, C):
    kb = pc.kb
    BPG = ws.ncol_max // 128
    ngroups = (len(blocks) + BPG - 1) // BPG
    nt = len(pc.tiles)
    if nt <= 3:
        acc_sets = [[0, 1, 2], [3, 4, 5]]
        kb._pool = [6, 7]
    else:
        acc_sets = [[0, 1, 2, 3], [4, 5, 6, 7]]
    kb._psi = 0

    def issue(g):
        nb = min(BPG, len(blocks) - BPG * g)
        return ws.issue(w_ap, 128 * BPG * g, 128 * nb)

    def mm_gen(cur, bi, accs):
        for ti, (c0, w, isc) in enumerate(pc.tiles):
            ps = accs[ti]
            for kc in range(KC):
                kb.op("pe", lambda e, ps=ps, kc=kc, c0=c0, w=w: e.matmul(ps[:, :w], cur[:, kc, bi * 128:(bi + 1) * 128], src[:, kc, c0:c0 + w],
                                                                   start=(kc == 0), stop=(kc == KC - 1)),
                      reads=[cur, src], writes=[ps], acc=(kc != 0))
                if kc % 4 == 3:
                    yield

    def drive(gens):
        gens = [g_ for g_ in gens if g_ is not None]
        while gens:
            for gen in list(gens):
                try:
                    next(gen)
                except StopIteration:
                    gens.remove(gen)

    wt = {0: issue(0)}
    prev = None
    for b, blk in enumerate(blocks):
        g, bi = divmod(b, BPG)
        if bi == 0 and g + 1 < ngroups:
            wt[g + 1] = issue(g + 1)
        accs = [kb.bank(i) for i in acc_sets[b % 2]][:nt]
        drive([mm_gen(wt[g], bi, accs), prev])
        prev = emit_epilogue_gen(pc, blk, accs, C)
    drive([prev])
    kb._pool = list(range(8))


def emit_widenorm(pc, raw, nb, gain, dst, C, gcol0=0):
    kb = pc.kb
    for (c0, w, isc) in pc.tiles:
        ps = kb.ps()
        for b in range(nb):
            sqb = pc.tmp("sqb", [128, 512], BF16)
            kb.op("act", lambda e, sqb=sqb, b=b, c0=c0, w=w: e.activation(sqb[:, :w], raw[:, b, c0:c0 + w], AF.Square), reads=[raw], writes=[sqb])
            kb.op("pe", lambda e, ps=ps, sqb=sqb, b=b, w=w: e.matmul(ps[:, :w], C["ones128"][:], sqb[:, :w], start=(b == 0), stop=(b == nb - 1)),
                  reads=[C["ones128"], sqb], writes=[ps], acc=True)
        t1 = pc.tmp("t1", [128, 512], F32)
        kb.rstd_from_ss(t1, t1[:, :w], ps, ps[:, :w], 1.0 / (128 * nb))
        for b in range(nb):
            kb.op("dve", lambda e, b=b, t1=t1, c0=c0, w=w: e.scalar_tensor_tensor(dst[:, b, c0:c0 + w], raw[:, b, c0:c0 + w], gain[:, gcol0 + b:gcol0 + b + 1], t1[:, :w], ALU.mult, ALU.mult),
                  reads=[raw, gain, t1], writes=[dst])


def proj_consts(kb, need64=False, rope=True, tm=1024):
    C = {}
    def cbf(name, shape):
        t32 = kb.const_load(name, shape)
        tb = kb.sb(name + "b", shape, BF16)
        kb.op("dve", lambda e: e.tensor_copy(tb[:], t32[:]), reads=[t32], writes=[tb])
        return tb
    C["ones128"] = cbf("c_ones128", [128, 128])
    if rope:
        C["rm128"] = cbf("c_rm128", [128, 128])
        C["cos128"] = kb.const_load("c_cos128", [128, tm])
        C["sin128"] = kb.const_load("c_sin128", [128, tm], q="pool")
    if need64:
        C["ones64"] = cbf("c_ones64", [128, 128])
        C["rm64"] = cbf("c_rm64", [128, 128])
        C["cos64"] = kb.const_load("c_cos64", [128, tm])
        C["sin64"] = kb.const_load("c_sin64", [128, tm], q="pool")
    return C


def proj_const_inputs(r, need64=False, rope=True, tm=1024):
    d = {"c_ones128": blockdiag_ones(128)}
    pos = np.arange(1024 * r, 1024 * (r + 1))
    if rope:
        d["c_rm128"] = rope_matrix(128, 1)
        d["c_cos128"], d["c_sin128"] = rope_tables(pos, 128)
    if need64:
        d["c_ones64"] = blockdiag_ones(64)
        d["c_rm64"] = rope_matrix(64, 2)
        c, s = rope_tables(pos, 64)
        d["c_cos64"], d["c_sin64"] = np.concatenate([c, c], 0), np.concatenate([s, s], 0)
    return {k: np.ascontiguousarray(v, dtype=np.float32) for k, v in d.items()}


def modv_input(norm_g, mod_m, mod_c):
    def l(v):
        return v.reshape(16, 128).T
    return np.ascontiguousarray(np.stack([l(norm_g), l(mod_m[0:2048]), l(mod_m[2048:4096]), l(mod_c[0:2048]), l(mod_c[2048:4096])], axis=1), dtype=np.float32)


def xT_input(x2d, ctx2d, r, halo=0):
    lo, hi = 1024 * r - halo, 1024 * (r + 1) + halo
    cols = []
    if lo < 0:
        cols.append(np.zeros((2048, halo), np.float32))
    cols.append(x2d[max(lo, 0):min(hi, 8192)].T)
    if hi > 8192:
        cols.append(np.zeros((2048, halo), np.float32))
    cols.append(ctx2d[TC * r:TC * (r + 1)].T)
    return np.ascontiguousarray(np.concatenate(cols, axis=1), dtype=np.float32)


def hn_blocks(n, bs, gain_t, col, rope, out, r0=0, dt=BF16):
    return [dict(kind="hn", bs=bs, gain=gain_t[:, col:col + 1], gain_t=gain_t, rope=rope, out=out, r0=r0 + 128 * i, dt=dt) for i in range(n)]


def raw_blocks(n, out, r0=0, dt=F32):
    return [dict(kind="raw", out=out, r0=r0 + 128 * i, dt=dt) for i in range(n)]


def build_proj_swa(nkv=4):
    kb = KB()
    pc = ProjCtx(kb, 1024, 0)
    T_ = pc.ttot
    xT = kb.inp("xT", [2048, T_])
    modv = kb.const_load("modv", [128, 5, 16])
    gains = kb.const_load("gains", [128, 2])
    C = proj_consts(kb)
    w = kb.inp("w_in", [(4096 + 256 * nkv) // 256, 128, 16, 256])
    qT = kb.out("qT", [2048, T_], BF16)
    kT = kb.out("kT", [128 * nkv, T_], BF16)
    vT = kb.out("vT", [128 * nkv, T_], BF16)
    gT = kb.out("gT", [2048, T_], BF16)
    hT = emit_modnorm(pc, xT, modv)
    blocks = (hn_blocks(16, 128, gains, 0, True, qT) + hn_blocks(nkv, 128, gains, 1, True, kT)
              + raw_blocks(nkv, vT, dt=BF16) + raw_blocks(16, gT, dt=BF16))
    ws = WStream(kb, "w", 16, 256)
    emit_proj_stage(pc, hT, 16, w, blocks, ws, C)
    kb.finish()
    return kb


def col2(a, b):
    return np.ascontiguousarray(np.stack([a, b], axis=1), dtype=np.float32)


def build_tail():
    kb = KB()
    pc = ProjCtx(kb, 1024, 0)
    T_ = pc.ttot
    oT = kb.inp("oT", [2048, T_], BF16)
    gT = kb.inp("gT", [2048, T_], BF16)
    xT = kb.inp("xT", [2048, T_])
    gv = kb.const_load("gv", [128, 2, 16])
    w = kb.inp("w_out", [8, 128, 16, 256])
    xo = kb.out("xo", [2048, T_], F32)
    xs = kb.sb("xs", [128, 16, T_], F32)
    for g in range(4):
        kb.load(xs, xs[:, 4 * g:4 * g + 4, :], xT[512 * g:512 * (g + 1), :].rearrange("(kc p) t -> p kc t", p=128), q="pool")
    aT = kb.sb("aT", [128, 16, T_], BF16)
    for kc in range(16):
        ob_ = pc.tmp("o_in", [128, T_], BF16)
        gb_ = pc.tmp("g_in", [128, T_], BF16)
        kb.load(ob_, ob_[:], oT[128 * kc:128 * (kc + 1), :], q="sp")
        kb.load(gb_, gb_[:], gT[128 * kc:128 * (kc + 1), :], q="sp")
        sg = pc.tmp("sg", [128, T_], F32)
        kb.op("act", lambda e, sg=sg, gb_=gb_: e.activation(sg[:], gb_[:], AF.Silu), reads=[gb_], writes=[sg])
        kb.op("dve", lambda e, sg=sg, ob_=ob_, kc=kc: e.tensor_tensor(aT[:, kc, :], sg[:], ob_[:], ALU.mult), reads=[sg, ob_], writes=[aT])
    blocks = [dict(kind="resid", xs=xs, gv=gv, ob=i, out=xo, r0=128 * i, dt=F32) for i in range(16)]
    ws = WStream(kb, "w", 16, 256)
    emit_proj_stage(pc, aT, 16, w, blocks, ws, {})
    kb.finish()
    return kb


def gv_input(mod_m, mod_c):
    def l(v):
        return v.reshape(16, 128).T
    return np.ascontiguousarray(np.stack([l(mod_m[4096:6144]), l(mod_c[4096:6144])], axis=1), dtype=np.float32)


def run_tail(kb, oT_list, gT_list, xT_list, mod_m, mod_c, w_out):
    gv = gv_input(mod_m, mod_c)
    w_t = tile_weight(w_out)
    maps = [{"oT": oT_list[r], "gT": gT_list[r], "xT": xT_list[r], "gv": gv, "w_out": w_t} for r in range(NCORES)]
    res = run_prog(kb, maps)
    return [np.asarray(res[r]["xo"]) for r in range(NCORES)]


def split_xo(xo_list):
    x2d = np.concatenate([xo[:, :1024].T for xo in xo_list], axis=0)
    c2d = np.concatenate([xo[:, 1024:1024 + TC].T for xo in xo_list], axis=0)
    return np.ascontiguousarray(x2d), np.ascontiguousarray(c2d)


def build_swa_att():
    kb = KB()
    T_ = 1024 + TC
    qT = kb.inp("qT", [2048, T_], BF16)
    kL = kb.inp("kL", [512, 1280], BF16)
    vL = kb.inp("vL", [1280, 512], BF16)
    ckT = kb.inp("ckT", [512, 256], BF16)
    cvL = kb.inp("cvL", [256, 512], BF16)
    oT = kb.out("oT", [2048, T_], BF16)
    q_sb = kb.sb("q_sb", [128, 16, T_], BF16)
    for g4 in range(4):
        kb.load(q_sb, q_sb[:, 4 * g4:4 * g4 + 4, :], qT[512 * g4:512 * (g4 + 1), :].rearrange("(h p) t -> p h t", p=128))
    k_sb = kb.sb("k_sb", [128, 4, 1280], BF16)
    kb.load(k_sb, k_sb[:], kL.rearrange("(g p) t -> p g t", p=128), q="pool")
    v_sb = kb.sb("v_sb", [128, 10, 512], BF16)
    kb.load(v_sb, v_sb[:], vL.rearrange("(j p) c -> p j c", p=128))
    ck_sb = kb.sb("ck_sb", [128, 4, 256], BF16)
    kb.load(ck_sb, ck_sb[:], ckT.rearrange("(g p) t -> p g t", p=128), q="pool")
    cv_sb = kb.sb("cv_sb", [128, 2, 512], BF16)
    kb.load(cv_sb, cv_sb[:], cvL.rearrange("(j p) c -> p j c", p=128))
    masks32 = kb.const_load("masks", [128, 4, 128])
    masks = kb.sb("masksb", [128, 4, 128], BF16)
    kb.op("dve", lambda e: e.tensor_copy(masks[:], masks32[:]), reads=[masks32], writes=[masks])
    esink = kb.const_load("sinkb", [128, 16])
    kb.op("act", lambda e: e.activation(esink[:], esink[:], AF.Exp), reads=[esink], writes=[esink])
    ones = kb.sb("onesb", [128, 128], BF16)
    kb.op("pool", lambda e: e.memset(ones[:], 1.0), writes=[ones])
    o_sb = kb.sb("o_sb", [128, 16, T_], BF16)
    kb._pool = [0, 1, 2, 3]
    pts = [kb.sb(f"pt{i}", [128, 512], BF16) for i in range(10)]
    dens = [kb.sb(f"den{i}", [128, 512], F32) for i in range(2)]
    scale = 128.0 ** -0.5
    state = {"it": 0, "npt": 0}

    def iter_gen(g, qb):
        if qb < 8:
            nq = 128
            qc0 = qb * 128
            kblocks = [("l", qb, 2 if qb == 0 else 0), ("l", qb + 1, None), ("l", qb + 2, 3 if qb == 7 else 1), ("c", 0, None), ("c", 1, None)]
        else:
            nq = TC
            qc0 = 1024
            kblocks = [("c", 0, None), ("c", 1, None)]
        N = 4 * nq
        it = state["it"]
        state["it"] += 1
        pso = kb.bank(4 + it % 2)
        psd = kb.bank(6 + it % 2)
        den = dens[it % 2]
        for bi, (kind, j, mi) in enumerate(kblocks):
            pss = kb.ps()
            if kind == "l":
                lk = k_sb[:, g, j * 128:(j + 1) * 128]
                lv = v_sb[:, j, g * 128:(g + 1) * 128]
                kt, vt = k_sb, v_sb
            else:
                lk = ck_sb[:, g, j * 128:(j + 1) * 128]
                lv = cv_sb[:, j, g * 128:(g + 1) * 128]
                kt, vt = ck_sb, cv_sb
            qv = q_sb[:, 4 * g:4 * g + 4, qc0:qc0 + nq]
            kb.op("pe", lambda e, pss=pss, lk=lk, qv=qv: e.matmul(pss[:, :N], lk, qv, start=True, stop=True), reads=[kt, q_sb], writes=[pss])
            pt = pts[state["npt"] % len(pts)]
            state["npt"] += 1
            yield
            kb.op("act", lambda e, pt=pt, pss=pss: e.activation(pt[:, :N], pss[:, :N], AF.Exp, scale=scale), reads=[pss], writes=[pt])
            yield
            if mi is not None:
                kb.op("dve", lambda e, pt=pt, mi=mi: e.tensor_tensor(pt[:, :4 * nq].rearrange("p (h q) -> p h q", h=4),
                                                                   pt[:, :4 * nq].rearrange("p (h q) -> p h q", h=4),
                                                                   masks[:, mi, :].unsqueeze(1).to_broadcast([128, 4, 128]), ALU.mult),
                      reads=[pt, masks], writes=[pt])
                yield
            first, last = (bi == 0), (bi == len(kblocks) - 1)
            kb.op("pe", lambda e, lv=lv, pt=pt, first=first, last=last: e.matmul(pso[:, :N], lv, pt[:, :N], start=first, stop=last),
                  reads=[vt, pt], writes=[pso], acc=not first)
            kb.op("pe", lambda e, pt=pt, first=first, last=last: e.matmul(psd[:, :N], ones[:], pt[:, :N], start=first, stop=last),
                  reads=[ones, pt], writes=[psd], acc=not first)
            yield
        kb.op("dve", lambda e: e.tensor_tensor(den[:, :N].rearrange("p (h q) -> p h q", h=4),
                                               psd[:, :N].rearrange("p (h q) -> p h q", h=4),
                                               esink[:, 4 * g:4 * g + 4].unsqueeze(2).to_broadcast([128, 4, nq]), ALU.add),
              reads=[psd, esink], writes=[den])
        yield
        kb.op("dve", lambda e: e.reciprocal(den[:, :N], den[:, :N]), reads=[den], writes=[den])
        yield
        kb.op("dve", lambda e: e.tensor_tensor(o_sb[:, 4 * g:4 * g + 4, qc0:qc0 + nq],
                                               pso[:, :N].rearrange("p (h q) -> p h q", h=4),
                                               den[:, :N].rearrange("p (h q) -> p h q", h=4), ALU.mult),
              reads=[pso, den], writes=[o_sb])
        yield

    work = [(g, qb) for g in range(4) for qb in range(9)]
    for i in range(0, len(work), 2):
        gens = [iter_gen(*w) for w in work[i:i + 2]]
        while gens:
            for gen in list(gens):
                try:
                    next(gen)
                except StopIteration:
                    gens.remove(gen)
    for g4 in range(4):
        kb.store(oT, oT[512 * g4:512 * (g4 + 1), :].rearrange("(h p) t -> p h t", p=128), o_sb, o_sb[:, 4 * g4:4 * g4 + 4, :])
    kb.finish()
    return kb


def swa_masks(r):
    k = np.arange(128)[:, None]
    q = np.arange(128)[None, :]
    mlo = (k >= q).astype(np.float32)
    mhi = (k <= q).astype(np.float32)
    z = np.zeros_like(mlo)
    return np.ascontiguousarray(np.stack([mlo, mhi, z if r == 0 else mlo, z if r == NCORES - 1 else mhi], axis=1))


def bf(a):
    return np.ascontiguousarray(a, dtype=ml_dtypes.bfloat16)


def run_layer_swa(inp, mod, x2d, ctx2d, progs):
    li = 0
    kb = progs["proj_swa"]
    w_t = tile_weight(inp["swa_w_in"][0])
    maps = []
    for r in range(NCORES):
        m = {"xT": xT_input(x2d, ctx2d, r), "modv": modv_input(inp["norm_g"][li], mod[li, 0], mod[li, 1]),
             "gains": col2(inp["swa_q_g"][0], inp["swa_k_g"][0]), "w_in": w_t}
        m.update(proj_const_inputs(r))
        maps.append(m)
    res = run_prog(kb, maps)
    qT = [np.asarray(res[r]["qT"]) for r in range(NCORES)]
    kT = [np.asarray(res[r]["kT"]) for r in range(NCORES)]
    vT = [np.asarray(res[r]["vT"]) for r in range(NCORES)]
    gT = [np.asarray(res[r]["gT"]) for r in range(NCORES)]
    z = np.zeros((512, 128), kT[0].dtype)
    kfull = np.concatenate([z] + [k[:, :1024] for k in kT] + [z], axis=1)
    vfull = np.concatenate([z] + [v[:, :1024] for v in vT] + [z], axis=1)
    ckT = np.ascontiguousarray(np.concatenate([k[:, 1024:] for k in kT], axis=1))
    cvL = np.ascontiguousarray(np.concatenate([v[:, 1024:] for v in vT], axis=1).T)
    sinkb = np.ascontiguousarray(np.broadcast_to(inp["swa_sink"][0][None, :], (128, 16)), dtype=np.float32)
    maps = []
    for r in range(NCORES):
        sl = slice(1024 * r, 1024 * r + 1280)
        maps.append({"qT": qT[r], "kL": np.ascontiguousarray(kfull[:, sl]), "vL": np.ascontiguousarray(vfull[:, sl].T),
                     "ckT": ckT, "cvL": cvL, "masks": swa_masks(r), "sinkb": sinkb})
    res = run_prog(progs["swa_att"], maps)
    oT = [np.asarray(res[r]["oT"]) for r in range(NCORES)]
    xT = [xT_input(x2d, ctx2d, r) for r in range(NCORES)]
    xo = run_tail(progs["tail"], oT, gT, xT, mod[li, 0], mod[li, 1], inp["swa_w_out"][0])
    return split_xo(xo)


NTOK = 8192 + 256


def build_proj_mla():
    kb = KB()
    pc = ProjCtx(kb, 1024, 0)
    T_ = pc.ttot
    xT = kb.inp("xT", [2048, T_])
    modv = kb.const_load("modv", [128, 5, 16])
    gains = kb.const_load("gains", [128, 10])
    C = proj_consts(kb, need64=True, rope=False)
    w_in = kb.inp("w_in", [12, 128, 16, 256])
    w_qb = kb.inp("w_qb", [12, 128, 4, 256])
    w_kvb = kb.inp("w_kvb", [16, 128, 2, 256])
    qnT = kb.out("qnT", [2048, T_], BF16)
    qpeT = kb.out("qpeT", [1024, T_], BF16)
    knT = kb.out("knT", [2048, T_], BF16)
    vT = kb.out("vT", [2048, T_], BF16)
    kpeT = kb.out("kpeT", [128, T_], BF16)
    gT = kb.out("gT", [2048, T_], BF16)
    hT = emit_modnorm(pc, xT, modv)
    cq_raw = kb.sb("cq_raw", [128, 4, T_], F32)
    ckv_raw = kb.sb("ckv_raw", [128, 2, T_], F32)
    blocks = ([dict(kind="keep", dst=cq_raw, idx=i) for i in range(4)] + [dict(kind="keep", dst=ckv_raw, idx=i) for i in range(2)]
              + hn_blocks(1, 64, gains, 9, True, kpeT) + raw_blocks(16, gT, dt=BF16))
    ws1 = WStream(kb, "w1", 16, 256)
    emit_proj_stage(pc, hT, 16, w_in, blocks, ws1, C)
    cqn = kb.sb("cqn", [128, 4, T_], BF16)
    ckvn = kb.sb("ckvn", [128, 2, T_], BF16)
    emit_widenorm(pc, cq_raw, 4, gains, cqn, C, gcol0=0)
    emit_widenorm(pc, ckv_raw, 2, gains, ckvn, C, gcol0=4)
    ws2 = WStream(kb, "w2", 4, 256)
    emit_proj_stage(pc, cqn, 4, w_qb, hn_blocks(16, 128, gains, 6, False, qnT) + hn_blocks(8, 64, gains, 7, True, qpeT), ws2, C)
    ws3 = WStream(kb, "w3", 2, 256)
    emit_proj_stage(pc, ckvn, 2, w_kvb, hn_blocks(16, 128, gains, 8, False, knT) + raw_blocks(16, vT, dt=BF16), ws3, C)
    kb.finish()
    return kb


def mla_weight_layouts(inp):
    w_in = inp["mla_w_in"][0]
    w_in_re = np.concatenate([w_in[:, 0:768], w_in[:, 768:832], w_in[:, 768:832], w_in[:, 832:]], axis=1)
    wq = inp["mla_w_qb"][0].reshape(512, 16, 192)
    w_qb_re = np.concatenate([wq[:, :, :128].reshape(512, 2048), wq[:, :, 128:].reshape(512, 1024)], axis=1)
    wkv = inp["mla_w_kvb"][0].reshape(256, 16, 256)
    w_kvb_re = np.concatenate([wkv[:, :, :128].reshape(256, 2048), wkv[:, :, 128:].reshape(256, 2048)], axis=1)
    g = np.zeros((128, 10), np.float32)
    g[:, 0:4] = inp["mla_qa_g"][0].reshape(4, 128).T
    g[:, 4:6] = inp["mla_kva_g"][0].reshape(2, 128).T
    g[:, 6] = inp["mla_qn_nope_g"][0]
    g[:, 7] = np.tile(inp["mla_qn_pe_g"][0], 2)
    g[:, 8] = inp["mla_kn_nope_g"][0]
    g[:, 9] = np.tile(inp["mla_kn_pe_g"][0], 2)
    return (np.ascontiguousarray(w_in_re, dtype=np.float32), np.ascontiguousarray(w_qb_re, dtype=np.float32),
            np.ascontiguousarray(w_kvb_re, dtype=np.float32), g)


def gather_tokens(per_core, rows=None):
    return np.concatenate([a[:, :1024] for a in per_core] + [a[:, 1024:1024 + TC] for a in per_core], axis=1)


def scatter_tokens(full, r):
    return np.ascontiguousarray(np.concatenate([full[:, 1024 * r:1024 * (r + 1)], full[:, 8192 + TC * r:8192 + TC * (r + 1)]], axis=1))


def build_mla_att():
    kb = KB()
    qn = kb.inp("qn", [2, 128, NTOK], BF16)
    qpe = kb.inp("qpe", [128, NTOK], BF16)
    kn = kb.inp("kn", [2, 128, NTOK], BF16)
    kpe = kb.inp("kpe", [128, NTOK], BF16)
    v = kb.inp("v", [2, 128, 66, 128], BF16)
    oT = kb.out("oT", [2, 128, NTOK], BF16)
    qn_sb = [kb.sb(f"qn{i}", [128, NTOK], BF16) for i in range(2)]
    kn_sb = [kb.sb(f"kn{i}", [128, NTOK], BF16) for i in range(2)]
    v_sb = [kb.sb(f"v{i}", [128, 66, 128], BF16) for i in range(2)]
    qpe_sb = kb.sb("qpe", [128, NTOK], BF16)
    kpe_sb = kb.sb("kpe", [128, NTOK], BF16)
    o_sb = [kb.sb(f"o{i}", [128, NTOK], BF16) for i in range(2)]
    H = NTOK // 2
    for i in range(2):
        for hf in range(2):
            kb.load(kn_sb[i], kn_sb[i][:, hf * H:(hf + 1) * H], kn[i, :, hf * H:(hf + 1) * H], q="sp")
            kb.load(qn_sb[i], qn_sb[i][:, hf * H:(hf + 1) * H], qn[i, :, hf * H:(hf + 1) * H], q="pool")
            kb.load(v_sb[i], v_sb[i][:, 33 * hf:33 * (hf + 1), :], v[i, :, 33 * hf:33 * (hf + 1), :], q="sp")
        if i == 0:
            for hf in range(2):
                kb.load(kpe_sb, kpe_sb[:, hf * H:(hf + 1) * H], kpe[:, hf * H:(hf + 1) * H], q="pool")
                kb.load(qpe_sb, qpe_sb[:, hf * H:(hf + 1) * H], qpe[:, hf * H:(hf + 1) * H], q="pool")
    ones = kb.sb("onesb", [128, 128], BF16)
    kb.op("pool", lambda e: e.memset(ones[:], 1.0), writes=[ones])
    kb._pool = [0, 1, 2, 3]
    pts = [kb.sb(f"pt{i}", [128, 512], BF16) for i in range(4)]
    recs = [kb.sb(f"rec{i}", [128, 512], F32) for i in range(2)]
    paccs = [kb.sb(f"pacc{i}", [128, 512], F32) for i in range(2)]
    ones32 = kb.sb("ones32", [128, 128], F32)
    kb.op("pool", lambda e: e.memset(ones32[:], 1.0), writes=[ones32])
    scale = 192.0 ** -0.5
    it = 0
    npt = 0
    qtiles = [(512 * i, 512, list(range(66))) for i in range(16)] + [(8192, 256, [64, 65])]
    hmask = kb.const_load("hmask", [128, 2])
    qpm = kb.sb("qpm", [128, NTOK], BF16)
    for hh in range(2):
        K, Q, V, O = kn_sb[hh], qn_sb[hh], v_sb[hh], o_sb[hh]
        for hf in range(2):
            kb.op("dve", lambda e, hh=hh, hf=hf: e.tensor_scalar(qpm[:, hf * H:(hf + 1) * H], qpe_sb[:, hf * H:(hf + 1) * H], hmask[:, hh:hh + 1], None, ALU.mult),
                  reads=[qpe_sb, hmask], writes=[qpm])
        for (q0, N, blocks) in qtiles:
            pso = kb.bank(4 + it % 2)
            psd = kb.bank(6 + it % 2)
            rec = recs[it % 2]
            pacc = paccs[it % 2]
            it += 1
            def emit_s(j, N=N, K=K, Q=Q, q0=q0):
                pss = kb.ps()
                kb.op("pe", lambda e, pss=pss, j=j: e.matmul(pss[:, :N], K[:, j * 128:(j + 1) * 128], Q[:, q0:q0 + N], start=True, stop=False),
                      reads=[K, Q], writes=[pss])
                kb.op("pe", lambda e, pss=pss, j=j: e.matmul(pss[:, :N], kpe_sb[:, j * 128:(j + 1) * 128], qpm[:, q0:q0 + N], start=False, stop=True),
                      reads=[kpe_sb, qpm], writes=[pss], acc=True)
                return pss
            pendq = [emit_s(blocks[0])]
            if len(blocks) > 1:
                pendq.append(emit_s(blocks[1]))
            for bi, j in enumerate(blocks):
                pss = pendq.pop(0)
                pt = pts[npt % 4]
                npt += 1
                kb.op("act", lambda e, pt=pt, pss=pss, N=N: e.activation(pt[:, :N], pss[:, :N], AF.Exp, scale=scale), reads=[pss], writes=[pt])
                if bi + 2 < len(blocks):
                    pendq.append(emit_s(blocks[bi + 2]))
                first, last = (bi == 0), (bi == len(blocks) - 1)
                kb.op("pe", lambda e, pt=pt, j=j, first=first, last=last, N=N, V=V, pso=pso: e.matmul(pso[:, :N], V[:, j, :], pt[:, :N], start=first, stop=last),
                      reads=[V, pt], writes=[pso], acc=not first)
                if first:
                    kb.op("dve", lambda e, pt=pt, N=N, pacc=pacc: e.tensor_copy(pacc[:, :N], pt[:, :N]), reads=[pt], writes=[pacc])
                else:
                    kb.op("dve", lambda e, pt=pt, N=N, pacc=pacc: e.tensor_tensor(pacc[:, :N], pacc[:, :N], pt[:, :N], ALU.add), reads=[pt, pacc], writes=[pacc])
            kb.op("pe", lambda e, N=N, psd=psd, pacc=pacc: e.matmul(psd[:, :N], ones32[:], pacc[:, :N], start=True, stop=True), reads=[ones32, pacc], writes=[psd])
            kb.op("dve", lambda e, rec=rec, psd=psd, N=N: e.reciprocal(rec[:, :N], psd[:, :N]), reads=[psd], writes=[rec])
            kb.op("dve", lambda e, rec=rec, pso=pso, N=N, O=O, q0=q0: e.tensor_tensor(O[:, q0:q0 + N], pso[:, :N], rec[:, :N], ALU.mult), reads=[pso, rec], writes=[O])
        for hf in range(2):
            kb.store(oT, oT[hh, :, hf * H:(hf + 1) * H], O, O[:, hf * H:(hf + 1) * H], q="sp")
    kb.finish()
    return kb


def run_layer_mla(inp, mod, x2d, ctx2d, progs):
    li = 1
    w_in_re, w_qb_re, w_kvb_re, g = mla_weight_layouts(inp)
    w_in_re, w_qb_re, w_kvb_re = tile_weight(w_in_re), tile_weight(w_qb_re), tile_weight(w_kvb_re)
    maps = []
    for r in range(NCORES):
        m = {"xT": xT_input(x2d, ctx2d, r), "modv": modv_input(inp["norm_g"][li], mod[li, 0], mod[li, 1]),
             "gains": g, "w_in": w_in_re, "w_qb": w_qb_re, "w_kvb": w_kvb_re}
        m.update(proj_const_inputs(r, need64=True, rope=False))
        maps.append(m)
    res = run_prog(progs["proj_mla"], maps)
    names = ("qnT", "qpeT", "knT", "vT", "kpeT", "gT")
    full = {n: gather_tokens([np.asarray(res[r][n]) for r in range(NCORES)]) for n in names}
    return run_layer_mla_b(inp, mod, x2d, ctx2d, progs, full)


def run_layer_mla_b(inp, mod, x2d, ctx2d, progs, full):
    li = 1
    gT = [scatter_tokens(full["gT"], r) for r in range(NCORES)]
    maps = []
    for r in range(NCORES):
        hs = slice(256 * r, 256 * (r + 1))
        maps.append({"qn": np.ascontiguousarray(full["qnT"][hs].reshape(2, 128, NTOK)),
                     "qpe": np.ascontiguousarray(full["qpeT"][128 * r:128 * (r + 1)]),
                     "kn": np.ascontiguousarray(full["knT"][hs].reshape(2, 128, NTOK)),
                     "kpe": np.ascontiguousarray(full["kpeT"]),
                     "hmask": np.ascontiguousarray(np.stack([(np.arange(128) < 64), (np.arange(128) >= 64)], axis=1), dtype=np.float32),
                     "v": np.ascontiguousarray(full["vT"][hs].reshape(2, 128, 66, 128).transpose(0, 3, 2, 1))})
    res = run_prog(progs["mla_att"], maps)
    ofull = np.concatenate([np.asarray(res[r]["oT"]).reshape(256, NTOK) for r in range(NCORES)], axis=0)
    oT = [scatter_tokens(ofull, r) for r in range(NCORES)]
    xT = [xT_input(x2d, ctx2d, r) for r in range(NCORES)]
    xo = run_tail(progs["tail"], oT, gT, xT, mod[li, 0], mod[li, 1], inp["mla_w_out"][0])
    return split_xo(xo)


def build_diff_att(lam_init):
    kb = KB()
    q = kb.inp("q", [2, 128, NTOK], BF16)
    k = kb.inp("k", [2, 128, NTOK], BF16)
    v = kb.inp("v", [128, 66, 256], BF16)
    oT = kb.out("oT", [2, 128, NTOK], BF16)
    lqk = kb.const_load("lqk", [128, 4])
    sg = kb.const_load("subg", [128, 2])
    q_sb = [kb.sb(f"q{i}", [128, NTOK], BF16) for i in range(2)]
    k_sb = [kb.sb(f"k{i}", [128, NTOK], BF16) for i in range(2)]
    v_sb = kb.sb("v", [128, 66, 256], BF16)
    o_sb = kb.sb("o", [128, 2, NTOK], BF16)
    H = NTOK // 2
    for i in range(2):
        for hf in range(2):
            kb.load(k_sb[i], k_sb[i][:, hf * H:(hf + 1) * H], k[i, :, hf * H:(hf + 1) * H], q="sp")
            kb.load(q_sb[i], q_sb[i][:, hf * H:(hf + 1) * H], q[i, :, hf * H:(hf + 1) * H], q="pool")
    for hf in range(2):
        kb.load(v_sb, v_sb[:, 33 * hf:33 * (hf + 1), :], v[:, 33 * hf:33 * (hf + 1), :], q="sp")
    ones = kb.sb("onesb", [128, 128], BF16)
    kb.op("pool", lambda e: e.memset(ones[:], 1.0), writes=[ones])
    ones32 = kb.sb("ones32", [128, 128], F32)
    kb.op("pool", lambda e: e.memset(ones32[:], 1.0), writes=[ones32])
    prod = kb.sb("prod", [128, 2], F32)
    kb.op("dve", lambda e: e.tensor_tensor(prod[:, 0:1], lqk[:, 0:1], lqk[:, 1:2], ALU.mult), reads=[lqk], writes=[prod])
    kb.op("dve", lambda e: e.tensor_tensor(prod[:, 1:2], lqk[:, 2:3], lqk[:, 3:4], ALU.mult), reads=[lqk], writes=[prod])
    psl = kb.bank(0)
    kb.op("pe", lambda e: e.matmul(psl[:, 0:2], ones32[:], prod[:], start=True, stop=True), reads=[ones32, prod], writes=[psl])
    el = kb.sb("el", [128, 2], F32)
    kb.op("act", lambda e: e.activation(el[:], psl[:, 0:2], AF.Exp), reads=[psl], writes=[el])
    lam = kb.sb("lam", [128, 1], F32)
    kb.op("dve", lambda e: e.tensor_tensor(lam[:], el[:, 0:1], el[:, 1:2], ALU.subtract), reads=[el], writes=[lam])
    kb.op("dve", lambda e: e.tensor_scalar(lam[:], lam[:], float(lam_init), None, ALU.add), reads=[lam], writes=[lam])
    kb.op("dve", lambda e: e.tensor_scalar(sg[:], sg[:], float(1.0 - lam_init), None, ALU.mult), reads=[sg], writes=[sg])
    kb._pool = [0, 1, 7]
    pts = [kb.sb(f"pt{i}", [128, 512], BF16) for i in range(4)]
    recs = [kb.sb(f"rec{i}", [128, 512], F32) for i in range(2)]
    paccs = [kb.sb(f"pacc{i}", [128, 512], F32) for i in range(2)]
    ods = [kb.sb(f"od{i}", [128, 2, 256], F32) for i in range(2)]
    t2s = [kb.sb(f"t2{i}", [128, 256], F32) for i in range(2)]
    sqs = [kb.sb(f"sqd{i}", [128, 256], BF16) for i in range(2)]
    rss = [kb.sb(f"rs{i}", [128, 256], F32) for i in range(2)]
    scale = 128.0 ** -0.5
    NQ = 256
    qtiles = [(NQ * i, list(range(66))) for i in range(8192 // NQ)] + [(8192, [64, 65])]
    it = 0
    npt = 0
    for (q0, blocks) in qtiles:
        pso1 = kb.bank(2 + it % 2)
        pso2 = kb.bank(4 + it % 2)
        psd = kb.bank(6)
        rec, od, t2, rs = recs[it % 2], ods[it % 2], t2s[it % 2], rss[it % 2]
        pacc = paccs[it % 2]
        it += 1

        def emit_s(j, q0=q0):
            pss = kb.ps()
            kb.op("pe", lambda e, pss=pss, j=j: e.matmul(pss[:, 0:NQ], k_sb[0][:, j * 128:(j + 1) * 128], q_sb[0][:, q0:q0 + NQ], start=True, stop=True),
                  reads=[k_sb[0], q_sb[0]], writes=[pss])
            kb.op("pe", lambda e, pss=pss, j=j: e.matmul(pss[:, NQ:2 * NQ], k_sb[1][:, j * 128:(j + 1) * 128], q_sb[1][:, q0:q0 + NQ], start=True, stop=True),
                  reads=[k_sb[1], q_sb[1]], writes=[pss], acc=True)
            return pss
        pendq = [emit_s(blocks[0])]
        if len(blocks) > 1:
            pendq.append(emit_s(blocks[1]))
        for bi, j in enumerate(blocks):
            pss = pendq.pop(0)
            pt = pts[npt % 4]
            npt += 1
            kb.op("act", lambda e, pt=pt, pss=pss: e.activation(pt[:, :], pss[:, :], AF.Exp, scale=scale), reads=[pss], writes=[pt])
            if bi + 2 < len(blocks):
                pendq.append(emit_s(blocks[bi + 2]))
            first, last = (bi == 0), (bi == len(blocks) - 1)
            for mi, pso in enumerate((pso1, pso2)):
                for c in range(2):
                    kb.op("pe", lambda e, pt=pt, j=j, first=first, last=last, pso=pso, c=c, mi=mi: e.matmul(pso[:, c * NQ:(c + 1) * NQ], v_sb[:, j, c * 128:(c + 1) * 128],
                                                                                                 pt[:, mi * NQ:(mi + 1) * NQ], start=first, stop=last),
                          reads=[v_sb, pt], writes=[pso], acc=not (first and c == 0))
            if first:
                kb.op("dve", lambda e, pt=pt, pacc=pacc: e.tensor_copy(pacc[:, :], pt[:, :]), reads=[pt], writes=[pacc])
            else:
                kb.op("dve", lambda e, pt=pt, pacc=pacc: e.tensor_tensor(pacc[:, :], pacc[:, :], pt[:, :], ALU.add), reads=[pt, pacc], writes=[pacc])
        kb.op("pe", lambda e, psd=psd, pacc=pacc: e.matmul(psd[:, :], ones32[:], pacc[:, :], start=True, stop=True), reads=[ones32, pacc], writes=[psd])
        kb.op("dve", lambda e, rec=rec, psd=psd: e.reciprocal(rec[:, :], psd[:, :]), reads=[psd], writes=[rec])
        kb.op("dve", lambda e, rec=rec: e.tensor_scalar(rec[:, NQ:2 * NQ], rec[:, NQ:2 * NQ], lam[:, 0:1], None, ALU.mult), reads=[rec, lam], writes=[rec])
        for c in range(2):
            kb.op("dve", lambda e, rec=rec, od=od, pso1=pso1, c=c: e.tensor_tensor(od[:, c, :], pso1[:, c * NQ:(c + 1) * NQ], rec[:, 0:NQ], ALU.mult),
                  reads=[pso1, rec], writes=[od])
            kb.op("dve", lambda e, rec=rec, t2=t2, pso2=pso2, c=c: e.tensor_tensor(t2[:, :], pso2[:, c * NQ:(c + 1) * NQ], rec[:, NQ:2 * NQ], ALU.mult),
                  reads=[pso2, rec], writes=[t2])
            kb.op("dve", lambda e, od=od, t2=t2, c=c: e.tensor_tensor(od[:, c, :], od[:, c, :], t2[:, :], ALU.subtract), reads=[od, t2], writes=[od])
        pq = kb.ps()
        for c in range(2):
            sq = sqs[c]
            kb.op("act", lambda e, sq=sq, od=od, c=c: e.activation(sq[:, :], od[:, c, :], AF.Square), reads=[od], writes=[sq])
            kb.op("pe", lambda e, pq=pq, sq=sq, c=c: e.matmul(pq[:, 0:NQ], ones[:], sq[:, :], start=(c == 0), stop=(c == 1)), reads=[ones, sq], writes=[pq], acc=(c == 1))
        kb.rstd_from_ss(rs, rs[:, :], pq, pq[:, 0:NQ], 1.0 / 256)
        for c in range(2):
            kb.op("dve", lambda e, od=od, rs=rs, c=c, q0=q0: e.scalar_tensor_tensor(o_sb[:, c, q0:q0 + NQ], od[:, c, :], sg[:, c:c + 1], rs[:, :], ALU.mult, ALU.mult),
                  reads=[od, sg, rs], writes=[o_sb])
    for c in range(2):
        for hf in range(2):
            kb.store(oT, oT[c, :, hf * H:(hf + 1) * H], o_sb, o_sb[:, c, hf * H:(hf + 1) * H], q="sp")
    kb.finish()
    return kb


def run_layer_diff(inp, mod, x2d, ctx2d, progs):
    li = 3
    w_t = tile_weight(inp["diff_w_in"][0])
    maps = []
    for r in range(NCORES):
        m = {"xT": xT_input(x2d, ctx2d, r), "modv": modv_input(inp["norm_g"][li], mod[li, 0], mod[li, 1]),
             "gains": col2(inp["diff_q_g"][0], inp["diff_k_g"][0]), "w_in": w_t}
        m.update(proj_const_inputs(r))
        maps.append(m)
    res = run_prog(progs["proj_diff"], maps)
    full = {n: gather_tokens([np.asarray(res[r][n]) for r in range(NCORES)]) for n in ("qT", "kT", "vT", "gT")}
    return run_layer_diff_b(inp, mod, x2d, ctx2d, progs, full)


def run_layer_diff_b(inp, mod, x2d, ctx2d, progs, full):
    li = 3
    gT = [scatter_tokens(full["gT"], r) for r in range(NCORES)]
    lqk = np.ascontiguousarray(np.stack([inp["diff_lq1"][0], inp["diff_lk1"][0], inp["diff_lq2"][0], inp["diff_lk2"][0]], axis=1), dtype=np.float32)
    subg = np.ascontiguousarray(inp["diff_subln_g"][0].reshape(2, 128).T, dtype=np.float32)
    maps = []
    for r in range(NCORES):
        hs = slice(256 * r, 256 * (r + 1))
        maps.append({"q": np.ascontiguousarray(full["qT"][hs].reshape(2, 128, NTOK)),
                     "k": np.ascontiguousarray(full["kT"][hs].reshape(2, 128, NTOK)),
                     "v": np.ascontiguousarray(full["vT"][hs].reshape(256, 66, 128).transpose(2, 1, 0)),
                     "lqk": lqk, "subg": subg})
    res = run_prog(progs["diff_att"], maps)
    ofull = np.concatenate([np.asarray(res[r]["oT"]).reshape(256, NTOK) for r in range(NCORES)], axis=0)
    oT = [scatter_tokens(ofull, r) for r in range(NCORES)]
    xT = [xT_input(x2d, ctx2d, r) for r in range(NCORES)]
    xo = run_tail(progs["tail"], oT, gT, xT, mod[li, 0], mod[li, 1], inp["diff_w_out"][0])
    return split_xo(xo)


def build_proj_hyena():
    kb = KB()
    pc = ProjCtx(kb, 1026, 1)
    T_ = pc.ttot
    TO = 1024 + TC
    xT = kb.inp("xT", [2048, T_])
    modv = kb.const_load("modv", [128, 5, 16])
    cw = kb.const_load("cw", [128, 48, 4])
    hm = kb.const_load("hm", [128, 2])
    w = kb.inp("w_in", [32, 128, 16, 256])
    uT = kb.out("uT", [6144, TO], BF16)
    gT = kb.out("gT", [2048, TO], BF16)
    hT = emit_modnorm(pc, xT, modv)
    blocks = ([dict(kind="conv3", cw=cw, ci=i, out=uT, r0=128 * i, dt=BF16) for i in range(48)] + raw_blocks(16, gT, dt=BF16))
    ws = WStream(kb, "w", 16, 256)
    emit_proj_stage(pc, hT, 16, w, blocks, ws, {"hm": hm})
    kb.finish()
    return kb


LH = 8192
NF = 2 * LH
GC = 4
CPC = 256


def hyena_consts(L):
    n = np.arange(2 * L)
    pos = np.where(n < L, n, 2 * L - n).astype(np.float64)
    pos[L] = 0
    t = pos / max(L - 1, 1)
    bands = np.linspace(1e-4, 15, 16)
    ang = (2.0 * math.pi / L) * pos[:, None] * bands[None, :]
    z = np.concatenate([t[:, None], np.cos(ang), -np.sin(ang)], axis=-1)
    return np.ascontiguousarray(z.T, dtype=np.float32), t.astype(np.float32)


def hyena_deltas():
    max_decay = math.log(1e-2) / 0.3
    min_decay = math.log(1e-2) / 1.5
    return np.abs(np.linspace(min_decay, max_decay, 2048)).astype(np.float32)


def emit_sin(kb, tmp, out_ap, out_t, arg_ap, arg_t, shape_ap):
    s, c, t = tmp("sn_s"), tmp("sn_c"), tmp("sn_t")
    sa, ca, ta = shape_ap(s), shape_ap(c), shape_ap(t)
    kb.op("act", lambda e: e.activation(sa, arg_ap, AF.Sin, scale=0.125), reads=[arg_t], writes=[s])
    hp = shape_ap(kb.halfpi_t)
    kb.op("act", lambda e: e.activation(ca, arg_ap, AF.Sin, scale=0.125, bias=hp), reads=[arg_t, kb.halfpi_t], writes=[c])
    for k in range(3):
        last = (k == 2)
        if not last:
            kb.op("dve", lambda e: e.tensor_tensor(ta, sa, sa, ALU.mult), reads=[s], writes=[t])
        dst = out_ap if last else sa
        dst_t = out_t if last else s
        kb.op("dve", lambda e, dst=dst: e.scalar_tensor_tensor(dst, sa, 2.0, ca, ALU.mult, ALU.mult), reads=[s, c], writes=[dst_t])
        if not last:
            kb.op("dve", lambda e: e.tensor_scalar(ca, ta, -2.0, 1.0, ALU.mult, ALU.add), reads=[t], writes=[c])


def build_hyena_core(with_ctx=True, debug=False, ngroups_override=None, skip_b=False):
    kb = KB()
    tmps = {}

    def tmp(name, shape, dt=F32, n=2):
        if name not in tmps:
            tmps[name] = [[kb.sb(f"{name}{i}", shape, dt) for i in range(n)], 0]
        lst = tmps[name]
        t = lst[0][lst[1] % n]
        lst[1] += 1
        return t

    sig_in = [kb.inp(nm, [64, CPC, 128], BF16) for nm in ("v", "x1", "x2")]
    zout = kb.out("z", [64, CPC, 128], BF16)
    ZT = kb.inp("ZT", [33, NF])
    tprow = kb.inp("tprow", [1, NF])
    fw1 = kb.const_load("fw1", [33, 64])
    fb1f = kb.const_load("fb1f", [64, 2])
    fw2d = kb.const_load("fw2d", [64, 128])
    fb2f = kb.const_load("fb2f", [128, 2])
    w3s32 = kb.const_load("w3s", [128, 2, CPC])
    w3b = kb.sb("w3b", [128, 2, CPC], BF16)
    kb.op("dve", lambda e: e.tensor_copy(w3b[:], w3s32[:]), reads=[w3s32], writes=[w3b])
    ndelta = kb.const_load("deltac", [128, 2])
    kb.op("dve", lambda e: e.tensor_scalar(ndelta[:], ndelta[:], -1.0, None, ALU.mult), reads=[ndelta], writes=[ndelta])
    skipb = kb.const_load("skipb", [64, 2, CPC])
    halfpi = kb.sb("halfpi", [128, 1], F32)
    kb.op("pool", lambda e: e.memset(halfpi[:], math.pi / 2), writes=[halfpi])
    kb.halfpi, kb.halfpi_t = halfpi, halfpi

    def cbf(name, shape):
        t32 = kb.const_load(name, shape)
        tb = kb.sb(name + "b", shape, BF16)
        kb.op("dve", lambda e: e.tensor_copy(tb[:], t32[:]), reads=[t32], writes=[tb])
        return tb
    Ff = cbf("Ff", [128, 256])
    Fi1 = cbf("Fi1", [128, 256])
    Fi2 = cbf("Fi2", [128, 256])
    Fc = cbf("Fc", [128, 128])
    Fs = cbf("Fs", [128, 128])
    Fsn = cbf("Fsn", [128, 128])
    Tc = kb.const_load("Tc", [128, 128])
    Ts = kb.const_load("Ts", [128, 128])
    kcs = kb.out("kcs", [2, CPC, NF], F32) if debug else kb.scratch("kcs", [2, CPC, NF], F32)

    HT2 = kb.sb("HT2", [128, NF], BF16)
    CH = 2048
    def big(nm):
        if nm in ("kchA", "kchB"):
            lst = tmps["kch"][0]
            return lst[0] if nm == "kchA" else lst[1]
        return tmp(nm, [128, CH], F32, n=1)
    def mlp_block(z_dram_ap, wcols, dst_ap, dst_t):
        zt = tmp("zt", [33, CH], F32, n=1)
        kb.load(zt, zt[:, :wcols], z_dram_ap, q="sp")
        arg = big("arg")
        for sbk in range(wcols // 512):
            ps = kb.ps()
            sc = slice(sbk * 512, (sbk + 1) * 512)
            kb.op("pe", lambda e, ps=ps, zt=zt, sc=sc: e.matmul(ps[0:64, :], fw1[:, :], zt[:, sc], start=True, stop=True), reads=[fw1, zt], writes=[ps])
            kb.op("dve", lambda e, ps=ps, sc=sc, arg=arg: e.tensor_scalar(arg[0:64, sc], ps[0:64, :], fb1f[:, 0:1], fb1f[:, 1:2], ALU.add, ALU.mult),
                  reads=[ps, fb1f], writes=[arg])
        h1 = big("h1")
        emit_sin(kb, big, h1[0:64, :wcols], h1, arg[0:64, :wcols], arg, lambda t: t[0:64, :wcols] if t is not kb.halfpi_t else t[0:64, :])
        arg2 = big("arg")
        for sbk in range(wcols // 512):
            ps = kb.ps()
            sc = slice(sbk * 512, (sbk + 1) * 512)
            kb.op("pe", lambda e, ps=ps, h1=h1, sc=sc: e.matmul(ps[:, :], fw2d[:, :], h1[0:64, sc], start=True, stop=True), reads=[fw2d, h1], writes=[ps])
            kb.op("dve", lambda e, ps=ps, sc=sc, arg2=arg2: e.tensor_scalar(arg2[:, sc], ps[:, :], fb2f[:, 0:1], fb2f[:, 1:2], ALU.add, ALU.mult),
                  reads=[ps, fb2f], writes=[arg2])
        emit_sin(kb, big, dst_ap, dst_t, arg2[:, :wcols], arg2, lambda t: t[:, :wcols] if t is not kb.halfpi_t else t[:, :])

    for ch in range(NF // CH):
        cols = slice(ch * CH, (ch + 1) * CH)
        mlp_block(ZT[:, cols], CH, HT2[:, cols], HT2)
    kb.op("pool", lambda e: e.memset(HT2[0:64, LH:NF], 0.0), writes=[HT2])
    kb.op("pool", lambda e: e.memset(HT2[64:128, 0:LH + 1], 0.0), writes=[HT2])

    if debug:
        dH = kb.out("dHT2", [128, NF], BF16)
        kb.store(dH, dH[:], HT2, HT2[:])
    ssp = kb.sb("ssp", [128, 4, 8], F32)
    rc = kb.sb("rc", [128, 4], F32)
    for o in range(0 if not skip_b else 2, 2):
        for cb in range(2):
            oc = 2 * o + cb
            for ch in range(NF // CH):
                cols = slice(ch * CH, (ch + 1) * CH)
                tpb = big("sn_s")
                kb.load(tpb, tpb[:], tprow[0:1, cols].partition_broadcast(128), q="sp")
                dec = big("sn_c")
                kb.op("act", lambda e, dec=dec, tpb=tpb, cb=cb: e.activation(dec[:], tpb[:], AF.Exp, scale=ndelta[:, cb:cb + 1]), reads=[tpb, ndelta], writes=[dec])
                kch = tmp("kch", [128, CH], F32, n=2)
                for sbk in range(4):
                    ps = kb.ps()
                    sc = slice(sbk * 512, (sbk + 1) * 512)
                    gc = slice(ch * CH + sbk * 512, ch * CH + (sbk + 1) * 512)
                    kb.op("pe", lambda e, ps=ps, o=o, cb=cb, gc=gc: e.matmul(ps[:, :], w3b[:, o, cb * 128:(cb + 1) * 128], HT2[:, gc], start=True, stop=True),
                          reads=[w3b, HT2], writes=[ps])
                    kb.op("dve", lambda e, ps=ps, sc=sc, kch=kch, dec=dec: e.tensor_tensor(kch[:, sc], ps[:, :], dec[:, sc], ALU.mult), reads=[ps, dec], writes=[kch])
                sq = big("sn_t")
                kb.op("pool", lambda e, sq=sq, kch=kch: e.tensor_tensor(sq[:], kch[:], kch[:], ALU.mult), reads=[kch], writes=[sq])
                kb.op("dve", lambda e, sq=sq, oc=oc, ch=ch: e.reduce_sum(ssp[:, oc, ch:ch + 1], sq[:], AX.X), reads=[sq], writes=[ssp])
                kb.store(kcs, kcs[o, cb * 128:(cb + 1) * 128, cols], kch, kch[:], q="pool", is_output=False)
            kb.op("dve", lambda e, oc=oc: e.reduce_sum(rc[:, oc:oc + 1], ssp[:, oc, :], AX.X), reads=[ssp], writes=[rc])
            kb.rstd_from_ss(rc, rc[:, oc:oc + 1], rc, rc[:, oc:oc + 1], 1.0)
            for ch in range(NF // CH):
                cols = slice(ch * CH, (ch + 1) * CH)
                k2 = tmp("kch", [128, CH], F32, n=2)
                kb.load(k2, k2[:], kcs[o, cb * 128:(cb + 1) * 128, cols], q="sp", src=kcs)
                kb.op("dve", lambda e, k2=k2, oc=oc: e.tensor_scalar(k2[:], k2[:], rc[:, oc:oc + 1], None, ALU.mult), reads=[k2, rc], writes=[k2])
                kb.store(kcs, kcs[o, cb * 128:(cb + 1) * 128, cols], k2, k2[:], q="pool", is_output=False)


    if with_ctx:
        LC = 256
        uc = kb.inp("uc", [3, 2, 128, LC], BF16)
        cwc = kb.const_load("cwc", [128, 3, 2, 4])
        ZTc = kb.inp("ZTc", [33, 2 * LC])
        tpc = kb.const_load("tpc", [128, 4])
        kb.op("dve", lambda e: e.tensor_scalar(tpc[:], tpc[:], -1.0, None, ALU.mult), reads=[tpc], writes=[tpc])
        deltab = kb.const_load("deltab", [128, CPC])
        skipc = kb.const_load("skipc", [128, 2, CPC])
        ident = kb.const_load("ident", [128, 128])
        ones32 = kb.sb("ones32c", [128, 128], F32)
        kb.op("pool", lambda e: e.memset(ones32[:], 1.0), writes=[ones32])
        Dc_in = kb.inp("Dc", [128, 4, 512])
        Dsn_in = kb.inp("Dsn", [128, 4, 512])
        zc_out = kb.out("zc", [2, 128, CPC], BF16)
        Dcb = kb.sb("Dcb", [128, 4, 512], BF16)
        Dsnb = kb.sb("Dsnb", [128, 4, 512], BF16)
        for src_, dst_ in ((Dc_in, Dcb), (Dsn_in, Dsnb)):
            st_ = big("sn_s")
            kb.load(st_, st_[:].rearrange("p (a b) -> p a b", a=4), src_, q="sp")
            kb.op("dve", lambda e, st_=st_, dst_=dst_: e.tensor_copy(dst_[:].rearrange("p a b -> p (a b)"), st_[:]), reads=[st_], writes=[dst_])
        utT = big("kchA")
        ut = utT[:, 0:1536].rearrange("p (tb si c) -> p tb si c", tb=2, si=3)
        for si in range(3):
            for cb in range(2):
                ub = tmp("ucb", [128, LC], BF16, n=2)
                kb.load(ub, ub[:], uc[si, cb], q="sp")
                u32 = tmp("uc32", [128, LC], F32, n=2)
                kb.op("act", lambda e, u32=u32, ub=ub: e.copy(u32[:], ub[:]), reads=[ub], writes=[u32])
                o32 = tmp("oc32", [128, LC], F32, n=2)
                kb.op("dve", lambda e, o32=o32, u32=u32, si=si, cb=cb: e.tensor_scalar(o32[:], u32[:], cwc[:, si, cb, 1:2], cwc[:, si, cb, 3:4], ALU.mult, ALU.add),
                      reads=[u32, cwc], writes=[o32])
                kb.op("dve", lambda e, o32=o32, u32=u32, si=si, cb=cb: e.scalar_tensor_tensor(o32[:, 1:LC], u32[:, 0:LC - 1], cwc[:, si, cb, 0:1], o32[:, 1:LC], ALU.mult, ALU.add),
                      reads=[u32, cwc, o32], writes=[o32])
                kb.op("dve", lambda e, o32=o32, u32=u32, si=si, cb=cb: e.scalar_tensor_tensor(o32[:, 0:LC - 1], u32[:, 1:LC], cwc[:, si, cb, 2:3], o32[:, 0:LC - 1], ALU.mult, ALU.add),
                      reads=[u32, cwc, o32], writes=[o32])
                for tb in range(2):
                    ps = kb.ps()
                    kb.op("pe", lambda e, ps=ps, o32=o32, tb=tb: e.transpose(ps[:, 0:128], o32[:, tb * 128:(tb + 1) * 128], ident[:, :]), reads=[o32, ident], writes=[ps])
                    kb.op("act", lambda e, ps=ps, tb=tb, si=si, cb=cb: e.copy(ut[:, tb, si, cb * 128:(cb + 1) * 128], ps[:, 0:128]), reads=[ps], writes=[utT])
        HTc = kb.sb("HTc", [128, 2 * LC], BF16)
        mlp_block(ZTc[:, :], 2 * LC, HTc[:, :], HTc)
        kb.op("pool", lambda e: e.memset(HTc[0:64, LC:2 * LC], 0.0), writes=[HTc])
        kb.op("pool", lambda e: e.memset(HTc[64:128, 0:LC + 1], 0.0), writes=[HTc])
        kccT = big("sn_c")
        HcT = [big("sn_t"), big("kchB")]
        kccb = kb.sb("kccb", [128, 2, 4, CPC], BF16)
        for o in range(2):
            kcc = kccT[:, o * 1024:(o + 1) * 1024].rearrange("p (a b) -> p a b", a=4)
            pss = kb.ps()
            for nb in range(4):
                ps = kb.ps()
                kb.op("pe", lambda e, ps=ps, nb=nb, o=o: e.matmul(ps[:, 0:CPC], HTc[:, nb * 128:(nb + 1) * 128], w3b[:, o, :], start=True, stop=True), reads=[HTc, w3b], writes=[ps])
                dec = tmp("decc", [128, CPC], F32, n=2)
                kb.op("act", lambda e, dec=dec, nb=nb: e.activation(dec[:], deltab[:], AF.Exp, scale=tpc[:, nb:nb + 1]), reads=[deltab, tpc], writes=[dec])
                kb.op("dve", lambda e, ps=ps, dec=dec, kcc=kcc, nb=nb: e.tensor_tensor(kcc[:, nb, :], ps[:, 0:CPC], dec[:], ALU.mult), reads=[ps, dec], writes=[kccT])
                sq = tmp("sqc", [128, CPC], F32, n=2)
                kb.op("pool", lambda e, sq=sq, kcc=kcc, nb=nb: e.tensor_tensor(sq[:], kcc[:, nb, :], kcc[:, nb, :], ALU.mult), reads=[kccT], writes=[sq])
                kb.op("pe", lambda e, pss=pss, sq=sq, nb=nb: e.matmul(pss[:, 0:CPC], ones32[:], sq[:], start=(nb == 0), stop=(nb == 3)), reads=[ones32, sq], writes=[pss], acc=(nb != 0))
            rsc = tmp("rsc", [128, CPC], F32, n=2)
            kb.rstd_from_ss(rsc, rsc[:], pss, pss[:, 0:CPC], 1.0)
            for nb in range(4):
                kb.op("dve", lambda e, kcc=kcc, rsc=rsc, nb=nb, o=o: e.tensor_tensor(kccb[:, o, nb, :], kcc[:, nb, :], rsc[:], ALU.mult), reads=[kccT, rsc], writes=[kccb])
            Hc = HcT[o]
            for fb in range(4):
                for ri, Dm in enumerate((Dcb, Dsnb)):
                    ps = kb.ps()
                    for kbk in range(4):
                        kb.op("pe", lambda e, ps=ps, Dm=Dm, kbk=kbk, fb=fb, o=o: e.matmul(ps[:, 0:CPC], Dm[:, kbk, fb * 128:(fb + 1) * 128], kccb[:, o, kbk, :],
                                                                                     start=(kbk == 0), stop=(kbk == 3)),
                              reads=[Dm, kccb], writes=[ps], acc=(kbk != 0))
                    kb.op("act", lambda e, ps=ps, Hc=Hc, ri=ri, fb=fb: e.copy(Hc[:, ri * 1024 + fb * 256: ri * 1024 + (fb + 1) * 256], ps[:, 0:CPC]), reads=[ps], writes=[Hc])
        curb = kb.sb("curc", [128, 2, CPC], BF16)
        for tb in range(2):
            kb.op("act", lambda e, tb=tb: e.copy(curb[:, tb, :], ut[:, tb, 0, :]), reads=[utT], writes=[curb])
        cur32 = [ut[:, tb, 0, :] for tb in range(2)]
        cur32_t = utT
        zc32 = kb.sb("zc32", [128, 2, CPC], F32)
        Ycb = kb.sb("Ycb", [128, 2, 4, CPC], BF16)
        for o in range(2):
            Hc = HcT[o]
            for fb in range(4):
                pre, pim = kb.ps(), kb.ps()
                for ri, (Dm, pp) in enumerate(((Dcb, pre), (Dsnb, pim))):
                    for kbk in range(2):
                        kb.op("pe", lambda e, pp=pp, Dm=Dm, kbk=kbk, fb=fb: e.matmul(pp[:, 0:CPC], Dm[:, kbk, fb * 128:(fb + 1) * 128], curb[:, kbk, :],
                                                                                 start=(kbk == 0), stop=(kbk == 1)),
                              reads=[Dm, curb], writes=[pp], acc=(kbk != 0))
                hre = Hc[:, fb * 256:(fb + 1) * 256]
                him = Hc[:, 1024 + fb * 256:1024 + (fb + 1) * 256]
                t1, t2 = tmp("hmc", [128, CPC], F32, n=4), tmp("hmc", [128, CPC], F32, n=4)
                kb.op("dve", lambda e, t1=t1, pre=pre, hre=hre: e.tensor_tensor(t1[:], pre[:, 0:CPC], hre, ALU.mult), reads=[pre, Hc], writes=[t1])
                kb.op("dve", lambda e, t2=t2, pim=pim, him=him: e.tensor_tensor(t2[:], pim[:, 0:CPC], him, ALU.mult), reads=[pim, Hc], writes=[t2])
                kb.op("pool", lambda e, t1=t1, t2=t2, fb=fb: e.tensor_tensor(Ycb[:, 0, fb, :], t1[:], t2[:], ALU.subtract), reads=[t1, t2], writes=[Ycb])
                t3, t4 = tmp("hmc", [128, CPC], F32, n=4), tmp("hmc", [128, CPC], F32, n=4)
                kb.op("dve", lambda e, t3=t3, pre=pre, him=him: e.tensor_tensor(t3[:], pre[:, 0:CPC], him, ALU.mult), reads=[pre, Hc], writes=[t3])
                kb.op("dve", lambda e, t4=t4, pim=pim, hre=hre: e.tensor_tensor(t4[:], pim[:, 0:CPC], hre, ALU.mult), reads=[pim, Hc], writes=[t4])
                kb.op("pool", lambda e, t3=t3, t4=t4, fb=fb: e.tensor_tensor(Ycb[:, 1, fb, :], t3[:], t4[:], ALU.add), reads=[t3, t4], writes=[Ycb])
            for tb in range(2):
                py = kb.ps()
                n_ = 0
                for ri, Dm in enumerate((Dcb, Dsnb)):
                    for fb in range(4):
                        kb.op("pe", lambda e, py=py, Dm=Dm, fb=fb, tb=tb, ri=ri, n_=n_: e.matmul(py[:, 0:CPC], Dm[:, fb, tb * 128:(tb + 1) * 128], Ycb[:, ri, fb, :],
                                                                                        start=(n_ == 0), stop=(n_ == 7)),
                              reads=[Dm, Ycb], writes=[py], acc=(n_ != 0))
                        n_ += 1
                t = tmp("epc", [128, CPC], F32, n=2)
                c32 = cur32[tb]
                kb.op("pool", lambda e, t=t, c32=c32, o=o: e.tensor_tensor(t[:], c32, skipc[:, o, :], ALU.mult), reads=[cur32_t, skipc], writes=[t])
                kb.op("dve", lambda e, t=t, py=py: e.scalar_tensor_tensor(t[:], py[:, 0:CPC], 1.0 / (2 * LC), t[:], ALU.mult, ALU.add), reads=[py, t], writes=[t])
                kb.op("dve", lambda e, t=t, tb=tb, o=o: e.tensor_tensor(zc32[:, tb, :], t[:], ut[:, tb, 1 + o, :], ALU.mult), reads=[t, utT], writes=[zc32])
            if o == 0:
                for tb in range(2):
                    kb.op("act", lambda e, tb=tb: e.copy(curb[:, tb, :], zc32[:, tb, :]), reads=[zc32], writes=[curb])
                z1c = kb.sb("z1c", [128, 2, CPC], F32)
                kb.op("dve", lambda e: e.tensor_copy(z1c[:], zc32[:]), reads=[zc32], writes=[z1c])
                cur32 = [z1c[:, tb, :] for tb in range(2)]
                cur32_t = z1c
        zcb = kb.sb("zcb", [128, 2, CPC], BF16)
        kb.op("act", lambda e: e.copy(zcb[:], zc32[:]), reads=[zc32], writes=[zcb])
        kb.store(zc_out, zc_out.h.rearrange("tb p c -> p tb c"), zcb, zcb[:], q="sp")

    W = GC * 128
    kb._pool = [0, 1, 2, 3]
    kb.p.barrier()
    carve_src = [tmps[nm][0][0] for nm in ("arg", "h1", "sn_s", "sn_c", "sn_t")] + list(tmps["kch"][0])
    carve_pos = [0, 0]

    def carve(nelem32):
        ti, off = carve_pos
        if off + nelem32 > CH:
            ti, off = ti + 1, 0
        v = carve_src[ti].h[:, off:off + nelem32]
        carve_pos[0], carve_pos[1] = ti, off + nelem32
        return v

    def ctmp(name, shape, dt, n):
        if name not in tmps:
            lst = []
            nel = int(np.prod(shape[1:]))
            n32 = nel if dt == F32 else nel // 2
            for i in range(n):
                v = carve(n32)
                if dt != F32:
                    v = v.bitcast(BF16)
                if len(shape) == 3:
                    v = v.rearrange("p (a b) -> p a b", a=shape[1])
                elif len(shape) == 4:
                    v = v.rearrange("p (a b c) -> p a b c", a=shape[1], b=shape[2])
                lst.append(T(v, Buf(f"{name}{i}")))
            tmps[name] = [lst, 0]
        lst = tmps[name]
        t = lst[0][lst[1] % n]
        lst[1] += 1
        return t

    dbg_done = []

    def fwd_fft(src, src_ap_fn, K, par):
        A32 = ctmp("A32", [128, GC, 2, 128], F32, 3)
        for c2 in range(GC // 2):
            bank = kb.ps()
            for u in range(2):
                c = 2 * c2 + u
                kb.op("pe", lambda e, bank=bank, c=c, u=u: e.matmul(bank[:, u * 256:(u + 1) * 256], src_ap_fn(c), Ff[0:K, :], start=True, stop=True),
                      reads=[src, Ff], writes=[bank], acc=(u == 1))
            kb.op("act", lambda e, bank=bank, c2=c2, A32=A32: e.copy(A32[:, 2 * c2:2 * c2 + 2, :, :].rearrange("p c r k -> p (c r k)"), bank[:, :]), reads=[bank], writes=[A32])
            yield
        Ab = [ctmp("Ab", [128, GC, 128], BF16, 8) for _ in range(2)]
        if debug and K == 64 and not dbg_done:
            d_ = kb.out("dA32", [128, GC, 2, 128], F32)
            kb.store(d_, d_[:], A32, A32[:])
        yield from twiddle(A32, Ab, conj=False)
        if debug and K == 64 and not dbg_done:
            d_ = kb.out("dAb0", [128, GC, 128], BF16)
            kb.store(d_, d_[:], Ab[0], Ab[0][:])
        banks = []
        for qd in range(GC // 4):
            bre, bim = kb.bank(4 + 2 * par), kb.bank(5 + 2 * par)
            cs = slice(4 * qd, 4 * qd + 4)
            kb.op("pe", lambda e, bre=bre, cs=cs: e.matmul(bre[:, :], Fc[:, :], Ab[0][:, cs, :], start=True, stop=False), reads=[Fc, Ab[0]], writes=[bre])
            kb.op("pe", lambda e, bre=bre, cs=cs: e.matmul(bre[:, :], Fs[:, :], Ab[1][:, cs, :], start=False, stop=True), reads=[Fs, Ab[1]], writes=[bre], acc=True)
            kb.op("pe", lambda e, bim=bim, cs=cs: e.matmul(bim[:, :], Fc[:, :], Ab[1][:, cs, :], start=True, stop=False), reads=[Fc, Ab[1]], writes=[bim])
            kb.op("pe", lambda e, bim=bim, cs=cs: e.matmul(bim[:, :], Fsn[:, :], Ab[0][:, cs, :], start=False, stop=True), reads=[Fsn, Ab[0]], writes=[bim], acc=True)
            banks.append((bre, bim))
            yield
            if debug and K == 64 and not dbg_done and qd == 0:
                xs_ = kb.sb("dbgx", [128, 2, 512], F32)
                kb.op("dve", lambda e, bre=bre: e.tensor_copy(xs_[:, 0, :], bre[:, :]), reads=[bre], writes=[xs_])
                kb.op("dve", lambda e, bim=bim: e.tensor_copy(xs_[:, 1, :], bim[:, :]), reads=[bim], writes=[xs_])
                d_ = kb.out("dX", [128, 2, 512], F32)
                kb.store(d_, d_[:], xs_, xs_[:])
        if debug and K == 64:
            dbg_done.append(1)
        return banks

    def twiddle(A32, Ab, conj):
        are, aim = A32[:, :, 0, :], A32[:, :, 1, :]
        tcb = Tc[:, :].unsqueeze(1).to_broadcast([128, GC, 128])
        tsb = Ts[:, :].unsqueeze(1).to_broadcast([128, GC, 128])
        t1, t2 = ctmp("tw", [128, GC, 128], F32, 8), ctmp("tw", [128, GC, 128], F32, 8)
        kb.op("dve", lambda e: e.tensor_tensor(t1[:], are, tcb, ALU.mult), reads=[A32, Tc], writes=[t1])
        kb.op("pool", lambda e: e.tensor_tensor(t2[:], aim, tsb, ALU.mult), reads=[A32, Ts], writes=[t2])
        yield
        kb.op("dve", lambda e: e.tensor_tensor(Ab[0][:], t1[:], t2[:], ALU.subtract if conj else ALU.add), reads=[t1, t2], writes=[Ab[0]])
        t3, t4 = ctmp("tw", [128, GC, 128], F32, 8), ctmp("tw", [128, GC, 128], F32, 8)
        kb.op("dve", lambda e: e.tensor_tensor(t3[:], aim, tcb, ALU.mult), reads=[A32, Tc], writes=[t3])
        kb.op("pool", lambda e: e.tensor_tensor(t4[:], are, tsb, ALU.mult), reads=[A32, Ts], writes=[t4])
        yield
        kb.op("dve", lambda e: e.tensor_tensor(Ab[1][:], t3[:], t4[:], ALU.add if conj else ALU.subtract), reads=[t3, t4], writes=[Ab[1]])
        yield

    ngroups = CPC // GC if ngroups_override is None else ngroups_override

    def group_gen(g, par):
        c0 = g * GC
        sig = []
        for si in range(3):
            t = tmp(f"sig{si}", [64, GC, 128], BF16, n=2)
            kb.load(t, t[:], sig_in[si][:, c0:c0 + GC, :], q="sp")
            sig.append(t)
        H = []
        for o in range(2):
            k32 = tmp("k32", [128, GC, 128], F32, n=2)
            kb.load(k32, k32[:], kcs[o, c0:c0 + GC, :].rearrange("c (a b) -> a c b", b=128), q="pool", src=kcs)
            kbf = tmp("kbf", [128, GC, 128], BF16, n=4)
            kb.op("act", lambda e, kbf=kbf, k32=k32: e.copy(kbf[:], k32[:]), reads=[k32], writes=[kbf])
            banks = yield from fwd_fft(kbf, lambda c, kbf=kbf: kbf[:, c, :], 128, par)
            Hre, Him = tmp(f"Hre{o}", [128, GC, 128], F32, n=2), tmp(f"Him{o}", [128, GC, 128], F32, n=2)
            for qd, (bre, bim) in enumerate(banks):
                cs = slice(4 * qd, 4 * qd + 4)
                kb.op("act", lambda e, bre=bre, cs=cs, Hre=Hre: e.copy(Hre[:, cs, :].rearrange("p c k -> p (c k)"), bre[:, :]), reads=[bre], writes=[Hre])
                kb.op("act", lambda e, bim=bim, cs=cs, Him=Him: e.copy(Him[:, cs, :].rearrange("p c k -> p (c k)"), bim[:, :]), reads=[bim], writes=[Him])
            H.append((Hre, Him))
            yield
            if debug and g == 0:
                for nm_, t_ in ((f"dHre{o}", Hre), (f"dHim{o}", Him)):
                    d_ = kb.out(nm_, [128, GC, 128], F32)
                    kb.store(d_, d_[:], t_, t_[:])
        cur = sig[0]
        for o in range(2):
            Hre, Him = H[o]
            banks = yield from fwd_fft(cur, lambda c, cur=cur: cur[0:64, c, :], 64, par)
            Yb = [tmp("Yb", [128, GC, 128], BF16, n=4) for _ in range(2)]
            for qd, (bre, bim) in enumerate(banks):
                cs = slice(4 * qd, 4 * qd + 4)
                fl = lambda t, cs=cs: t[:, cs, :].rearrange("p c k -> p (c k)")
                t1, t2 = ctmp("hm", [128, 512], F32, 8), ctmp("hm", [128, 512], F32, 8)
                kb.op("dve", lambda e, t1=t1, bre=bre, fl=fl, Hre=Hre: e.tensor_tensor(t1[:], bre[:, :], fl(Hre), ALU.mult), reads=[bre, Hre], writes=[t1])
                kb.op("dve", lambda e, t2=t2, bim=bim, fl=fl, Him=Him: e.tensor_tensor(t2[:], bim[:, :], fl(Him), ALU.mult), reads=[bim, Him], writes=[t2])
                kb.op("pool", lambda e, t1=t1, t2=t2, fl=fl, Y0=Yb[0]: e.tensor_tensor(fl(Y0), t1[:], t2[:], ALU.subtract), reads=[t1, t2], writes=[Yb[0]])
                t3, t4 = ctmp("hm", [128, 512], F32, 8), ctmp("hm", [128, 512], F32, 8)
                kb.op("dve", lambda e, t3=t3, bre=bre, fl=fl, Him=Him: e.tensor_tensor(t3[:], bre[:, :], fl(Him), ALU.mult), reads=[bre, Him], writes=[t3])
                kb.op("dve", lambda e, t4=t4, bim=bim, fl=fl, Hre=Hre: e.tensor_tensor(t4[:], bim[:, :], fl(Hre), ALU.mult), reads=[bim, Hre], writes=[t4])
                kb.op("pool", lambda e, t3=t3, t4=t4, fl=fl, Y1=Yb[1]: e.tensor_tensor(fl(Y1), t3[:], t4[:], ALU.add), reads=[t3, t4], writes=[Yb[1]])
                yield
            B32 = ctmp("A32", [128, GC, 2, 128], F32, 3)
            for c2 in range(GC // 2):
                bank = kb.ps()
                for u in range(2):
                    c = 2 * c2 + u
                    kb.op("pe", lambda e, bank=bank, c=c, u=u, Y0=Yb[0]: e.matmul(bank[:, u * 256:(u + 1) * 256], Y0[:, c, :], Fi1[:, :], start=True, stop=False),
                          reads=[Yb[0], Fi1], writes=[bank], acc=(u == 1))
                    kb.op("pe", lambda e, bank=bank, c=c, u=u, Y1=Yb[1]: e.matmul(bank[:, u * 256:(u + 1) * 256], Y1[:, c, :], Fi2[:, :], start=False, stop=True),
                          reads=[Yb[1], Fi2], writes=[bank], acc=True)
                kb.op("act", lambda e, bank=bank, c2=c2, B32=B32: e.copy(B32[:, 2 * c2:2 * c2 + 2, :, :].rearrange("p c r k -> p (c r k)"), bank[:, :]), reads=[bank], writes=[B32])
                yield
            Bb = [ctmp("Ab", [128, GC, 128], BF16, 8) for _ in range(2)]
            yield from twiddle(B32, Bb, conj=True)
            xk = sig[1 + o]
            znew = tmp("zb", [64, GC, 128], BF16, n=2)
            for qd in range(GC // 4):
                yb = kb.bank(4 + 2 * par)
                cs = slice(4 * qd, 4 * qd + 4)
                kb.op("pe", lambda e, yb=yb, cs=cs, B0=Bb[0]: e.matmul(yb[0:64, :], Fc[:, 0:64], B0[:, cs, :], start=True, stop=False), reads=[Fc, Bb[0]], writes=[yb])
                kb.op("pe", lambda e, yb=yb, cs=cs, B1=Bb[1]: e.matmul(yb[0:64, :], Fsn[:, 0:64], B1[:, cs, :], start=False, stop=True), reads=[Fsn, Bb[1]], writes=[yb], acc=True)
                t = tmp("ep", [64, 4, 128], F32, n=2)
                sk = skipb[:, o, c0 + 4 * qd:c0 + 4 * qd + 4].unsqueeze(2).to_broadcast([64, 4, 128])
                kb.op("pool", lambda e, t=t, cs=cs, sk=sk, cur=cur: e.tensor_tensor(t[:], cur[0:64, cs, :], sk, ALU.mult), reads=[cur, skipb], writes=[t])
                kb.op("dve", lambda e, t=t, yb=yb: e.scalar_tensor_tensor(t[:].rearrange("p c k -> p (c k)"), yb[0:64, :], 1.0 / NF, t[:].rearrange("p c k -> p (c k)"), ALU.mult, ALU.add),
                      reads=[yb, t], writes=[t])
                kb.op("dve", lambda e, t=t, cs=cs, xk=xk, znew=znew: e.tensor_tensor(znew[:, cs, :], t[:], xk[0:64, cs, :], ALU.mult), reads=[t, xk], writes=[znew])
                yield
            if debug and g == 0:
                d_ = kb.out(f"dz{o}", [64, GC, 128], BF16)
                kb.store(d_, d_[:], znew, znew[:])
                d_ = kb.out(f"dY{o}", [128, GC, 128], BF16)
                kb.store(d_, d_[:], Yb[0], Yb[0][:])
                d_ = kb.out(f"dB{o}", [128, GC, 128], BF16)
                kb.store(d_, d_[:], Bb[0], Bb[0][:])
            cur = znew
        kb.store(zout, zout[:, c0:c0 + GC, :], cur, cur[:], q="pool")

    NCH = 2
    for g2 in range(0, ngroups, NCH):
        gens = [group_gen(g2 + p, p % 2) for p in range(NCH) if g2 + p < ngroups]
        while gens:
            for gen in list(gens):
                try:
                    next(gen)
                except StopIteration:
                    gens.remove(gen)
    kb.finish()
    return kb


def dft_consts():
    a = np.arange(128)
    th = 2 * math.pi * np.outer(a, a) / 128.0
    c, s = np.cos(th), np.sin(th)
    ph = 2 * math.pi * np.outer(a, a) / NF
    d = {"Ff": np.concatenate([c, -s], 1), "Fi1": np.concatenate([c, s], 1), "Fi2": np.concatenate([-s, c], 1),
         "Fc": c, "Fs": s, "Fsn": -s, "Tc": np.cos(ph), "Ts": np.sin(ph)}
    return {k: np.ascontiguousarray(v, dtype=np.float32) for k, v in d.items()}


def hyena_core_inputs(inp, r, u_main, u_ctx):
    sl = slice(CPC * r, CPC * (r + 1))
    d = {}
    d["uc"] = np.ascontiguousarray(np.stack([u_ctx[si * 2048 + CPC * r: si * 2048 + CPC * (r + 1)].reshape(2, 128, 256) for si in range(3)], axis=0))
    cwv = np.concatenate([inp["hyena_conv_w"][0], inp["hyena_conv_b"][0][None, :]], axis=0).reshape(4, 3, 2048)[:, :, sl]
    d["cwc"] = np.ascontiguousarray(cwv.reshape(4, 3, 2, 128).transpose(3, 1, 2, 0), dtype=np.float32)
    ZTc, tc = hyena_consts(256)
    d["ZTc"] = ZTc
    d["tpc"] = np.ascontiguousarray(tc.reshape(4, 128).T, dtype=np.float32)
    d["deltab"] = np.ascontiguousarray(np.broadcast_to(hyena_deltas()[sl][None, :], (128, CPC)), dtype=np.float32)
    d["skipc"] = np.ascontiguousarray(np.broadcast_to(inp["hyena_skip"][0][None, :, sl], (128, 2, CPC)), dtype=np.float32)
    d["ident"] = np.eye(128, dtype=np.float32)
    nn = np.arange(512)
    th = 2 * math.pi * np.outer(nn, nn) / 512.0
    d["Dc"] = np.ascontiguousarray(np.cos(th).reshape(4, 128, 512).transpose(1, 0, 2), dtype=np.float32)
    d["Dsn"] = np.ascontiguousarray((-np.sin(th)).reshape(4, 128, 512).transpose(1, 0, 2), dtype=np.float32)
    for si, nm in enumerate(("v", "x1", "x2")):
        a = u_main[si * 2048 + CPC * r: si * 2048 + CPC * (r + 1)]
        d[nm] = np.ascontiguousarray(a.reshape(CPC, 64, 128).transpose(1, 0, 2))
    ZT, t = hyena_consts(LH)
    d["ZT"] = ZT
    d["tprow"] = np.ascontiguousarray(t[None, :])
    f32 = lambda a: np.ascontiguousarray(a, dtype=np.float32)
    d["fw1"] = f32(inp["hyena_f_w1"][0])
    d["fb1f"] = f32(np.stack([inp["hyena_f_b1"][0], inp["hyena_f_freq"][0, 0]], axis=1))
    w2 = inp["hyena_f_w2"][0]
    d["fw2d"] = f32(np.concatenate([w2, w2], axis=1))
    d["fb2f"] = f32(np.stack([np.tile(inp["hyena_f_b2"][0], 2), np.tile(inp["hyena_f_freq"][0, 1], 2)], axis=1))
    w3 = inp["hyena_f_w3"][0].reshape(64, 2, 2, 2048)
    d["w3s"] = f32(np.concatenate([w3[:, :, 0, sl], w3[:, :, 1, sl]], axis=0))
    d["deltac"] = f32(hyena_deltas()[sl].reshape(2, 128).T)
    d["skipb"] = f32(np.broadcast_to(inp["hyena_skip"][0][None, :, sl], (64, 2, CPC)))
    d.update(dft_consts())
    return d


def run_layer_hyena(inp, mod, x2d, ctx2d, progs):
    li = 2
    w_t = tile_weight(inp["hyena_w_in"][0])
    cwv = np.concatenate([inp["hyena_conv_w"][0], inp["hyena_conv_b"][0][None, :]], axis=0)
    cw = np.ascontiguousarray(cwv.reshape(4, 48, 128).transpose(2, 1, 0), dtype=np.float32)
    maps = []
    for r in range(NCORES):
        hm = np.ones((128, 2), np.float32)
        if r == 0:
            hm[:, 0] = 0
        if r == NCORES - 1:
            hm[:, 1] = 0
        maps.append({"xT": xT_input(x2d, ctx2d, r, halo=1), "modv": modv_input(inp["norm_g"][li], mod[li, 0], mod[li, 1]),
                     "cw": cw, "hm": hm, "w_in": w_t})
    res = run_prog(progs["proj_hyena"], maps)
    full = {n: gather_tokens([np.asarray(res[r][n]) for r in range(NCORES)]) for n in ("uT", "gT")}
    return run_layer_hyena_b(inp, mod, x2d, ctx2d, progs, full)


def run_layer_hyena_b(inp, mod, x2d, ctx2d, progs, full):
    li = 2
    u_main = full["uT"][:, :8192]
    u_ctx = full["uT"][:, 8192:]
    maps = [hyena_core_inputs(inp, r, u_main, u_ctx) for r in range(NCORES)]
    res = run_prog(progs["hyena_core"], maps)
    zmain = np.concatenate([np.asarray(res[r]["z"]).transpose(1, 0, 2).reshape(CPC, 8192) for r in range(NCORES)], axis=0)
    zc = np.concatenate([np.asarray(res[r]["zc"]).reshape(256, CPC).T for r in range(NCORES)], axis=0)
    zfull = np.concatenate([zmain, zc], axis=1)
    oT = [scatter_tokens(zfull, r) for r in range(NCORES)]
    gT = [scatter_tokens(full["gT"], r) for r in range(NCORES)]
    xT = [xT_input(x2d, ctx2d, r) for r in range(NCORES)]
    xo = run_tail(progs["tail"], oT, gT, xT, mod[li, 0], mod[li, 1], inp["hyena_w_out"][0])
    return split_xo(xo)


def kernel(**inputs):
    inp = {k: np.asarray(v) for k, v in inputs.items()}
    mod = run_mod(inp)
    x2d = np.ascontiguousarray(inp["x"][0], dtype=np.float32)
    ctx2d = np.ascontiguousarray(inp["ctx"][0], dtype=np.float32)
    progs = {"proj_swa": build_proj_swa(4), "swa_att": build_swa_att(), "tail": build_tail()}
    x2d, ctx2d = run_layer_swa(inp, mod, x2d, ctx2d, progs)
    progs = {"proj_mla": build_proj_mla(), "mla_att": build_mla_att(), "tail": build_tail()}
    x2d, ctx2d = run_layer_mla(inp, mod, x2d, ctx2d, progs)
    progs = {"proj_hyena": build_proj_hyena(), "hyena_core": build_hyena_core(), "tail": build_tail()}
    x2d, ctx2d = run_layer_hyena(inp, mod, x2d, ctx2d, progs)
    lam_init = 0.8 - 0.6 * math.exp(-0.3 * 3)
    progs = {"proj_diff": build_proj_swa(16), "diff_att": build_diff_att(lam_init), "tail": build_tail()}
    x2d, ctx2d = run_layer_diff(inp, mod, x2d, ctx2d, progs)
    return np.ascontiguousarray(x2d[None], dtype=np.float32)


def tile_weight(w, ncol=256):
    K, N = w.shape
    ng = (N + ncol - 1) // ncol
    wp = np.zeros((K, ng * ncol), np.float32)
    wp[:, :N] = w
    return np.ascontiguousarray(wp.reshape(K // 128, 128, ng, ncol).transpose(2, 1, 0, 3))
```

```python
import math
import numpy as np
import ml_dtypes
import concourse.bass as bass
import concourse.mybir as mybir
from concourse.bass_utils import run_bass_kernel_spmd

F32 = mybir.dt.float32
BF16 = mybir.dt.bfloat16
ALU = mybir.AluOpType
AF = mybir.ActivationFunctionType
AX = mybir.AxisListType
NCORES = 8


class Buf:
    _n = 0

    def __init__(self, name):
        Buf._n += 1
        self.name = f"{name}_{Buf._n}"
        self.writes = {}
        self.reads = {}
        self.dsem = None
        self.dcnt = 0


class Prog:
    def __init__(self, nc):
        self.nc = nc
        self.lists = {e: [] for e in ("pe", "act", "dve", "pool", "sp")}
        self.esem = {}
        self.ecnt = {e: 0 for e in self.lists}
        self.seen = {e: {} for e in self.lists}
        self.sems = {}
        self.out_events = []
        self.dbufs = []
        for e in ("pe", "act", "dve", "pool"):
            self.esem[e] = nc.alloc_semaphore(f"es_{e}")
            self.sems[("e", e)] = self.esem[e]

    def _wait(self, eng, ev):
        if ev is None:
            return
        key, val = ev
        if self.seen[eng].get(key, 0) >= val:
            return
        self.seen[eng][key] = val
        sem = self.sems[key]
        self.lists[eng].append(lambda e, sem=sem, val=val: e.wait_ge(sem, val))

    def _deps(self, eng, reads, writes, acc=False):
        for b in reads:
            for k, v in b.writes.items():
                self._wait(eng, (k, v))
        for b in writes:
            for k, v in b.writes.items():
                if acc and k == ("e", eng):
                    continue
                self._wait(eng, (k, v))
            for k, v in b.reads.items():
                self._wait(eng, (k, v))

    def _mark(self, ev, reads, writes):
        k, v = ev
        for b in reads:
            if b.reads.get(k, 0) < v:
                b.reads[k] = v
        for b in writes:
            if b.writes.get(k, 0) < v:
                b.writes[k] = v
            b.reads = {}

    def op(self, eng, fn, reads=(), writes=(), acc=False):
        self._deps(eng, reads, writes, acc)
        sem = self.esem[eng]
        self.ecnt[eng] += 1
        ev = (("e", eng), self.ecnt[eng])
        self.lists[eng].append(lambda e, fn=fn, sem=sem: fn(e).then_inc(sem, 1))
        self._mark(ev, reads, writes)
        return ev

    def dma(self, q, out, in_, sb, reads=(), writes=(), is_output=False, **kw):
        self._deps(q, reads, writes)
        if sb.dsem is None:
            sb.dsem = self.nc.alloc_semaphore(f"ds_{sb.name}")
            self.sems[("d", sb.name)] = sb.dsem
        sb.dcnt += 1
        if not hasattr(self, "dbufs"):
            self.dbufs = []
        if sb not in self.dbufs:
            self.dbufs.append(sb)
        ev = (("d", sb.name), 16 * sb.dcnt)
        sem = sb.dsem
        self.lists[q].append(lambda e, out=out, in_=in_, sem=sem, kw=kw: e.dma_start(out=out, in_=in_, **kw).then_inc(sem, 16))
        self._mark(ev, reads, writes)
        if is_output:
            self.out_events.append(ev)
        return ev

    def barrier(self):
        for eng in self.lists:
            for e2, cnt in self.ecnt.items():
                if e2 in self.esem and cnt > 0:
                    self._wait(eng, (("e", e2), cnt))
            for b in self.dbufs:
                self._wait(eng, (("d", b.name), 16 * b.dcnt))

    def finish(self):
        for ev in self.out_events:
            self._wait("sp", ev)
        nc = self.nc
        lists = self.lists
        with nc.Block() as block:
            @block.tensor
            def _(e):
                for f in lists["pe"]:
                    f(e)

            @block.scalar
            def _(e):
                for f in lists["act"]:
                    f(e)

            @block.vector
            def _(e):
                for f in lists["dve"]:
                    f(e)

            @block.gpsimd
            def _(e):
                for f in lists["pool"]:
                    f(e)

            @block.sync
            def _(e):
                for f in lists["sp"]:
                    f(e)


class T:
    def __init__(self, h, b):
        self.h = h
        self.b = b

    def __getitem__(self, key):
        return self.h[key]


class KB:
    def __init__(self):
        self.nc = bass.Bass("TRN2", target_bir_lowering=False)
        self.p = Prog(self.nc)
        self.in_names = []
        self.out_names = []
        self._ps = [T(self.nc.alloc_psum_tensor(f"psb{i}", [128, 512], F32), Buf(f"psb{i}")) for i in range(8)]
        self._psi = 0
        self._n = 0

    def inp(self, name, shape, dt=F32):
        self.in_names.append(name)
        return self.nc.dram_tensor(name, list(shape), dt, kind="ExternalInput").ap()

    def out(self, name, shape, dt=F32):
        self.out_names.append(name)
        return T(self.nc.dram_tensor(name, list(shape), dt, kind="ExternalOutput").ap(), Buf(name))

    def scratch(self, name, shape, dt=F32):
        return T(self.nc.dram_tensor(name, list(shape), dt, kind="Internal").ap(), Buf(name))

    def sb(self, name, shape, dt=F32):
        self._n += 1
        nm = f"{name}_{self._n}"
        return T(self.nc.alloc_sbuf_tensor(nm, list(shape), dt), Buf(nm))

    def ps(self):
        pool = getattr(self, "_pool", list(range(8)))
        t = self._ps[pool[self._psi % len(pool)]]
        self._psi += 1
        return t

    def bank(self, i):
        return self._ps[i]

    def op(self, eng, fn, reads=(), writes=(), acc=False):
        return self.p.op(eng, fn, [t.b for t in reads], [t.b for t in writes], acc)

    def load(self, dst, dst_ap, src_ap, q="sp", src=None, **kw):
        rd = [src.b] if src is not None else []
        return self.p.dma(q, dst_ap, src_ap, dst.b, reads=rd, writes=[dst.b], **kw)

    def store(self, dst, dst_ap, src, src_ap, q="sp", is_output=True, **kw):
        return self.p.dma(q, dst_ap, src_ap, src.b, reads=[src.b], writes=[dst.b], is_output=is_output, **kw)

    def finish(self):
        self.p.finish()
        return self.nc

    def const_load(self, name, shape, dt=F32, q="sp"):
        ap = self.inp(name, shape, dt)
        t = self.sb(name, shape, dt)
        self.load(t, t[:], ap, q=q)
        return t

    def rstd_from_ss(self, dst, dst_ap, ss, ss_ap, inv_n, eps=1e-6):
        self.op("dve", lambda e: e.tensor_scalar(dst_ap, ss_ap, inv_n, eps, ALU.mult, ALU.add), reads=[ss], writes=[dst])
        self.op("act", lambda e: e.activation(dst_ap, dst_ap, AF.Sqrt), reads=[dst], writes=[dst])
        self.op("dve", lambda e: e.reciprocal(dst_ap, dst_ap), reads=[dst], writes=[dst])


def run_prog(kb, in_maps):
    res = run_bass_kernel_spmd(kb.nc, in_maps, core_ids=list(range(NCORES)))
    return res.results


class WStream:
    def __init__(self, kb, name, kc, ncol_max, cast_eng="pool"):
        self.kb = kb
        self.kc = kc
        self.ncol_max = ncol_max
        self.stg = [kb.sb(f"{name}_stg{i}", [128, kc, ncol_max], F32) for i in range(2)]
        self.wb = [kb.sb(f"{name}_wb{i}", [128, kc, ncol_max], BF16) for i in range(2)]
        self.n = 0
        self.cast_eng = cast_eng

    def issue(self, w_ap, c0, ncol):
        kb = self.kb
        s = self.n % 2
        self.n += 1
        stg, wb = self.stg[s], self.wb[s]
        gi = c0 // self.ncol_max
        h = max(1, self.kc // 2)
        for k0 in range(0, self.kc, h):
            kb.load(stg, stg[:, k0:k0 + h, :], w_ap[gi, :, k0:k0 + h, :], q="sp")
        return (stg, wb, ncol)

    def cast(self, handle):
        kb = self.kb
        stg, wb, ncol = handle
        h = max(1, self.kc // 2)
        kb.op("act", lambda e: e.copy(wb[:, 0:h, :ncol], stg[:, 0:h, :ncol]), reads=[stg], writes=[wb])
        if h < self.kc:
            kb.op("dve", lambda e: e.tensor_copy(wb[:, h:, :ncol], stg[:, h:, :ncol]), reads=[stg], writes=[wb])
        return wb


def build_mod():
    kb = KB()
    cs = kb.const_load("cs", [128, 16, 2])
    adaw = kb.inp("adaw", [4, 2048, 768])
    bias = kb.const_load("adab", [2, 4, 768])
    modo = kb.out("modo", [2, 4 * 768])
    kb.op("act", lambda e: e.activation(cs[:], cs[:], AF.Silu), reads=[cs], writes=[cs])
    wt = [kb.sb(f"adaw{i}", [128, 16, 768], F32) for i in range(2)]
    osb = kb.sb("osb", [2, 4 * 768], F32)
    for i in range(4):
        w = wt[i % 2]
        for g in range(4):
            kb.load(w, w[:, 4 * g:4 * g + 4, :], adaw[i, 512 * g:512 * (g + 1), :].rearrange("(kc p) n -> p kc n", p=128), q="sp")
        for nb in range(2):
            ps = kb.ps()
            for kc in range(16):
                kb.op("pe", lambda e, ps=ps, w=w, kc=kc, nb=nb: e.matmul(ps[0:2, 0:384], cs[:, kc, :], w[:, kc, nb * 384:(nb + 1) * 384],
                                                                       start=(kc == 0), stop=(kc == 15)),
                      reads=[cs, w], writes=[ps], acc=True)
            c0 = i * 768 + nb * 384
            kb.op("dve", lambda e, ps=ps, c0=c0, i=i, nb=nb: e.tensor_tensor(osb[0:2, c0:c0 + 384], ps[0:2, 0:384],
                                                                           bias[0:2, i, nb * 384:(nb + 1) * 384], ALU.add),
                  reads=[ps, bias], writes=[osb])
    kb.store(modo, modo[:], osb, osb[:])
    kb.finish()
    return kb


def run_mod(inp):
    kb = build_mod()
    c = inp["c"].reshape(2048)
    cc = inp["c_ctx"].reshape(2048)
    cs = np.stack([c.reshape(16, 128).T, cc.reshape(16, 128).T], axis=-1).astype(np.float32)
    maps = []
    for r in range(NCORES):
        sl = slice(768 * r, 768 * (r + 1))
        adab = np.ascontiguousarray(np.broadcast_to(inp["ada_b"][None, :, sl], (2, 4, 768)))
        maps.append({"cs": np.ascontiguousarray(cs), "adaw": np.ascontiguousarray(inp["ada_w"][:, :, sl]), "adab": adab})
    res = run_prog(kb, maps)
    mod = np.zeros((4, 2, 6144), np.float32)
    for r in range(NCORES):
        mod[:, :, 768 * r:768 * (r + 1)] = res[r]["modo"].reshape(2, 4, 768).transpose(1, 0, 2)
    return mod


TC = 32


def rope_tables(positions, rot_dim):
    row = (positions // 64).astype(np.float32)
    col = (positions % 64).astype(np.float32)
    half = rot_dim // 2
    inv = (1.0 / (10000.0 ** (np.arange(0, half, 2, dtype=np.float32) / half))).astype(np.float32)
    ar = row[:, None] * inv[None, :]
    ac = col[:, None] * inv[None, :]
    ang = np.concatenate([ar, ar, ac, ac], axis=-1)
    return np.cos(ang).T.astype(np.float32), np.sin(ang).T.astype(np.float32)


def rope_matrix(rot_dim, reps):
    half = rot_dim // 2
    qr = half // 2
    R = np.zeros((rot_dim, rot_dim), np.float32)
    for seg in range(2):
        o = seg * half
        for i in range(qr):
            R[o + qr + i, o + i] = -1.0
            R[o + i, o + qr + i] = 1.0
    full = np.zeros((rot_dim * reps, rot_dim * reps), np.float32)
    for r in range(reps):
        full[r * rot_dim:(r + 1) * rot_dim, r * rot_dim:(r + 1) * rot_dim] = R
    return full


def blockdiag_ones(bs):
    m = np.zeros((128, 128), np.float32)
    for r in range(128 // bs):
        m[r * bs:(r + 1) * bs, r * bs:(r + 1) * bs] = 1.0
    return m


class ProjCtx:
    def __init__(self, kb, tm, halo):
        self.kb = kb
        self.tm = tm
        self.halo = halo
        self.ttot = tm + TC
        tiles = []
        c = 0
        while c < tm:
            w = min(512, tm - c)
            tiles.append((c, w, False))
            c += w
        tiles.append((tm, TC, True))
        self.tiles = tiles
        self._tmp = {}

    def tmp(self, name, shape, dt, n=2):
        key = name
        if key not in self._tmp:
            self._tmp[key] = [[self.kb.sb(f"{name}{i}", shape, dt) for i in range(n)], 0]
        lst = self._tmp[key]
        t = lst[0][lst[1] % n]
        lst[1] += 1
        return t


def emit_modnorm(pc, xT_ap, modv):
    kb = pc.kb
    T_ = pc.ttot
    ones = kb.sb("ones32", [128, 128], F32)
    kb.op("pool", lambda e: e.memset(ones[:], 1.0), writes=[ones])
    AB = kb.sb("AB", [128, 2, 16], F32)
    for w_, (sc) in enumerate((2, 4)):
        kb.op("dve", lambda e, w_=w_, sc=sc: e.tensor_scalar(AB[:, w_, :], modv[:, sc, :], 1.0, None, ALU.add), reads=[modv], writes=[AB])
        kb.op("dve", lambda e, w_=w_: e.tensor_tensor(AB[:, w_, :], AB[:, w_, :], modv[:, 0, :], ALU.mult), reads=[AB, modv], writes=[AB])
    rstd = kb.sb("rstd", [128, T_], F32)
    pss = [kb.bank(i) for i in range(len(pc.tiles))]
    for kc in range(16):
        xc = pc.tmp("xc", [128, T_], F32, n=3)
        kb.load(xc, xc[:], xT_ap[128 * kc:128 * (kc + 1), :], q=("sp" if kc % 2 == 0 else "pool"))
        for ti, (c0, w, isc) in enumerate(pc.tiles):
            ps = pss[ti]
            sq = pc.tmp("sq32", [128, 512], F32)
            kb.op("act", lambda e, sq=sq, xc=xc, c0=c0, w=w: e.activation(sq[:, :w], xc[:, c0:c0 + w], AF.Square), reads=[xc], writes=[sq])
            kb.op("pe", lambda e, ps=ps, sq=sq, kc=kc, w=w: e.matmul(ps[:, :w], ones[:], sq[:, :w], start=(kc == 0), stop=(kc == 15)),
                  reads=[ones, sq], writes=[ps], acc=(kc != 0))
    for ti, (c0, w, isc) in enumerate(pc.tiles):
        kb.rstd_from_ss(rstd, rstd[:, c0:c0 + w], pss[ti], pss[ti][:, :w], 1.0 / 2048)
    hT = kb.sb("hT", [128, 16, T_], BF16)
    for kc in range(16):
        xc = pc.tmp("xc", [128, T_], F32, n=3)
        kb.load(xc, xc[:], xT_ap[128 * kc:128 * (kc + 1), :], q=("sp" if kc % 2 == 0 else "pool"))
        for (c0, w, isc) in pc.tiles:
            wi = 1 if isc else 0
            bi = 3 if isc else 1
            t = pc.tmp("mn32", [128, 512], F32)
            kb.op("dve", lambda e, t=t, xc=xc, kc=kc, c0=c0, w=w, wi=wi: e.scalar_tensor_tensor(t[:, :w], xc[:, c0:c0 + w], AB[:, wi, kc:kc + 1],
                                                                                             rstd[:, c0:c0 + w], ALU.mult, ALU.mult),
                  reads=[xc, AB, rstd], writes=[t])
            kb.op("act", lambda e, t=t, kc=kc, c0=c0, w=w, bi=bi: e.activation(hT[:, kc, c0:c0 + w], t[:, :w], AF.Identity, bias=modv[:, bi, kc:kc + 1]),
                  reads=[t, modv], writes=[hT])
    return hT


def emit_epilogue_gen(pc, blk, accs, C):
    kb = pc.kb
    kind = blk["kind"]
    T_ = pc.ttot
    if kind == "keep":
        dst, idx = blk["dst"], blk["idx"]
        for ti, (c0, w, isc) in enumerate(pc.tiles):
            ps = accs[ti]
            eng = "act" if ti % 2 == 0 else "dve"
            if eng == "act":
                kb.op("act", lambda e, ps=ps, c0=c0, w=w: e.copy(dst[:, idx, c0:c0 + w], ps[:, :w]), reads=[ps], writes=[dst])
            else:
                kb.op("dve", lambda e, ps=ps, c0=c0, w=w: e.tensor_copy(dst[:, idx, c0:c0 + w], ps[:, :w]), reads=[ps], writes=[dst])
            yield
        return
    dt = blk.get("dt", F32)
    stage = pc.tmp("stg32" if dt == F32 else "stg16", [128, T_], dt, n=3)
    if kind == "resid":
        xs, gv, ob = blk["xs"], blk["gv"], blk["ob"]
        for ti, (c0, w, isc) in enumerate(pc.tiles):
            ps = accs[ti]
            wi = 1 if isc else 0
            kb.op("dve", lambda e, ps=ps, c0=c0, w=w, wi=wi: e.scalar_tensor_tensor(stage[:, c0:c0 + w], ps[:, :w], gv[:, wi, ob:ob + 1], xs[:, ob, c0:c0 + w], ALU.mult, ALU.add),
                  reads=[ps, gv, xs], writes=[stage])
            yield
    elif kind == "raw":
        for ti, (c0, w, isc) in enumerate(pc.tiles):
            ps = accs[ti]
            if ti % 2 == 0:
                kb.op("act", lambda e, ps=ps, c0=c0, w=w: e.copy(stage[:, c0:c0 + w], ps[:, :w]), reads=[ps], writes=[stage])
            else:
                kb.op("dve", lambda e, ps=ps, c0=c0, w=w: e.tensor_copy(stage[:, c0:c0 + w], ps[:, :w]), reads=[ps], writes=[stage])
            yield
    elif kind == "hn":
        bs, gain, rope = blk["bs"], blk["gain"], blk["rope"]
        onesb = C["ones128"] if bs == 128 else C["ones64"]

        def tile_gen(ti, c0, w, isc):
            ps = accs[ti]
            sqb = pc.tmp("sqb", [128, 512], BF16, n=3)
            kb.op("act", lambda e: e.activation(sqb[:, :w], ps[:, :w], AF.Square), reads=[ps], writes=[sqb])
            yield
            pss = kb.ps()
            kb.op("pe", lambda e: e.matmul(pss[:, :w], onesb[:], sqb[:, :w], start=True, stop=True), reads=[onesb, sqb], writes=[pss])
            yield
            t1 = pc.tmp("t1", [128, 512], F32, n=3)
            kb.op("dve", lambda e: e.tensor_scalar(t1[:, :w], pss[:, :w], 1.0 / bs, 1e-6, ALU.mult, ALU.add), reads=[pss], writes=[t1])
            yield
            kb.op("act", lambda e: e.activation(t1[:, :w], t1[:, :w], AF.Sqrt), reads=[t1], writes=[t1])
            yield
            kb.op("dve", lambda e: e.reciprocal(t1[:, :w], t1[:, :w]), reads=[t1], writes=[t1])
            yield
            if rope and not isc:
                yn = pc.tmp("yn", [128, 512], F32, n=3)
                kb.op("dve", lambda e: e.scalar_tensor_tensor(yn[:, :w], ps[:, :w], gain, t1[:, :w], ALU.mult, ALU.mult),
                      reads=[ps, t1, blk["gain_t"]], writes=[yn])
                yield
                ynb = pc.tmp("ynb", [128, 512], BF16, n=3)
                kb.op("act", lambda e: e.copy(ynb[:, :w], yn[:, :w]), reads=[yn], writes=[ynb])
                yield
                psr = kb.ps()
                Rm = C["rm128"] if bs == 128 else C["rm64"]
                kb.op("pe", lambda e: e.matmul(psr[:, :w], Rm[:], ynb[:, :w], start=True, stop=True), reads=[Rm, ynb], writes=[psr])
                cos, sin = (C["cos128"], C["sin128"]) if bs == 128 else (C["cos64"], C["sin64"])
                kb.op("pool", lambda e: e.tensor_tensor(yn[:, :w], yn[:, :w], cos[:, c0:c0 + w], ALU.mult), reads=[yn, cos], writes=[yn])
                yield
                o2 = pc.tmp("o2", [128, 512], F32, n=3)
                kb.op("dve", lambda e: e.tensor_tensor(o2[:, :w], psr[:, :w], sin[:, c0:c0 + w], ALU.mult), reads=[psr, sin], writes=[o2])
                yield
                kb.op("dve", lambda e: e.tensor_tensor(stage[:, c0:c0 + w], yn[:, :w], o2[:, :w], ALU.add), reads=[yn, o2], writes=[stage])
            else:
                kb.op("dve", lambda e: e.scalar_tensor_tensor(stage[:, c0:c0 + w], ps[:, :w], gain, t1[:, :w], ALU.mult, ALU.mult),
                      reads=[ps, t1, blk["gain_t"]], writes=[stage])
            yield

        main = [tile_gen(ti, c0, w, isc) for ti, (c0, w, isc) in enumerate(pc.tiles) if not isc]
        rest = [tile_gen(ti, c0, w, isc) for ti, (c0, w, isc) in enumerate(pc.tiles) if isc]
        for grp in (main[0:2], main[2:] + rest):
            gens = list(grp)
            while gens:
                for gen in list(gens):
                    try:
                        next(gen)
                    except StopIteration:
                        gens.remove(gen)
                yield
    elif kind == "conv3":
        cw = blk["cw"]
        ci = blk["ci"]
        hm = C["hm"]
        u = pc.tmp("u32", [128, T_], F32)
        for ti, (c0, w, isc) in enumerate(pc.tiles):
            ps = accs[ti]
            if ti % 2 == 0:
                kb.op("act", lambda e, ps=ps, c0=c0, w=w: e.copy(u[:, c0:c0 + w], ps[:, :w]), reads=[ps], writes=[u])
            else:
                kb.op("dve", lambda e, ps=ps, c0=c0, w=w: e.tensor_copy(u[:, c0:c0 + w], ps[:, :w]), reads=[ps], writes=[u])
            yield
        tm = pc.tm
        kb.op("dve", lambda e: e.tensor_scalar(u[:, 0:1], u[:, 0:1], hm[:, 0:1], None, ALU.mult), reads=[u, hm], writes=[u])
        kb.op("dve", lambda e: e.tensor_scalar(u[:, tm - 1:tm], u[:, tm - 1:tm], hm[:, 1:2], None, ALU.mult), reads=[u, hm], writes=[u])
        n = tm - 2
        yield
        kb.op("dve", lambda e: e.tensor_scalar(stage[:, 1:1 + n], u[:, 0:n], cw[:, ci, 0:1], cw[:, ci, 3:4], ALU.mult, ALU.add), reads=[u, cw], writes=[stage])
        yield
        kb.op("dve", lambda e: e.scalar_tensor_tensor(stage[:, 1:1 + n], u[:, 1:1 + n], cw[:, ci, 1:2], stage[:, 1:1 + n], ALU.mult, ALU.add),
              reads=[u, cw, stage], writes=[stage])
        yield
        kb.op("dve", lambda e: e.scalar_tensor_tensor(stage[:, 1:1 + n], u[:, 2:2 + n], cw[:, ci, 2:3], stage[:, 1:1 + n], ALU.mult, ALU.add),
              reads=[u, cw, stage], writes=[stage])
        kb.op("pool", lambda e: e.tensor_copy(stage[:, tm:tm + TC], u[:, tm:tm + TC]), reads=[u], writes=[stage])
    else:
        raise ValueError(kind)
    out, r0 = blk["out"], blk["r0"]
    h = pc.halo
    if h:
        kb.store(out, out[r0:r0 + 128, 0:pc.tm - 2], stage, stage[:, 1:pc.tm - 1], q="pool")
        kb.store(out, out[r0:r0 + 128, pc.tm - 2:pc.tm - 2 + TC], stage, stage[:, pc.tm:pc.tm + TC], q="pool")
    else:
        kb.store(out, out[r0:r0 + 128, :], stage, stage[:], q="pool")


def emit_proj_stage(pc, src, KC, w_ap, blocks, ws, C):
    kb = pc.kb
    BPG = ws.ncol_max // 128
    ngroups = (len(blocks) + BPG - 1) // BPG
    nt = len(pc.tiles)
    if nt <= 3:
        acc_sets = [[0, 1, 2], [3, 4, 5]]
        kb._pool = [6, 7]
    else:
        acc_sets = [[0, 1, 2, 3], [4, 5, 6, 7]]
    kb._psi = 0

    def issue(g):
        nb = min(BPG, len(blocks) - BPG * g)
        return ws.issue(w_ap, 128 * BPG * g, 128 * nb)

    def mm_gen(cur, bi, accs):
        for ti, (c0, w, isc) in enumerate(pc.tiles):
            ps = accs[ti]
            for kc in range(KC):
                kb.op("pe", lambda e, ps=ps, kc=kc, c0=c0, w=w: e.matmul(ps[:, :w], cur[:, kc, bi * 128:(bi + 1) * 128], src[:, kc, c0:c0 + w],
                                                                   start=(kc == 0), stop=(kc == KC - 1)),
                      reads=[cur, src], writes=[ps], acc=(kc != 0))
                if kc % 4 == 3:
                    yield

    def drive(gens):
        gens = [g_ for g_ in gens if g_ is not None]
        while gens:
            for gen in list(gens):
                try:
                    next(gen)
                except StopIteration:
                    gens.remove(gen)

    wt = {0: ws.cast(issue(0))}
    pend = {}
    prev = None
    for b, blk in enumerate(blocks):
        g, bi = divmod(b, BPG)
        if bi == 0 and g + 1 < ngroups:
            pend[g + 1] = issue(g + 1)
        accs = [kb.bank(i) for i in acc_sets[b % 2]][:nt]
        drive([mm_gen(wt[g], bi, accs), prev])
        prev = emit_epilogue_gen(pc, blk, accs, C)
        last_of_group = (bi == BPG - 1) or (b == len(blocks) - 1)
        if last_of_group and (g + 1) in pend:
            wt[g + 1] = ws.cast(pend.pop(g + 1))
    drive([prev])
    kb._pool = list(range(8))


def emit_widenorm(pc, raw, nb, gain, dst, C, gcol0=0):
    kb = pc.kb
    for (c0, w, isc) in pc.tiles:
        ps = kb.ps()
        for b in range(nb):
            sqb = pc.tmp("sqb", [128, 512], BF16)
            kb.op("act", lambda e, sqb=sqb, b=b, c0=c0, w=w: e.activation(sqb[:, :w], raw[:, b, c0:c0 + w], AF.Square), reads=[raw], writes=[sqb])
            kb.op("pe", lambda e, ps=ps, sqb=sqb, b=b, w=w: e.matmul(ps[:, :w], C["ones128"][:], sqb[:, :w], start=(b == 0), stop=(b == nb - 1)),
                  reads=[C["ones128"], sqb], writes=[ps], acc=True)
        t1 = pc.tmp("t1", [128, 512], F32)
        kb.rstd_from_ss(t1, t1[:, :w], ps, ps[:, :w], 1.0 / (128 * nb))
        for b in range(nb):
            kb.op("dve", lambda e, b=b, t1=t1, c0=c0, w=w: e.scalar_tensor_tensor(dst[:, b, c0:c0 + w], raw[:, b, c0:c0 + w], gain[:, gcol0 + b:gcol0 + b + 1], t1[:, :w], ALU.mult, ALU.mult),
                  reads=[raw, gain, t1], writes=[dst])


def proj_consts(kb, need64=False, rope=True, tm=1024):
    C = {}
    def cbf(name, shape):
        t32 = kb.const_load(name, shape)
        tb = kb.sb(name + "b", shape, BF16)
        kb.op("dve", lambda e: e.tensor_copy(tb[:], t32[:]), reads=[t32], writes=[tb])
        return tb
    C["ones128"] = cbf("c_ones128", [128, 128])
    if rope:
        C["rm128"] = cbf("c_rm128", [128, 128])
        C["cos128"] = kb.const_load("c_cos128", [128, tm])
        C["sin128"] = kb.const_load("c_sin128", [128, tm], q="pool")
    if need64:
        C["ones64"] = cbf("c_ones64", [128, 128])
        C["rm64"] = cbf("c_rm64", [128, 128])
        C["cos64"] = kb.const_load("c_cos64", [128, tm])
        C["sin64"] = kb.const_load("c_sin64", [128, tm], q="pool")
    return C


def proj_const_inputs(r, need64=False, rope=True, tm=1024):
    d = {"c_ones128": blockdiag_ones(128)}
    pos = np.arange(1024 * r, 1024 * (r + 1))
    if rope:
        d["c_rm128"] = rope_matrix(128, 1)
        d["c_cos128"], d["c_sin128"] = rope_tables(pos, 128)
    if need64:
        d["c_ones64"] = blockdiag_ones(64)
        d["c_rm64"] = rope_matrix(64, 2)
        c, s = rope_tables(pos, 64)
        d["c_cos64"], d["c_sin64"] = np.concatenate([c, c], 0), np.concatenate([s, s], 0)
    return {k: np.ascontiguousarray(v, dtype=np.float32) for k, v in d.items()}


def modv_input(norm_g, mod_m, mod_c):
    def l(v):
        return v.reshape(16, 128).T
    return np.ascontiguousarray(np.stack([l(norm_g), l(mod_m[0:2048]), l(mod_m[2048:4096]), l(mod_c[0:2048]), l(mod_c[2048:4096])], axis=1), dtype=np.float32)


def xT_input(x2d, ctx2d, r, halo=0):
    lo, hi = 1024 * r - halo, 1024 * (r + 1) + halo
    cols = []
    if lo < 0:
        cols.append(np.zeros((2048, halo), np.float32))
    cols.append(x2d[max(lo, 0):min(hi, 8192)].T)
    if hi > 8192:
        cols.append(np.zeros((2048, halo), np.float32))
    cols.append(ctx2d[TC * r:TC * (r + 1)].T)
    return np.ascontiguousarray(np.concatenate(cols, axis=1), dtype=np.float32)


def hn_blocks(n, bs, gain_t, col, rope, out, r0=0, dt=BF16):
    return [dict(kind="hn", bs=bs, gain=gain_t[:, col:col + 1], gain_t=gain_t, rope=rope, out=out, r0=r0 + 128 * i, dt=dt) for i in range(n)]


def raw_blocks(n, out, r0=0, dt=F32):
    return [dict(kind="raw", out=out, r0=r0 + 128 * i, dt=dt) for i in range(n)]


def build_proj_swa(nkv=4):
    kb = KB()
    pc = ProjCtx(kb, 1024, 0)
    T_ = pc.ttot
    xT = kb.inp("xT", [2048, T_])
    modv = kb.const_load("modv", [128, 5, 16])
    gains = kb.const_load("gains", [128, 2])
    C = proj_consts(kb)
    w = kb.inp("w_in", [(4096 + 256 * nkv) // 256, 128, 16, 256])
    qT = kb.out("qT", [2048, T_], BF16)
    kT = kb.out("kT", [128 * nkv, T_], BF16)
    vT = kb.out("vT", [128 * nkv, T_], BF16)
    gT = kb.out("gT", [2048, T_], BF16)
    hT = emit_modnorm(pc, xT, modv)
    blocks = (hn_blocks(16, 128, gains, 0, True, qT) + hn_blocks(nkv, 128, gains, 1, True, kT)
              + raw_blocks(nkv, vT, dt=BF16) + raw_blocks(16, gT, dt=BF16))
    ws = WStream(kb, "w", 16, 256)
    emit_proj_stage(pc, hT, 16, w, blocks, ws, C)
    kb.finish()
    return kb


def col2(a, b):
    return np.ascontiguousarray(np.stack([a, b], axis=1), dtype=np.float32)


def build_tail():
    kb = KB()
    pc = ProjCtx(kb, 1024, 0)
    T_ = pc.ttot
    oT = kb.inp("oT", [2048, T_], BF16)
    gT = kb.inp("gT", [2048, T_], BF16)
    xT = kb.inp("xT", [2048, T_])
    gv = kb.const_load("gv", [128, 2, 16])
    w = kb.inp("w_out", [8, 128, 16, 256])
    xo = kb.out("xo", [2048, T_], F32)
    xs = kb.sb("xs", [128, 16, T_], F32)
    for g in range(4):
        kb.load(xs, xs[:, 4 * g:4 * g + 4, :], xT[512 * g:512 * (g + 1), :].rearrange("(kc p) t -> p kc t", p=128), q="pool")
    aT = kb.sb("aT", [128, 16, T_], BF16)
    for kc in range(16):
        ob_ = pc.tmp("o_in", [128, T_], BF16)
        gb_ = pc.tmp("g_in", [128, T_], BF16)
        kb.load(ob_, ob_[:], oT[128 * kc:128 * (kc + 1), :], q="sp")
        kb.load(gb_, gb_[:], gT[128 * kc:128 * (kc + 1), :], q="sp")
        sg = pc.tmp("sg", [128, T_], F32)
        kb.op("act", lambda e, sg=sg, gb_=gb_: e.activation(sg[:], gb_[:], AF.Silu), reads=[gb_], writes=[sg])
        kb.op("dve", lambda e, sg=sg, ob_=ob_, kc=kc: e.tensor_tensor(aT[:, kc, :], sg[:], ob_[:], ALU.mult), reads=[sg, ob_], writes=[aT])
    blocks = [dict(kind="resid", xs=xs, gv=gv, ob=i, out=xo, r0=128 * i, dt=F32) for i in range(16)]
    ws = WStream(kb, "w", 16, 256)
    emit_proj_stage(pc, aT, 16, w, blocks, ws, {})
    kb.finish()
    return kb


def gv_input(mod_m, mod_c):
    def l(v):
        return v.reshape(16, 128).T
    return np.ascontiguousarray(np.stack([l(mod_m[4096:6144]), l(mod_c[4096:6144])], axis=1), dtype=np.float32)


def run_tail(kb, oT_list, gT_list, xT_list, mod_m, mod_c, w_out):
    gv = gv_input(mod_m, mod_c)
    w_t = tile_weight(w_out)
    maps = [{"oT": oT_list[r], "gT": gT_list[r], "xT": xT_list[r], "gv": gv, "w_out": w_t} for r in range(NCORES)]
    res = run_prog(kb, maps)
    return [np.asarray(res[r]["xo"]) for r in range(NCORES)]


def split_xo(xo_list):
    x2d = np.concatenate([xo[:, :1024].T for xo in xo_list], axis=0)
    c2d = np.concatenate([xo[:, 1024:1024 + TC].T for xo in xo_list], axis=0)
    return np.ascontiguousarray(x2d), np.ascontiguousarray(c2d)


def build_swa_att():
    kb = KB()
    T_ = 1024 + TC
    qT = kb.inp("qT", [2048, T_], BF16)
    kL = kb.inp("kL", [512, 1280], BF16)
    vL = kb.inp("vL", [1280, 512], BF16)
    ckT = kb.inp("ckT", [512, 256], BF16)
    cvL = kb.inp("cvL", [256, 512], BF16)
    oT = kb.out("oT", [2048, T_], BF16)
    q_sb = kb.sb("q_sb", [128, 16, T_], BF16)
    for g4 in range(4):
        kb.load(q_sb, q_sb[:, 4 * g4:4 * g4 + 4, :], qT[512 * g4:512 * (g4 + 1), :].rearrange("(h p) t -> p h t", p=128))
    k_sb = kb.sb("k_sb", [128, 4, 1280], BF16)
    kb.load(k_sb, k_sb[:], kL.rearrange("(g p) t -> p g t", p=128), q="pool")
    v_sb = kb.sb("v_sb", [128, 10, 512], BF16)
    kb.load(v_sb, v_sb[:], vL.rearrange("(j p) c -> p j c", p=128))
    ck_sb = kb.sb("ck_sb", [128, 4, 256], BF16)
    kb.load(ck_sb, ck_sb[:], ckT.rearrange("(g p) t -> p g t", p=128), q="pool")
    cv_sb = kb.sb("cv_sb", [128, 2, 512], BF16)
    kb.load(cv_sb, cv_sb[:], cvL.rearrange("(j p) c -> p j c", p=128))
    masks32 = kb.const_load("masks", [128, 4, 128])
    masks = kb.sb("masksb", [128, 4, 128], BF16)
    kb.op("dve", lambda e: e.tensor_copy(masks[:], masks32[:]), reads=[masks32], writes=[masks])
    esink = kb.const_load("sinkb", [128, 16])
    kb.op("act", lambda e: e.activation(esink[:], esink[:], AF.Exp), reads=[esink], writes=[esink])
    ones = kb.sb("onesb", [128, 128], BF16)
    kb.op("pool", lambda e: e.memset(ones[:], 1.0), writes=[ones])
    o_sb = kb.sb("o_sb", [128, 16, T_], BF16)
    kb._pool = [0, 1, 2, 3]
    pts = [kb.sb(f"pt{i}", [128, 512], BF16) for i in range(10)]
    dens = [kb.sb(f"den{i}", [128, 512], F32) for i in range(2)]
    scale = 128.0 ** -0.5
    state = {"it": 0, "npt": 0}

    def iter_gen(g, qb):
        if qb < 8:
            nq = 128
            qc0 = qb * 128
            kblocks = [("l", qb, 2 if qb == 0 else 0), ("l", qb + 1, None), ("l", qb + 2, 3 if qb == 7 else 1), ("c", 0, None), ("c", 1, None)]
        else:
            nq = TC
            qc0 = 1024
            kblocks = [("c", 0, None), ("c", 1, None)]
        N = 4 * nq
        it = state["it"]
        state["it"] += 1
        pso = kb.bank(4 + it % 2)
        psd = kb.bank(6 + it % 2)
        den = dens[it % 2]
        for bi, (kind, j, mi) in enumerate(kblocks):
            pss = kb.ps()
            if kind == "l":
                lk = k_sb[:, g, j * 128:(j + 1) * 128]
                lv = v_sb[:, j, g * 128:(g + 1) * 128]
                kt, vt = k_sb, v_sb
            else:
                lk = ck_sb[:, g, j * 128:(j + 1) * 128]
                lv = cv_sb[:, j, g * 128:(g + 1) * 128]
                kt, vt = ck_sb, cv_sb
            qv = q_sb[:, 4 * g:4 * g + 4, qc0:qc0 + nq]
            kb.op("pe", lambda e, pss=pss, lk=lk, qv=qv: e.matmul(pss[:, :N], lk, qv, start=True, stop=True), reads=[kt, q_sb], writes=[pss])
            pt = pts[state["npt"] % len(pts)]
            state["npt"] += 1
            yield
            kb.op("act", lambda e, pt=pt, pss=pss: e.activation(pt[:, :N], pss[:, :N], AF.Exp, scale=scale), reads=[pss], writes=[pt])
            yield
            if mi is not None:
                kb.op("dve", lambda e, pt=pt, mi=mi: e.tensor_tensor(pt[:, :4 * nq].rearrange("p (h q) -> p h q", h=4),
                                                                   pt[:, :4 * nq].rearrange("p (h q) -> p h q", h=4),
                                                                   masks[:, mi, :].unsqueeze(1).to_broadcast([128, 4, 128]), ALU.mult),
                      reads=[pt, masks], writes=[pt])
                yield
            first, last = (bi == 0), (bi == len(kblocks) - 1)
            kb.op("pe", lambda e, lv=lv, pt=pt, first=first, last=last: e.matmul(pso[:, :N], lv, pt[:, :N], start=first, stop=last),
                  reads=[vt, pt], writes=[pso], acc=not first)
            kb.op("pe", lambda e, pt=pt, first=first, last=last: e.matmul(psd[:, :N], ones[:], pt[:, :N], start=first, stop=last),
                  reads=[ones, pt], writes=[psd], acc=not first)
            yield
        kb.op("dve", lambda e: e.tensor_tensor(den[:, :N].rearrange("p (h q) -> p h q", h=4),
                                               psd[:, :N].rearrange("p (h q) -> p h q", h=4),
                                               esink[:, 4 * g:4 * g + 4].unsqueeze(2).to_broadcast([128, 4, nq]), ALU.add),
              reads=[psd, esink], writes=[den])
        yield
        kb.op("dve", lambda e: e.reciprocal(den[:, :N], den[:, :N]), reads=[den], writes=[den])
        yield
        kb.op("dve", lambda e: e.tensor_tensor(o_sb[:, 4 * g:4 * g + 4, qc0:qc0 + nq],
                                               pso[:, :N].rearrange("p (h q) -> p h q", h=4),
                                               den[:, :N].rearrange("p (h q) -> p h q", h=4), ALU.mult),
              reads=[pso, den], writes=[o_sb])
        yield

    work = [(g, qb) for g in range(4) for qb in range(9)]
    for i in range(0, len(work), 2):
        gens = [iter_gen(*w) for w in work[i:i + 2]]
        while gens:
            for gen in list(gens):
                try:
                    next(gen)
                except StopIteration:
                    gens.remove(gen)
    for g4 in range(4):
        kb.store(oT, oT[512 * g4:512 * (g4 + 1), :].rearrange("(h p) t -> p h t", p=128), o_sb, o_sb[:, 4 * g4:4 * g4 + 4, :])
    kb.finish()
    return kb


def swa_masks(r):
    k = np.arange(128)[:, None]
    q = np.arange(128)[None, :]
    mlo = (k >= q).astype(np.float32)
    mhi = (k <= q).astype(np.float32)
    z = np.zeros_like(mlo)
    return np.ascontiguousarray(np.stack([mlo, mhi, z if r == 0 else mlo, z if r == NCORES - 1 else mhi], axis=1))


def bf(a):
    return np.ascontiguousarray(a, dtype=ml_dtypes.bfloat16)


def run_layer_swa(inp, mod, x2d, ctx2d, progs):
    li = 0
    kb = progs["proj_swa"]
    w_t = tile_weight(inp["swa_w_in"][0])
    maps = []
    for r in range(NCORES):
        m = {"xT": xT_input(x2d, ctx2d, r), "modv": modv_input(inp["norm_g"][li], mod[li, 0], mod[li, 1]),
             "gains": col2(inp["swa_q_g"][0], inp["swa_k_g"][0]), "w_in": w_t}
        m.update(proj_const_inputs(r))
        maps.append(m)
    res = run_prog(kb, maps)
    qT = [np.asarray(res[r]["qT"]) for r in range(NCORES)]
    kT = [np.asarray(res[r]["kT"]) for r in range(NCORES)]
    vT = [np.asarray(res[r]["vT"]) for r in range(NCORES)]
    gT = [np.asarray(res[r]["gT"]) for r in range(NCORES)]
    z = np.zeros((512, 128), kT[0].dtype)
    kfull = np.concatenate([z] + [k[:, :1024] for k in kT] + [z], axis=1)
    vfull = np.concatenate([z] + [v[:, :1024] for v in vT] + [z], axis=1)
    ckT = np.ascontiguousarray(np.concatenate([k[:, 1024:] for k in kT], axis=1))
    cvL = np.ascontiguousarray(np.concatenate([v[:, 1024:] for v in vT], axis=1).T)
    sinkb = np.ascontiguousarray(np.broadcast_to(inp["swa_sink"][0][None, :], (128, 16)), dtype=np.float32)
    maps = []
    for r in range(NCORES):
        sl = slice(1024 * r, 1024 * r + 1280)
        maps.append({"qT": qT[r], "kL": np.ascontiguousarray(kfull[:, sl]), "vL": np.ascontiguousarray(vfull[:, sl].T),
                     "ckT": ckT, "cvL": cvL, "masks": swa_masks(r), "sinkb": sinkb})
    res = run_prog(progs["swa_att"], maps)
    oT = [np.asarray(res[r]["oT"]) for r in range(NCORES)]
    xT = [xT_input(x2d, ctx2d, r) for r in range(NCORES)]
    xo = run_tail(progs["tail"], oT, gT, xT, mod[li, 0], mod[li, 1], inp["swa_w_out"][0])
    return split_xo(xo)


NTOK = 8192 + 256


def build_proj_mla():
    kb = KB()
    pc = ProjCtx(kb, 1024, 0)
    T_ = pc.ttot
    xT = kb.inp("xT", [2048, T_])
    modv = kb.const_load("modv", [128, 5, 16])
    gains = kb.const_load("gains", [128, 10])
    C = proj_consts(kb, need64=True, rope=False)
    w_in = kb.inp("w_in", [12, 128, 16, 256])
    w_qb = kb.inp("w_qb", [12, 128, 4, 256])
    w_kvb = kb.inp("w_kvb", [16, 128, 2, 256])
    qnT = kb.out("qnT", [2048, T_], BF16)
    qpeT = kb.out("qpeT", [1024, T_], BF16)
    knT = kb.out("knT", [2048, T_], BF16)
    vT = kb.out("vT", [2048, T_], BF16)
    kpeT = kb.out("kpeT", [128, T_], BF16)
    gT = kb.out("gT", [2048, T_], BF16)
    hT = emit_modnorm(pc, xT, modv)
    cq_raw = kb.sb("cq_raw", [128, 4, T_], F32)
    ckv_raw = kb.sb("ckv_raw", [128, 2, T_], F32)
    blocks = ([dict(kind="keep", dst=cq_raw, idx=i) for i in range(4)] + [dict(kind="keep", dst=ckv_raw, idx=i) for i in range(2)]
              + hn_blocks(1, 64, gains, 9, True, kpeT) + raw_blocks(16, gT, dt=BF16))
    ws1 = WStream(kb, "w1", 16, 256)
    emit_proj_stage(pc, hT, 16, w_in, blocks, ws1, C)
    cqn = kb.sb("cqn", [128, 4, T_], BF16)
    ckvn = kb.sb("ckvn", [128, 2, T_], BF16)
    emit_widenorm(pc, cq_raw, 4, gains, cqn, C, gcol0=0)
    emit_widenorm(pc, ckv_raw, 2, gains, ckvn, C, gcol0=4)
    ws2 = WStream(kb, "w2", 4, 256)
    emit_proj_stage(pc, cqn, 4, w_qb, hn_blocks(16, 128, gains, 6, False, qnT) + hn_blocks(8, 64, gains, 7, True, qpeT), ws2, C)
    ws3 = WStream(kb, "w3", 2, 256)
    emit_proj_stage(pc, ckvn, 2, w_kvb, hn_blocks(16, 128, gains, 8, False, knT) + raw_blocks(16, vT, dt=BF16), ws3, C)
    kb.finish()
    return kb


def mla_weight_layouts(inp):
    w_in = inp["mla_w_in"][0]
    w_in_re = np.concatenate([w_in[:, 0:768], w_in[:, 768:832], w_in[:, 768:832], w_in[:, 832:]], axis=1)
    wq = inp["mla_w_qb"][0].reshape(512, 16, 192)
    w_qb_re = np.concatenate([wq[:, :, :128].reshape(512, 2048), wq[:, :, 128:].reshape(512, 1024)], axis=1)
    wkv = inp["mla_w_kvb"][0].reshape(256, 16, 256)
    w_kvb_re = np.concatenate([wkv[:, :, :128].reshape(256, 2048), wkv[:, :, 128:].reshape(256, 2048)], axis=1)
    g = np.zeros((128, 10), np.float32)
    g[:, 0:4] = inp["mla_qa_g"][0].reshape(4, 128).T
    g[:, 4:6] = inp["mla_kva_g"][0].reshape(2, 128).T
    g[:, 6] = inp["mla_qn_nope_g"][0]
    g[:, 7] = np.tile(inp["mla_qn_pe_g"][0], 2)
    g[:, 8] = inp["mla_kn_nope_g"][0]
    g[:, 9] = np.tile(inp["mla_kn_pe_g"][0], 2)
    return (np.ascontiguousarray(w_in_re, dtype=np.float32), np.ascontiguousarray(w_qb_re, dtype=np.float32),
            np.ascontiguousarray(w_kvb_re, dtype=np.float32), g)


def gather_tokens(per_core, rows=None):
    return np.concatenate([a[:, :1024] for a in per_core] + [a[:, 1024:1024 + TC] for a in per_core], axis=1)


def scatter_tokens(full, r):
    return np.ascontiguousarray(np.concatenate([full[:, 1024 * r:1024 * (r + 1)], full[:, 8192 + TC * r:8192 + TC * (r + 1)]], axis=1))


def build_mla_att():
    kb = KB()
    qn = kb.inp("qn", [2, 128, NTOK], BF16)
    qpe = kb.inp("qpe", [128, NTOK], BF16)
    kn = kb.inp("kn", [2, 128, NTOK], BF16)
    kpe = kb.inp("kpe", [128, NTOK], BF16)
    v = kb.inp("v", [2, 128, 66, 128], BF16)
    oT = kb.out("oT", [2, 128, NTOK], BF16)
    qn_sb = [kb.sb(f"qn{i}", [128, NTOK], BF16) for i in range(2)]
    kn_sb = [kb.sb(f"kn{i}", [128, NTOK], BF16) for i in range(2)]
    v_sb = [kb.sb(f"v{i}", [128, 66, 128], BF16) for i in range(2)]
    qpe_sb = kb.sb("qpe", [128, NTOK], BF16)
    kpe_sb = kb.sb("kpe", [128, NTOK], BF16)
    o_sb = [kb.sb(f"o{i}", [128, NTOK], BF16) for i in range(2)]
    H = NTOK // 2
    for i in range(2):
        for hf in range(2):
            kb.load(kn_sb[i], kn_sb[i][:, hf * H:(hf + 1) * H], kn[i, :, hf * H:(hf + 1) * H], q="sp")
            kb.load(qn_sb[i], qn_sb[i][:, hf * H:(hf + 1) * H], qn[i, :, hf * H:(hf + 1) * H], q="pool")
            kb.load(v_sb[i], v_sb[i][:, 33 * hf:33 * (hf + 1), :], v[i, :, 33 * hf:33 * (hf + 1), :], q="sp")
        if i == 0:
            for hf in range(2):
                kb.load(kpe_sb, kpe_sb[:, hf * H:(hf + 1) * H], kpe[:, hf * H:(hf + 1) * H], q="pool")
                kb.load(qpe_sb, qpe_sb[:, hf * H:(hf + 1) * H], qpe[:, hf * H:(hf + 1) * H], q="pool")
    ones = kb.sb("onesb", [128, 128], BF16)
    kb.op("pool", lambda e: e.memset(ones[:], 1.0), writes=[ones])
    kb._pool = [0, 1, 2, 3]
    pts = [kb.sb(f"pt{i}", [128, 512], BF16) for i in range(4)]
    recs = [kb.sb(f"rec{i}", [128, 512], F32) for i in range(2)]
    paccs = [kb.sb(f"pacc{i}", [128, 512], F32) for i in range(2)]
    ones32 = kb.sb("ones32", [128, 128], F32)
    kb.op("pool", lambda e: e.memset(ones32[:], 1.0), writes=[ones32])
    scale = 192.0 ** -0.5
    it = 0
    npt = 0
    qtiles = [(512 * i, 512, list(range(66))) for i in range(16)] + [(8192, 256, [64, 65])]
    hmask = kb.const_load("hmask", [128, 2])
    qpm = kb.sb("qpm", [128, NTOK], BF16)
    for hh in range(2):
        K, Q, V, O = kn_sb[hh], qn_sb[hh], v_sb[hh], o_sb[hh]
        for hf in range(2):
            kb.op("dve", lambda e, hh=hh, hf=hf: e.tensor_scalar(qpm[:, hf * H:(hf + 1) * H], qpe_sb[:, hf * H:(hf + 1) * H], hmask[:, hh:hh + 1], None, ALU.mult),
                  reads=[qpe_sb, hmask], writes=[qpm])
        for (q0, N, blocks) in qtiles:
            pso = kb.bank(4 + it % 2)
            psd = kb.bank(6 + it % 2)
            rec = recs[it % 2]
            pacc = paccs[it % 2]
            it += 1
            def emit_s(j, N=N, K=K, Q=Q, q0=q0):
                pss = kb.ps()
                kb.op("pe", lambda e, pss=pss, j=j: e.matmul(pss[:, :N], K[:, j * 128:(j + 1) * 128], Q[:, q0:q0 + N], start=True, stop=False),
                      reads=[K, Q], writes=[pss])
                kb.op("pe", lambda e, pss=pss, j=j: e.matmul(pss[:, :N], kpe_sb[:, j * 128:(j + 1) * 128], qpm[:, q0:q0 + N], start=False, stop=True),
                      reads=[kpe_sb, qpm], writes=[pss], acc=True)
                return pss
            pendq = [emit_s(blocks[0])]
            if len(blocks) > 1:
                pendq.append(emit_s(blocks[1]))
            for bi, j in enumerate(blocks):
                pss = pendq.pop(0)
                pt = pts[npt % 4]
                npt += 1
                kb.op("act", lambda e, pt=pt, pss=pss, N=N: e.activation(pt[:, :N], pss[:, :N], AF.Exp, scale=scale), reads=[pss], writes=[pt])
                if bi + 2 < len(blocks):
                    pendq.append(emit_s(blocks[bi + 2]))
                first, last = (bi == 0), (bi == len(blocks) - 1)
                kb.op("pe", lambda e, pt=pt, j=j, first=first, last=last, N=N, V=V, pso=pso: e.matmul(pso[:, :N], V[:, j, :], pt[:, :N], start=first, stop=last),
                      reads=[V, pt], writes=[pso], acc=not first)
                if first:
                    kb.op("dve", lambda e, pt=pt, N=N, pacc=pacc: e.tensor_copy(pacc[:, :N], pt[:, :N]), reads=[pt], writes=[pacc])
                else:
                    kb.op("dve", lambda e, pt=pt, N=N, pacc=pacc: e.tensor_tensor(pacc[:, :N], pacc[:, :N], pt[:, :N], ALU.add), reads=[pt, pacc], writes=[pacc])
            kb.op("pe", lambda e, N=N, psd=psd, pacc=pacc: e.matmul(psd[:, :N], ones32[:], pacc[:, :N], start=True, stop=True), reads=[ones32, pacc], writes=[psd])
            kb.op("dve", lambda e, rec=rec, psd=psd, N=N: e.reciprocal(rec[:, :N], psd[:, :N]), reads=[psd], writes=[rec])
            kb.op("dve", lambda e, rec=rec, pso=pso, N=N, O=O, q0=q0: e.tensor_tensor(O[:, q0:q0 + N], pso[:, :N], rec[:, :N], ALU.mult), reads=[pso, rec], writes=[O])
        for hf in range(2):
            kb.store(oT, oT[hh, :, hf * H:(hf + 1) * H], O, O[:, hf * H:(hf + 1) * H], q="sp")
    kb.finish()
    return kb


def run_layer_mla(inp, mod, x2d, ctx2d, progs):
    li = 1
    w_in_re, w_qb_re, w_kvb_re, g = mla_weight_layouts(inp)
    w_in_re, w_qb_re, w_kvb_re = tile_weight(w_in_re), tile_weight(w_qb_re), tile_weight(w_kvb_re)
    maps = []
    for r in range(NCORES):
        m = {"xT": xT_input(x2d, ctx2d, r), "modv": modv_input(inp["norm_g"][li], mod[li, 0], mod[li, 1]),
             "gains": g, "w_in": w_in_re, "w_qb": w_qb_re, "w_kvb": w_kvb_re}
        m.update(proj_const_inputs(r, need64=True, rope=False))
        maps.append(m)
    res = run_prog(progs["proj_mla"], maps)
    names = ("qnT", "qpeT", "knT", "vT", "kpeT", "gT")
    full = {n: gather_tokens([np.asarray(res[r][n]) for r in range(NCORES)]) for n in names}
    return run_layer_mla_b(inp, mod, x2d, ctx2d, progs, full)


def run_layer_mla_b(inp, mod, x2d, ctx2d, progs, full):
    li = 1
    gT = [scatter_tokens(full["gT"], r) for r in range(NCORES)]
    maps = []
    for r in range(NCORES):
        hs = slice(256 * r, 256 * (r + 1))
        maps.append({"qn": np.ascontiguousarray(full["qnT"][hs].reshape(2, 128, NTOK)),
                     "qpe": np.ascontiguousarray(full["qpeT"][128 * r:128 * (r + 1)]),
                     "kn": np.ascontiguousarray(full["knT"][hs].reshape(2, 128, NTOK)),
                     "kpe": np.ascontiguousarray(full["kpeT"]),
                     "hmask": np.ascontiguousarray(np.stack([(np.arange(128) < 64), (np.arange(128) >= 64)], axis=1), dtype=np.float32),
                     "v": np.ascontiguousarray(full["vT"][hs].reshape(2, 128, 66, 128).transpose(0, 3, 2, 1))})
    res = run_prog(progs["mla_att"], maps)
    ofull = np.concatenate([np.asarray(res[r]["oT"]).reshape(256, NTOK) for r in range(NCORES)], axis=0)
    oT = [scatter_tokens(ofull, r) for r in range(NCORES)]
    xT = [xT_input(x2d, ctx2d, r) for r in range(NCORES)]
    xo = run_tail(progs["tail"], oT, gT, xT, mod[li, 0], mod[li, 1], inp["mla_w_out"][0])
    return split_xo(xo)


def build_diff_att(lam_init):
    kb = KB()
    q = kb.inp("q", [2, 128, NTOK], BF16)
    k = kb.inp("k", [2, 128, NTOK], BF16)
    v = kb.inp("v", [128, 66, 256], BF16)
    oT = kb.out("oT", [2, 128, NTOK], BF16)
    lqk = kb.const_load("lqk", [128, 4])
    sg = kb.const_load("subg", [128, 2])
    q_sb = [kb.sb(f"q{i}", [128, NTOK], BF16) for i in range(2)]
    k_sb = [kb.sb(f"k{i}", [128, NTOK], BF16) for i in range(2)]
    v_sb = kb.sb("v", [128, 66, 256], BF16)
    o_sb = kb.sb("o", [128, 2, NTOK], BF16)
    H = NTOK // 2
    for i in range(2):
        for hf in range(2):
            kb.load(k_sb[i], k_sb[i][:, hf * H:(hf + 1) * H], k[i, :, hf * H:(hf + 1) * H], q="sp")
            kb.load(q_sb[i], q_sb[i][:, hf * H:(hf + 1) * H], q[i, :, hf * H:(hf + 1) * H], q="pool")
    for hf in range(2):
        kb.load(v_sb, v_sb[:, 33 * hf:33 * (hf + 1), :], v[:, 33 * hf:33 * (hf + 1), :], q="sp")
    ones = kb.sb("onesb", [128, 128], BF16)
    kb.op("pool", lambda e: e.memset(ones[:], 1.0), writes=[ones])
    ones32 = kb.sb("ones32", [128, 128], F32)
    kb.op("pool", lambda e: e.memset(ones32[:], 1.0), writes=[ones32])
    prod = kb.sb("prod", [128, 2], F32)
    kb.op("dve", lambda e: e.tensor_tensor(prod[:, 0:1], lqk[:, 0:1], lqk[:, 1:2], ALU.mult), reads=[lqk], writes=[prod])
    kb.op("dve", lambda e: e.tensor_tensor(prod[:, 1:2], lqk[:, 2:3], lqk[:, 3:4], ALU.mult), reads=[lqk], writes=[prod])
    psl = kb.bank(0)
    kb.op("pe", lambda e: e.matmul(psl[:, 0:2], ones32[:], prod[:], start=True, stop=True), reads=[ones32, prod], writes=[psl])
    el = kb.sb("el", [128, 2], F32)
    kb.op("act", lambda e: e.activation(el[:], psl[:, 0:2], AF.Exp), reads=[psl], writes=[el])
    lam = kb.sb("lam", [128, 1], F32)
    kb.op("dve", lambda e: e.tensor_tensor(lam[:], el[:, 0:1], el[:, 1:2], ALU.subtract), reads=[el], writes=[lam])
    kb.op("dve", lambda e: e.tensor_scalar(lam[:], lam[:], float(lam_init), None, ALU.add), reads=[lam], writes=[lam])
    kb.op("dve", lambda e: e.tensor_scalar(sg[:], sg[:], float(1.0 - lam_init), None, ALU.mult), reads=[sg], writes=[sg])
    kb._pool = [0, 1, 7]
    pts = [kb.sb(f"pt{i}", [128, 512], BF16) for i in range(4)]
    recs = [kb.sb(f"rec{i}", [128, 512], F32) for i in range(2)]
    paccs = [kb.sb(f"pacc{i}", [128, 512], F32) for i in range(2)]
    ods = [kb.sb(f"od{i}", [128, 2, 256], F32) for i in range(2)]
    t2s = [kb.sb(f"t2{i}", [128, 256], F32) for i in range(2)]
    sqs = [kb.sb(f"sqd{i}", [128, 256], BF16) for i in range(2)]
    rss = [kb.sb(f"rs{i}", [128, 256], F32) for i in range(2)]
    scale = 128.0 ** -0.5
    NQ = 256
    qtiles = [(NQ * i, list(range(66))) for i in range(8192 // NQ)] + [(8192, [64, 65])]
    it = 0
    npt = 0
    for (q0, blocks) in qtiles:
        pso1 = kb.bank(2 + it % 2)
        pso2 = kb.bank(4 + it % 2)
        psd = kb.bank(6)
        rec, od, t2, rs = recs[it % 2], ods[it % 2], t2s[it % 2], rss[it % 2]
        pacc = paccs[it % 2]
        it += 1

        def emit_s(j, q0=q0):
            pss = kb.ps()
            kb.op("pe", lambda e, pss=pss, j=j: e.matmul(pss[:, 0:NQ], k_sb[0][:, j * 128:(j + 1) * 128], q_sb[0][:, q0:q0 + NQ], start=True, stop=True),
                  reads=[k_sb[0], q_sb[0]], writes=[pss])
            kb.op("pe", lambda e, pss=pss, j=j: e.matmul(pss[:, NQ:2 * NQ], k_sb[1][:, j * 128:(j + 1) * 128], q_sb[1][:, q0:q0 + NQ], start=True, stop=True),
                  reads=[k_sb[1], q_sb[1]], writes=[pss], acc=True)
            return pss
        pendq = [emit_s(blocks[0])]
        if len(blocks) > 1:
            pendq.append(emit_s(blocks[1]))
        for bi, j in enumerate(blocks):
            pss = pendq.pop(0)
            pt = pts[npt % 4]
            npt += 1
            kb.op("act", lambda e, pt=pt, pss=pss: e.activation(pt[:, :], pss[:, :], AF.Exp, scale=scale), reads=[pss], writes=[pt])
            if bi + 2 < len(blocks):
                pendq.append(emit_s(blocks[bi + 2]))
            first, last = (bi == 0), (bi == len(blocks) - 1)
            for mi, pso in enumerate((pso1, pso2)):
                for c in range(2):
                    kb.op("pe", lambda e, pt=pt, j=j, first=first, last=last, pso=pso, c=c, mi=mi: e.matmul(pso[:, c * NQ:(c + 1) * NQ], v_sb[:, j, c * 128:(c + 1) * 128],
                                                                                                 pt[:, mi * NQ:(mi + 1) * NQ], start=first, stop=last),
                          reads=[v_sb, pt], writes=[pso], acc=not (first and c == 0))
            if first:
                kb.op("dve", lambda e, pt=pt, pacc=pacc: e.tensor_copy(pacc[:, :], pt[:, :]), reads=[pt], writes=[pacc])
            else:
                kb.op("dve", lambda e, pt=pt, pacc=pacc: e.tensor_tensor(pacc[:, :], pacc[:, :], pt[:, :], ALU.add), reads=[pt, pacc], writes=[pacc])
        kb.op("pe", lambda e, psd=psd, pacc=pacc: e.matmul(psd[:, :], ones32[:], pacc[:, :], start=True, stop=True), reads=[ones32, pacc], writes=[psd])
        kb.op("dve", lambda e, rec=rec, psd=psd: e.reciprocal(rec[:, :], psd[:, :]), reads=[psd], writes=[rec])
        kb.op("dve", lambda e, rec=rec: e.tensor_scalar(rec[:, NQ:2 * NQ], rec[:, NQ:2 * NQ], lam[:, 0:1], None, ALU.mult), reads=[rec, lam], writes=[rec])
        for c in range(2):
            kb.op("dve", lambda e, rec=rec, od=od, pso1=pso1, c=c: e.tensor_tensor(od[:, c, :], pso1[:, c * NQ:(c + 1) * NQ], rec[:, 0:NQ], ALU.mult),
                  reads=[pso1, rec], writes=[od])
            kb.op("dve", lambda e, rec=rec, t2=t2, pso2=pso2, c=c: e.tensor_tensor(t2[:, :], pso2[:, c * NQ:(c + 1) * NQ], rec[:, NQ:2 * NQ], ALU.mult),
                  reads=[pso2, rec], writes=[t2])
            kb.op("dve", lambda e, od=od, t2=t2, c=c: e.tensor_tensor(od[:, c, :], od[:, c, :], t2[:, :], ALU.subtract), reads=[od, t2], writes=[od])
        pq = kb.ps()
        for c in range(2):
            sq = sqs[c]
            kb.op("act", lambda e, sq=sq, od=od, c=c: e.activation(sq[:, :], od[:, c, :], AF.Square), reads=[od], writes=[sq])
            kb.op("pe", lambda e, pq=pq, sq=sq, c=c: e.matmul(pq[:, 0:NQ], ones[:], sq[:, :], start=(c == 0), stop=(c == 1)), reads=[ones, sq], writes=[pq], acc=(c == 1))
        kb.rstd_from_ss(rs, rs[:, :], pq, pq[:, 0:NQ], 1.0 / 256)
        for c in range(2):
            kb.op("dve", lambda e, od=od, rs=rs, c=c, q0=q0: e.scalar_tensor_tensor(o_sb[:, c, q0:q0 + NQ], od[:, c, :], sg[:, c:c + 1], rs[:, :], ALU.mult, ALU.mult),
                  reads=[od, sg, rs], writes=[o_sb])
    for c in range(2):
        for hf in range(2):
            kb.store(oT, oT[c, :, hf * H:(hf + 1) * H], o_sb, o_sb[:, c, hf * H:(hf + 1) * H], q="sp")
    kb.finish()
    return kb


def run_layer_diff(inp, mod, x2d, ctx2d, progs):
    li = 3
    w_t = tile_weight(inp["diff_w_in"][0])
    maps = []
    for r in range(NCORES):
        m = {"xT": xT_input(x2d, ctx2d, r), "modv": modv_input(inp["norm_g"][li], mod[li, 0], mod[li, 1]),
             "gains": col2(inp["diff_q_g"][0], inp["diff_k_g"][0]), "w_in": w_t}
        m.update(proj_const_inputs(r))
        maps.append(m)
    res = run_prog(progs["proj_diff"], maps)
    full = {n: gather_tokens([np.asarray(res[r][n]) for r in range(NCORES)]) for n in ("qT", "kT", "vT", "gT")}
    return run_layer_diff_b(inp, mod, x2d, ctx2d, progs, full)


def run_layer_diff_b(inp, mod, x2d, ctx2d, progs, full):
    li = 3
    gT = [scatter_tokens(full["gT"], r) for r in range(NCORES)]
    lqk = np.ascontiguousarray(np.stack([inp["diff_lq1"][0], inp["diff_lk1"][0], inp["diff_lq2"][0], inp["diff_lk2"][0]], axis=1), dtype=np.float32)
    subg = np.ascontiguousarray(inp["diff_subln_g"][0].reshape(2, 128).T, dtype=np.float32)
    maps = []
    for r in range(NCORES):
        hs = slice(256 * r, 256 * (r + 1))
        maps.append({"q": np.ascontiguousarray(full["qT"][hs].reshape(2, 128, NTOK)),
                     "k": np.ascontiguousarray(full["kT"][hs].reshape(2, 128, NTOK)),
                     "v": np.ascontiguousarray(full["vT"][hs].reshape(256, 66, 128).transpose(2, 1, 0)),
                     "lqk": lqk, "subg": subg})
    res = run_prog(progs["diff_att"], maps)
    ofull = np.concatenate([np.asarray(res[r]["oT"]).reshape(256, NTOK) for r in range(NCORES)], axis=0)
    oT = [scatter_tokens(ofull, r) for r in range(NCORES)]
    xT = [xT_input(x2d, ctx2d, r) for r in range(NCORES)]
    xo = run_tail(progs["tail"], oT, gT, xT, mod[li, 0], mod[li, 1], inp["diff_w_out"][0])
    return split_xo(xo)


def build_proj_hyena():
    kb = KB()
    pc = ProjCtx(kb, 1026, 1)
    T_ = pc.ttot
    TO = 1024 + TC
    xT = kb.inp("xT", [2048, T_])
    modv = kb.const_load("modv", [128, 5, 16])
    cw = kb.const_load("cw", [128, 48, 4])
    hm = kb.const_load("hm", [128, 2])
    w = kb.inp("w_in", [32, 128, 16, 256])
    uT = kb.out("uT", [6144, TO], BF16)
    gT = kb.out("gT", [2048, TO], BF16)
    hT = emit_modnorm(pc, xT, modv)
    blocks = ([dict(kind="conv3", cw=cw, ci=i, out=uT, r0=128 * i, dt=BF16) for i in range(48)] + raw_blocks(16, gT, dt=BF16))
    ws = WStream(kb, "w", 16, 256)
    emit_proj_stage(pc, hT, 16, w, blocks, ws, {"hm": hm})
    kb.finish()
    return kb


LH = 8192
NF = 2 * LH
GC = 4
CPC = 256


def hyena_consts(L):
    n = np.arange(2 * L)
    pos = np.where(n < L, n, 2 * L - n).astype(np.float64)
    pos[L] = 0
    t = pos / max(L - 1, 1)
    bands = np.linspace(1e-4, 15, 16)
    ang = (2.0 * math.pi / L) * pos[:, None] * bands[None, :]
    z = np.concatenate([t[:, None], np.cos(ang), -np.sin(ang)], axis=-1)
    return np.ascontiguousarray(z.T, dtype=np.float32), t.astype(np.float32)


def hyena_deltas():
    max_decay = math.log(1e-2) / 0.3
    min_decay = math.log(1e-2) / 1.5
    return np.abs(np.linspace(min_decay, max_decay, 2048)).astype(np.float32)


def emit_sin(kb, tmp, out_ap, out_t, arg_ap, arg_t, shape_ap):
    s, c, t = tmp("sn_s"), tmp("sn_c"), tmp("sn_t")
    sa, ca, ta = shape_ap(s), shape_ap(c), shape_ap(t)
    kb.op("act", lambda e: e.activation(sa, arg_ap, AF.Sin, scale=0.125), reads=[arg_t], writes=[s])
    hp = shape_ap(kb.halfpi_t)
    kb.op("act", lambda e: e.activation(ca, arg_ap, AF.Sin, scale=0.125, bias=hp), reads=[arg_t, kb.halfpi_t], writes=[c])
    for k in range(3):
        last = (k == 2)
        if not last:
            kb.op("dve", lambda e: e.tensor_tensor(ta, sa, sa, ALU.mult), reads=[s], writes=[t])
        dst = out_ap if last else sa
        dst_t = out_t if last else s
        kb.op("dve", lambda e, dst=dst: e.scalar_tensor_tensor(dst, sa, 2.0, ca, ALU.mult, ALU.mult), reads=[s, c], writes=[dst_t])
        if not last:
            kb.op("dve", lambda e: e.tensor_scalar(ca, ta, -2.0, 1.0, ALU.mult, ALU.add), reads=[t], writes=[c])


def build_hyena_core(with_ctx=True, debug=False, ngroups_override=None, skip_b=False):
    kb = KB()
    tmps = {}

    def tmp(name, shape, dt=F32, n=2):
        if name not in tmps:
            tmps[name] = [[kb.sb(f"{name}{i}", shape, dt) for i in range(n)], 0]
        lst = tmps[name]
        t = lst[0][lst[1] % n]
        lst[1] += 1
        return t

    sig_in = [kb.inp(nm, [64, CPC, 128], BF16) for nm in ("v", "x1", "x2")]
    zout = kb.out("z", [64, CPC, 128], BF16)
    ZT = kb.inp("ZT", [33, NF])
    tprow = kb.inp("tprow", [1, NF])
    fw1 = kb.const_load("fw1", [33, 64])
    fb1f = kb.const_load("fb1f", [64, 2])
    fw2d = kb.const_load("fw2d", [64, 128])
    fb2f = kb.const_load("fb2f", [128, 2])
    w3s32 = kb.const_load("w3s", [128, 2, CPC])
    w3b = kb.sb("w3b", [128, 2, CPC], BF16)
    kb.op("dve", lambda e: e.tensor_copy(w3b[:], w3s32[:]), reads=[w3s32], writes=[w3b])
    ndelta = kb.const_load("deltac", [128, 2])
    kb.op("dve", lambda e: e.tensor_scalar(ndelta[:], ndelta[:], -1.0, None, ALU.mult), reads=[ndelta], writes=[ndelta])
    skipb = kb.const_load("skipb", [64, 2, CPC])
    halfpi = kb.sb("halfpi", [128, 1], F32)
    kb.op("pool", lambda e: e.memset(halfpi[:], math.pi / 2), writes=[halfpi])
    kb.halfpi, kb.halfpi_t = halfpi, halfpi

    def cbf(name, shape):
        t32 = kb.const_load(name, shape)
        tb = kb.sb(name + "b", shape, BF16)
        kb.op("dve", lambda e: e.tensor_copy(tb[:], t32[:]), reads=[t32], writes=[tb])
        return tb
    Ff = cbf("Ff", [128, 256])
    Fi1 = cbf("Fi1", [128, 256])
    Fi2 = cbf("Fi2", [128, 256])
    Fc = cbf("Fc", [128, 128])
    Fs = cbf("Fs", [128, 128])
    Fsn = cbf("Fsn", [128, 128])
    Tc = kb.const_load("Tc", [128, 128])
    Ts = kb.const_load("Ts", [128, 128])
    kcs = kb.out("kcs", [2, CPC, NF], F32) if debug else kb.scratch("kcs", [2, CPC, NF], F32)

    HT2 = kb.sb("HT2", [128, NF], BF16)
    CH = 2048
    def big(nm):
        if nm in ("kchA", "kchB"):
            lst = tmps["kch"][0]
            return lst[0] if nm == "kchA" else lst[1]
        return tmp(nm, [128, CH], F32, n=1)
    def mlp_block(z_dram_ap, wcols, dst_ap, dst_t):
        zt = tmp("zt", [33, CH], F32, n=1)
        kb.load(zt, zt[:, :wcols], z_dram_ap, q="sp")
        arg = big("arg")
        for sbk in range(wcols // 512):
            ps = kb.ps()
            sc = slice(sbk * 512, (sbk + 1) * 512)
            kb.op("pe", lambda e, ps=ps, zt=zt, sc=sc: e.matmul(ps[0:64, :], fw1[:, :], zt[:, sc], start=True, stop=True), reads=[fw1, zt], writes=[ps])
            kb.op("dve", lambda e, ps=ps, sc=sc, arg=arg: e.tensor_scalar(arg[0:64, sc], ps[0:64, :], fb1f[:, 0:1], fb1f[:, 1:2], ALU.add, ALU.mult),
                  reads=[ps, fb1f], writes=[arg])
        h1 = big("h1")
        emit_sin(kb, big, h1[0:64, :wcols], h1, arg[0:64, :wcols], arg, lambda t: t[0:64, :wcols] if t is not kb.halfpi_t else t[0:64, :])
        arg2 = big("arg")
        for sbk in range(wcols // 512):
            ps = kb.ps()
            sc = slice(sbk * 512, (sbk + 1) * 512)
            kb.op("pe", lambda e, ps=ps, h1=h1, sc=sc: e.matmul(ps[:, :], fw2d[:, :], h1[0:64, sc], start=True, stop=True), reads=[fw2d, h1], writes=[ps])
            kb.op("dve", lambda e, ps=ps, sc=sc, arg2=arg2: e.tensor_scalar(arg2[:, sc], ps[:, :], fb2f[:, 0:1], fb2f[:, 1:2], ALU.add, ALU.mult),
                  reads=[ps, fb2f], writes=[arg2])
        emit_sin(kb, big, dst_ap, dst_t, arg2[:, :wcols], arg2, lambda t: t[:, :wcols] if t is not kb.halfpi_t else t[:, :])

    for ch in range(NF // CH):
        cols = slice(ch * CH, (ch + 1) * CH)
        mlp_block(ZT[:, cols], CH, HT2[:, cols], HT2)
    kb.op("pool", lambda e: e.memset(HT2[0:64, LH:NF], 0.0), writes=[HT2])
    kb.op("pool", lambda e: e.memset(HT2[64:128, 0:LH + 1], 0.0), writes=[HT2])

    if debug:
        dH = kb.out("dHT2", [128, NF], BF16)
        kb.store(dH, dH[:], HT2, HT2[:])
    ssp = kb.sb("ssp", [128, 4, 8], F32)
    rc = kb.sb("rc", [128, 4], F32)
    for o in range(0 if not skip_b else 2, 2):
        for cb in range(2):
            oc = 2 * o + cb
            for ch in range(NF // CH):
                cols = slice(ch * CH, (ch + 1) * CH)
                tpb = big("sn_s")
                kb.load(tpb, tpb[:], tprow[0:1, cols].partition_broadcast(128), q="sp")
                dec = big("sn_c")
                kb.op("act", lambda e, dec=dec, tpb=tpb, cb=cb: e.activation(dec[:], tpb[:], AF.Exp, scale=ndelta[:, cb:cb + 1]), reads=[tpb, ndelta], writes=[dec])
                kch = tmp("kch", [128, CH], F32, n=2)
                for sbk in range(4):
                    ps = kb.ps()
                    sc = slice(sbk * 512, (sbk + 1) * 512)
                    gc = slice(ch * CH + sbk * 512, ch * CH + (sbk + 1) * 512)
                    kb.op("pe", lambda e, ps=ps, o=o, cb=cb, gc=gc: e.matmul(ps[:, :], w3b[:, o, cb * 128:(cb + 1) * 128], HT2[:, gc], start=True, stop=True),
                          reads=[w3b, HT2], writes=[ps])
                    kb.op("dve", lambda e, ps=ps, sc=sc, kch=kch, dec=dec: e.tensor_tensor(kch[:, sc], ps[:, :], dec[:, sc], ALU.mult), reads=[ps, dec], writes=[kch])
                sq = big("sn_t")
                kb.op("pool", lambda e, sq=sq, kch=kch: e.tensor_tensor(sq[:], kch[:], kch[:], ALU.mult), reads=[kch], writes=[sq])
                kb.op("dve", lambda e, sq=sq, oc=oc, ch=ch: e.reduce_sum(ssp[:, oc, ch:ch + 1], sq[:], AX.X), reads=[sq], writes=[ssp])
                kb.store(kcs, kcs[o, cb * 128:(cb + 1) * 128, cols], kch, kch[:], q="pool", is_output=False)
            kb.op("dve", lambda e, oc=oc: e.reduce_sum(rc[:, oc:oc + 1], ssp[:, oc, :], AX.X), reads=[ssp], writes=[rc])
            kb.rstd_from_ss(rc, rc[:, oc:oc + 1], rc, rc[:, oc:oc + 1], 1.0)
            for ch in range(NF // CH):
                cols = slice(ch * CH, (ch + 1) * CH)
                k2 = tmp("kch", [128, CH], F32, n=2)
                kb.load(k2, k2[:], kcs[o, cb * 128:(cb + 1) * 128, cols], q="sp", src=kcs)
                kb.op("dve", lambda e, k2=k2, oc=oc: e.tensor_scalar(k2[:], k2[:], rc[:, oc:oc + 1], None, ALU.mult), reads=[k2, rc], writes=[k2])
                kb.store(kcs, kcs[o, cb * 128:(cb + 1) * 128, cols], k2, k2[:], q="pool", is_output=False)


    if with_ctx:
        LC = 256
        uc = kb.inp("uc", [3, 2, 128, LC], BF16)
        cwc = kb.const_load("cwc", [128, 3, 2, 4])
        ZTc = kb.inp("ZTc", [33, 2 * LC])
        tpc = kb.const_load("tpc", [128, 4])
        kb.op("dve", lambda e: e.tensor_scalar(tpc[:], tpc[:], -1.0, None, ALU.mult), reads=[tpc], writes=[tpc])
        deltab = kb.const_load("deltab", [128, CPC])
        skipc = kb.const_load("skipc", [128, 2, CPC])
        ident = kb.const_load("ident", [128, 128])
        ones32 = kb.sb("ones32c", [128, 128], F32)
        kb.op("pool", lambda e: e.memset(ones32[:], 1.0), writes=[ones32])
        Dc_in = kb.inp("Dc", [128, 4, 512])
        Dsn_in = kb.inp("Dsn", [128, 4, 512])
        zc_out = kb.out("zc", [2, 128, CPC], BF16)
        Dcb = kb.sb("Dcb", [128, 4, 512], BF16)
        Dsnb = kb.sb("Dsnb", [128, 4, 512], BF16)
        for src_, dst_ in ((Dc_in, Dcb), (Dsn_in, Dsnb)):
            st_ = big("sn_s")
            kb.load(st_, st_[:].rearrange("p (a b) -> p a b", a=4), src_, q="sp")
            kb.op("dve", lambda e, st_=st_, dst_=dst_: e.tensor_copy(dst_[:].rearrange("p a b -> p (a b)"), st_[:]), reads=[st_], writes=[dst_])
        utT = big("kchA")
        ut = utT[:, 0:1536].rearrange("p (tb si c) -> p tb si c", tb=2, si=3)
        for si in range(3):
            for cb in range(2):
                ub = tmp("ucb", [128, LC], BF16, n=2)
                kb.load(ub, ub[:], uc[si, cb], q="sp")
                u32 = tmp("uc32", [128, LC], F32, n=2)
                kb.op("act", lambda e, u32=u32, ub=ub: e.copy(u32[:], ub[:]), reads=[ub], writes=[u32])
                o32 = tmp("oc32", [128, LC], F32, n=2)
                kb.op("dve", lambda e, o32=o32, u32=u32, si=si, cb=cb: e.tensor_scalar(o32[:], u32[:], cwc[:, si, cb, 1:2], cwc[:, si, cb, 3:4], ALU.mult, ALU.add),
                      reads=[u32, cwc], writes=[o32])
                kb.op("dve", lambda e, o32=o32, u32=u32, si=si, cb=cb: e.scalar_tensor_tensor(o32[:, 1:LC], u32[:, 0:LC - 1], cwc[:, si, cb, 0:1], o32[:, 1:LC], ALU.mult, ALU.add),
                      reads=[u32, cwc, o32], writes=[o32])
                kb.op("dve", lambda e, o32=o32, u32=u32, si=si, cb=cb: e.scalar_tensor_tensor(o32[:, 0:LC - 1], u32[:, 1:LC], cwc[:, si, cb, 2:3], o32[:, 0:LC - 1], ALU.mult, ALU.add),
                      reads=[u32, cwc, o32], writes=[o32])
                for tb in range(2):
                    ps = kb.ps()
                    kb.op("pe", lambda e, ps=ps, o32=o32, tb=tb: e.transpose(ps[:, 0:128], o32[:, tb * 128:(tb + 1) * 128], ident[:, :]), reads=[o32, ident], writes=[ps])
                    kb.op("act", lambda e, ps=ps, tb=tb, si=si, cb=cb: e.copy(ut[:, tb, si, cb * 128:(cb + 1) * 128], ps[:, 0:128]), reads=[ps], writes=[utT])
        HTc = kb.sb("HTc", [128, 2 * LC], BF16)
        mlp_block(ZTc[:, :], 2 * LC, HTc[:, :], HTc)
        kb.op("pool", lambda e: e.memset(HTc[0:64, LC:2 * LC], 0.0), writes=[HTc])
        kb.op("pool", lambda e: e.memset(HTc[64:128, 0:LC + 1], 0.0), writes=[HTc])
        kccT = big("sn_c")
        HcT = [big("sn_t"), big("kchB")]
        kccb = kb.sb("kccb", [128, 2, 4, CPC], BF16)
        for o in range(2):
            kcc = kccT[:, o * 1024:(o + 1) * 1024].rearrange("p (a b) -> p a b", a=4)
            pss = kb.ps()
            for nb in range(4):
                ps = kb.ps()
                kb.op("pe", lambda e, ps=ps, nb=nb, o=o: e.matmul(ps[:, 0:CPC], HTc[:, nb * 128:(nb + 1) * 128], w3b[:, o, :], start=True, stop=True), reads=[HTc, w3b], writes=[ps])
                dec = tmp("decc", [128, CPC], F32, n=2)
                kb.op("act", lambda e, dec=dec, nb=nb: e.activation(dec[:], deltab[:], AF.Exp, scale=tpc[:, nb:nb + 1]), reads=[deltab, tpc], writes=[dec])
                kb.op("dve", lambda e, ps=ps, dec=dec, kcc=kcc, nb=nb: e.tensor_tensor(kcc[:, nb, :], ps[:, 0:CPC], dec[:], ALU.mult), reads=[ps, dec], writes=[kccT])
                sq = tmp("sqc", [128, CPC], F32, n=2)
                kb.op("pool", lambda e, sq=sq, kcc=kcc, nb=nb: e.tensor_tensor(sq[:], kcc[:, nb, :], kcc[:, nb, :], ALU.mult), reads=[kccT], writes=[sq])
                kb.op("pe", lambda e, pss=pss, sq=sq, nb=nb: e.matmul(pss[:, 0:CPC], ones32[:], sq[:], start=(nb == 0), stop=(nb == 3)), reads=[ones32, sq], writes=[pss], acc=(nb != 0))
            rsc = tmp("rsc", [128, CPC], F32, n=2)
            kb.rstd_from_ss(rsc, rsc[:], pss, pss[:, 0:CPC], 1.0)
            for nb in range(4):
                kb.op("dve", lambda e, kcc=kcc, rsc=rsc, nb=nb, o=o: e.tensor_tensor(kccb[:, o, nb, :], kcc[:, nb, :], rsc[:], ALU.mult), reads=[kccT, rsc], writes=[kccb])
            Hc = HcT[o]
            for fb in range(4):
                for ri, Dm in enumerate((Dcb, Dsnb)):
                    ps = kb.ps()
                    for kbk in range(4):
                        kb.op("pe", lambda e, ps=ps, Dm=Dm, kbk=kbk, fb=fb, o=o: e.matmul(ps[:, 0:CPC], Dm[:, kbk, fb * 128:(fb + 1) * 128], kccb[:, o, kbk, :],
                                                                                     start=(kbk == 0), stop=(kbk == 3)),
                              reads=[Dm, kccb], writes=[ps], acc=(kbk != 0))
                    kb.op("act", lambda e, ps=ps, Hc=Hc, ri=ri, fb=fb: e.copy(Hc[:, ri * 1024 + fb * 256: ri * 1024 + (fb + 1) * 256], ps[:, 0:CPC]), reads=[ps], writes=[Hc])
        curb = kb.sb("curc", [128, 2, CPC], BF16)
        for tb in range(2):
            kb.op("act", lambda e, tb=tb: e.copy(curb[:, tb, :], ut[:, tb, 0, :]), reads=[utT], writes=[curb])
        cur32 = [ut[:, tb, 0, :] for tb in range(2)]
        cur32_t = utT
        zc32 = kb.sb("zc32", [128, 2, CPC], F32)
        Ycb = kb.sb("Ycb", [128, 2, 4, CPC], BF16)
        for o in range(2):
            Hc = HcT[o]
            for fb in range(4):
                pre, pim = kb.ps(), kb.ps()
                for ri, (Dm, pp) in enumerate(((Dcb, pre), (Dsnb, pim))):
                    for kbk in range(2):
                        kb.op("pe", lambda e, pp=pp, Dm=Dm, kbk=kbk, fb=fb: e.matmul(pp[:, 0:CPC], Dm[:, kbk, fb * 128:(fb + 1) * 128], curb[:, kbk, :],
                                                                                 start=(kbk == 0), stop=(kbk == 1)),
                              reads=[Dm, curb], writes=[pp], acc=(kbk != 0))
                hre = Hc[:, fb * 256:(fb + 1) * 256]
                him = Hc[:, 1024 + fb * 256:1024 + (fb + 1) * 256]
                t1, t2 = tmp("hmc", [128, CPC], F32, n=4), tmp("hmc", [128, CPC], F32, n=4)
                kb.op("dve", lambda e, t1=t1, pre=pre, hre=hre: e.tensor_tensor(t1[:], pre[:, 0:CPC], hre, ALU.mult), reads=[pre, Hc], writes=[t1])
                kb.op("dve", lambda e, t2=t2, pim=pim, him=him: e.tensor_tensor(t2[:], pim[:, 0:CPC], him, ALU.mult), reads=[pim, Hc], writes=[t2])
                kb.op("pool", lambda e, t1=t1, t2=t2, fb=fb: e.tensor_tensor(Ycb[:, 0, fb, :], t1[:], t2[:], ALU.subtract), reads=[t1, t2], writes=[Ycb])
                t3, t4 = tmp("hmc", [128, CPC], F32, n=4), tmp("hmc", [128, CPC], F32, n=4)
                kb.op("dve", lambda e, t3=t3, pre=pre, him=him: e.tensor_tensor(t3[:], pre[:, 0:CPC], him, ALU.mult), reads=[pre, Hc], writes=[t3])
                kb.op("dve", lambda e, t4=t4, pim=pim, hre=hre: e.tensor_tensor(t4[:], pim[:, 0:CPC], hre, ALU.mult), reads=[pim, Hc], writes=[t4])
                kb.op("pool", lambda e, t3=t3, t4=t4, fb=fb: e.tensor_tensor(Ycb[:, 1, fb, :], t3[:], t4[:], ALU.add), reads=[t3, t4], writes=[Ycb])
            for tb in range(2):
                py = kb.ps()
                n_ = 0
                for ri, Dm in enumerate((Dcb, Dsnb)):
                    for fb in range(4):
                        kb.op("pe", lambda e, py=py, Dm=Dm, fb=fb, tb=tb, ri=ri, n_=n_: e.matmul(py[:, 0:CPC], Dm[:, fb, tb * 128:(tb + 1) * 128], Ycb[:, ri, fb, :],
                                                                                        start=(n_ == 0), stop=(n_ == 7)),
                              reads=[Dm, Ycb], writes=[py], acc=(n_ != 0))
                        n_ += 1
                t = tmp("epc", [128, CPC], F32, n=2)
                c32 = cur32[tb]
                kb.op("pool", lambda e, t=t, c32=c32, o=o: e.tensor_tensor(t[:], c32, skipc[:, o, :], ALU.mult), reads=[cur32_t, skipc], writes=[t])
                kb.op("dve", lambda e, t=t, py=py: e.scalar_tensor_tensor(t[:], py[:, 0:CPC], 1.0 / (2 * LC), t[:], ALU.mult, ALU.add), reads=[py, t], writes=[t])
                kb.op("dve", lambda e, t=t, tb=tb, o=o: e.tensor_tensor(zc32[:, tb, :], t[:], ut[:, tb, 1 + o, :], ALU.mult), reads=[t, utT], writes=[zc32])
            if o == 0:
                for tb in range(2):
                    kb.op("act", lambda e, tb=tb: e.copy(curb[:, tb, :], zc32[:, tb, :]), reads=[zc32], writes=[curb])
                z1c = kb.sb("z1c", [128, 2, CPC], F32)
                kb.op("dve", lambda e: e.tensor_copy(z1c[:], zc32[:]), reads=[zc32], writes=[z1c])
                cur32 = [z1c[:, tb, :] for tb in range(2)]
                cur32_t = z1c
        zcb = kb.sb("zcb", [128, 2, CPC], BF16)
        kb.op("act", lambda e: e.copy(zcb[:], zc32[:]), reads=[zc32], writes=[zcb])
        kb.store(zc_out, zc_out.h.rearrange("tb p c -> p tb c"), zcb, zcb[:], q="sp")

    W = GC * 128
    kb._pool = [0, 1, 2, 3]
    kb.p.barrier()
    carve_src = [tmps[nm][0][0] for nm in ("arg", "h1", "sn_s", "sn_c", "sn_t")] + list(tmps["kch"][0])
    carve_pos = [0, 0]

    def carve(nelem32):
        ti, off = carve_pos
        if off + nelem32 > CH:
            ti, off = ti + 1, 0
        v = carve_src[ti].h[:, off:off + nelem32]
        carve_pos[0], carve_pos[1] = ti, off + nelem32
        return v

    def ctmp(name, shape, dt, n):
        if name not in tmps:
            lst = []
            nel = int(np.prod(shape[1:]))
            n32 = nel if dt == F32 else nel // 2
            for i in range(n):
                v = carve(n32)
                if dt != F32:
                    v = v.bitcast(BF16)
                if len(shape) == 3:
                    v = v.rearrange("p (a b) -> p a b", a=shape[1])
                elif len(shape) == 4:
                    v = v.rearrange("p (a b c) -> p a b c", a=shape[1], b=shape[2])
                lst.append(T(v, Buf(f"{name}{i}")))
            tmps[name] = [lst, 0]
        lst = tmps[name]
        t = lst[0][lst[1] % n]
        lst[1] += 1
        return t

    dbg_done = []

    def fwd_fft(src, src_ap_fn, K, par):
        A32 = ctmp("A32", [128, GC, 2, 128], F32, 3)
        for c2 in range(GC // 2):
            bank = kb.ps()
            for u in range(2):
                c = 2 * c2 + u
                kb.op("pe", lambda e, bank=bank, c=c, u=u: e.matmul(bank[:, u * 256:(u + 1) * 256], src_ap_fn(c), Ff[0:K, :], start=True, stop=True),
                      reads=[src, Ff], writes=[bank], acc=(u == 1))
            kb.op("act", lambda e, bank=bank, c2=c2, A32=A32: e.copy(A32[:, 2 * c2:2 * c2 + 2, :, :].rearrange("p c r k -> p (c r k)"), bank[:, :]), reads=[bank], writes=[A32])
            yield
        Ab = [ctmp("Ab", [128, GC, 128], BF16, 8) for _ in range(2)]
        if debug and K == 64 and not dbg_done:
            d_ = kb.out("dA32", [128, GC, 2, 128], F32)
            kb.store(d_, d_[:], A32, A32[:])
        yield from twiddle(A32, Ab, conj=False)
        if debug and K == 64 and not dbg_done:
            d_ = kb.out("dAb0", [128, GC, 128], BF16)
            kb.store(d_, d_[:], Ab[0], Ab[0][:])
        banks = []
        for qd in range(GC // 4):
            bre, bim = kb.bank(4 + 2 * par), kb.bank(5 + 2 * par)
            cs = slice(4 * qd, 4 * qd + 4)
            kb.op("pe", lambda e, bre=bre, cs=cs: e.matmul(bre[:, :], Fc[:, :], Ab[0][:, cs, :], start=True, stop=False), reads=[Fc, Ab[0]], writes=[bre])
            kb.op("pe", lambda e, bre=bre, cs=cs: e.matmul(bre[:, :], Fs[:, :], Ab[1][:, cs, :], start=False, stop=True), reads=[Fs, Ab[1]], writes=[bre], acc=True)
            kb.op("pe", lambda e, bim=bim, cs=cs: e.matmul(bim[:, :], Fc[:, :], Ab[1][:, cs, :], start=True, stop=False), reads=[Fc, Ab[1]], writes=[bim])
            kb.op("pe", lambda e, bim=bim, cs=cs: e.matmul(bim[:, :], Fsn[:, :], Ab[0][:, cs, :], start=False, stop=True), reads=[Fsn, Ab[0]], writes=[bim], acc=True)
            banks.append((bre, bim))
            yield
            if debug and K == 64 and not dbg_done and qd == 0:
                xs_ = kb.sb("dbgx", [128, 2, 512], F32)
                kb.op("dve", lambda e, bre=bre: e.tensor_copy(xs_[:, 0, :], bre[:, :]), reads=[bre], writes=[xs_])
                kb.op("dve", lambda e, bim=bim: e.tensor_copy(xs_[:, 1, :], bim[:, :]), reads=[bim], writes=[xs_])
                d_ = kb.out("dX", [128, 2, 512], F32)
                kb.store(d_, d_[:], xs_, xs_[:])
        if debug and K == 64:
            dbg_done.append(1)
        return banks

    def twiddle(A32, Ab, conj):
        are, aim = A32[:, :, 0, :], A32[:, :, 1, :]
        tcb = Tc[:, :].unsqueeze(1).to_broadcast([128, GC, 128])
        tsb = Ts[:, :].unsqueeze(1).to_broadcast([128, GC, 128])
        t1, t2 = ctmp("tw", [128, GC, 128], F32, 8), ctmp("tw", [128, GC, 128], F32, 8)
        kb.op("dve", lambda e: e.tensor_tensor(t1[:], are, tcb, ALU.mult), reads=[A32, Tc], writes=[t1])
        kb.op("pool", lambda e: e.tensor_tensor(t2[:], aim, tsb, ALU.mult), reads=[A32, Ts], writes=[t2])
        yield
        kb.op("dve", lambda e: e.tensor_tensor(Ab[0][:], t1[:], t2[:], ALU.subtract if conj else ALU.add), reads=[t1, t2], writes=[Ab[0]])
        t3, t4 = ctmp("tw", [128, GC, 128], F32, 8), ctmp("tw", [128, GC, 128], F32, 8)
        kb.op("dve", lambda e: e.tensor_tensor(t3[:], aim, tcb, ALU.mult), reads=[A32, Tc], writes=[t3])
        kb.op("pool", lambda e: e.tensor_tensor(t4[:], are, tsb, ALU.mult), reads=[A32, Ts], writes=[t4])
        yield
        kb.op("dve", lambda e: e.tensor_tensor(Ab[1][:], t3[:], t4[:], ALU.add if conj else ALU.subtract), reads=[t3, t4], writes=[Ab[1]])
        yield

    ngroups = CPC // GC if ngroups_override is None else ngroups_override

    def group_gen(g, par):
        c0 = g * GC
        sig = []
        for si in range(3):
            t = tmp(f"sig{si}", [64, GC, 128], BF16, n=2)
            kb.load(t, t[:], sig_in[si][:, c0:c0 + GC, :], q="sp")
            sig.append(t)
        H = []
        for o in range(2):
            k32 = tmp("k32", [128, GC, 128], F32, n=2)
            kb.load(k32, k32[:], kcs[o, c0:c0 + GC, :].rearrange("c (a b) -> a c b", b=128), q="pool", src=kcs)
            kbf = tmp("kbf", [128, GC, 128], BF16, n=4)
            kb.op("act", lambda e, kbf=kbf, k32=k32: e.copy(kbf[:], k32[:]), reads=[k32], writes=[kbf])
            banks = yield from fwd_fft(kbf, lambda c, kbf=kbf: kbf[:, c, :], 128, par)
            Hre, Him = tmp(f"Hre{o}", [128, GC, 128], F32, n=2), tmp(f"Him{o}", [128, GC, 128], F32, n=2)
            for qd, (bre, bim) in enumerate(banks):
                cs = slice(4 * qd, 4 * qd + 4)
                kb.op("act", lambda e, bre=bre, cs=cs, Hre=Hre: e.copy(Hre[:, cs, :].rearrange("p c k -> p (c k)"), bre[:, :]), reads=[bre], writes=[Hre])
                kb.op("act", lambda e, bim=bim, cs=cs, Him=Him: e.copy(Him[:, cs, :].rearrange("p c k -> p (c k)"), bim[:, :]), reads=[bim], writes=[Him])
            H.append((Hre, Him))
            yield
            if debug and g == 0:
                for nm_, t_ in ((f"dHre{o}", Hre), (f"dHim{o}", Him)):
                    d_ = kb.out(nm_, [128, GC, 128], F32)
                    kb.store(d_, d_[:], t_, t_[:])
        cur = sig[0]
        for o in range(2):
            Hre, Him = H[o]
            banks = yield from fwd_fft(cur, lambda c, cur=cur: cur[0:64, c, :], 64, par)
            Yb = [tmp("Yb", [128, GC, 128], BF16, n=4) for _ in range(2)]
            for qd, (bre, bim) in enumerate(banks):
                cs = slice(4 * qd, 4 * qd + 4)
                fl = lambda t, cs=cs: t[:, cs, :].rearrange("p c k -> p (c k)")
                t1, t2 = ctmp("hm", [128, 512], F32, 8), ctmp("hm", [128, 512], F32, 8)
                kb.op("dve", lambda e, t1=t1, bre=bre, fl=fl, Hre=Hre: e.tensor_tensor(t1[:], bre[:, :], fl(Hre), ALU.mult), reads=[bre, Hre], writes=[t1])
                kb.op("dve", lambda e, t2=t2, bim=bim, fl=fl, Him=Him: e.tensor_tensor(t2[:], bim[:, :], fl(Him), ALU.mult), reads=[bim, Him], writes=[t2])
                kb.op("pool", lambda e, t1=t1, t2=t2, fl=fl, Y0=Yb[0]: e.tensor_tensor(fl(Y0), t1[:], t2[:], ALU.subtract), reads=[t1, t2], writes=[Yb[0]])
                t3, t4 = ctmp("hm", [128, 512], F32, 8), ctmp("hm", [128, 512], F32, 8)
                kb.op("dve", lambda e, t3=t3, bre=bre, fl=fl, Him=Him: e.tensor_tensor(t3[:], bre[:, :], fl(Him), ALU.mult), reads=[bre, Him], writes=[t3])
                kb.op("dve", lambda e, t4=t4, bim=bim, fl=fl, Hre=Hre: e.tensor_tensor(t4[:], bim[:, :], fl(Hre), ALU.mult), reads=[bim, Hre], writes=[t4])
                kb.op("pool", lambda e, t3=t3, t4=t4, fl=fl, Y1=Yb[1]: e.tensor_tensor(fl(Y1), t3[:], t4[:], ALU.add), reads=[t3, t4], writes=[Yb[1]])
                yield
            B32 = ctmp("A32", [128, GC, 2, 128], F32, 3)
            for c2 in range(GC // 2):
                bank = kb.ps()
                for u in range(2):
                    c = 2 * c2 + u
                    kb.op("pe", lambda e, bank=bank, c=c, u=u, Y0=Yb[0]: e.matmul(bank[:, u * 256:(u + 1) * 256], Y0[:, c, :], Fi1[:, :], start=True, stop=False),
                          reads=[Yb[0], Fi1], writes=[bank], acc=(u == 1))
                    kb.op("pe", lambda e, bank=bank, c=c, u=u, Y1=Yb[1]: e.matmul(bank[:, u * 256:(u + 1) * 256], Y1[:, c, :], Fi2[:, :], start=False, stop=True),
                          reads=[Yb[1], Fi2], writes=[bank], acc=True)
                kb.op("act", lambda e, bank=bank, c2=c2, B32=B32: e.copy(B32[:, 2 * c2:2 * c2 + 2, :, :].rearrange("p c r k -> p (c r k)"), bank[:, :]), reads=[bank], writes=[B32])
                yield
            Bb = [ctmp("Ab", [128, GC, 128], BF16, 8) for _ in range(2)]
            yield from twiddle(B32, Bb, conj=True)
            xk = sig[1 + o]
            znew = tmp("zb", [64, GC, 128], BF16, n=2)
            for qd in range(GC // 4):
                yb = kb.bank(4 + 2 * par)
                cs = slice(4 * qd, 4 * qd + 4)
                kb.op("pe", lambda e, yb=yb, cs=cs, B0=Bb[0]: e.matmul(yb[0:64, :], Fc[:, 0:64], B0[:, cs, :], start=True, stop=False), reads=[Fc, Bb[0]], writes=[yb])
                kb.op("pe", lambda e, yb=yb, cs=cs, B1=Bb[1]: e.matmul(yb[0:64, :], Fsn[:, 0:64], B1[:, cs, :], start=False, stop=True), reads=[Fsn, Bb[1]], writes=[yb], acc=True)
                t = tmp("ep", [64, 4, 128], F32, n=2)
                sk = skipb[:, o, c0 + 4 * qd:c0 + 4 * qd + 4].unsqueeze(2).to_broadcast([64, 4, 128])
                kb.op("pool", lambda e, t=t, cs=cs, sk=sk, cur=cur: e.tensor_tensor(t[:], cur[0:64, cs, :], sk, ALU.mult), reads=[cur, skipb], writes=[t])
                kb.op("dve", lambda e, t=t, yb=yb: e.scalar_tensor_tensor(t[:].rearrange("p c k -> p (c k)"), yb[0:64, :], 1.0 / NF, t[:].rearrange("p c k -> p (c k)"), ALU.mult, ALU.add),
                      reads=[yb, t], writes=[t])
                kb.op("dve", lambda e, t=t, cs=cs, xk=xk, znew=znew: e.tensor_tensor(znew[:, cs, :], t[:], xk[0:64, cs, :], ALU.mult), reads=[t, xk], writes=[znew])
                yield
            if debug and g == 0:
                d_ = kb.out(f"dz{o}", [64, GC, 128], BF16)
                kb.store(d_, d_[:], znew, znew[:])
                d_ = kb.out(f"dY{o}", [128, GC, 128], BF16)
                kb.store(d_, d_[:], Yb[0], Yb[0][:])
                d_ = kb.out(f"dB{o}", [128, GC, 128], BF16)
                kb.store(d_, d_[:], Bb[0], Bb[0][:])
            cur = znew
        kb.store(zout, zout[:, c0:c0 + GC, :], cur, cur[:], q="pool")

    NCH = 2
    for g2 in range(0, ngroups, NCH):
        gens = [group_gen(g2 + p, p % 2) for p in range(NCH) if g2 + p < ngroups]
        while gens:
            for gen in list(gens):
                try:
                    next(gen)
                except StopIteration:
                    gens.remove(gen)
    kb.finish()
    return kb


def dft_consts():
    a = np.arange(128)
    th = 2 * math.pi * np.outer(a, a) / 128.0
    c, s = np.cos(th), np.sin(th)
    ph = 2 * math.pi * np.outer(a, a) / NF
    d = {"Ff": np.concatenate([c, -s], 1), "Fi1": np.concatenate([c, s], 1), "Fi2": np.concatenate([-s, c], 1),
         "Fc": c, "Fs": s, "Fsn": -s, "Tc": np.cos(ph), "Ts": np.sin(ph)}
    return {k: np.ascontiguousarray(v, dtype=np.float32) for k, v in d.items()}


def hyena_core_inputs(inp, r, u_main, u_ctx):
    sl = slice(CPC * r, CPC * (r + 1))
    d = {}
    d["uc"] = np.ascontiguousarray(np.stack([u_ctx[si * 2048 + CPC * r: si * 2048 + CPC * (r + 1)].reshape(2, 128, 256) for si in range(3)], axis=0))
    cwv = np.concatenate([inp["hyena_conv_w"][0], inp["hyena_conv_b"][0][None, :]], axis=0).reshape(4, 3, 2048)[:, :, sl]
    d["cwc"] = np.ascontiguousarray(cwv.reshape(4, 3, 2, 128).transpose(3, 1, 2, 0), dtype=np.float32)
    ZTc, tc = hyena_consts(256)
    d["ZTc"] = ZTc
    d["tpc"] = np.ascontiguousarray(tc.reshape(4, 128).T, dtype=np.float32)
    d["deltab"] = np.ascontiguousarray(np.broadcast_to(hyena_deltas()[sl][None, :], (128, CPC)), dtype=np.float32)
    d["skipc"] = np.ascontiguousarray(np.broadcast_to(inp["hyena_skip"][0][None, :, sl], (128, 2, CPC)), dtype=np.float32)
    d["ident"] = np.eye(128, dtype=np.float32)
    nn = np.arange(512)
    th = 2 * math.pi * np.outer(nn, nn) / 512.0
    d["Dc"] = np.ascontiguousarray(np.cos(th).reshape(4, 128, 512).transpose(1, 0, 2), dtype=np.float32)
    d["Dsn"] = np.ascontiguousarray((-np.sin(th)).reshape(4, 128, 512).transpose(1, 0, 2), dtype=np.float32)
    for si, nm in enumerate(("v", "x1", "x2")):
        a = u_main[si * 2048 + CPC * r: si * 2048 + CPC * (r + 1)]
        d[nm] = np.ascontiguousarray(a.reshape(CPC, 64, 128).transpose(1, 0, 2))
    ZT, t = hyena_consts(LH)
    d["ZT"] = ZT
    d["tprow"] = np.ascontiguousarray(t[None, :])
    f32 = lambda a: np.ascontiguousarray(a, dtype=np.float32)
    d["fw1"] = f32(inp["hyena_f_w1"][0])
    d["fb1f"] = f32(np.stack([inp["hyena_f_b1"][0], inp["hyena_f_freq"][0, 0]], axis=1))
    w2 = inp["hyena_f_w2"][0]
    d["fw2d"] = f32(np.concatenate([w2, w2], axis=1))
    d["fb2f"] = f32(np.stack([np.tile(inp["hyena_f_b2"][0], 2), np.tile(inp["hyena_f_freq"][0, 1], 2)], axis=1))
    w3 = inp["hyena_f_w3"][0].reshape(64, 2, 2, 2048)
    d["w3s"] = f32(np.concatenate([w3[:, :, 0, sl], w3[:, :, 1, sl]], axis=0))
    d["deltac"] = f32(hyena_deltas()[sl].reshape(2, 128).T)
    d["skipb"] = f32(np.broadcast_to(inp["hyena_skip"][0][None, :, sl], (64, 2, CPC)))
    d.update(dft_consts())
    return d


def run_layer_hyena(inp, mod, x2d, ctx2d, progs):
    li = 2
    w_t = tile_weight(inp["hyena_w_in"][0])
    cwv = np.concatenate([inp["hyena_conv_w"][0], inp["hyena_conv_b"][0][None, :]], axis=0)
    cw = np.ascontiguousarray(cwv.reshape(4, 48, 128).transpose(2, 1, 0), dtype=np.float32)
    maps = []
    for r in range(NCORES):
        hm = np.ones((128, 2), np.float32)
        if r == 0:
            hm[:, 0] = 0
        if r == NCORES - 1:
            hm[:, 1] = 0
        maps.append({"xT": xT_input(x2d, ctx2d, r, halo=1), "modv": modv_input(inp["norm_g"][li], mod[li, 0], mod[li, 1]),
                     "cw": cw, "hm": hm, "w_in": w_t})
    res = run_prog(progs["proj_hyena"], maps)
    full = {n: gather_tokens([np.asarray(res[r][n]) for r in range(NCORES)]) for n in ("uT", "gT")}
    return run_layer_hyena_b(inp, mod, x2d, ctx2d, progs, full)


def run_layer_hyena_b(inp, mod, x2d, ctx2d, progs, full):
    li = 2
    u_main = full["uT"][:, :8192]
    u_ctx = full["uT"][:, 8192:]
    maps = [hyena_core_inputs(inp, r, u_main, u_ctx) for r in range(NCORES)]
    res = run_prog(progs["hyena_core"], maps)
    zmain = np.concatenate([np.asarray(res[r]["z"]).transpose(1, 0, 2).reshape(CPC, 8192) for r in range(NCORES)], axis=0)
    zc = np.concatenate([np.asarray(res[r]["zc"]).reshape(256, CPC).T for r in range(NCORES)], axis=0)
    zfull = np.concatenate([zmain, zc], axis=1)
    oT = [scatter_tokens(zfull, r) for r in range(NCORES)]
    gT = [scatter_tokens(full["gT"], r) for r in range(NCORES)]
    xT = [xT_input(x2d, ctx2d, r) for r in range(NCORES)]
    xo = run_tail(progs["tail"], oT, gT, xT, mod[li, 0], mod[li, 1], inp["hyena_w_out"][0])
    return split_xo(xo)


def kernel(**inputs):
    inp = {k: np.asarray(v) for k, v in inputs.items()}
    mod = run_mod(inp)
    x2d = np.ascontiguousarray(inp["x"][0], dtype=np.float32)
    ctx2d = np.ascontiguousarray(inp["ctx"][0], dtype=np.float32)
    progs = {"proj_swa": build_proj_swa(4), "swa_att": build_swa_att(), "tail": build_tail()}
    x2d, ctx2d = run_layer_swa(inp, mod, x2d, ctx2d, progs)
    progs = {"proj_mla": build_proj_mla(), "mla_att": build_mla_att(), "tail": build_tail()}
    x2d, ctx2d = run_layer_mla(inp, mod, x2d, ctx2d, progs)
    progs = {"proj_hyena": build_proj_hyena(), "hyena_core": build_hyena_core(), "tail": build_tail()}
    x2d, ctx2d = run_layer_hyena(inp, mod, x2d, ctx2d, progs)
    lam_init = 0.8 - 0.6 * math.exp(-0.3 * 3)
    progs = {"proj_diff": build_proj_swa(16), "diff_att": build_diff_att(lam_init), "tail": build_tail()}
    x2d, ctx2d = run_layer_diff(inp, mod, x2d, ctx2d, progs)
    return np.ascontiguousarray(x2d[None], dtype=np.float32)


def tile_weight(w, ncol=256):
    K, N = w.shape
    ng = (N + ncol - 1) // ncol
    wp = np.zeros((K, ng * ncol), np.float32)
    wp[:, :N] = w
    return np.ascontiguousarray(wp.reshape(K // 128, 128, ng, ncol).transpose(2, 1, 0, 3))
```

```python
import math
import numpy as np
import ml_dtypes
import concourse.bass as bass
import concourse.mybir as mybir
from concourse.bass_utils import run_bass_kernel_spmd

F32 = mybir.dt.float32
BF16 = mybir.dt.bfloat16
ALU = mybir.AluOpType
AF = mybir.ActivationFunctionType
AX = mybir.AxisListType
NCORES = 8


class Buf:
    _n = 0

    def __init__(self, name):
        Buf._n += 1
        self.name = f"{name}_{Buf._n}"
        self.writes = {}
        self.reads = {}
        self.dsem = None
        self.dcnt = 0


class Prog:
    def __init__(self, nc):
        self.nc = nc
        self.lists = {e: [] for e in ("pe", "act", "dve", "pool", "sp")}
        self.esem = {}
        self.ecnt = {e: 0 for e in self.lists}
        self.seen = {e: {} for e in self.lists}
        self.sems = {}
        self.out_events = []
        self.dbufs = []
        for e in ("pe", "act", "dve", "pool"):
            self.esem[e] = nc.alloc_semaphore(f"es_{e}")
            self.sems[("e", e)] = self.esem[e]

    def _wait(self, eng, ev):
        if ev is None:
            return
        key, val = ev
        if self.seen[eng].get(key, 0) >= val:
            return
        self.seen[eng][key] = val
        sem = self.sems[key]
        self.lists[eng].append(lambda e, sem=sem, val=val: e.wait_ge(sem, val))

    def _deps(self, eng, reads, writes, acc=False):
        for b in reads:
            for k, v in b.writes.items():
                self._wait(eng, (k, v))
        for b in writes:
            for k, v in b.writes.items():
                if acc and k == ("e", eng):
                    continue
                self._wait(eng, (k, v))
            for k, v in b.reads.items():
                self._wait(eng, (k, v))

    def _mark(self, ev, reads, writes):
        k, v = ev
        for b in reads:
            if b.reads.get(k, 0) < v:
                b.reads[k] = v
        for b in writes:
            if b.writes.get(k, 0) < v:
                b.writes[k] = v
            b.reads = {}

    def op(self, eng, fn, reads=(), writes=(), acc=False):
        self._deps(eng, reads, writes, acc)
        sem = self.esem[eng]
        self.ecnt[eng] += 1
        ev = (("e", eng), self.ecnt[eng])
        self.lists[eng].append(lambda e, fn=fn, sem=sem: fn(e).then_inc(sem, 1))
        self._mark(ev, reads, writes)
        return ev

    def dma(self, q, out, in_, sb, reads=(), writes=(), is_output=False, **kw):
        if q == "pool":
            q = "sp"
        self._deps(q, reads, writes)
        if sb.dsem is None:
            sb.dsem = self.nc.alloc_semaphore(f"ds_{sb.name}")
            self.sems[("d", sb.name)] = sb.dsem
        sb.dcnt += 1
        if not hasattr(self, "dbufs"):
            self.dbufs = []
        if sb not in self.dbufs:
            self.dbufs.append(sb)
        ev = (("d", sb.name), 16 * sb.dcnt)
        sem = sb.dsem
        self.lists[q].append(lambda e, out=out, in_=in_, sem=sem, kw=kw: e.dma_start(out=out, in_=in_, **kw).then_inc(sem, 16))
        self._mark(ev, reads, writes)
        if is_output:
            self.out_events.append(ev)
        return ev

    def barrier(self):
        for eng in self.lists:
            for e2, cnt in self.ecnt.items():
                if e2 in self.esem and cnt > 0:
                    self._wait(eng, (("e", e2), cnt))
            for b in self.dbufs:
                self._wait(eng, (("d", b.name), 16 * b.dcnt))

    def finish(self):
        for ev in self.out_events:
            self._wait("sp", ev)
        nc = self.nc
        lists = self.lists
        with nc.Block() as block:
            @block.tensor
            def _(e):
                for f in lists["pe"]:
                    f(e)

            @block.scalar
            def _(e):
                for f in lists["act"]:
                    f(e)

            @block.vector
            def _(e):
                for f in lists["dve"]:
                    f(e)

            @block.gpsimd
            def _(e):
                for f in lists["pool"]:
                    f(e)

            @block.sync
            def _(e):
                for f in lists["sp"]:
                    f(e)


class T:
    def __init__(self, h, b):
        self.h = h
        self.b = b

    def __getitem__(self, key):
        return self.h[key]


class KB:
    def __init__(self):
        self.nc = bass.Bass("TRN2", target_bir_lowering=False)
        self.p = Prog(self.nc)
        self.in_names = []
        self.out_names = []
        self._ps = [T(self.nc.alloc_psum_tensor(f"psb{i}", [128, 512], F32), Buf(f"psb{i}")) for i in range(8)]
        self._psi = 0
        self._n = 0

    def inp(self, name, shape, dt=F32):
        self.in_names.append(name)
        return self.nc.dram_tensor(name, list(shape), dt, kind="ExternalInput").ap()

    def out(self, name, shape, dt=F32):
        self.out_names.append(name)
        return T(self.nc.dram_tensor(name, list(shape), dt, kind="ExternalOutput").ap(), Buf(name))

    def scratch(self, name, shape, dt=F32):
        return T(self.nc.dram_tensor(name, list(shape), dt, kind="Internal").ap(), Buf(name))

    def sb(self, name, shape, dt=F32):
        self._n += 1
        nm = f"{name}_{self._n}"
        return T(self.nc.alloc_sbuf_tensor(nm, list(shape), dt), Buf(nm))

    def ps(self):
        pool = getattr(self, "_pool", list(range(8)))
        t = self._ps[pool[self._psi % len(pool)]]
        self._psi += 1
        return t

    def bank(self, i):
        return self._ps[i]

    def op(self, eng, fn, reads=(), writes=(), acc=False):
        return self.p.op(eng, fn, [t.b for t in reads], [t.b for t in writes], acc)

    def load(self, dst, dst_ap, src_ap, q="sp", src=None, **kw):
        rd = [src.b] if src is not None else []
        return self.p.dma(q, dst_ap, src_ap, dst.b, reads=rd, writes=[dst.b], **kw)

    def store(self, dst, dst_ap, src, src_ap, q="sp", is_output=True, **kw):
        return self.p.dma(q, dst_ap, src_ap, src.b, reads=[src.b], writes=[dst.b], is_output=is_output, **kw)

    def finish(self):
        self.p.finish()
        return self.nc

    def const_load(self, name, shape, dt=F32, q="sp"):
        ap = self.inp(name, shape, dt)
        t = self.sb(name, shape, dt)
        self.load(t, t[:], ap, q=q)
        return t

    def rstd_from_ss(self, dst, dst_ap, ss, ss_ap, inv_n, eps=1e-6):
        self.op("dve", lambda e: e.tensor_scalar(dst_ap, ss_ap, inv_n, eps, ALU.mult, ALU.add), reads=[ss], writes=[dst])
        self.op("act", lambda e: e.activation(dst_ap, dst_ap, AF.Sqrt), reads=[dst], writes=[dst])
        self.op("dve", lambda e: e.reciprocal(dst_ap, dst_ap), reads=[dst], writes=[dst])


def run_prog(kb, in_maps):
    res = run_bass_kernel_spmd(kb.nc, in_maps, core_ids=list(range(NCORES)))
    return res.results


class WStream:
    def __init__(self, kb, name, kc, ncol_max, cast_eng="pool"):
        self.kb = kb
        self.kc = kc
        self.ncol_max = ncol_max
        self.stg = [kb.sb(f"{name}_stg{i}", [128, kc, ncol_max], F32) for i in range(2)]
        self.wb = [kb.sb(f"{name}_wb{i}", [128, kc, ncol_max], BF16) for i in range(2)]
        self.n = 0
        self.cast_eng = cast_eng

    def issue(self, w_ap, c0, ncol):
        kb = self.kb
        s = self.n % 2
        self.n += 1
        stg, wb = self.stg[s], self.wb[s]
        gi = c0 // self.ncol_max
        h = max(1, self.kc // 2)
        for k0 in range(0, self.kc, h):
            kb.load(stg, stg[:, k0:k0 + h, :], w_ap[gi, :, k0:k0 + h, :], q="sp")
        return (stg, wb, ncol)

    def cast(self, handle):
        kb = self.kb
        stg, wb, ncol = handle
        h = max(1, self.kc // 2)
        kb.op("act", lambda e: e.copy(wb[:, 0:h, :ncol], stg[:, 0:h, :ncol]), reads=[stg], writes=[wb])
        if h < self.kc:
            kb.op("dve", lambda e: e.tensor_copy(wb[:, h:, :ncol], stg[:, h:, :ncol]), reads=[stg], writes=[wb])
        return wb


def build_mod():
    kb = KB()
    cs = kb.const_load("cs", [128, 16, 2])
    adaw = kb.inp("adaw", [4, 2048, 768])
    bias = kb.const_load("adab", [2, 4, 768])
    modo = kb.out("modo", [2, 4 * 768])
    kb.op("act", lambda e: e.activation(cs[:], cs[:], AF.Silu), reads=[cs], writes=[cs])
    wt = [kb.sb(f"adaw{i}", [128, 16, 768], F32) for i in range(2)]
    osb = kb.sb("osb", [2, 4 * 768], F32)
    for i in range(4):
        w = wt[i % 2]
        for g in range(4):
            kb.load(w, w[:, 4 * g:4 * g + 4, :], adaw[i, 512 * g:512 * (g + 1), :].rearrange("(kc p) n -> p kc n", p=128), q="sp")
        for nb in range(2):
            ps = kb.ps()
            for kc in range(16):
                kb.op("pe", lambda e, ps=ps, w=w, kc=kc, nb=nb: e.matmul(ps[0:2, 0:384], cs[:, kc, :], w[:, kc, nb * 384:(nb + 1) * 384],
                                                                       start=(kc == 0), stop=(kc == 15)),
                      reads=[cs, w], writes=[ps], acc=True)
            c0 = i * 768 + nb * 384
            kb.op("dve", lambda e, ps=ps, c0=c0, i=i, nb=nb: e.tensor_tensor(osb[0:2, c0:c0 + 384], ps[0:2, 0:384],
                                                                           bias[0:2, i, nb * 384:(nb + 1) * 384], ALU.add),
                  reads=[ps, bias], writes=[osb])
    kb.store(modo, modo[:], osb, osb[:])
    kb.finish()
    return kb


def run_mod(inp):
    kb = build_mod()
    c = inp["c"].reshape(2048)
    cc = inp["c_ctx"].reshape(2048)
    cs = np.stack([c.reshape(16, 128).T, cc.reshape(16, 128).T], axis=-1).astype(np.float32)
    maps = []
    for r in range(NCORES):
        sl = slice(768 * r, 768 * (r + 1))
        adab = np.ascontiguousarray(np.broadcast_to(inp["ada_b"][None, :, sl], (2, 4, 768)))
        maps.append({"cs": np.ascontiguousarray(cs), "adaw": np.ascontiguousarray(inp["ada_w"][:, :, sl]), "adab": adab})
    res = run_prog(kb, maps)
    mod = np.zeros((4, 2, 6144), np.float32)
    for r in range(NCORES):
        mod[:, :, 768 * r:768 * (r + 1)] = res[r]["modo"].reshape(2, 4, 768).transpose(1, 0, 2)
    return mod


TC = 32


def rope_tables(positions, rot_dim):
    row = (positions // 64).astype(np.float32)
    col = (positions % 64).astype(np.float32)
    half = rot_dim // 2
    inv = (1.0 / (10000.0 ** (np.arange(0, half, 2, dtype=np.float32) / half))).astype(np.float32)
    ar = row[:, None] * inv[None, :]
    ac = col[:, None] * inv[None, :]
    ang = np.concatenate([ar, ar, ac, ac], axis=-1)
    return np.cos(ang).T.astype(np.float32), np.sin(ang).T.astype(np.float32)


def rope_matrix(rot_dim, reps):
    half = rot_dim // 2
    qr = half // 2
    R = np.zeros((rot_dim, rot_dim), np.float32)
    for seg in range(2):
        o = seg * half
        for i in range(qr):
            R[o + qr + i, o + i] = -1.0
            R[o + i, o + qr + i] = 1.0
    full = np.zeros((rot_dim * reps, rot_dim * reps), np.float32)
    for r in range(reps):
        full[r * rot_dim:(r + 1) * rot_dim, r * rot_dim:(r + 1) * rot_dim] = R
    return full


def blockdiag_ones(bs):
    m = np.zeros((128, 128), np.float32)
    for r in range(128 // bs):
        m[r * bs:(r + 1) * bs, r * bs:(r + 1) * bs] = 1.0
    return m


class ProjCtx:
    def __init__(self, kb, tm, halo):
        self.kb = kb
        self.tm = tm
        self.halo = halo
        self.ttot = tm + TC
        tiles = []
        c = 0
        while c < tm:
            w = min(512, tm - c)
            tiles.append((c, w, False))
            c += w
        tiles.append((tm, TC, True))
        self.tiles = tiles
        self._tmp = {}

    def tmp(self, name, shape, dt, n=2):
        key = name
        if key not in self._tmp:
            self._tmp[key] = [[self.kb.sb(f"{name}{i}", shape, dt) for i in range(n)], 0]
        lst = self._tmp[key]
        t = lst[0][lst[1] % n]
        lst[1] += 1
        return t


def emit_modnorm(pc, xT_ap, modv):
    kb = pc.kb
    T_ = pc.ttot
    ones = kb.sb("ones32", [128, 128], F32)
    kb.op("pool", lambda e: e.memset(ones[:], 1.0), writes=[ones])
    AB = kb.sb("AB", [128, 2, 16], F32)
    for w_, (sc) in enumerate((2, 4)):
        kb.op("dve", lambda e, w_=w_, sc=sc: e.tensor_scalar(AB[:, w_, :], modv[:, sc, :], 1.0, None, ALU.add), reads=[modv], writes=[AB])
        kb.op("dve", lambda e, w_=w_: e.tensor_tensor(AB[:, w_, :], AB[:, w_, :], modv[:, 0, :], ALU.mult), reads=[AB, modv], writes=[AB])
    rstd = kb.sb("rstd", [128, T_], F32)
    pss = [kb.bank(i) for i in range(len(pc.tiles))]
    for kc in range(16):
        xc = pc.tmp("xc", [128, T_], F32, n=3)
        kb.load(xc, xc[:], xT_ap[128 * kc:128 * (kc + 1), :], q=("sp" if kc % 2 == 0 else "pool"))
        for ti, (c0, w, isc) in enumerate(pc.tiles):
            ps = pss[ti]
            sq = pc.tmp("sq32", [128, 512], F32)
            kb.op("act", lambda e, sq=sq, xc=xc, c0=c0, w=w: e.activation(sq[:, :w], xc[:, c0:c0 + w], AF.Square), reads=[xc], writes=[sq])
            kb.op("pe", lambda e, ps=ps, sq=sq, kc=kc, w=w: e.matmul(ps[:, :w], ones[:], sq[:, :w], start=(kc == 0), stop=(kc == 15)),
                  reads=[ones, sq], writes=[ps], acc=(kc != 0))
    for ti, (c0, w, isc) in enumerate(pc.tiles):
        kb.rstd_from_ss(rstd, rstd[:, c0:c0 + w], pss[ti], pss[ti][:, :w], 1.0 / 2048)
    hT = kb.sb("hT", [128, 16, T_], BF16)
    for kc in range(16):
        xc = pc.tmp("xc", [128, T_], F32, n=3)
        kb.load(xc, xc[:], xT_ap[128 * kc:128 * (kc + 1), :], q=("sp" if kc % 2 == 0 else "pool"))
        for (c0, w, isc) in pc.tiles:
            wi = 1 if isc else 0
            bi = 3 if isc else 1
            t = pc.tmp("mn32", [128, 512], F32)
            kb.op("dve", lambda e, t=t, xc=xc, kc=kc, c0=c0, w=w, wi=wi: e.scalar_tensor_tensor(t[:, :w], xc[:, c0:c0 + w], AB[:, wi, kc:kc + 1],
                                                                                             rstd[:, c0:c0 + w], ALU.mult, ALU.mult),
                  reads=[xc, AB, rstd], writes=[t])
            kb.op("act", lambda e, t=t, kc=kc, c0=c0, w=w, bi=bi: e.activation(hT[:, kc, c0:c0 + w], t[:, :w], AF.Identity, bias=modv[:, bi, kc:kc + 1]),
                  reads=[t, modv], writes=[hT])
    return hT


def emit_epilogue_gen(pc, blk, accs, C):
    kb = pc.kb
    kind = blk["kind"]
    T_ = pc.ttot
    if kind == "keep":
        dst, idx = blk["dst"], blk["idx"]
        for ti, (c0, w, isc) in enumerate(pc.tiles):
            ps = accs[ti]
            eng = "act" if ti % 2 == 0 else "dve"
            if eng == "act":
                kb.op("act", lambda e, ps=ps, c0=c0, w=w: e.copy(dst[:, idx, c0:c0 + w], ps[:, :w]), reads=[ps], writes=[dst])
            else:
                kb.op("dve", lambda e, ps=ps, c0=c0, w=w: e.tensor_copy(dst[:, idx, c0:c0 + w], ps[:, :w]), reads=[ps], writes=[dst])
            yield
        return
    dt = blk.get("dt", F32)
    stage = pc.tmp("stg32" if dt == F32 else "stg16", [128, T_], dt, n=3)
    if kind == "resid":
        xs, gv, ob = blk["xs"], blk["gv"], blk["ob"]
        for ti, (c0, w, isc) in enumerate(pc.tiles):
            ps = accs[ti]
            wi = 1 if isc else 0
            kb.op("dve", lambda e, ps=ps, c0=c0, w=w, wi=wi: e.scalar_tensor_tensor(stage[:, c0:c0 + w], ps[:, :w], gv[:, wi, ob:ob + 1], xs[:, ob, c0:c0 + w], ALU.mult, ALU.add),
                  reads=[ps, gv, xs], writes=[stage])
            yield
    elif kind == "raw":
        for ti, (c0, w, isc) in enumerate(pc.tiles):
            ps = accs[ti]
            if ti % 2 == 0:
                kb.op("act", lambda e, ps=ps, c0=c0, w=w: e.copy(stage[:, c0:c0 + w], ps[:, :w]), reads=[ps], writes=[stage])
            else:
                kb.op("dve", lambda e, ps=ps, c0=c0, w=w: e.tensor_copy(stage[:, c0:c0 + w], ps[:, :w]), reads=[ps], writes=[stage])
            yield
    elif kind == "hn":
        bs, gain, rope = blk["bs"], blk["gain"], blk["rope"]
        onesb = C["ones128"] if bs == 128 else C["ones64"]

        def tile_gen(ti, c0, w, isc):
            ps = accs[ti]
            sqb = pc.tmp("sqb", [128, 512], BF16, n=3)
            kb.op("act", lambda e: e.activation(sqb[:, :w], ps[:, :w], AF.Square), reads=[ps], writes=[sqb])
            yield
            pss = kb.ps()
            kb.op("pe", lambda e: e.matmul(pss[:, :w], onesb[:], sqb[:, :w], start=True, stop=True), reads=[onesb, sqb], writes=[pss])
            yield
            t1 = pc.tmp("t1", [128, 512], F32, n=3)
            kb.op("dve", lambda e: e.tensor_scalar(t1[:, :w], pss[:, :w], 1.0 / bs, 1e-6, ALU.mult, ALU.add), reads=[pss], writes=[t1])
            yield
            kb.op("act", lambda e: e.activation(t1[:, :w], t1[:, :w], AF.Sqrt), reads=[t1], writes=[t1])
            yield
            kb.op("dve", lambda e: e.reciprocal(t1[:, :w], t1[:, :w]), reads=[t1], writes=[t1])
            yield
            if rope and not isc:
                yn = pc.tmp("yn", [128, 512], F32, n=3)
                kb.op("dve", lambda e: e.scalar_tensor_tensor(yn[:, :w], ps[:, :w], gain, t1[:, :w], ALU.mult, ALU.mult),
                      reads=[ps, t1, blk["gain_t"]], writes=[yn])
                yield
                ynb = pc.tmp("ynb", [128, 512], BF16, n=3)
                kb.op("act", lambda e: e.copy(ynb[:, :w], yn[:, :w]), reads=[yn], writes=[ynb])
                yield
                psr = kb.ps()
                Rm = C["rm128"] if bs == 128 else C["rm64"]
                kb.op("pe", lambda e: e.matmul(psr[:, :w], Rm[:], ynb[:, :w], start=True, stop=True), reads=[Rm, ynb], writes=[psr])
                cos, sin = (C["cos128"], C["sin128"]) if bs == 128 else (C["cos64"], C["sin64"])
                kb.op("dve", lambda e: e.tensor_tensor(yn[:, :w], yn[:, :w], cos[:, c0:c0 + w], ALU.mult), reads=[yn, cos], writes=[yn])
                yield
                o2 = pc.tmp("o2", [128, 512], F32, n=3)
                kb.op("dve", lambda e: e.tensor_tensor(o2[:, :w], psr[:, :w], sin[:, c0:c0 + w], ALU.mult), reads=[psr, sin], writes=[o2])
                yield
                kb.op("dve", lambda e: e.tensor_tensor(stage[:, c0:c0 + w], yn[:, :w], o2[:, :w], ALU.add), reads=[yn, o2], writes=[stage])
            else:
                kb.op("dve", lambda e: e.scalar_tensor_tensor(stage[:, c0:c0 + w], ps[:, :w], gain, t1[:, :w], ALU.mult, ALU.mult),
                      reads=[ps, t1, blk["gain_t"]], writes=[stage])
            yield

        main = [tile_gen(ti, c0, w, isc) for ti, (c0, w, isc) in enumerate(pc.tiles) if not isc]
        rest = [tile_gen(ti, c0, w, isc) for ti, (c0, w, isc) in enumerate(pc.tiles) if isc]
        for grp in (main[0:2], main[2:] + rest):
            gens = list(grp)
            while gens:
                for gen in list(gens):
                    try:
                        next(gen)
                    except StopIteration:
                        gens.remove(gen)
                yield
    elif kind == "conv3":
        cw = blk["cw"]
        ci = blk["ci"]
        hm = C["hm"]
        u = pc.tmp("u32", [128, T_], F32)
        for ti, (c0, w, isc) in enumerate(pc.tiles):
            ps = accs[ti]
            if ti % 2 == 0:
                kb.op("act", lambda e, ps=ps, c0=c0, w=w: e.copy(u[:, c0:c0 + w], ps[:, :w]), reads=[ps], writes=[u])
            else:
                kb.op("dve", lambda e, ps=ps, c0=c0, w=w: e.tensor_copy(u[:, c0:c0 + w], ps[:, :w]), reads=[ps], writes=[u])
            yield
        tm = pc.tm
        kb.op("dve", lambda e: e.tensor_scalar(u[:, 0:1], u[:, 0:1], hm[:, 0:1], None, ALU.mult), reads=[u, hm], writes=[u])
        kb.op("dve", lambda e: e.tensor_scalar(u[:, tm - 1:tm], u[:, tm - 1:tm], hm[:, 1:2], None, ALU.mult), reads=[u, hm], writes=[u])
        n = tm - 2
        yield
        kb.op("dve", lambda e: e.tensor_scalar(stage[:, 1:1 + n], u[:, 0:n], cw[:, ci, 0:1], cw[:, ci, 3:4], ALU.mult, ALU.add), reads=[u, cw], writes=[stage])
        yield
        kb.op("dve", lambda e: e.scalar_tensor_tensor(stage[:, 1:1 + n], u[:, 1:1 + n], cw[:, ci, 1:2], stage[:, 1:1 + n], ALU.mult, ALU.add),
              reads=[u, cw, stage], writes=[stage])
        yield
        kb.op("dve", lambda e: e.scalar_tensor_tensor(stage[:, 1:1 + n], u[:, 2:2 + n], cw[:, ci, 2:3], stage[:, 1:1 + n], ALU.mult, ALU.add),
              reads=[u, cw, stage], writes=[stage])
        kb.op("pool", lambda e: e.tensor_copy(stage[:, tm:tm + TC], u[:, tm:tm + TC]), reads=[u], writes=[stage])
    else:
        raise ValueError(kind)
    out, r0 = blk["out"], blk["r0"]
    h = pc.halo
    if h:
        kb.store(out, out[r0:r0 + 128, 0:pc.tm - 2], stage, stage[:, 1:pc.tm - 1], q="pool")
        kb.store(out, out[r0:r0 + 128, pc.tm - 2:pc.tm - 2 + TC], stage, stage[:, pc.tm:pc.tm + TC], q="pool")
    else:
        kb.store(out, out[r0:r0 + 128, :], stage, stage[:], q="pool")


def emit_proj_stage(pc, src, KC, w_ap, blocks, ws, C):
    kb = pc.kb
    BPG = ws.ncol_max // 128
    ngroups = (len(blocks) + BPG - 1) // BPG
    nt = len(pc.tiles)
    if nt <= 3:
        acc_sets = [[0, 1, 2], [3, 4, 5]]
        kb._pool = [6, 7]
    else:
        acc_sets = [[0, 1, 2, 3], [4, 5, 6, 7]]
    kb._psi = 0

    def issue(g):
        nb = min(BPG, len(blocks) - BPG * g)
        return ws.issue(w_ap, 128 * BPG * g, 128 * nb)

    def mm_gen(cur, bi, accs):
        for ti, (c0, w, isc) in enumerate(pc.tiles):
            ps = accs[ti]
            for kc in range(KC):
                kb.op("pe", lambda e, ps=ps, kc=kc, c0=c0, w=w: e.matmul(ps[:, :w], cur[:, kc, bi * 128:(bi + 1) * 128], src[:, kc, c0:c0 + w],
                                                                   start=(kc == 0), stop=(kc == KC - 1)),
                      reads=[cur, src], writes=[ps], acc=(kc != 0))
                if kc % 4 == 3:
                    yield

    def drive(gens):
        gens = [g_ for g_ in gens if g_ is not None]
        while gens:
            for gen in list(gens):
                try:
                    next(gen)
                except StopIteration:
                    gens.remove(gen)

    wt = {0: ws.cast(issue(0))}
    pend = {}
    prev = None
    for b, blk in enumerate(blocks):
        g, bi = divmod(b, BPG)
        if bi == 0 and g + 1 < ngroups:
            pend[g + 1] = issue(g + 1)
        accs = [kb.bank(i) for i in acc_sets[b % 2]][:nt]
        drive([mm_gen(wt[g], bi, accs), prev])
        prev = emit_epilogue_gen(pc, blk, accs, C)
        last_of_group = (bi == BPG - 1) or (b == len(blocks) - 1)
        if last_of_group and (g + 1) in pend:
            wt[g + 1] = ws.cast(pend.pop(g + 1))
    drive([prev])
    kb._pool = list(range(8))


def emit_widenorm(pc, raw, nb, gain, dst, C, gcol0=0):
    kb = pc.kb
    for (c0, w, isc) in pc.tiles:
        ps = kb.ps()
        for b in range(nb):
            sqb = pc.tmp("sqb", [128, 512], BF16)
            kb.op("act", lambda e, sqb=sqb, b=b, c0=c0, w=w: e.activation(sqb[:, :w], raw[:, b, c0:c0 + w], AF.Square), reads=[raw], writes=[sqb])
            kb.op("pe", lambda e, ps=ps, sqb=sqb, b=b, w=w: e.matmul(ps[:, :w], C["ones128"][:], sqb[:, :w], start=(b == 0), stop=(b == nb - 1)),
                  reads=[C["ones128"], sqb], writes=[ps], acc=True)
        t1 = pc.tmp("t1", [128, 512], F32)
        kb.rstd_from_ss(t1, t1[:, :w], ps, ps[:, :w], 1.0 / (128 * nb))
        for b in range(nb):
            kb.op("dve", lambda e, b=b, t1=t1, c0=c0, w=w: e.scalar_tensor_tensor(dst[:, b, c0:c0 + w], raw[:, b, c0:c0 + w], gain[:, gcol0 + b:gcol0 + b + 1], t1[:, :w], ALU.mult, ALU.mult),
                  reads=[raw, gain, t1], writes=[dst])


def proj_consts(kb, need64=False, rope=True, tm=1024):
    C = {}
    def cbf(name, shape):
        t32 = kb.const_load(name, shape)
        tb = kb.sb(name + "b", shape, BF16)
        kb.op("dve", lambda e: e.tensor_copy(tb[:], t32[:]), reads=[t32], writes=[tb])
        return tb
    C["ones128"] = cbf("c_ones128", [128, 128])
    if rope:
        C["rm128"] = cbf("c_rm128", [128, 128])
        C["cos128"] = kb.const_load("c_cos128", [128, tm])
        C["sin128"] = kb.const_load("c_sin128", [128, tm], q="pool")
    if need64:
        C["ones64"] = cbf("c_ones64", [128, 128])
        C["rm64"] = cbf("c_rm64", [128, 128])
        C["cos64"] = kb.const_load("c_cos64", [128, tm])
        C["sin64"] = kb.const_load("c_sin64", [128, tm], q="pool")
    return C


def proj_const_inputs(r, need64=False, rope=True, tm=1024):
    d = {"c_ones128": blockdiag_ones(128)}
    pos = np.arange(1024 * r, 1024 * (r + 1))
    if rope:
        d["c_rm128"] = rope_matrix(128, 1)
        d["c_cos128"], d["c_sin128"] = rope_tables(pos, 128)
    if need64:
        d["c_ones64"] = blockdiag_ones(64)
        d["c_rm64"] = rope_matrix(64, 2)
        c, s = rope_tables(pos, 64)
        d["c_cos64"], d["c_sin64"] = np.concatenate([c, c], 0), np.concatenate([s, s], 0)
    return {k: np.ascontiguousarray(v, dtype=np.float32) for k, v in d.items()}


def modv_input(norm_g, mod_m, mod_c):
    def l(v):
        return v.reshape(16, 128).T
    return np.ascontiguousarray(np.stack([l(norm_g), l(mod_m[0:2048]), l(mod_m[2048:4096]), l(mod_c[0:2048]), l(mod_c[2048:4096])], axis=1), dtype=np.float32)


def xT_input(x2d, ctx2d, r, halo=0):
    lo, hi = 1024 * r - halo, 1024 * (r + 1) + halo
    cols = []
    if lo < 0:
        cols.append(np.zeros((2048, halo), np.float32))
    cols.append(x2d[max(lo, 0):min(hi, 8192)].T)
    if hi > 8192:
        cols.append(np.zeros((2048, halo), np.float32))
    cols.append(ctx2d[TC * r:TC * (r + 1)].T)
    return np.ascontiguousarray(np.concatenate(cols, axis=1), dtype=np.float32)


def hn_blocks(n, bs, gain_t, col, rope, out, r0=0, dt=BF16):
    return [dict(kind="hn", bs=bs, gain=gain_t[:, col:col + 1], gain_t=gain_t, rope=rope, out=out, r0=r0 + 128 * i, dt=dt) for i in range(n)]


def raw_blocks(n, out, r0=0, dt=F32):
    return [dict(kind="raw", out=out, r0=r0 + 128 * i, dt=dt) for i in range(n)]


def build_proj_swa(nkv=4):
    kb = KB()
    pc = ProjCtx(kb, 1024, 0)
    T_ = pc.ttot
    xT = kb.inp("xT", [2048, T_])
    modv = kb.const_load("modv", [128, 5, 16])
    gains = kb.const_load("gains", [128, 2])
    C = proj_consts(kb)
    w = kb.inp("w_in", [(4096 + 256 * nkv) // 256, 128, 16, 256])
    qT = kb.out("qT", [2048, T_], BF16)
    kT = kb.out("kT", [128 * nkv, T_], BF16)
    vT = kb.out("vT", [128 * nkv, T_], BF16)
    gT = kb.out("gT", [2048, T_], BF16)
    hT = emit_modnorm(pc, xT, modv)
    blocks = (hn_blocks(16, 128, gains, 0, True, qT) + hn_blocks(nkv, 128, gains, 1, True, kT)
              + raw_blocks(nkv, vT, dt=BF16) + raw_blocks(16, gT, dt=BF16))
    ws = WStream(kb, "w", 16, 256)
    emit_proj_stage(pc, hT, 16, w, blocks, ws, C)
    kb.finish()
    return kb


def col2(a, b):
    return np.ascontiguousarray(np.stack([a, b], axis=1), dtype=np.float32)


def build_tail():
    kb = KB()
    pc = ProjCtx(kb, 1024, 0)
    T_ = pc.ttot
    oT = kb.inp("oT", [2048, T_], BF16)
    gT = kb.inp("gT", [2048, T_], BF16)
    xT = kb.inp("xT", [2048, T_])
    gv = kb.const_load("gv", [128, 2, 16])
    w = kb.inp("w_out", [8, 128, 16, 256])
    xo = kb.out("xo", [2048, T_], F32)
    xs = kb.sb("xs", [128, 16, T_], F32)
    for g in range(4):
        kb.load(xs, xs[:, 4 * g:4 * g + 4, :], xT[512 * g:512 * (g + 1), :].rearrange("(kc p) t -> p kc t", p=128), q="pool")
    aT = kb.sb("aT", [128, 16, T_], BF16)
    for kc in range(16):
        ob_ = pc.tmp("o_in", [128, T_], BF16)
        gb_ = pc.tmp("g_in", [128, T_], BF16)
        kb.load(ob_, ob_[:], oT[128 * kc:128 * (kc + 1), :], q="sp")
        kb.load(gb_, gb_[:], gT[128 * kc:128 * (kc + 1), :], q="sp")
        sg = pc.tmp("sg", [128, T_], F32)
        kb.op("act", lambda e, sg=sg, gb_=gb_: e.activation(sg[:], gb_[:], AF.Silu), reads=[gb_], writes=[sg])
        kb.op("dve", lambda e, sg=sg, ob_=ob_, kc=kc: e.tensor_tensor(aT[:, kc, :], sg[:], ob_[:], ALU.mult), reads=[sg, ob_], writes=[aT])
    blocks = [dict(kind="resid", xs=xs, gv=gv, ob=i, out=xo, r0=128 * i, dt=F32) for i in range(16)]
    ws = WStream(kb, "w", 16, 256)
    emit_proj_stage(pc, aT, 16, w, blocks, ws, {})
    kb.finish()
    return kb


def gv_input(mod_m, mod_c):
    def l(v):
        return v.reshape(16, 128).T
    return np.ascontiguousarray(np.stack([l(mod_m[4096:6144]), l(mod_c[4096:6144])], axis=1), dtype=np.float32)


def run_tail(kb, oT_list, gT_list, xT_list, mod_m, mod_c, w_out):
    gv = gv_input(mod_m, mod_c)
    w_t = tile_weight(w_out)
    maps = [{"oT": oT_list[r], "gT": gT_list[r], "xT": xT_list[r], "gv": gv, "w_out": w_t} for r in range(NCORES)]
    res = run_prog(kb, maps)
    return [np.asarray(res[r]["xo"]) for r in range(NCORES)]


def split_xo(xo_list):
    x2d = np.concatenate([xo[:, :1024].T for xo in xo_list], axis=0)
    c2d = np.concatenate([xo[:, 1024:1024 + TC].T for xo in xo_list], axis=0)
    return np.ascontiguousarray(x2d), np.ascontiguousarray(c2d)


def build_swa_att():
    kb = KB()
    T_ = 1024 + TC
    qT = kb.inp("qT", [2048, T_], BF16)
    kL = kb.inp("kL", [512, 1280], BF16)
    vL = kb.inp("vL", [1280, 512], BF16)
    ckT = kb.inp("ckT", [512, 256], BF16)
    cvL = kb.inp("cvL", [256, 512], BF16)
    oT = kb.out("oT", [2048, T_], BF16)
    q_sb = kb.sb("q_sb", [128, 16, T_], BF16)
    for g4 in range(4):
        kb.load(q_sb, q_sb[:, 4 * g4:4 * g4 + 4, :], qT[512 * g4:512 * (g4 + 1), :].rearrange("(h p) t -> p h t", p=128))
    k_sb = kb.sb("k_sb", [128, 4, 1280], BF16)
    kb.load(k_sb, k_sb[:], kL.rearrange("(g p) t -> p g t", p=128), q="pool")
    v_sb = kb.sb("v_sb", [128, 10, 512], BF16)
    kb.load(v_sb, v_sb[:], vL.rearrange("(j p) c -> p j c", p=128))
    ck_sb = kb.sb("ck_sb", [128, 4, 256], BF16)
    kb.load(ck_sb, ck_sb[:], ckT.rearrange("(g p) t -> p g t", p=128), q="pool")
    cv_sb = kb.sb("cv_sb", [128, 2, 512], BF16)
    kb.load(cv_sb, cv_sb[:], cvL.rearrange("(j p) c -> p j c", p=128))
    masks32 = kb.const_load("masks", [128, 4, 128])
    masks = kb.sb("masksb", [128, 4, 128], BF16)
    kb.op("dve", lambda e: e.tensor_copy(masks[:], masks32[:]), reads=[masks32], writes=[masks])
    esink = kb.const_load("sinkb", [128, 16])
    kb.op("act", lambda e: e.activation(esink[:], esink[:], AF.Exp), reads=[esink], writes=[esink])
    ones = kb.sb("onesb", [128, 128], BF16)
    kb.op("pool", lambda e: e.memset(ones[:], 1.0), writes=[ones])
    o_sb = kb.sb("o_sb", [128, 16, T_], BF16)
    kb._pool = [0, 1, 2, 3]
    pts = [kb.sb(f"pt{i}", [128, 512], BF16) for i in range(10)]
    dens = [kb.sb(f"den{i}", [128, 512], F32) for i in range(2)]
    scale = 128.0 ** -0.5
    state = {"it": 0, "npt": 0}

    def iter_gen(g, qb):
        if qb < 8:
            nq = 128
            qc0 = qb * 128
            kblocks = [("l", qb, 2 if qb == 0 else 0), ("l", qb + 1, None), ("l", qb + 2, 3 if qb == 7 else 1), ("c", 0, None), ("c", 1, None)]
        else:
            nq = TC
            qc0 = 1024
            kblocks = [("c", 0, None), ("c", 1, None)]
        N = 4 * nq
        it = state["it"]
        state["it"] += 1
        pso = kb.bank(4 + it % 2)
        psd = kb.bank(6 + it % 2)
        den = dens[it % 2]
        for bi, (kind, j, mi) in enumerate(kblocks):
            pss = kb.ps()
            if kind == "l":
                lk = k_sb[:, g, j * 128:(j + 1) * 128]
                lv = v_sb[:, j, g * 128:(g + 1) * 128]
                kt, vt = k_sb, v_sb
            else:
                lk = ck_sb[:, g, j * 128:(j + 1) * 128]
                lv = cv_sb[:, j, g * 128:(g + 1) * 128]
                kt, vt = ck_sb, cv_sb
            qv = q_sb[:, 4 * g:4 * g + 4, qc0:qc0 + nq]
            kb.op("pe", lambda e, pss=pss, lk=lk, qv=qv: e.matmul(pss[:, :N], lk, qv, start=True, stop=True), reads=[kt, q_sb], writes=[pss])
            pt = pts[state["npt"] % len(pts)]
            state["npt"] += 1
            yield
            kb.op("act", lambda e, pt=pt, pss=pss: e.activation(pt[:, :N], pss[:, :N], AF.Exp, scale=scale), reads=[pss], writes=[pt])
            yield
            if mi is not None:
                kb.op("dve", lambda e, pt=pt, mi=mi: e.tensor_tensor(pt[:, :4 * nq].rearrange("p (h q) -> p h q", h=4),
                                                                   pt[:, :4 * nq].rearrange("p (h q) -> p h q", h=4),
                                                                   masks[:, mi, :].unsqueeze(1).to_broadcast([128, 4, 128]), ALU.mult),
                      reads=[pt, masks], writes=[pt])
                yield
            first, last = (bi == 0), (bi == len(kblocks) - 1)
            kb.op("pe", lambda e, lv=lv, pt=pt, first=first, last=last: e.matmul(pso[:, :N], lv, pt[:, :N], start=first, stop=last),
                  reads=[vt, pt], writes=[pso], acc=not first)
            kb.op("pe", lambda e, pt=pt, first=first, last=last: e.matmul(psd[:, :N], ones[:], pt[:, :N], start=first, stop=last),
                  reads=[ones, pt], writes=[psd], acc=not first)
            yield
        kb.op("dve", lambda e: e.tensor_tensor(den[:, :N].rearrange("p (h q) -> p h q", h=4),
                                               psd[:, :N].rearrange("p (h q) -> p h q", h=4),
                                               esink[:, 4 * g:4 * g + 4].unsqueeze(2).to_broadcast([128, 4, nq]), ALU.add),
              reads=[psd, esink], writes=[den])
        yield
        kb.op("dve", lambda e: e.reciprocal(den[:, :N], den[:, :N]), reads=[den], writes=[den])
        yield
        kb.op("dve", lambda e: e.tensor_tensor(o_sb[:, 4 * g:4 * g + 4, qc0:qc0 + nq],
                                               pso[:, :N].rearrange("p (h q) -> p h q", h=4),
                                               den[:, :N].rearrange("p (h q) -> p h q", h=4), ALU.mult),
              reads=[pso, den], writes=[o_sb])
        yield

    work = [(g, qb) for g in range(4) for qb in range(9)]
    for i in range(0, len(work), 2):
        gens = [iter_gen(*w) for w in work[i:i + 2]]
        while gens:
            for gen in list(gens):
                try:
                    next(gen)
                except StopIteration:
                    gens.remove(gen)
    for g4 in range(4):
        kb.store(oT, oT[512 * g4:512 * (g4 + 1), :].rearrange("(h p) t -> p h t", p=128), o_sb, o_sb[:, 4 * g4:4 * g4 + 4, :])
    kb.finish()
    return kb


def swa_masks(r):
    k = np.arange(128)[:, None]
    q = np.arange(128)[None, :]
    mlo = (k >= q).astype(np.float32)
    mhi = (k <= q).astype(np.float32)
    z = np.zeros_like(mlo)
    return np.ascontiguousarray(np.stack([mlo, mhi, z if r == 0 else mlo, z if r == NCORES - 1 else mhi], axis=1))


def bf(a):
    return np.ascontiguousarray(a, dtype=ml_dtypes.bfloat16)


def run_layer_swa(inp, mod, x2d, ctx2d, progs):
    li = 0
    kb = progs["proj_swa"]
    w_t = tile_weight(inp["swa_w_in"][0])
    maps = []
    for r in range(NCORES):
        m = {"xT": xT_input(x2d, ctx2d, r), "modv": modv_input(inp["norm_g"][li], mod[li, 0], mod[li, 1]),
             "gains": col2(inp["swa_q_g"][0], inp["swa_k_g"][0]), "w_in": w_t}
        m.update(proj_const_inputs(r))
        maps.append(m)
    res = run_prog(kb, maps)
    qT = [np.asarray(res[r]["qT"]) for r in range(NCORES)]
    kT = [np.asarray(res[r]["kT"]) for r in range(NCORES)]
    vT = [np.asarray(res[r]["vT"]) for r in range(NCORES)]
    gT = [np.asarray(res[r]["gT"]) for r in range(NCORES)]
    z = np.zeros((512, 128), kT[0].dtype)
    kfull = np.concatenate([z] + [k[:, :1024] for k in kT] + [z], axis=1)
    vfull = np.concatenate([z] + [v[:, :1024] for v in vT] + [z], axis=1)
    ckT = np.ascontiguousarray(np.concatenate([k[:, 1024:] for k in kT], axis=1))
    cvL = np.ascontiguousarray(np.concatenate([v[:, 1024:] for v in vT], axis=1).T)
    sinkb = np.ascontiguousarray(np.broadcast_to(inp["swa_sink"][0][None, :], (128, 16)), dtype=np.float32)
    maps = []
    for r in range(NCORES):
        sl = slice(1024 * r, 1024 * r + 1280)
        maps.append({"qT": qT[r], "kL": np.ascontiguousarray(kfull[:, sl]), "vL": np.ascontiguousarray(vfull[:, sl].T),
                     "ckT": ckT, "cvL": cvL, "masks": swa_masks(r), "sinkb": sinkb})
    res = run_prog(progs["swa_att"], maps)
    oT = [np.asarray(res[r]["oT"]) for r in range(NCORES)]
    xT = [xT_input(x2d, ctx2d, r) for r in range(NCORES)]
    xo = run_tail(progs["tail"], oT, gT, xT, mod[li, 0], mod[li, 1], inp["swa_w_out"][0])
    return split_xo(xo)


NTOK = 8192 + 256


def build_proj_mla():
    kb = KB()
    pc = ProjCtx(kb, 1024, 0)
    T_ = pc.ttot
    xT = kb.inp("xT", [2048, T_])
    modv = kb.const_load("modv", [128, 5, 16])
    gains = kb.const_load("gains", [128, 10])
    C = proj_consts(kb, need64=True, rope=False)
    w_in = kb.inp("w_in", [12, 128, 16, 256])
    w_qb = kb.inp("w_qb", [12, 128, 4, 256])
    w_kvb = kb.inp("w_kvb", [16, 128, 2, 256])
    qnT = kb.out("qnT", [2048, T_], BF16)
    qpeT = kb.out("qpeT", [1024, T_], BF16)
    knT = kb.out("knT", [2048, T_], BF16)
    vT = kb.out("vT", [2048, T_], BF16)
    kpeT = kb.out("kpeT", [128, T_], BF16)
    gT = kb.out("gT", [2048, T_], BF16)
    hT = emit_modnorm(pc, xT, modv)
    cq_raw = kb.sb("cq_raw", [128, 4, T_], F32)
    ckv_raw = kb.sb("ckv_raw", [128, 2, T_], F32)
    blocks = ([dict(kind="keep", dst=cq_raw, idx=i) for i in range(4)] + [dict(kind="keep", dst=ckv_raw, idx=i) for i in range(2)]
              + hn_blocks(1, 64, gains, 9, True, kpeT) + raw_blocks(16, gT, dt=BF16))
    ws1 = WStream(kb, "w1", 16, 256)
    emit_proj_stage(pc, hT, 16, w_in, blocks, ws1, C)
    cqn = kb.sb("cqn", [128, 4, T_], BF16)
    ckvn = kb.sb("ckvn", [128, 2, T_], BF16)
    emit_widenorm(pc, cq_raw, 4, gains, cqn, C, gcol0=0)
    emit_widenorm(pc, ckv_raw, 2, gains, ckvn, C, gcol0=4)
    ws2 = WStream(kb, "w2", 4, 256)
    emit_proj_stage(pc, cqn, 4, w_qb, hn_blocks(16, 128, gains, 6, False, qnT) + hn_blocks(8, 64, gains, 7, True, qpeT), ws2, C)
    ws3 = WStream(kb, "w3", 2, 256)
    emit_proj_stage(pc, ckvn, 2, w_kvb, hn_blocks(16, 128, gains, 8, False, knT) + raw_blocks(16, vT, dt=BF16), ws3, C)
    kb.finish()
    return kb


def mla_weight_layouts(inp):
    w_in = inp["mla_w_in"][0]
    w_in_re = np.concatenate([w_in[:, 0:768], w_in[:, 768:832], w_in[:, 768:832], w_in[:, 832:]], axis=1)
    wq = inp["mla_w_qb"][0].reshape(512, 16, 192)
    w_qb_re = np.concatenate([wq[:, :, :128].reshape(512, 2048), wq[:, :, 128:].reshape(512, 1024)], axis=1)
    wkv = inp["mla_w_kvb"][0].reshape(256, 16, 256)
    w_kvb_re = np.concatenate([wkv[:, :, :128].reshape(256, 2048), wkv[:, :, 128:].reshape(256, 2048)], axis=1)
    g = np.zeros((128, 10), np.float32)
    g[:, 0:4] = inp["mla_qa_g"][0].reshape(4, 128).T
    g[:, 4:6] = inp["mla_kva_g"][0].reshape(2, 128).T
    g[:, 6] = inp["mla_qn_nope_g"][0]
    g[:, 7] = np.tile(inp["mla_qn_pe_g"][0], 2)
    g[:, 8] = inp["mla_kn_nope_g"][0]
    g[:, 9] = np.tile(inp["mla_kn_pe_g"][0], 2)
    return (np.ascontiguousarray(w_in_re, dtype=np.float32), np.ascontiguousarray(w_qb_re, dtype=np.float32),
            np.ascontiguousarray(w_kvb_re, dtype=np.float32), g)


def gather_tokens(per_core, rows=None):
    return np.concatenate([a[:, :1024] for a in per_core] + [a[:, 1024:1024 + TC] for a in per_core], axis=1)


def scatter_tokens(full, r):
    return np.ascontiguousarray(np.concatenate([full[:, 1024 * r:1024 * (r + 1)], full[:, 8192 + TC * r:8192 + TC * (r + 1)]], axis=1))


def build_mla_att():
    kb = KB()
    qn = kb.inp("qn", [2, 128, NTOK], BF16)
    qpe = kb.inp("qpe", [128, NTOK], BF16)
    kn = kb.inp("kn", [2, 128, NTOK], BF16)
    kpe = kb.inp("kpe", [128, NTOK], BF16)
    v = kb.inp("v", [2, 128, 66, 128], BF16)
    oT = kb.out("oT", [2, 128, NTOK], BF16)
    qn_sb = [kb.sb(f"qn{i}", [128, NTOK], BF16) for i in range(2)]
    kn_sb = [kb.sb(f"kn{i}", [128, NTOK], BF16) for i in range(2)]
    v_sb = [kb.sb(f"v{i}", [128, 66, 128], BF16) for i in range(2)]
    qpe_sb = kb.sb("qpe", [128, NTOK], BF16)
    kpe_sb = kb.sb("kpe", [128, NTOK], BF16)
    o_sb = [kb.sb(f"o{i}", [128, NTOK], BF16) for i in range(2)]
    H = NTOK // 2
    for i in range(2):
        for hf in range(2):
            kb.load(kn_sb[i], kn_sb[i][:, hf * H:(hf + 1) * H], kn[i, :, hf * H:(hf + 1) * H], q="sp")
            kb.load(qn_sb[i], qn_sb[i][:, hf * H:(hf + 1) * H], qn[i, :, hf * H:(hf + 1) * H], q="pool")
            kb.load(v_sb[i], v_sb[i][:, 33 * hf:33 * (hf + 1), :], v[i, :, 33 * hf:33 * (hf + 1), :], q="sp")
        if i == 0:
            for hf in range(2):
                kb.load(kpe_sb, kpe_sb[:, hf * H:(hf + 1) * H], kpe[:, hf * H:(hf + 1) * H], q="pool")
                kb.load(qpe_sb, qpe_sb[:, hf * H:(hf + 1) * H], qpe[:, hf * H:(hf + 1) * H], q="pool")
    ones = kb.sb("onesb", [128, 128], BF16)
    kb.op("pool", lambda e: e.memset(ones[:], 1.0), writes=[ones])
    kb._pool = [0, 1, 2, 3]
    pts = [kb.sb(f"pt{i}", [128, 512], BF16) for i in range(4)]
    recs = [kb.sb(f"rec{i}", [128, 512], F32) for i in range(2)]
    paccs = [kb.sb(f"pacc{i}", [128, 512], F32) for i in range(2)]
    ones32 = kb.sb("ones32", [128, 128], F32)
    kb.op("pool", lambda e: e.memset(ones32[:], 1.0), writes=[ones32])
    scale = 192.0 ** -0.5
    it = 0
    npt = 0
    qtiles = [(512 * i, 512, list(range(66))) for i in range(16)] + [(8192, 256, [64, 65])]
    hmask = kb.const_load("hmask", [128, 2])
    qpm = kb.sb("qpm", [128, NTOK], BF16)
    for hh in range(2):
        K, Q, V, O = kn_sb[hh], qn_sb[hh], v_sb[hh], o_sb[hh]
        for hf in range(2):
            kb.op("dve", lambda e, hh=hh, hf=hf: e.tensor_scalar(qpm[:, hf * H:(hf + 1) * H], qpe_sb[:, hf * H:(hf + 1) * H], hmask[:, hh:hh + 1], None, ALU.mult),
                  reads=[qpe_sb, hmask], writes=[qpm])
        for (q0, N, blocks) in qtiles:
            pso = kb.bank(4 + it % 2)
            psd = kb.bank(6 + it % 2)
            rec = recs[it % 2]
            pacc = paccs[it % 2]
            it += 1
            def emit_s(j, N=N, K=K, Q=Q, q0=q0):
                pss = kb.ps()
                kb.op("pe", lambda e, pss=pss, j=j: e.matmul(pss[:, :N], K[:, j * 128:(j + 1) * 128], Q[:, q0:q0 + N], start=True, stop=False),
                      reads=[K, Q], writes=[pss])
                kb.op("pe", lambda e, pss=pss, j=j: e.matmul(pss[:, :N], kpe_sb[:, j * 128:(j + 1) * 128], qpm[:, q0:q0 + N], start=False, stop=True),
                      reads=[kpe_sb, qpm], writes=[pss], acc=True)
                return pss
            pendq = [emit_s(blocks[0])]
            if len(blocks) > 1:
                pendq.append(emit_s(blocks[1]))
            for bi, j in enumerate(blocks):
                pss = pendq.pop(0)
                pt = pts[npt % 4]
                npt += 1
                kb.op("act", lambda e, pt=pt, pss=pss, N=N: e.activation(pt[:, :N], pss[:, :N], AF.Exp, scale=scale), reads=[pss], writes=[pt])
                if bi + 2 < len(blocks):
                    pendq.append(emit_s(blocks[bi + 2]))
                first, last = (bi == 0), (bi == len(blocks) - 1)
                kb.op("pe", lambda e, pt=pt, j=j, first=first, last=last, N=N, V=V, pso=pso: e.matmul(pso[:, :N], V[:, j, :], pt[:, :N], start=first, stop=last),
                      reads=[V, pt], writes=[pso], acc=not first)
                if first:
                    kb.op("dve", lambda e, pt=pt, N=N, pacc=pacc: e.tensor_copy(pacc[:, :N], pt[:, :N]), reads=[pt], writes=[pacc])
                else:
                    kb.op("dve", lambda e, pt=pt, N=N, pacc=pacc: e.tensor_tensor(pacc[:, :N], pacc[:, :N], pt[:, :N], ALU.add), reads=[pt, pacc], writes=[pacc])
            kb.op("pe", lambda e, N=N, psd=psd, pacc=pacc: e.matmul(psd[:, :N], ones32[:], pacc[:, :N], start=True, stop=True), reads=[ones32, pacc], writes=[psd])
            kb.op("dve", lambda e, rec=rec, psd=psd, N=N: e.reciprocal(rec[:, :N], psd[:, :N]), reads=[psd], writes=[rec])
            kb.op("dve", lambda e, rec=rec, pso=pso, N=N, O=O, q0=q0: e.tensor_tensor(O[:, q0:q0 + N], pso[:, :N], rec[:, :N], ALU.mult), reads=[pso, rec], writes=[O])
        for hf in range(2):
            kb.store(oT, oT[hh, :, hf * H:(hf + 1) * H], O, O[:, hf * H:(hf + 1) * H], q="sp")
    kb.finish()
    return kb


def run_layer_mla(inp, mod, x2d, ctx2d, progs):
    li = 1
    w_in_re, w_qb_re, w_kvb_re, g = mla_weight_layouts(inp)
    w_in_re, w_qb_re, w_kvb_re = tile_weight(w_in_re), tile_weight(w_qb_re), tile_weight(w_kvb_re)
    maps = []
    for r in range(NCORES):
        m = {"xT": xT_input(x2d, ctx2d, r), "modv": modv_input(inp["norm_g"][li], mod[li, 0], mod[li, 1]),
             "gains": g, "w_in": w_in_re, "w_qb": w_qb_re, "w_kvb": w_kvb_re}
        m.update(proj_const_inputs(r, need64=True, rope=False))
        maps.append(m)
    res = run_prog(progs["proj_mla"], maps)
    names = ("qnT", "qpeT", "knT", "vT", "kpeT", "gT")
    full = {n: gather_tokens([np.asarray(res[r][n]) for r in range(NCORES)]) for n in names}
    return run_layer_mla_b(inp, mod, x2d, ctx2d, progs, full)


def run_layer_mla_b(inp, mod, x2d, ctx2d, progs, full):
    li = 1
    gT = [scatter_tokens(full["gT"], r) for r in range(NCORES)]
    maps = []
    for r in range(NCORES):
        hs = slice(256 * r, 256 * (r + 1))
        maps.append({"qn": np.ascontiguousarray(full["qnT"][hs].reshape(2, 128, NTOK)),
                     "qpe": np.ascontiguousarray(full["qpeT"][128 * r:128 * (r + 1)]),
                     "kn": np.ascontiguousarray(full["knT"][hs].reshape(2, 128, NTOK)),
                     "kpe": np.ascontiguousarray(full["kpeT"]),
                     "hmask": np.ascontiguousarray(np.stack([(np.arange(128) < 64), (np.arange(128) >= 64)], axis=1), dtype=np.float32),
                     "v": np.ascontiguousarray(full["vT"][hs].reshape(2, 128, 66, 128).transpose(0, 3, 2, 1))})
    res = run_prog(progs["mla_att"], maps)
    ofull = np.concatenate([np.asarray(res[r]["oT"]).reshape(256, NTOK) for r in range(NCORES)], axis=0)
    oT = [scatter_tokens(ofull, r) for r in range(NCORES)]
    xT = [xT_input(x2d, ctx2d, r) for r in range(NCORES)]
    xo = run_tail(progs["tail"], oT, gT, xT, mod[li, 0], mod[li, 1], inp["mla_w_out"][0])
    return split_xo(xo)


def build_diff_att(lam_init):
    kb = KB()
    q = kb.inp("q", [2, 128, NTOK], BF16)
    k = kb.inp("k", [2, 128, NTOK], BF16)
    v = kb.inp("v", [128, 66, 256], BF16)
    oT = kb.out("oT", [2, 128, NTOK], BF16)
    lqk = kb.const_load("lqk", [128, 4])
    sg = kb.const_load("subg", [128, 2])
    q_sb = [kb.sb(f"q{i}", [128, NTOK], BF16) for i in range(2)]
    k_sb = [kb.sb(f"k{i}", [128, NTOK], BF16) for i in range(2)]
    v_sb = kb.sb("v", [128, 66, 256], BF16)
    o_sb = kb.sb("o", [128, 2, NTOK], BF16)
    H = NTOK // 2
    for i in range(2):
        for hf in range(2):
            kb.load(k_sb[i], k_sb[i][:, hf * H:(hf + 1) * H], k[i, :, hf * H:(hf + 1) * H], q="sp")
            kb.load(q_sb[i], q_sb[i][:, hf * H:(hf + 1) * H], q[i, :, hf * H:(hf + 1) * H], q="pool")
    for hf in range(2):
        kb.load(v_sb, v_sb[:, 33 * hf:33 * (hf + 1), :], v[:, 33 * hf:33 * (hf + 1), :], q="sp")
    ones = kb.sb("onesb", [128, 128], BF16)
    kb.op("pool", lambda e: e.memset(ones[:], 1.0), writes=[ones])
    ones32 = kb.sb("ones32", [128, 128], F32)
    kb.op("pool", lambda e: e.memset(ones32[:], 1.0), writes=[ones32])
    prod = kb.sb("prod", [128, 2], F32)
    kb.op("dve", lambda e: e.tensor_tensor(prod[:, 0:1], lqk[:, 0:1], lqk[:, 1:2], ALU.mult), reads=[lqk], writes=[prod])
    kb.op("dve", lambda e: e.tensor_tensor(prod[:, 1:2], lqk[:, 2:3], lqk[:, 3:4], ALU.mult), reads=[lqk], writes=[prod])
    psl = kb.bank(0)
    kb.op("pe", lambda e: e.matmul(psl[:, 0:2], ones32[:], prod[:], start=True, stop=True), reads=[ones32, prod], writes=[psl])
    el = kb.sb("el", [128, 2], F32)
    kb.op("act", lambda e: e.activation(el[:], psl[:, 0:2], AF.Exp), reads=[psl], writes=[el])
    lam = kb.sb("lam", [128, 1], F32)
    kb.op("dve", lambda e: e.tensor_tensor(lam[:], el[:, 0:1], el[:, 1:2], ALU.subtract), reads=[el], writes=[lam])
    kb.op("dve", lambda e: e.tensor_scalar(lam[:], lam[:], float(lam_init), None, ALU.add), reads=[lam], writes=[lam])
    kb.op("dve", lambda e: e.tensor_scalar(sg[:], sg[:], float(1.0 - lam_init), None, ALU.mult), reads=[sg], writes=[sg])
    kb._pool = [0, 1, 7]
    pts = [kb.sb(f"pt{i}", [128, 512], BF16) for i in range(4)]
    recs = [kb.sb(f"rec{i}", [128, 512], F32) for i in range(2)]
    paccs = [kb.sb(f"pacc{i}", [128, 512], F32) for i in range(2)]
    ods = [kb.sb(f"od{i}", [128, 2, 256], F32) for i in range(2)]
    t2s = [kb.sb(f"t2{i}", [128, 256], F32) for i in range(2)]
    sqs = [kb.sb(f"sqd{i}", [128, 256], BF16) for i in range(2)]
    rss = [kb.sb(f"rs{i}", [128, 256], F32) for i in range(2)]
    scale = 128.0 ** -0.5
    NQ = 256
    qtiles = [(NQ * i, list(range(66))) for i in range(8192 // NQ)] + [(8192, [64, 65])]
    it = 0
    npt = 0
    for (q0, blocks) in qtiles:
        pso1 = kb.bank(2 + it % 2)
        pso2 = kb.bank(4 + it % 2)
        psd = kb.bank(6)
        rec, od, t2, rs = recs[it % 2], ods[it % 2], t2s[it % 2], rss[it % 2]
        pacc = paccs[it % 2]
        it += 1

        def emit_s(j, q0=q0):
            pss = kb.ps()
            kb.op("pe", lambda e, pss=pss, j=j: e.matmul(pss[:, 0:NQ], k_sb[0][:, j * 128:(j + 1) * 128], q_sb[0][:, q0:q0 + NQ], start=True, stop=True),
                  reads=[k_sb[0], q_sb[0]], writes=[pss])
            kb.op("pe", lambda e, pss=pss, j=j: e.matmul(pss[:, NQ:2 * NQ], k_sb[1][:, j * 128:(j + 1) * 128], q_sb[1][:, q0:q0 + NQ], start=True, stop=True),
                  reads=[k_sb[1], q_sb[1]], writes=[pss], acc=True)
            return pss
        pendq = [emit_s(blocks[0])]
        if len(blocks) > 1:
            pendq.append(emit_s(blocks[1]))
        for bi, j in enumerate(blocks):
            pss = pendq.pop(0)
            pt = pts[npt % 4]
            npt += 1
            kb.op("act", lambda e, pt=pt, pss=pss: e.activation(pt[:, :], pss[:, :], AF.Exp, scale=scale), reads=[pss], writes=[pt])
            if bi + 2 < len(blocks):
                pendq.append(emit_s(blocks[bi + 2]))
            first, last = (bi == 0), (bi == len(blocks) - 1)
            for mi, pso in enumerate((pso1, pso2)):
                for c in range(2):
                    kb.op("pe", lambda e, pt=pt, j=j, first=first, last=last, pso=pso, c=c, mi=mi: e.matmul(pso[:, c * NQ:(c + 1) * NQ], v_sb[:, j, c * 128:(c + 1) * 128],
                                                                                                 pt[:, mi * NQ:(mi + 1) * NQ], start=first, stop=last),
                          reads=[v_sb, pt], writes=[pso], acc=not (first and c == 0))
            if first:
                kb.op("dve", lambda e, pt=pt, pacc=pacc: e.tensor_copy(pacc[:, :], pt[:, :]), reads=[pt], writes=[pacc])
            else:
                kb.op("dve", lambda e, pt=pt, pacc=pacc: e.tensor_tensor(pacc[:, :], pacc[:, :], pt[:, :], ALU.add), reads=[pt, pacc], writes=[pacc])
        kb.op("pe", lambda e, psd=psd, pacc=pacc: e.matmul(psd[:, :], ones32[:], pacc[:, :], start=True, stop=True), reads=[ones32, pacc], writes=[psd])
        kb.op("dve", lambda e, rec=rec, psd=psd: e.reciprocal(rec[:, :], psd[:, :]), reads=[psd], writes=[rec])
        kb.op("dve", lambda e, rec=rec: e.tensor_scalar(rec[:, NQ:2 * NQ], rec[:, NQ:2 * NQ], lam[:, 0:1], None, ALU.mult), reads=[rec, lam], writes=[rec])
        for c in range(2):
            kb.op("dve", lambda e, rec=rec, od=od, pso1=pso1, c=c: e.tensor_tensor(od[:, c, :], pso1[:, c * NQ:(c + 1) * NQ], rec[:, 0:NQ], ALU.mult),
                  reads=[pso1, rec], writes=[od])
            kb.op("dve", lambda e, rec=rec, t2=t2, pso2=pso2, c=c: e.tensor_tensor(t2[:, :], pso2[:, c * NQ:(c + 1) * NQ], rec[:, NQ:2 * NQ], ALU.mult),
                  reads=[pso2, rec], writes=[t2])
            kb.op("dve", lambda e, od=od, t2=t2, c=c: e.tensor_tensor(od[:, c, :], od[:, c, :], t2[:, :], ALU.subtract), reads=[od, t2], writes=[od])
        pq = kb.ps()
        for c in range(2):
            sq = sqs[c]
            kb.op("act", lambda e, sq=sq, od=od, c=c: e.activation(sq[:, :], od[:, c, :], AF.Square), reads=[od], writes=[sq])
            kb.op("pe", lambda e, pq=pq, sq=sq, c=c: e.matmul(pq[:, 0:NQ], ones[:], sq[:, :], start=(c == 0), stop=(c == 1)), reads=[ones, sq], writes=[pq], acc=(c == 1))
        kb.rstd_from_ss(rs, rs[:, :], pq, pq[:, 0:NQ], 1.0 / 256)
        for c in range(2):
            kb.op("dve", lambda e, od=od, rs=rs, c=c, q0=q0: e.scalar_tensor_tensor(o_sb[:, c, q0:q0 + NQ], od[:, c, :], sg[:, c:c + 1], rs[:, :], ALU.mult, ALU.mult),
                  reads=[od, sg, rs], writes=[o_sb])
    for c in range(2):
        for hf in range(2):
            kb.store(oT, oT[c, :, hf * H:(hf + 1) * H], o_sb, o_sb[:, c, hf * H:(hf + 1) * H], q="sp")
    kb.finish()
    return kb


def run_layer_diff(inp, mod, x2d, ctx2d, progs):
    li = 3
    w_t = tile_weight(inp["diff_w_in"][0])
    maps = []
    for r in range(NCORES):
        m = {"xT": xT_input(x2d, ctx2d, r), "modv": modv_input(inp["norm_g"][li], mod[li, 0], mod[li, 1]),
             "gains": col2(inp["diff_q_g"][0], inp["diff_k_g"][0]), "w_in": w_t}
        m.update(proj_const_inputs(r))
        maps.append(m)
    res = run_prog(progs["proj_diff"], maps)
    full = {n: gather_tokens([np.asarray(res[r][n]) for r in range(NCORES)]) for n in ("qT", "kT", "vT", "gT")}
    return run_layer_diff_b(inp, mod, x2d, ctx2d, progs, full)


def run_layer_diff_b(inp, mod, x2d, ctx2d, progs, full):
    li = 3
    gT = [scatter_tokens(full["gT"], r) for r in range(NCORES)]
    lqk = np.ascontiguousarray(np.stack([inp["diff_lq1"][0], inp["diff_lk1"][0], inp["diff_lq2"][0], inp["diff_lk2"][0]], axis=1), dtype=np.float32)
    subg = np.ascontiguousarray(inp["diff_subln_g"][0].reshape(2, 128).T, dtype=np.float32)
    maps = []
    for r in range(NCORES):
        hs = slice(256 * r, 256 * (r + 1))
        maps.append({"q": np.ascontiguousarray(full["qT"][hs].reshape(2, 128, NTOK)),
                     "k": np.ascontiguousarray(full["kT"][hs].reshape(2, 128, NTOK)),
                     "v": np.ascontiguousarray(full["vT"][hs].reshape(256, 66, 128).transpose(2, 1, 0)),
                     "lqk": lqk, "subg": subg})
    res = run_prog(progs["diff_att"], maps)
    ofull = np.concatenate([np.asarray(res[r]["oT"]).reshape(256, NTOK) for r in range(NCORES)], axis=0)
    oT = [scatter_tokens(ofull, r) for r in range(NCORES)]
    xT = [xT_input(x2d, ctx2d, r) for r in range(NCORES)]
    xo = run_tail(progs["tail"], oT, gT, xT, mod[li, 0], mod[li, 1], inp["diff_w_out"][0])
    return split_xo(xo)


def build_proj_hyena():
    kb = KB()
    pc = ProjCtx(kb, 1026, 1)
    T_ = pc.ttot
    TO = 1024 + TC
    xT = kb.inp("xT", [2048, T_])
    modv = kb.const_load("modv", [128, 5, 16])
    cw = kb.const_load("cw", [128, 48, 4])
    hm = kb.const_load("hm", [128, 2])
    w = kb.inp("w_in", [32, 128, 16, 256])
    uT = kb.out("uT", [6144, TO], BF16)
    gT = kb.out("gT", [2048, TO], BF16)
    hT = emit_modnorm(pc, xT, modv)
    blocks = ([dict(kind="conv3", cw=cw, ci=i, out=uT, r0=128 * i, dt=BF16) for i in range(48)] + raw_blocks(16, gT, dt=BF16))
    ws = WStream(kb, "w", 16, 256)
    emit_proj_stage(pc, hT, 16, w, blocks, ws, {"hm": hm})
    kb.finish()
    return kb


LH = 8192
NF = 2 * LH
GC = 4
CPC = 256


def hyena_consts(L):
    n = np.arange(2 * L)
    pos = np.where(n < L, n, 2 * L - n).astype(np.float64)
    pos[L] = 0
    t = pos / max(L - 1, 1)
    bands = np.linspace(1e-4, 15, 16)
    ang = (2.0 * math.pi / L) * pos[:, None] * bands[None, :]
    z = np.concatenate([t[:, None], np.cos(ang), -np.sin(ang)], axis=-1)
    return np.ascontiguousarray(z.T, dtype=np.float32), t.astype(np.float32)


def hyena_deltas():
    max_decay = math.log(1e-2) / 0.3
    min_decay = math.log(1e-2) / 1.5
    return np.abs(np.linspace(min_decay, max_decay, 2048)).astype(np.float32)


def emit_sin(kb, tmp, out_ap, out_t, arg_ap, arg_t, shape_ap):
    s, c, t = tmp("sn_s"), tmp("sn_c"), tmp("sn_t")
    sa, ca, ta = shape_ap(s), shape_ap(c), shape_ap(t)
    kb.op("act", lambda e: e.activation(sa, arg_ap, AF.Sin, scale=0.125), reads=[arg_t], writes=[s])
    hp = shape_ap(kb.halfpi_t)
    kb.op("act", lambda e: e.activation(ca, arg_ap, AF.Sin, scale=0.125, bias=hp), reads=[arg_t, kb.halfpi_t], writes=[c])
    for k in range(3):
        last = (k == 2)
        if not last:
            kb.op("dve", lambda e: e.tensor_tensor(ta, sa, sa, ALU.mult), reads=[s], writes=[t])
        dst = out_ap if last else sa
        dst_t = out_t if last else s
        kb.op("dve", lambda e, dst=dst: e.scalar_tensor_tensor(dst, sa, 2.0, ca, ALU.mult, ALU.mult), reads=[s, c], writes=[dst_t])
        if not last:
            kb.op("dve", lambda e: e.tensor_scalar(ca, ta, -2.0, 1.0, ALU.mult, ALU.add), reads=[t], writes=[c])


def build_hyena_core(with_ctx=True, debug=False, ngroups_override=None, skip_b=False):
    kb = KB()
    tmps = {}

    def tmp(name, shape, dt=F32, n=2):
        if name not in tmps:
            tmps[name] = [[kb.sb(f"{name}{i}", shape, dt) for i in range(n)], 0]
        lst = tmps[name]
        t = lst[0][lst[1] % n]
        lst[1] += 1
        return t

    sig_in = [kb.inp(nm, [64, CPC, 128], BF16) for nm in ("v", "x1", "x2")]
    zout = kb.out("z", [64, CPC, 128], BF16)
    ZT = kb.inp("ZT", [33, NF])
    tprow = kb.inp("tprow", [1, NF])
    fw1 = kb.const_load("fw1", [33, 64])
    fb1f = kb.const_load("fb1f", [64, 2])
    fw2d = kb.const_load("fw2d", [64, 128])
    fb2f = kb.const_load("fb2f", [128, 2])
    w3s32 = kb.const_load("w3s", [128, 2, CPC])
    w3b = kb.sb("w3b", [128, 2, CPC], BF16)
    kb.op("dve", lambda e: e.tensor_copy(w3b[:], w3s32[:]), reads=[w3s32], writes=[w3b])
    ndelta = kb.const_load("deltac", [128, 2])
    kb.op("dve", lambda e: e.tensor_scalar(ndelta[:], ndelta[:], -1.0, None, ALU.mult), reads=[ndelta], writes=[ndelta])
    skipb = kb.const_load("skipb", [64, 2, CPC])
    halfpi = kb.sb("halfpi", [128, 1], F32)
    kb.op("pool", lambda e: e.memset(halfpi[:], math.pi / 2), writes=[halfpi])
    kb.halfpi, kb.halfpi_t = halfpi, halfpi

    def cbf(name, shape):
        t32 = kb.const_load(name, shape)
        tb = kb.sb(name + "b", shape, BF16)
        kb.op("dve", lambda e: e.tensor_copy(tb[:], t32[:]), reads=[t32], writes=[tb])
        return tb
    Ff = cbf("Ff", [128, 256])
    Fi1 = cbf("Fi1", [128, 256])
    Fi2 = cbf("Fi2", [128, 256])
    Fc = cbf("Fc", [128, 128])
    Fs = cbf("Fs", [128, 128])
    Fsn = cbf("Fsn", [128, 128])
    Tc = kb.const_load("Tc", [128, 128])
    Ts = kb.const_load("Ts", [128, 128])
    kcs = kb.out("kcs", [2, CPC, NF], F32) if debug else kb.scratch("kcs", [2, CPC, NF], F32)

    HT2 = kb.sb("HT2", [128, NF], BF16)
    CH = 2048
    def big(nm):
        if nm in ("kchA", "kchB"):
            lst = tmps["kch"][0]
            return lst[0] if nm == "kchA" else lst[1]
        return tmp(nm, [128, CH], F32, n=1)
    def mlp_block(z_dram_ap, wcols, dst_ap, dst_t):
        zt = tmp("zt", [33, CH], F32, n=1)
        kb.load(zt, zt[:, :wcols], z_dram_ap, q="sp")
        arg = big("arg")
        for sbk in range(wcols // 512):
            ps = kb.ps()
            sc = slice(sbk * 512, (sbk + 1) * 512)
            kb.op("pe", lambda e, ps=ps, zt=zt, sc=sc: e.matmul(ps[0:64, :], fw1[:, :], zt[:, sc], start=True, stop=True), reads=[fw1, zt], writes=[ps])
            kb.op("dve", lambda e, ps=ps, sc=sc, arg=arg: e.tensor_scalar(arg[0:64, sc], ps[0:64, :], fb1f[:, 0:1], fb1f[:, 1:2], ALU.add, ALU.mult),
                  reads=[ps, fb1f], writes=[arg])
        h1 = big("h1")
        emit_sin(kb, big, h1[0:64, :wcols], h1, arg[0:64, :wcols], arg, lambda t: t[0:64, :wcols] if t is not kb.halfpi_t else t[0:64, :])
        arg2 = big("arg")
        for sbk in range(wcols // 512):
            ps = kb.ps()
            sc = slice(sbk * 512, (sbk + 1) * 512)
            kb.op("pe", lambda e, ps=ps, h1=h1, sc=sc: e.matmul(ps[:, :], fw2d[:, :], h1[0:64, sc], start=True, stop=True), reads=[fw2d, h1], writes=[ps])
            kb.op("dve", lambda e, ps=ps, sc=sc, arg2=arg2: e.tensor_scalar(arg2[:, sc], ps[:, :], fb2f[:, 0:1], fb2f[:, 1:2], ALU.add, ALU.mult),
                  reads=[ps, fb2f], writes=[arg2])
        emit_sin(kb, big, dst_ap, dst_t, arg2[:, :wcols], arg2, lambda t: t[:, :wcols] if t is not kb.halfpi_t else t[:, :])

    for ch in range(NF // CH):
        cols = slice(ch * CH, (ch + 1) * CH)
        mlp_block(ZT[:, cols], CH, HT2[:, cols], HT2)
    kb.op("pool", lambda e: e.memset(HT2[0:64, LH:NF], 0.0), writes=[HT2])
    kb.op("pool", lambda e: e.memset(HT2[64:128, 0:LH + 1], 0.0), writes=[HT2])

    if debug:
        dH = kb.out("dHT2", [128, NF], BF16)
        kb.store(dH, dH[:], HT2, HT2[:])
    ssp = kb.sb("ssp", [128, 4, 8], F32)
    rc = kb.sb("rc", [128, 4], F32)
    for o in range(0 if not skip_b else 2, 2):
        for cb in range(2):
            oc = 2 * o + cb
            for ch in range(NF // CH):
                cols = slice(ch * CH, (ch + 1) * CH)
                tpb = big("sn_s")
                kb.load(tpb, tpb[:], tprow[0:1, cols].partition_broadcast(128), q="sp")
                dec = big("sn_c")
                kb.op("act", lambda e, dec=dec, tpb=tpb, cb=cb: e.activation(dec[:], tpb[:], AF.Exp, scale=ndelta[:, cb:cb + 1]), reads=[tpb, ndelta], writes=[dec])
                kch = tmp("kch", [128, CH], F32, n=2)
                for sbk in range(4):
                    ps = kb.ps()
                    sc = slice(sbk * 512, (sbk + 1) * 512)
                    gc = slice(ch * CH + sbk * 512, ch * CH + (sbk + 1) * 512)
                    kb.op("pe", lambda e, ps=ps, o=o, cb=cb, gc=gc: e.matmul(ps[:, :], w3b[:, o, cb * 128:(cb + 1) * 128], HT2[:, gc], start=True, stop=True),
                          reads=[w3b, HT2], writes=[ps])
                    kb.op("dve", lambda e, ps=ps, sc=sc, kch=kch, dec=dec: e.tensor_tensor(kch[:, sc], ps[:, :], dec[:, sc], ALU.mult), reads=[ps, dec], writes=[kch])
                sq = big("sn_t")
                kb.op("act", lambda e, sq=sq, kch=kch: e.activation(sq[:], kch[:], AF.Square), reads=[kch], writes=[sq])
                kb.op("dve", lambda e, sq=sq, oc=oc, ch=ch: e.reduce_sum(ssp[:, oc, ch:ch + 1], sq[:], AX.X), reads=[sq], writes=[ssp])
                kb.store(kcs, kcs[o, cb * 128:(cb + 1) * 128, cols], kch, kch[:], q="pool", is_output=False)
            kb.op("dve", lambda e, oc=oc: e.reduce_sum(rc[:, oc:oc + 1], ssp[:, oc, :], AX.X), reads=[ssp], writes=[rc])
            kb.rstd_from_ss(rc, rc[:, oc:oc + 1], rc, rc[:, oc:oc + 1], 1.0)
            for ch in range(NF // CH):
                cols = slice(ch * CH, (ch + 1) * CH)
                k2 = tmp("kch", [128, CH], F32, n=2)
                kb.load(k2, k2[:], kcs[o, cb * 128:(cb + 1) * 128, cols], q="sp", src=kcs)
                kb.op("dve", lambda e, k2=k2, oc=oc: e.tensor_scalar(k2[:], k2[:], rc[:, oc:oc + 1], None, ALU.mult), reads=[k2, rc], writes=[k2])
                kb.store(kcs, kcs[o, cb * 128:(cb + 1) * 128, cols], k2, k2[:], q="pool", is_output=False)


    if with_ctx:
        LC = 256
        uc = kb.inp("uc", [3, 2, 128, LC], BF16)
        cwc = kb.const_load("cwc", [128, 3, 2, 4])
        ZTc = kb.inp("ZTc", [33, 2 * LC])
        tpc = kb.const_load("tpc", [128, 4])
        kb.op("dve", lambda e: e.tensor_scalar(tpc[:], tpc[:], -1.0, None, ALU.mult), reads=[tpc], writes=[tpc])
        deltab = kb.const_load("deltab", [128, CPC])
        skipc = kb.const_load("skipc", [128, 2, CPC])
        ident = kb.const_load("ident", [128, 128])
        ones32 = kb.sb("ones32c", [128, 128], F32)
        kb.op("pool", lambda e: e.memset(ones32[:], 1.0), writes=[ones32])
        Dc_in = kb.inp("Dc", [128, 4, 512])
        Dsn_in = kb.inp("Dsn", [128, 4, 512])
        zc_out = kb.out("zc", [2, 128, CPC], BF16)
        Dcb = kb.sb("Dcb", [128, 4, 512], BF16)
        Dsnb = kb.sb("Dsnb", [128, 4, 512], BF16)
        for src_, dst_ in ((Dc_in, Dcb), (Dsn_in, Dsnb)):
            st_ = big("sn_s")
            kb.load(st_, st_[:].rearrange("p (a b) -> p a b", a=4), src_, q="sp")
            kb.op("dve", lambda e, st_=st_, dst_=dst_: e.tensor_copy(dst_[:].rearrange("p a b -> p (a b)"), st_[:]), reads=[st_], writes=[dst_])
        utT = big("kchA")
        ut = utT[:, 0:1536].rearrange("p (tb si c) -> p tb si c", tb=2, si=3)
        for si in range(3):
            for cb in range(2):
                ub = tmp("ucb", [128, LC], BF16, n=2)
                kb.load(ub, ub[:], uc[si, cb], q="sp")
                u32 = tmp("uc32", [128, LC], F32, n=2)
                kb.op("act", lambda e, u32=u32, ub=ub: e.copy(u32[:], ub[:]), reads=[ub], writes=[u32])
                o32 = tmp("oc32", [128, LC], F32, n=2)
                kb.op("dve", lambda e, o32=o32, u32=u32, si=si, cb=cb: e.tensor_scalar(o32[:], u32[:], cwc[:, si, cb, 1:2], cwc[:, si, cb, 3:4], ALU.mult, ALU.add),
                      reads=[u32, cwc], writes=[o32])
                kb.op("dve", lambda e, o32=o32, u32=u32, si=si, cb=cb: e.scalar_tensor_tensor(o32[:, 1:LC], u32[:, 0:LC - 1], cwc[:, si, cb, 0:1], o32[:, 1:LC], ALU.mult, ALU.add),
                      reads=[u32, cwc, o32], writes=[o32])
                kb.op("dve", lambda e, o32=o32, u32=u32, si=si, cb=cb: e.scalar_tensor_tensor(o32[:, 0:LC - 1], u32[:, 1:LC], cwc[:, si, cb, 2:3], o32[:, 0:LC - 1], ALU.mult, ALU.add),
                      reads=[u32, cwc, o32], writes=[o32])
                for tb in range(2):
                    ps = kb.ps()
                    kb.op("pe", lambda e, ps=ps, o32=o32, tb=tb: e.transpose(ps[:, 0:128], o32[:, tb * 128:(tb + 1) * 128], ident[:, :]), reads=[o32, ident], writes=[ps])
                    kb.op("act", lambda e, ps=ps, tb=tb, si=si, cb=cb: e.copy(ut[:, tb, si, cb * 128:(cb + 1) * 128], ps[:, 0:128]), reads=[ps], writes=[utT])
        HTc = kb.sb("HTc", [128, 2 * LC], BF16)
        mlp_block(ZTc[:, :], 2 * LC, HTc[:, :], HTc)
        kb.op("pool", lambda e: e.memset(HTc[0:64, LC:2 * LC], 0.0), writes=[HTc])
        kb.op("pool", lambda e: e.memset(HTc[64:128, 0:LC + 1], 0.0), writes=[HTc])
        kccT = big("sn_c")
        HcT = [big("sn_t"), big("kchB")]
        kccb = kb.sb("kccb", [128, 2, 4, CPC], BF16)
        for o in range(2):
            kcc = kccT[:, o * 1024:(o + 1) * 1024].rearrange("p (a b) -> p a b", a=4)
            pss = kb.ps()
            for nb in range(4):
                ps = kb.ps()
                kb.op("pe", lambda e, ps=ps, nb=nb, o=o: e.matmul(ps[:, 0:CPC], HTc[:, nb * 128:(nb + 1) * 128], w3b[:, o, :], start=True, stop=True), reads=[HTc, w3b], writes=[ps])
                dec = tmp("decc", [128, CPC], F32, n=2)
                kb.op("act", lambda e, dec=dec, nb=nb: e.activation(dec[:], deltab[:], AF.Exp, scale=tpc[:, nb:nb + 1]), reads=[deltab, tpc], writes=[dec])
                kb.op("dve", lambda e, ps=ps, dec=dec, kcc=kcc, nb=nb: e.tensor_tensor(kcc[:, nb, :], ps[:, 0:CPC], dec[:], ALU.mult), reads=[ps, dec], writes=[kccT])
                sq = tmp("sqc", [128, CPC], F32, n=2)
                kb.op("pool", lambda e, sq=sq, kcc=kcc, nb=nb: e.tensor_tensor(sq[:], kcc[:, nb, :], kcc[:, nb, :], ALU.mult), reads=[kccT], writes=[sq])
                kb.op("pe", lambda e, pss=pss, sq=sq, nb=nb: e.matmul(pss[:, 0:CPC], ones32[:], sq[:], start=(nb == 0), stop=(nb == 3)), reads=[ones32, sq], writes=[pss], acc=(nb != 0))
            rsc = tmp("rsc", [128, CPC], F32, n=2)
            kb.rstd_from_ss(rsc, rsc[:], pss, pss[:, 0:CPC], 1.0)
            for nb in range(4):
                kb.op("dve", lambda e, kcc=kcc, rsc=rsc, nb=nb, o=o: e.tensor_tensor(kccb[:, o, nb, :], kcc[:, nb, :], rsc[:], ALU.mult), reads=[kccT, rsc], writes=[kccb])
            Hc = HcT[o]
            for fb in range(4):
                for ri, Dm in enumerate((Dcb, Dsnb)):
                    ps = kb.ps()
                    for kbk in range(4):
                        kb.op("pe", lambda e, ps=ps, Dm=Dm, kbk=kbk, fb=fb, o=o: e.matmul(ps[:, 0:CPC], Dm[:, kbk, fb * 128:(fb + 1) * 128], kccb[:, o, kbk, :],
                                                                                     start=(kbk == 0), stop=(kbk == 3)),
                              reads=[Dm, kccb], writes=[ps], acc=(kbk != 0))
                    kb.op("act", lambda e, ps=ps, Hc=Hc, ri=ri, fb=fb: e.copy(Hc[:, ri * 1024 + fb * 256: ri * 1024 + (fb + 1) * 256], ps[:, 0:CPC]), reads=[ps], writes=[Hc])
        curb = kb.sb("curc", [128, 2, CPC], BF16)
        for tb in range(2):
            kb.op("act", lambda e, tb=tb: e.copy(curb[:, tb, :], ut[:, tb, 0, :]), reads=[utT], writes=[curb])
        cur32 = [ut[:, tb, 0, :] for tb in range(2)]
        cur32_t = utT
        zc32 = kb.sb("zc32", [128, 2, CPC], F32)
        Ycb = kb.sb("Ycb", [128, 2, 4, CPC], BF16)
        for o in range(2):
            Hc = HcT[o]
            for fb in range(4):
                pre, pim = kb.ps(), kb.ps()
                for ri, (Dm, pp) in enumerate(((Dcb, pre), (Dsnb, pim))):
                    for kbk in range(2):
                        kb.op("pe", lambda e, pp=pp, Dm=Dm, kbk=kbk, fb=fb: e.matmul(pp[:, 0:CPC], Dm[:, kbk, fb * 128:(fb + 1) * 128], curb[:, kbk, :],
                                                                                 start=(kbk == 0), stop=(kbk == 1)),
                              reads=[Dm, curb], writes=[pp], acc=(kbk != 0))
                hre = Hc[:, fb * 256:(fb + 1) * 256]
                him = Hc[:, 1024 + fb * 256:1024 + (fb + 1) * 256]
                t1, t2 = tmp("hmc", [128, CPC], F32, n=4), tmp("hmc", [128, CPC], F32, n=4)
                kb.op("dve", lambda e, t1=t1, pre=pre, hre=hre: e.tensor_tensor(t1[:], pre[:, 0:CPC], hre, ALU.mult), reads=[pre, Hc], writes=[t1])
                kb.op("dve", lambda e, t2=t2, pim=pim, him=him: e.tensor_tensor(t2[:], pim[:, 0:CPC], him, ALU.mult), reads=[pim, Hc], writes=[t2])
                kb.op("pool", lambda e, t1=t1, t2=t2, fb=fb: e.tensor_tensor(Ycb[:, 0, fb, :], t1[:], t2[:], ALU.subtract), reads=[t1, t2], writes=[Ycb])
                t3, t4 = tmp("hmc", [128, CPC], F32, n=4), tmp("hmc", [128, CPC], F32, n=4)
                kb.op("dve", lambda e, t3=t3, pre=pre, him=him: e.tensor_tensor(t3[:], pre[:, 0:CPC], him, ALU.mult), reads=[pre, Hc], writes=[t3])
                kb.op("dve", lambda e, t4=t4, pim=pim, hre=hre: e.tensor_tensor(t4[:], pim[:, 0:CPC], hre, ALU.mult), reads=[pim, Hc], writes=[t4])
                kb.op("pool", lambda e, t3=t3, t4=t4, fb=fb: e.tensor_tensor(Ycb[:, 1, fb, :], t3[:], t4[:], ALU.add), reads=[t3, t4], writes=[Ycb])
            for tb in range(2):
                py = kb.ps()
                n_ = 0
                for ri, Dm in enumerate((Dcb, Dsnb)):
                    for fb in range(4):
                        kb.op("pe", lambda e, py=py, Dm=Dm, fb=fb, tb=tb, ri=ri, n_=n_: e.matmul(py[:, 0:CPC], Dm[:, fb, tb * 128:(tb + 1) * 128], Ycb[:, ri, fb, :],
                                                                                        start=(n_ == 0), stop=(n_ == 7)),
                              reads=[Dm, Ycb], writes=[py], acc=(n_ != 0))
                        n_ += 1
                t = tmp("epc", [128, CPC], F32, n=2)
                c32 = cur32[tb]
                kb.op("pool", lambda e, t=t, c32=c32, o=o: e.tensor_tensor(t[:], c32, skipc[:, o, :], ALU.mult), reads=[cur32_t, skipc], writes=[t])
                kb.op("dve", lambda e, t=t, py=py: e.scalar_tensor_tensor(t[:], py[:, 0:CPC], 1.0 / (2 * LC), t[:], ALU.mult, ALU.add), reads=[py, t], writes=[t])
                kb.op("dve", lambda e, t=t, tb=tb, o=o: e.tensor_tensor(zc32[:, tb, :], t[:], ut[:, tb, 1 + o, :], ALU.mult), reads=[t, utT], writes=[zc32])
            if o == 0:
                for tb in range(2):
                    kb.op("act", lambda e, tb=tb: e.copy(curb[:, tb, :], zc32[:, tb, :]), reads=[zc32], writes=[curb])
                z1c = kb.sb("z1c", [128, 2, CPC], F32)
                kb.op("dve", lambda e: e.tensor_copy(z1c[:], zc32[:]), reads=[zc32], writes=[z1c])
                cur32 = [z1c[:, tb, :] for tb in range(2)]
                cur32_t = z1c
        zcb = kb.sb("zcb", [128, 2, CPC], BF16)
        kb.op("act", lambda e: e.copy(zcb[:], zc32[:]), reads=[zc32], writes=[zcb])
        kb.store(zc_out, zc_out.h.rearrange("tb p c -> p tb c"), zcb, zcb[:], q="sp")

    W = GC * 128
    kb._pool = [0, 1, 2, 3]
    kb.p.barrier()
    carve_src = [tmps[nm][0][0] for nm in ("arg", "h1", "sn_s", "sn_c", "sn_t")] + list(tmps["kch"][0])
    carve_pos = [0, 0]

    def carve(nelem32):
        ti, off = carve_pos
        if off + nelem32 > CH:
            ti, off = ti + 1, 0
        v = carve_src[ti].h[:, off:off + nelem32]
        carve_pos[0], carve_pos[1] = ti, off + nelem32
        return v

    def ctmp(name, shape, dt, n):
        if name not in tmps:
            lst = []
            nel = int(np.prod(shape[1:]))
            n32 = nel if dt == F32 else nel // 2
            for i in range(n):
                v = carve(n32)
                if dt != F32:
                    v = v.bitcast(BF16)
                if len(shape) == 3:
                    v = v.rearrange("p (a b) -> p a b", a=shape[1])
                elif len(shape) == 4:
                    v = v.rearrange("p (a b c) -> p a b c", a=shape[1], b=shape[2])
                lst.append(T(v, Buf(f"{name}{i}")))
            tmps[name] = [lst, 0]
        lst = tmps[name]
        t = lst[0][lst[1] % n]
        lst[1] += 1
        return t

    dbg_done = []

    def fwd_fft(src, src_ap_fn, K, par):
        A32 = ctmp("A32", [128, GC, 2, 128], F32, 3)
        for c2 in range(GC // 2):
            bank = kb.ps()
            for u in range(2):
                c = 2 * c2 + u
                kb.op("pe", lambda e, bank=bank, c=c, u=u: e.matmul(bank[:, u * 256:(u + 1) * 256], src_ap_fn(c), Ff[0:K, :], start=True, stop=True),
                      reads=[src, Ff], writes=[bank], acc=(u == 1))
            kb.op("act", lambda e, bank=bank, c2=c2, A32=A32: e.copy(A32[:, 2 * c2:2 * c2 + 2, :, :].rearrange("p c r k -> p (c r k)"), bank[:, :]), reads=[bank], writes=[A32])
            yield
        Ab = [ctmp("Ab", [128, GC, 128], BF16, 8) for _ in range(2)]
        if debug and K == 64 and not dbg_done:
            d_ = kb.out("dA32", [128, GC, 2, 128], F32)
            kb.store(d_, d_[:], A32, A32[:])
        yield from twiddle(A32, Ab, conj=False)
        if debug and K == 64 and not dbg_done:
            d_ = kb.out("dAb0", [128, GC, 128], BF16)
            kb.store(d_, d_[:], Ab[0], Ab[0][:])
        banks = []
        for qd in range(GC // 4):
            bre, bim = kb.bank(4 + 2 * par), kb.bank(5 + 2 * par)
            cs = slice(4 * qd, 4 * qd + 4)
            kb.op("pe", lambda e, bre=bre, cs=cs: e.matmul(bre[:, :], Fc[:, :], Ab[0][:, cs, :], start=True, stop=False), reads=[Fc, Ab[0]], writes=[bre])
            kb.op("pe", lambda e, bre=bre, cs=cs: e.matmul(bre[:, :], Fs[:, :], Ab[1][:, cs, :], start=False, stop=True), reads=[Fs, Ab[1]], writes=[bre], acc=True)
            kb.op("pe", lambda e, bim=bim, cs=cs: e.matmul(bim[:, :], Fc[:, :], Ab[1][:, cs, :], start=True, stop=False), reads=[Fc, Ab[1]], writes=[bim])
            kb.op("pe", lambda e, bim=bim, cs=cs: e.matmul(bim[:, :], Fsn[:, :], Ab[0][:, cs, :], start=False, stop=True), reads=[Fsn, Ab[0]], writes=[bim], acc=True)
            banks.append((bre, bim))
            yield
            if debug and K == 64 and not dbg_done and qd == 0:
                xs_ = kb.sb("dbgx", [128, 2, 512], F32)
                kb.op("dve", lambda e, bre=bre: e.tensor_copy(xs_[:, 0, :], bre[:, :]), reads=[bre], writes=[xs_])
                kb.op("dve", lambda e, bim=bim: e.tensor_copy(xs_[:, 1, :], bim[:, :]), reads=[bim], writes=[xs_])
                d_ = kb.out("dX", [128, 2, 512], F32)
                kb.store(d_, d_[:], xs_, xs_[:])
        if debug and K == 64:
            dbg_done.append(1)
        return banks

    def twiddle(A32, Ab, conj):
        are, aim = A32[:, :, 0, :], A32[:, :, 1, :]
        tcb = Tc[:, :].unsqueeze(1).to_broadcast([128, GC, 128])
        tsb = Ts[:, :].unsqueeze(1).to_broadcast([128, GC, 128])
        t1, t2 = ctmp("tw", [128, GC, 128], F32, 8), ctmp("tw", [128, GC, 128], F32, 8)
        kb.op("dve", lambda e: e.tensor_tensor(t1[:], are, tcb, ALU.mult), reads=[A32, Tc], writes=[t1])
        kb.op("dve", lambda e: e.tensor_tensor(t2[:], aim, tsb, ALU.mult), reads=[A32, Ts], writes=[t2])
        yield
        kb.op("dve", lambda e: e.tensor_tensor(Ab[0][:], t1[:], t2[:], ALU.subtract if conj else ALU.add), reads=[t1, t2], writes=[Ab[0]])
        t3, t4 = ctmp("tw", [128, GC, 128], F32, 8), ctmp("tw", [128, GC, 128], F32, 8)
        kb.op("dve", lambda e: e.tensor_tensor(t3[:], aim, tcb, ALU.mult), reads=[A32, Tc], writes=[t3])
        kb.op("dve", lambda e: e.tensor_tensor(t4[:], are, tsb, ALU.mult), reads=[A32, Ts], writes=[t4])
        yield
        kb.op("dve", lambda e: e.tensor_tensor(Ab[1][:], t3[:], t4[:], ALU.add if conj else ALU.subtract), reads=[t3, t4], writes=[Ab[1]])
        yield

    ngroups = CPC // GC if ngroups_override is None else ngroups_override

    def group_gen(g, par):
        c0 = g * GC
        sig = []
        for si in range(3):
            t = tmp(f"sig{si}", [64, GC, 128], BF16, n=2)
            kb.load(t, t[:], sig_in[si][:, c0:c0 + GC, :], q="sp")
            sig.append(t)
        H = []
        for o in range(2):
            k32 = tmp("k32", [128, GC, 128], F32, n=2)
            kb.load(k32, k32[:], kcs[o, c0:c0 + GC, :].rearrange("c (a b) -> a c b", b=128), q="pool", src=kcs)
            kbf = tmp("kbf", [128, GC, 128], BF16, n=4)
            kb.op("act", lambda e, kbf=kbf, k32=k32: e.copy(kbf[:], k32[:]), reads=[k32], writes=[kbf])
            banks = yield from fwd_fft(kbf, lambda c, kbf=kbf: kbf[:, c, :], 128, par)
            Hre, Him = tmp(f"Hre{o}", [128, GC, 128], F32, n=2), tmp(f"Him{o}", [128, GC, 128], F32, n=2)
            for qd, (bre, bim) in enumerate(banks):
                cs = slice(4 * qd, 4 * qd + 4)
                kb.op("act", lambda e, bre=bre, cs=cs, Hre=Hre: e.copy(Hre[:, cs, :].rearrange("p c k -> p (c k)"), bre[:, :]), reads=[bre], writes=[Hre])
                kb.op("act", lambda e, bim=bim, cs=cs, Him=Him: e.copy(Him[:, cs, :].rearrange("p c k -> p (c k)"), bim[:, :]), reads=[bim], writes=[Him])
            H.append((Hre, Him))
            yield
            if debug and g == 0:
                for nm_, t_ in ((f"dHre{o}", Hre), (f"dHim{o}", Him)):
                    d_ = kb.out(nm_, [128, GC, 128], F32)
                    kb.store(d_, d_[:], t_, t_[:])
        cur = sig[0]
        for o in range(2):
            Hre, Him = H[o]
            banks = yield from fwd_fft(cur, lambda c, cur=cur: cur[0:64, c, :], 64, par)
            Yb = [tmp("Yb", [128, GC, 128], BF16, n=4) for _ in range(2)]
            for qd, (bre, bim) in enumerate(banks):
                cs = slice(4 * qd, 4 * qd + 4)
                fl = lambda t, cs=cs: t[:, cs, :].rearrange("p c k -> p (c k)")
                t1, t2 = ctmp("hm", [128, 512], F32, 8), ctmp("hm", [128, 512], F32, 8)
                kb.op("dve", lambda e, t1=t1, bre=bre, fl=fl, Hre=Hre: e.tensor_tensor(t1[:], bre[:, :], fl(Hre), ALU.mult), reads=[bre, Hre], writes=[t1])
                kb.op("dve", lambda e, t2=t2, bim=bim, fl=fl, Him=Him: e.tensor_tensor(t2[:], bim[:, :], fl(Him), ALU.mult), reads=[bim, Him], writes=[t2])
                kb.op("dve", lambda e, t1=t1, t2=t2, fl=fl, Y0=Yb[0]: e.tensor_tensor(fl(Y0), t1[:], t2[:], ALU.subtract), reads=[t1, t2], writes=[Yb[0]])
                t3, t4 = ctmp("hm", [128, 512], F32, 8), ctmp("hm", [128, 512], F32, 8)
                kb.op("dve", lambda e, t3=t3, bre=bre, fl=fl, Him=Him: e.tensor_tensor(t3[:], bre[:, :], fl(Him), ALU.mult), reads=[bre, Him], writes=[t3])
                kb.op("dve", lambda e, t4=t4, bim=bim, fl=fl, Hre=Hre: e.tensor_tensor(t4[:], bim[:, :], fl(Hre), ALU.mult), reads=[bim, Hre], writes=[t4])
                kb.op("dve", lambda e, t3=t3, t4=t4, fl=fl, Y1=Yb[1]: e.tensor_tensor(fl(Y1), t3[:], t4[:], ALU.add), reads=[t3, t4], writes=[Yb[1]])
                yield
            B32 = ctmp("A32", [128, GC, 2, 128], F32, 3)
            for c2 in range(GC // 2):
                bank = kb.ps()
                for u in range(2):
                    c = 2 * c2 + u
                    kb.op("pe", lambda e, bank=bank, c=c, u=u, Y0=Yb[0]: e.matmul(bank[:, u * 256:(u + 1) * 256], Y0[:, c, :], Fi1[:, :], start=True, stop=False),
                          reads=[Yb[0], Fi1], writes=[bank], acc=(u == 1))
                    kb.op("pe", lambda e, bank=bank, c=c, u=u, Y1=Yb[1]: e.matmul(bank[:, u * 256:(u + 1) * 256], Y1[:, c, :], Fi2[:, :], start=False, stop=True),
                          reads=[Yb[1], Fi2], writes=[bank], acc=True)
                kb.op("act", lambda e, bank=bank, c2=c2, B32=B32: e.copy(B32[:, 2 * c2:2 * c2 + 2, :, :].rearrange("p c r k -> p (c r k)"), bank[:, :]), reads=[bank], writes=[B32])
                yield
            Bb = [ctmp("Ab", [128, GC, 128], BF16, 8) for _ in range(2)]
            yield from twiddle(B32, Bb, conj=True)
            xk = sig[1 + o]
            znew = tmp("zb", [64, GC, 128], BF16, n=2)
            for qd in range(GC // 4):
                yb = kb.bank(4 + 2 * par)
                cs = slice(4 * qd, 4 * qd + 4)
                kb.op("pe", lambda e, yb=yb, cs=cs, B0=Bb[0]: e.matmul(yb[0:64, :], Fc[:, 0:64], B0[:, cs, :], start=True, stop=False), reads=[Fc, Bb[0]], writes=[yb])
                kb.op("pe", lambda e, yb=yb, cs=cs, B1=Bb[1]: e.matmul(yb[0:64, :], Fsn[:, 0:64], B1[:, cs, :], start=False, stop=True), reads=[Fsn, Bb[1]], writes=[yb], acc=True)
                t = tmp("ep", [64, 4, 128], F32, n=2)
                sk = skipb[:, o, c0 + 4 * qd:c0 + 4 * qd + 4].unsqueeze(2).to_broadcast([64, 4, 128])
                kb.op("pool", lambda e, t=t, cs=cs, sk=sk, cur=cur: e.tensor_tensor(t[:], cur[0:64, cs, :], sk, ALU.mult), reads=[cur, skipb], writes=[t])
                kb.op("dve", lambda e, t=t, yb=yb: e.scalar_tensor_tensor(t[:].rearrange("p c k -> p (c k)"), yb[0:64, :], 1.0 / NF, t[:].rearrange("p c k -> p (c k)"), ALU.mult, ALU.add),
                      reads=[yb, t], writes=[t])
                kb.op("dve", lambda e, t=t, cs=cs, xk=xk, znew=znew: e.tensor_tensor(znew[:, cs, :], t[:], xk[0:64, cs, :], ALU.mult), reads=[t, xk], writes=[znew])
                yield
            if debug and g == 0:
                d_ = kb.out(f"dz{o}", [64, GC, 128], BF16)
                kb.store(d_, d_[:], znew, znew[:])
                d_ = kb.out(f"dY{o}", [128, GC, 128], BF16)
                kb.store(d_, d_[:], Yb[0], Yb[0][:])
                d_ = kb.out(f"dB{o}", [128, GC, 128], BF16)
                kb.store(d_, d_[:], Bb[0], Bb[0][:])
            cur = znew
        kb.store(zout, zout[:, c0:c0 + GC, :], cur, cur[:], q="pool")

    NCH = 2
    for g2 in range(0, ngroups, NCH):
        gens = [group_gen(g2 + p, p % 2) for p in range(NCH) if g2 + p < ngroups]
        while gens:
            for gen in list(gens):
                try:
                    next(gen)
                except StopIteration:
                    gens.remove(gen)
    kb.finish()
    return kb


def dft_consts():
    a = np.arange(128)
    th = 2 * math.pi * np.outer(a, a) / 128.0
    c, s = np.cos(th), np.sin(th)
    ph = 2 * math.pi * np.outer(a, a) / NF
    d = {"Ff": np.concatenate([c, -s], 1), "Fi1": np.concatenate([c, s], 1), "Fi2": np.concatenate([-s, c], 1),
         "Fc": c, "Fs": s, "Fsn": -s, "Tc": np.cos(ph), "Ts": np.sin(ph)}
    return {k: np.ascontiguousarray(v, dtype=np.float32) for k, v in d.items()}


def hyena_core_inputs(inp, r, u_main, u_ctx):
    sl = slice(CPC * r, CPC * (r + 1))
    d = {}
    d["uc"] = np.ascontiguousarray(np.stack([u_ctx[si * 2048 + CPC * r: si * 2048 + CPC * (r + 1)].reshape(2, 128, 256) for si in range(3)], axis=0))
    cwv = np.concatenate([inp["hyena_conv_w"][0], inp["hyena_conv_b"][0][None, :]], axis=0).reshape(4, 3, 2048)[:, :, sl]
    d["cwc"] = np.ascontiguousarray(cwv.reshape(4, 3, 2, 128).transpose(3, 1, 2, 0), dtype=np.float32)
    ZTc, tc = hyena_consts(256)
    d["ZTc"] = ZTc
    d["tpc"] = np.ascontiguousarray(tc.reshape(4, 128).T, dtype=np.float32)
    d["deltab"] = np.ascontiguousarray(np.broadcast_to(hyena_deltas()[sl][None, :], (128, CPC)), dtype=np.float32)
    d["skipc"] = np.ascontiguousarray(np.broadcast_to(inp["hyena_skip"][0][None, :, sl], (128, 2, CPC)), dtype=np.float32)
    d["ident"] = np.eye(128, dtype=np.float32)
    nn = np.arange(512)
    th = 2 * math.pi * np.outer(nn, nn) / 512.0
    d["Dc"] = np.ascontiguousarray(np.cos(th).reshape(4, 128, 512).transpose(1, 0, 2), dtype=np.float32)
    d["Dsn"] = np.ascontiguousarray((-np.sin(th)).reshape(4, 128, 512).transpose(1, 0, 2), dtype=np.float32)
    for si, nm in enumerate(("v", "x1", "x2")):
        a = u_main[si * 2048 + CPC * r: si * 2048 + CPC * (r + 1)]
        d[nm] = np.ascontiguousarray(a.reshape(CPC, 64, 128).transpose(1, 0, 2))
    ZT, t = hyena_consts(LH)
    d["ZT"] = ZT
    d["tprow"] = np.ascontiguousarray(t[None, :])
    f32 = lambda a: np.ascontiguousarray(a, dtype=np.float32)
    d["fw1"] = f32(inp["hyena_f_w1"][0])
    d["fb1f"] = f32(np.stack([inp["hyena_f_b1"][0], inp["hyena_f_freq"][0, 0]], axis=1))
    w2 = inp["hyena_f_w2"][0]
    d["fw2d"] = f32(np.concatenate([w2, w2], axis=1))
    d["fb2f"] = f32(np.stack([np.tile(inp["hyena_f_b2"][0], 2), np.tile(inp["hyena_f_freq"][0, 1], 2)], axis=1))
    w3 = inp["hyena_f_w3"][0].reshape(64, 2, 2, 2048)
    d["w3s"] = f32(np.concatenate([w3[:, :, 0, sl], w3[:, :, 1, sl]], axis=0))
    d["deltac"] = f32(hyena_deltas()[sl].reshape(2, 128).T)
    d["skipb"] = f32(np.broadcast_to(inp["hyena_skip"][0][None, :, sl], (64, 2, CPC)))
    d.update(dft_consts())
    return d


def run_layer_hyena(inp, mod, x2d, ctx2d, progs):
    li = 2
    w_t = tile_weight(inp["hyena_w_in"][0])
    cwv = np.concatenate([inp["hyena_conv_w"][0], inp["hyena_conv_b"][0][None, :]], axis=0)
    cw = np.ascontiguousarray(cwv.reshape(4, 48, 128).transpose(2, 1, 0), dtype=np.float32)
    maps = []
    for r in range(NCORES):
        hm = np.ones((128, 2), np.float32)
        if r == 0:
            hm[:, 0] = 0
        if r == NCORES - 1:
            hm[:, 1] = 0
        maps.append({"xT": xT_input(x2d, ctx2d, r, halo=1), "modv": modv_input(inp["norm_g"][li], mod[li, 0], mod[li, 1]),
                     "cw": cw, "hm": hm, "w_in": w_t})
    res = run_prog(progs["proj_hyena"], maps)
    full = {n: gather_tokens([np.asarray(res[r][n]) for r in range(NCORES)]) for n in ("uT", "gT")}
    return run_layer_hyena_b(inp, mod, x2d, ctx2d, progs, full)


def run_layer_hyena_b(inp, mod, x2d, ctx2d, progs, full):
    li = 2
    u_main = full["uT"][:, :8192]
    u_ctx = full["uT"][:, 8192:]
    maps = [hyena_core_inputs(inp, r, u_main, u_ctx) for r in range(NCORES)]
    res = run_prog(progs["hyena_core"], maps)
    zmain = np.concatenate([np.asarray(res[r]["z"]).transpose(1, 0, 2).reshape(CPC, 8192) for r in range(NCORES)], axis=0)
    zc = np.concatenate([np.asarray(res[r]["zc"]).reshape(256, CPC).T for r in range(NCORES)], axis=0)
    zfull = np.concatenate([zmain, zc], axis=1)
    oT = [scatter_tokens(zfull, r) for r in range(NCORES)]
    gT = [scatter_tokens(full["gT"], r) for r in range(NCORES)]
    xT = [xT_input(x2d, ctx2d, r) for r in range(NCORES)]
    xo = run_tail(progs["tail"], oT, gT, xT, mod[li, 0], mod[li, 1], inp["hyena_w_out"][0])
    return split_xo(xo)


def kernel(**inputs):
    inp = {k: np.asarray(v) for k, v in inputs.items()}
    mod = run_mod(inp)
    x2d = np.ascontiguousarray(inp["x"][0], dtype=np.float32)
    ctx2d = np.ascontiguousarray(inp["ctx"][0], dtype=np.float32)
    progs = {"proj_swa": build_proj_swa(4), "swa_att": build_swa_att(), "tail": build_tail()}
    x2d, ctx2d = run_layer_swa(inp, mod, x2d, ctx2d, progs)
    progs = {"proj_mla": build_proj_mla(), "mla_att": build_mla_att(), "tail": build_tail()}
    x2d, ctx2d = run_layer_mla(inp, mod, x2d, ctx2d, progs)
    progs = {"proj_hyena": build_proj_hyena(), "hyena_core": build_hyena_core(), "tail": build_tail()}
    x2d, ctx2d = run_layer_hyena(inp, mod, x2d, ctx2d, progs)
    lam_init = 0.8 - 0.6 * math.exp(-0.3 * 3)
    progs = {"proj_diff": build_proj_swa(16), "diff_att": build_diff_att(lam_init), "tail": build_tail()}
    x2d, ctx2d = run_layer_diff(inp, mod, x2d, ctx2d, progs)
    return np.ascontiguousarray(x2d[None], dtype=np.float32)


def tile_weight(w, ncol=256):
    K, N = w.shape
    ng = (N + ncol - 1) // ncol
    wp = np.zeros((K, ng * ncol), np.float32)
    wp[:, :N] = w
    return np.ascontiguousarray(wp.reshape(K // 128, 128, ng, ncol).transpose(2, 1, 0, 3))
```
